# Optimizing a Trainium2 kernel written in Bass

```python
import math, functools
import jax, jax.numpy as jnp
from jax import lax
import numpy as np

D_MODEL = 2048
BATCH = 4
SEQ = 4096
DEPTH = 2

N_META = 16
N_MIXERS = 2
N_SC_LAYERS = (DEPTH + 1) // 2
N_GDN_LAYERS = DEPTH // 2
SC_WIDTH = 3
GDN_NK = 16
GDN_NV = 32
GDN_DK = 128
GDN_DV = 128
GDN_QK_DIM = GDN_NK * GDN_DK
GDN_V_DIM = GDN_NV * GDN_DV
GDN_CONV_DIM = 2 * GDN_QK_DIM + GDN_V_DIM
GDN_PROJ_DIM = GDN_CONV_DIM + GDN_V_DIM + 2 * GDN_NV
GDN_CONV_WIDTH = 4
CHUNK = 64
D_FF = ((math.ceil(8 * D_MODEL / 3) + 255) // 256) * 256
NORM_EPS = 1e-6
L2_EPS = 1e-6

kernel_name = "hybrid_shortconv_gdn_meta"


def rmsnorm(x, w):
    xf = x.astype(jnp.float32)
    y = xf * lax.rsqrt(jnp.mean(xf * xf, axis=-1, keepdims=True) + NORM_EPS)
    return (y * w.astype(jnp.float32)).astype(x.dtype)


def l2norm(x):
    return x * lax.rsqrt(jnp.sum(x * x, axis=-1, keepdims=True) + L2_EPS)


def causal_depthwise_conv(u, w):
    width = w.shape[0]
    length = u.shape[1]
    up = jnp.pad(u, ((0, 0), (width - 1, 0), (0, 0)))
    return sum(w[j] * up[:, j:j + length] for j in range(width))


def short_conv_mixer(h, w_in, conv_w, w_out):
    b_gate, c_gate, u = jnp.split(h @ w_in, 3, axis=-1)
    y = b_gate * causal_depthwise_conv(c_gate * u, conv_w)
    return y @ w_out


def chunked_gated_delta_rule(q, k, v, g, beta):
    bsz, length, nh, dk = q.shape
    dv = v.shape[-1]
    pad_front = (-N_META) % CHUNK
    pad_back = (-(length + pad_front)) % CHUNK
    n = (length + pad_front + pad_back) // CHUNK

    def prep(t):
        t = jnp.pad(t, [(0, 0), (pad_front, pad_back)] + [(0, 0)] * (t.ndim - 2))
        t = t.reshape((bsz, n, CHUNK) + t.shape[2:])
        return jnp.moveaxis(t, 3, 1)

    q, k, v, g, beta = map(prep, (q, k, v, g, beta))
    q = q * (dk ** -0.5)
    g = jnp.cumsum(g, axis=-1)
    causal = jnp.tril(jnp.ones((CHUNK, CHUNK), dtype=bool))
    strict = jnp.tril(jnp.ones((CHUNK, CHUNK), dtype=bool), k=-1)
    decay = jnp.exp(jnp.where(causal, g[..., :, None] - g[..., None, :], -jnp.inf))

    k_beta = k * beta[..., None]
    v_beta = v * beta[..., None]
    a_mat = jnp.where(strict, jnp.einsum('bhncd,bhnsd->bhncs', k_beta, k) * decay, 0.0)
    eye = jnp.eye(CHUNK, dtype=a_mat.dtype)
    t_mat = lax.linalg.triangular_solve(eye + a_mat, jnp.broadcast_to(eye, a_mat.shape),
                                        left_side=True, lower=True)
    u_c = jnp.einsum('bhncs,bhnsv->bhncv', t_mat, v_beta)
    w_c = jnp.einsum('bhncs,bhnsd->bhncd', t_mat, k_beta * jnp.exp(g)[..., None])
    qk = jnp.einsum('bhncd,bhnsd->bhncs', q, k) * decay
    q_g = q * jnp.exp(g)[..., None]
    k_tail = k * jnp.exp(g[..., -1:] - g)[..., None]
    g_last = jnp.exp(g[..., -1])

    def step(state, inp):
        w_i, u_i, qg_i, qk_i, kt_i, gl_i = inp
        v_new = u_i - jnp.einsum('bhcd,bhdv->bhcv', w_i, state)
        o_i = jnp.einsum('bhcd,bhdv->bhcv', qg_i, state) + jnp.einsum('bhcs,bhsv->bhcv', qk_i, v_new)
        state = state * gl_i[..., None, None] + jnp.einsum('bhcd,bhcv->bhdv', kt_i, v_new)
        return state, o_i

    xs = tuple(jnp.moveaxis(t, 2, 0) for t in (w_c, u_c, q_g, qk, k_tail, g_last))
    s0 = jnp.zeros((bsz, nh, dk, dv), jnp.float32)
    _, o = lax.scan(step, s0, xs)
    o = jnp.transpose(o, (1, 0, 3, 2, 4)).reshape(bsz, n * CHUNK, nh, dv)
    return o[:, pad_front:pad_front + length]


def gated_deltanet_mixer(h, w_in, conv_w, a_log, dt_bias, norm_w, w_out):
    bsz, length, _ = h.shape
    proj = h @ w_in
    qkv, z, b, a = jnp.split(proj, [GDN_CONV_DIM, GDN_CONV_DIM + GDN_V_DIM,
                                    GDN_CONV_DIM + GDN_V_DIM + GDN_NV], axis=-1)
    qkv = jax.nn.silu(causal_depthwise_conv(qkv, conv_w))
    q, k, v = jnp.split(qkv.astype(jnp.float32), [GDN_QK_DIM, 2 * GDN_QK_DIM], axis=-1)
    rep = GDN_NV // GDN_NK
    q = jnp.repeat(l2norm(q.reshape(bsz, length, GDN_NK, GDN_DK)), rep, axis=2)
    k = jnp.repeat(l2norm(k.reshape(bsz, length, GDN_NK, GDN_DK)), rep, axis=2)
    v = v.reshape(bsz, length, GDN_NV, GDN_DV)
    beta = jax.nn.sigmoid(b.astype(jnp.float32))
    g = -jnp.exp(a_log.astype(jnp.float32)) * jax.nn.softplus(
        a.astype(jnp.float32) + dt_bias.astype(jnp.float32))
    o = chunked_gated_delta_rule(q, k, v, g, beta)
    zf = z.astype(jnp.float32).reshape(bsz, length, GDN_NV, GDN_DV)
    o = rmsnorm(o, norm_w) * jax.nn.silu(zf)
    return o.reshape(bsz, length, GDN_V_DIM).astype(h.dtype) @ w_out


def swiglu_ffn(h, w_gate, w_up, w_down):
    return (jax.nn.silu(h @ w_gate) * (h @ w_up)) @ w_down


def setup_inputs(seed: int = 0) -> dict:
    key = jax.random.key(seed)
    ks = jax.random.split(key, 20)
    f32 = jnp.float32

    def nrm(k, shape, fan_in):
        return jax.random.normal(k, shape, f32) * (fan_in ** -0.5)

    def gain(k, shape):
        return 1.0 + 0.02 * jax.random.normal(k, shape, f32)

    return {
        "x": jax.random.normal(ks[0], (BATCH, SEQ, D_MODEL), f32),
        "meta_tokens": jax.random.normal(ks[1], (N_META, D_MODEL), f32),
        "mixer_norm": gain(ks[2], (DEPTH, D_MODEL)),
        "ffn_norm": gain(ks[3], (DEPTH, D_MODEL)),
        "sc_w_in": nrm(ks[4], (N_SC_LAYERS, D_MODEL, 3 * D_MODEL), D_MODEL),
        "sc_conv_w": nrm(ks[5], (N_SC_LAYERS, SC_WIDTH, D_MODEL), SC_WIDTH),
        "sc_w_out": nrm(ks[6], (N_SC_LAYERS, D_MODEL, D_MODEL), D_MODEL),
        "gdn_w_in": nrm(ks[7], (N_GDN_LAYERS, D_MODEL, GDN_PROJ_DIM), D_MODEL),
        "gdn_conv_w": nrm(ks[8], (N_GDN_LAYERS, GDN_CONV_WIDTH, GDN_CONV_DIM), GDN_CONV_WIDTH),
        "gdn_a_log": jnp.log(jax.random.uniform(ks[9], (N_GDN_LAYERS, GDN_NV), f32, 1.0, 16.0)),
        "gdn_dt_bias": 1.0 + 0.1 * jax.random.normal(ks[10], (N_GDN_LAYERS, GDN_NV), f32),
        "gdn_norm_w": gain(ks[11], (N_GDN_LAYERS, GDN_DV)),
        "gdn_w_out": nrm(ks[12], (N_GDN_LAYERS, GDN_V_DIM, D_MODEL), GDN_V_DIM),
        "ffn_w_gate": nrm(ks[13], (DEPTH, D_MODEL, D_FF), D_MODEL),
        "ffn_w_up": nrm(ks[14], (DEPTH, D_MODEL, D_FF), D_MODEL),
        "ffn_w_down": nrm(ks[15], (DEPTH, D_FF, D_MODEL), D_FF),
        "final_norm": gain(ks[16], (D_MODEL,)),
    }


def reference(x, meta_tokens, mixer_norm, ffn_norm, sc_w_in, sc_conv_w, sc_w_out,
              gdn_w_in, gdn_conv_w, gdn_a_log, gdn_dt_bias, gdn_norm_w, gdn_w_out,
              ffn_w_gate, ffn_w_up, ffn_w_down, final_norm):
    bsz = x.shape[0]
    meta = jnp.broadcast_to(meta_tokens.astype(x.dtype)[None], (bsz, N_META, D_MODEL))
    h = jnp.concatenate([meta, x], axis=1)
    for i in range(DEPTH):
        j = i // N_MIXERS
        hn = rmsnorm(h, mixer_norm[i])
        if i % N_MIXERS == 0:
            h = h + short_conv_mixer(hn, sc_w_in[j], sc_conv_w[j], sc_w_out[j])
        else:
            h = h + gated_deltanet_mixer(hn, gdn_w_in[j], gdn_conv_w[j], gdn_a_log[j],
                                         gdn_dt_bias[j], gdn_norm_w[j], gdn_w_out[j])
        h = h + swiglu_ffn(rmsnorm(h, ffn_norm[i]), ffn_w_gate[i], ffn_w_up[i], ffn_w_down[i])
    return rmsnorm(h, final_norm)[:, N_META:]
```

```python
import numpy as np
import ml_dtypes
import concourse.bass as bass
import concourse.mybir as mybir
from concourse.bass_utils import run_bass_kernel_spmd

F32 = mybir.dt.float32
BF16 = mybir.dt.bfloat16
AF = mybir.ActivationFunctionType
ALU = mybir.AluOpType
AX = mybir.AxisListType

D = 2048
NF = 16
DFF = 5632
NCF = 44
SEQ = 4096
BATCH = 4
N_META = 16
EPS = 1e-6
TA = 2064
TC = 2048
TG = 4160
PAIRS = [[0, 1], [2, 3], [4, 5], [6, 7]]
TA_TILE = 516
XWS = (512, 64)

ENGS = ("pe", "act", "dve", "pool", "sp")


class Buf:
    __slots__ = ("name", "last_w", "readers", "excl")

    def __init__(self, name="", excl=False):
        self.name = name
        self.last_w = None
        self.readers = []
        self.excl = excl


class Ins:
    __slots__ = ("eng", "emit", "deps", "inc", "ordinal", "lane", "lane_val", "is_dma", "cc_inc", "phase")

    def __init__(self, eng, emit):
        self.eng = eng
        self.emit = emit
        self.deps = []
        self.inc = False
        self.ordinal = None
        self.lane = None
        self.lane_val = None
        self.is_dma = False
        self.cc_inc = 16
        self.phase = 0


class Lane:
    def __init__(self, name):
        self.name = name
        self.sem = None
        self.count = 0
        self.last = None


class Prog:
    def __init__(self, nc):
        self.nc = nc
        self.streams = {e: [] for e in ENGS}
        self.lanes = []
        self._ctx = []
        self.out_lanes = []
        self.phase = 0
        self.sems = None
        self.ord = {e: 0 for e in ENGS}
        self._semctx = []

    def sb(self, name, shape, dt):
        g = self.nc.sbuf_tensor("%s_p%d" % (name, self.phase), list(shape), dt)
        t = g.__enter__()
        self._ctx.append(g)
        return t

    def ps(self, name, shape, dt=F32):
        g = self.nc.psum_tensor("%s_p%d" % (name, self.phase), list(shape), dt)
        t = g.__enter__()
        self._ctx.append(g)
        return t

    def lane(self, name, out=False):
        l = Lane("%s_p%d" % (name, self.phase))
        self.lanes.append(l)
        if out:
            self.out_lanes.append(l)
        return l

    def _track(self, ins, reads, writes):
        deps = ins.deps
        for b in reads:
            if b.last_w is not None:
                deps.append(b.last_w)
            if b.excl:
                for r in b.readers:
                    if r.eng != ins.eng:
                        deps.append(r)
        for b in writes:
            if b.last_w is not None:
                deps.append(b.last_w)
            deps.extend(b.readers)
        for b in reads:
            if not ins.is_dma:
                b.readers = [r for r in b.readers if r.is_dma or r.eng != ins.eng]
            b.readers.append(ins)
        for b in writes:
            b.last_w = ins
            b.readers = []

    def op(self, eng, emit, reads=(), writes=()):
        ins = Ins(eng, emit)
        ins.phase = self.phase
        self._track(ins, reads, writes)
        self.streams[eng].append(ins)
        return ins

    def dma(self, eng, lane, out, in_, reads=(), writes=()):
        return self.dma_group(eng, lane, [(out, in_)], reads, writes)

    def dma_group(self, eng, lane, pairs, reads=(), writes=()):
        pairs = list(pairs)
        ins = Ins(eng, lambda e: [e.dma_start(out=o, in_=i) for (o, i) in pairs])
        ins.is_dma = True
        ins.phase = self.phase
        ins.lane = lane
        if lane.last is not None:
            ins.deps.append(lane.last)
        lane.count += 16 * len(pairs)
        ins.lane_val = lane.count
        lane.last = ins
        self._track(ins, reads, writes)
        self.streams[eng].append(ins)
        return ins

    def cc_allgather(self, in_ap, in_buf, out_ap, out_buf, groups):
        lane = self.lane("cc%d" % len(self.lanes))
        ins = Ins("pool", lambda e: [e.collective_compute("AllGather", ALU.bypass, replica_groups=groups,
                                                          ins=[in_ap], outs=[out_ap])])
        ins.is_dma = True
        ins.phase = self.phase
        ins.lane = lane
        ins.cc_inc = 1
        lane.count += 1
        ins.lane_val = lane.count
        lane.last = ins
        self._track(ins, [in_buf], [out_buf])
        self.streams["pool"].append(ins)
        return ins

    def emit(self, final=True):
        nc = self.nc
        cur = self.phase
        for e in ENGS:
            for ins in self.streams[e]:
                for d in ins.deps:
                    if not d.is_dma and d.phase == cur:
                        d.inc = True
        for e in ENGS:
            c = self.ord[e]
            for ins in self.streams[e]:
                if ins.inc and not ins.is_dma:
                    c += 1
                    ins.ordinal = c
            self.ord[e] = c
        if self.sems is None:
            self.sems = {}
            for e in ENGS:
                g = nc.semaphore("sem_" + e)
                self.sems[e] = g.__enter__()
                self._semctx.append(g)
        sems = self.sems
        for l in self.lanes:
            if l.sem is None:
                g = nc.semaphore("lane_" + l.name)
                l.sem = g.__enter__()
                self._semctx.append(g)
        out_lanes = self.out_lanes if final else []

        def run(ename, eng):
            seen = {}
            for ins in self.streams[ename]:
                need = {}
                for d in ins.deps:
                    if d.is_dma:
                        key = ("l", id(d.lane))
                        sem = d.lane.sem
                        val = d.lane_val
                    else:
                        if d.phase != cur:
                            continue
                        if d.eng == ename and ename == "pe":
                            continue
                        key = ("e", d.eng)
                        sem = sems[d.eng]
                        val = d.ordinal
                    if seen.get(key, 0) >= val:
                        continue
                    if key not in need or need[key][1] < val:
                        need[key] = (sem, val)
                for key, (sem, val) in need.items():
                    eng.wait_ge(sem, val)
                    seen[key] = val
                bi = ins.emit(eng)
                if ins.is_dma:
                    for b1 in bi:
                        b1.then_inc(ins.lane.sem, getattr(ins, "cc_inc", 16))
                elif ins.inc:
                    bi.then_inc(sems[ename], 1)
            if ename == "sp":
                for l in out_lanes:
                    if l.count:
                        eng.wait_ge(l.sem, l.count)

        with nc.Block() as block:
            @block.tensor
            def _(t):
                run("pe", t)

            @block.scalar
            def _(t):
                run("act", t)

            @block.vector
            def _(t):
                run("dve", t)

            @block.gpsimd
            def _(t):
                run("pool", t)

            @block.sync
            def _(t):
                run("sp", t)
        self.streams = {e: [] for e in ENGS}
        self.phase += 1

    def close(self):
        for g in reversed(self._ctx):
            g.__exit__(None, None, None)
        self._ctx = []

    def finish(self):
        self.close()
        for g in reversed(self._semctx):
            g.__exit__(None, None, None)
        self._semctx = []


class Ring:
    def __init__(self, P, name, n, shape, dt, lanes=False, psum=False):
        self.slots = []
        for i in range(n):
            t = (P.ps if psum else P.sb)("%s%d" % (name, i), shape, dt)
            self.slots.append((t, Buf("%s%d" % (name, i), excl=psum), P.lane("%s%d" % (name, i)) if lanes else None))
        self.i = 0

    def next(self):
        s = self.slots[self.i % len(self.slots)]
        self.i += 1
        return s


def E(method, **kw):
    return lambda e: getattr(e, method)(**kw)


def mm_group(P, out_ap, out_buf, pairs, reads):
    n = len(pairs)
    for i, (l, r) in enumerate(pairs):
        P.op("pe", E("matmul", out=out_ap, lhsT=l, rhs=r, start=(i == 0), stop=(i == n - 1)),
             reads=reads, writes=[out_buf])


def make_segs(T, maxn=512):
    nseg = (T + maxn - 1) // maxn
    base = (T + nseg - 1) // nseg
    segs = []
    a = 0
    while a < T:
        b = min(T, a + base)
        segs.append((a, b))
        a = b
    return segs


class Common:
    def __init__(self, P, nc, ident_dram, wslots=3, wsize=6144):
        self.P = P
        self.psum = Ring(P, "bank", 8, [128, 512], F32, psum=True)
        self.wring = Ring(P, "wr", wslots, [128, wsize], BF16, lanes=True)
        self.ident = P.sb("ident_sb", [128, 128], F32)
        self.ident_b = Buf("ident")
        self.ones = P.sb("ones_sb", [128, 128], F32)
        self.ones_b = Buf("ones")
        self.cl = P.lane("const")
        P.dma("sp", self.cl, self.ident[:], ident_dram, writes=[self.ident_b])
        P.op("dve", E("memset", ap=self.ones[:], constant=1.0), writes=[self.ones_b])
        self.epsc = P.sb("epsc", [128, 2], F32)
        self.eps_b = Buf("epsc")
        self.eps_col = {EPS: 0}
        P.op("dve", E("memset", ap=self.epsc[:], constant=EPS), writes=[self.eps_b])
        self.sq = Ring(P, "sq", 3, [128, 512], F32)
        self.tmp = Ring(P, "tmp", 4, [128, 512], F32)


def rsqrt_op(P, K, out, out_b, in_, in_b, scale, eps):
    P.op("act", E("activation", out=out, in_=in_, func=AF.Sqrt, bias=K.epsc[:in_.shape[0], K.eps_col[eps]:K.eps_col[eps] + 1],
                  scale=scale), reads=[in_b, K.eps_b], writes=[out_b])
    P.op("dve", E("reciprocal", out=out, in_=out), reads=[out_b], writes=[out_b])


def rmsnorm_fm(P, K, src, src_b, dst, dst_b, segs, vec, vec_b, wcol, rstd, rstd_b):
    for (a, b) in segs:
        n = b - a
        pb, pbb, _ = K.psum.next()
        for f in range(NF):
            sq, sqb, _ = K.sq.next()
            P.op("act", E("activation", out=sq[:, :n], in_=src[:, f, a:b], func=AF.Square),
                 reads=[src_b], writes=[sqb])
            P.op("pe", E("matmul", out=pb[:, :n], lhsT=K.ones[:], rhs=sq[:, :n], start=(f == 0), stop=(f == NF - 1)),
                 reads=[sqb, K.ones_b], writes=[pbb])
        rsqrt_op(P, K, rstd[:, a:b], rstd_b, pb[:, :n], pbb, 1.0 / D, EPS)
        for f in range(NF):
            P.op("dve", E("scalar_tensor_tensor", out=dst[:, f, a:b], in0=src[:, f, a:b],
                          scalar=vec[:, wcol + f:wcol + f + 1], in1=rstd[:, a:b], op0=ALU.mult, op1=ALU.mult),
                 reads=[src_b, rstd_b, vec_b], writes=[dst_b])


def ffn_fm(P, K, hT, hT_b, hn, hn_b, act, act_b, segs, wg, wu, wd):
    wgv = wg.rearrange("(k p) m -> p k m", p=128)
    wuv = wu.rearrange("(k p) m -> p k m", p=128)
    wdv = wd.rearrange("(k p) m -> p k m", p=128)
    for c in range(NCF):
        wt, wb, wl = K.wring.next()
        wv = wt[:, 0:4096].rearrange("p (k j m) -> p k j m", k=NF, j=2)
        P.dma_group("pool", wl, [(wv[:, :, 0, :], wgv[:, :, c * 128:(c + 1) * 128]),
                                 (wv[:, :, 1, :], wuv[:, :, c * 128:(c + 1) * 128])], writes=[wb])
        for (a, b) in segs:
            n = b - a
            pg, pgb, _ = K.psum.next()
            pu, pub, _ = K.psum.next()
            mm_group(P, pg[:, :n], pgb, [(wv[:, k, 0, :], hn[:, k, a:b]) for k in range(NF)], [wb, hn_b])
            mm_group(P, pu[:, :n], pub, [(wv[:, k, 1, :], hn[:, k, a:b]) for k in range(NF)], [wb, hn_b])
            st, stb, _ = K.tmp.next()
            P.op("act", E("activation", out=st[:, :n], in_=pg[:, :n], func=AF.Silu), reads=[pgb], writes=[stb])
            P.op("dve", E("tensor_tensor", out=act[:, c, a:b], in0=pu[:, :n], in1=st[:, :n], op=ALU.mult),
                 reads=[pub, stb], writes=[act_b])
    for m in range(NF):
        wt, wb, wl = K.wring.next()
        wv = wt[:, 0:NCF * 128].rearrange("p (k m) -> p k m", k=NCF)
        P.dma("pool", wl, wv, wdv[:, :, m * 128:(m + 1) * 128], writes=[wb])
        for (a, b) in segs:
            n = b - a
            pb, pbb, _ = K.psum.next()
            mm_group(P, pb[:, :n], pbb, [(wv[:, k, :], act[:, k, a:b]) for k in range(NCF)], [wb, act_b])
            P.op("dve", E("tensor_tensor", out=hT[:, m, a:b], in0=pb[:, :n], in1=hT[:, m, a:b], op=ALU.add),
                 reads=[pbb, hT_b], writes=[hT_b])


def phase_A(P, K, io, tile_T=516, xch=None):
    xa, w_in, w_out = io["xa"], io["sc_w_in"], io["sc_w_out"]
    h1, hn1 = io["h1"], io.get("hn1")
    T = tile_T
    ntile = TA // T
    assert ntile * T == TA
    segs = make_segs(T)
    vec = P.sb("vecA_sb", [128, 96], F32)
    vec_b = Buf("vecA")
    P.dma("sp", K.cl, vec[:], io["vecA"], writes=[vec_b])
    hT = P.sb("hT", [128, NF, T], F32)
    hT_b = Buf("hT")
    hn = P.sb("hn", [128, NF, T], BF16)
    hn_b = Buf("hn")
    act = P.sb("act", [128, NCF, T], BF16)
    act_b = Buf("act")
    y = act[:, 0:NF, :]
    rstd = P.sb("rstd", [128, T], F32)
    rstd_b = Buf("rstd")
    xs = Ring(P, "xs", 2, [128, D], F32, lanes=True)
    cur = Ring(P, "cu", 2, [128, T + 2], F32)
    bsb = Ring(P, "bsb", 2, [128, T], F32)
    acc = Ring(P, "acc", 2, [128, T], F32)
    halo = P.sb("halo", [128, NF, 2], F32)
    halo_b = [Buf("halo%d" % f) for f in range(NF)]
    P.op("dve", E("memset", ap=halo[:], constant=0.0), writes=halo_b)
    st_lane = P.lane("stA", out=(xch is None))
    w_in_v = w_in.rearrange("(k p) m -> p k m", p=128)
    w_out_v = w_out.rearrange("(k p) m -> p k m", p=128)
    h1_v = h1.rearrange("f p t -> p f t")
    hn1_v = hn1.rearrange("f p t -> p f t") if hn1 is not None else None

    for ti in range(ntile):
        t0 = ti * T
        for g0 in range(0, T, 128):
            gs = min(128, T - g0)
            xt, xb, xl = xs.next()
            P.dma("sp", xl, xt[:gs, :], xa[t0 + g0:t0 + g0 + gs, :], writes=[xb])
            for fq in range(4):
                pb, pbb, _ = K.psum.next()
                for j in range(4):
                    f = fq * 4 + j
                    P.op("pe", E("transpose", out=pb[:, j * 128:j * 128 + gs], in_=xt[:gs, f * 128:(f + 1) * 128],
                                 identity=K.ident[:gs, :gs]), reads=[xb, K.ident_b], writes=[pbb])
                P.op("act", E("activation", out=hT[:, fq * 4:(fq + 1) * 4, g0:g0 + gs],
                              in_=pb[:, :].rearrange("p (j t) -> p j t", t=128)[:, :, :gs], func=AF.Copy),
                     reads=[pbb], writes=[hT_b])
        rmsnorm_fm(P, K, hT, hT_b, hn, hn_b, segs, vec, vec_b, 0, rstd, rstd_b)
        for f in range(NF):
            wt, wb, wl = K.wring.next()
            wv = wt[:, 0:6144].rearrange("p (k j m) -> p k j m", k=NF, j=3)
            P.dma_group("pool", wl, [(wv[:, :, j, :], w_in_v[:, :, j * D + f * 128:j * D + (f + 1) * 128])
                                     for j in range(3)], writes=[wb])
            cu, cub, _ = cur.next()
            bs, bsbb, _ = bsb.next()
            ac, acb, _ = acc.next()
            P.op("act", E("activation", out=cu[:, 0:2], in_=halo[:, f, :], func=AF.Copy),
                 reads=[halo_b[f]], writes=[cub])
            for (a, b) in segs:
                n = b - a
                pbk = [K.psum.next() for _ in range(3)]
                for j in range(3):
                    mm_group(P, pbk[j][0][:, :n], pbk[j][1],
                             [(wv[:, k, j, :], hn[:, k, a:b]) for k in range(NF)], [wb, hn_b])
                ut, utb, _ = K.tmp.next()
                P.op("act", E("activation", out=ut[:, :n], in_=pbk[2][0][:, :n], func=AF.Copy),
                     reads=[pbk[2][1]], writes=[utb])
                P.op("dve", E("tensor_tensor", out=cu[:, 2 + a:2 + b], in0=pbk[1][0][:, :n], in1=ut[:, :n], op=ALU.mult),
                     reads=[pbk[1][1], utb], writes=[cub])
                P.op("act", E("activation", out=bs[:, a:b], in_=pbk[0][0][:, :n], func=AF.Copy),
                     reads=[pbk[0][1]], writes=[bsbb])
            c0, c1, c2 = 32 + f, 48 + f, 64 + f
            P.op("dve", E("tensor_scalar", out=ac[:, :], in0=cu[:, 2:2 + T], scalar1=vec[:, c2:c2 + 1], scalar2=None,
                          op0=ALU.mult), reads=[cub, vec_b], writes=[acb])
            P.op("dve", E("scalar_tensor_tensor", out=ac[:, :], in0=cu[:, 1:1 + T], scalar=vec[:, c1:c1 + 1],
                          in1=ac[:, :], op0=ALU.mult, op1=ALU.add), reads=[cub, vec_b, acb], writes=[acb])
            P.op("dve", E("scalar_tensor_tensor", out=ac[:, :], in0=cu[:, 0:T], scalar=vec[:, c0:c0 + 1],
                          in1=ac[:, :], op0=ALU.mult, op1=ALU.add), reads=[cub, vec_b, acb], writes=[acb])
            P.op("dve", E("tensor_tensor", out=y[:, f, :], in0=ac[:, :], in1=bs[:, :], op=ALU.mult),
                 reads=[acb, bsbb], writes=[act_b])
            P.op("act", E("activation", out=halo[:, f, :], in_=cu[:, T:T + 2], func=AF.Copy),
                 reads=[cub], writes=[halo_b[f]])
        for mp in range(NF // 2):
            wt, wb, wl = K.wring.next()
            wv = wt[:, 0:4096].rearrange("p (k m) -> p k m", k=NF)
            P.dma("pool", wl, wv, w_out_v[:, :, mp * 256:(mp + 1) * 256], writes=[wb])
            for mi in range(2):
                m = mp * 2 + mi
                for (a, b) in segs:
                    n = b - a
                    pb, pbb, _ = K.psum.next()
                    mm_group(P, pb[:, :n], pbb, [(wv[:, k, mi * 128:(mi + 1) * 128], y[:, k, a:b]) for k in range(NF)],
                             [wb, act_b])
                    P.op("dve", E("tensor_tensor", out=hT[:, m, a:b], in0=pb[:, :n], in1=hT[:, m, a:b], op=ALU.add),
                         reads=[pbb, hT_b], writes=[hT_b])
        rmsnorm_fm(P, K, hT, hT_b, hn, hn_b, segs, vec, vec_b, 16, rstd, rstd_b)
        ffn_fm(P, K, hT, hT_b, hn, hn_b, act, act_b, segs, io["wg"], io["wu"], io["wd"])
        if xch is None:
            P.dma("sp", st_lane, h1_v[:, :, t0:t0 + T], hT[:, :, :], reads=[hT_b])
            rmsnorm_fm(P, K, hT, hT_b, hn, hn_b, segs, vec, vec_b, 80, rstd, rstd_b)
            P.dma("sp", st_lane, hn1_v[:, :, t0:t0 + T], hn[:, :, :], reads=[hn_b])
        else:
            P.dma("sp", st_lane, h1_v[:, :, t0:t0 + T], hT[:, :, :], reads=[hT_b], writes=[xch["h1_b"]])
            rmsnorm_fm(P, K, hT, hT_b, hn, hn_b, segs, vec, vec_b, 80, rstd, rstd_b)
            for part, (c0, c1) in enumerate(((0, 512), (512, T))):
                xi = 2 * ti + part
                P.dma("sp", st_lane, xch["hn1_loc"][xi].rearrange("(f p) t -> p f t", p=128)[:, :, 0:c1 - c0],
                      hn[:, :, c0:c1], reads=[hn_b], writes=[xch["hn1_loc_b"][xi]])
                P.cc_allgather(xch["hn1_loc"][xi], xch["hn1_loc_b"][xi], xch["hn1_all"][xi], xch["hn1_all_b"][xi], PAIRS)
    return st_lane


def phase_C(P, K, io, tile_T=512, xch=None):
    h1c, og, gw_out, out = io["h1c"], io.get("ogc"), io["gdn_w_out"], io["out"]
    T = tile_T
    ntile = TC // T
    segs = make_segs(T)
    vec = P.sb("vecC_sb", [128, 32], F32)
    vec_b = Buf("vecC")
    P.dma("sp", K.cl, vec[:], io["vecC"], writes=[vec_b])
    hT = P.sb("hTc", [128, NF, T], F32)
    hT_b = Buf("hTc")
    hn = P.sb("hnc", [128, NF, T], BF16)
    hn_b = Buf("hnc")
    act = P.sb("actc", [128, NCF, T], BF16)
    act_b = Buf("actc")
    ogt = act[:, 0:32, :]
    rstd = P.sb("rstdc", [128, T], F32)
    rstd_b = Buf("rstdc")
    ot = Ring(P, "ot", 2, [128, D], F32, lanes=True)
    ld = P.lane("ldC")
    h1_v = h1c.rearrange("f p t -> p f t")
    og_v = og.rearrange("f p t -> p f t") if og is not None else None
    wo_v = gw_out.rearrange("(k p) m -> p k m", p=128)
    hoff = 0 if xch is None else 16
    if xch is not None:
        mk = P.sb("mk_sb", [128, 2], F32)
        mk_b = Buf("mk")
        P.dma("sp", K.cl, mk[:], io["mk"], writes=[mk_b])
    for l in ot.slots:
        P.out_lanes.append(l[2])
    for ti in range(ntile):
        t0 = ti * T
        if xch is None:
            P.dma("sp", ld, hT[:, :, :], h1_v[:, :, t0:t0 + T], writes=[hT_b])
            P.dma_group("sp", ld, [(ogt[:, 0:16, :], og_v[:, 0:16, t0:t0 + T]),
                                   (ogt[:, 16:32, :], og_v[:, 16:32, t0:t0 + T])], writes=[act_b])
        else:
            P.dma("sp", ld, hT[:, :, :], h1_v[:, :, hoff + t0:hoff + t0 + T], reads=[xch["h1_b"]], writes=[hT_b])
            for hf in range(2):
                for cand, dst, dst_b in ((0, ogt, act_b), (1, hn, hn_b)):
                    pieces = []
                    s0 = 64 + cand * 2048 + t0
                    end = s0 + T
                    while s0 < end:
                        bt, cq = s0 // TB, s0 % TB
                        n = min(end - s0, TB - cq)
                        src = xch["og_all"][bt].rearrange("(rf p) t -> p rf t", p=128)[:, hf * 16:(hf + 1) * 16, cq:cq + n]
                        d0 = s0 - (64 + cand * 2048 + t0)
                        dd = dst[:, hf * 16:(hf + 1) * 16, d0:d0 + n] if cand == 0 else dst[:, :, d0:d0 + n]
                        pieces.append((dd, src, xch["og_all_b"][bt]))
                        s0 += n
                    P.dma_group("sp", ld, [(d_, s_) for d_, s_, _ in pieces], reads=[b_ for _, _, b_ in pieces],
                                writes=[dst_b])
                oh = ogt[:, hf * 16:(hf + 1) * 16, :]
                P.op("dve", E("tensor_scalar", out=oh, in0=oh, scalar1=mk[:, 0:1], scalar2=None, op0=ALU.mult),
                     reads=[act_b, mk_b], writes=[act_b])
                P.op("dve", E("scalar_tensor_tensor", out=oh, in0=hn[:, :, :], scalar=mk[:, 1:2], in1=oh,
                              op0=ALU.mult, op1=ALU.add), reads=[hn_b, act_b, mk_b], writes=[act_b])
        for m in range(NF):
            wt, wb, wl = K.wring.next()
            wv = wt[:, 0:4096].rearrange("p (k m) -> p k m", k=32)
            P.dma("pool", wl, wv, wo_v[:, :, m * 128:(m + 1) * 128], writes=[wb])
            for (a, b) in segs:
                n = b - a
                pb, pbb, _ = K.psum.next()
                mm_group(P, pb[:, :n], pbb, [(wv[:, k, :], ogt[:, k, a:b]) for k in range(32)], [wb, act_b])
                P.op("dve", E("tensor_tensor", out=hT[:, m, a:b], in0=pb[:, :n], in1=hT[:, m, a:b], op=ALU.add),
                     reads=[pbb, hT_b], writes=[hT_b])
        rmsnorm_fm(P, K, hT, hT_b, hn, hn_b, segs, vec, vec_b, 0, rstd, rstd_b)
        ffn_fm(P, K, hT, hT_b, hn, hn_b, act, act_b, segs, io["wg"], io["wu"], io["wd"])
        rmsnorm_fm(P, K, hT, hT_b, hT, hT_b, segs, vec, vec_b, 16, rstd, rstd_b)
        for g0 in range(0, T, 128):
            o_t, ob, ol = ot.next()
            for fq in range(4):
                pb, pbb, _ = K.psum.next()
                for j in range(4):
                    f = fq * 4 + j
                    P.op("pe", E("transpose", out=pb[:, j * 128:(j + 1) * 128], in_=hT[:, f, g0:g0 + 128],
                                 identity=K.ident[:, :]), reads=[hT_b, K.ident_b], writes=[pbb])
                if fq % 2:
                    P.op("act", E("activation", out=o_t[:, fq * 512:(fq + 1) * 512], in_=pb[:, :], func=AF.Copy),
                         reads=[pbb], writes=[ob])
                else:
                    P.op("dve", E("tensor_copy", out=o_t[:, fq * 512:(fq + 1) * 512], in_=pb[:, :]),
                         reads=[pbb], writes=[ob])
            P.dma("sp", ol, out[t0 + g0:t0 + g0 + 128, :], o_t[:, :], reads=[ob])


def _fm(v):
    v = np.asarray(v, np.float32)
    return np.ascontiguousarray(v.reshape(-1, 128).T)


def build_A(tile_T=516):
    nc = bass.Bass("TRN2", target_bir_lowering=False)
    io = {}

    def inp(name, shape, dt=F32):
        io[name] = nc.dram_tensor(name, list(shape), dt, kind="ExternalInput").ap()

    inp("xa", [TA, D])
    inp("ident", [128, 128])
    inp("vecA", [128, 96])
    inp("sc_w_in", [D, 3 * D])
    inp("sc_w_out", [D, D])
    inp("wg", [D, DFF])
    inp("wu", [D, DFF])
    inp("wd", [DFF, D])
    io["h1"] = nc.dram_tensor("h1", [NF, 128, TA], F32, kind="ExternalOutput").ap()
    io["hn1"] = nc.dram_tensor("hn1", [NF, 128, TA], BF16, kind="ExternalOutput").ap()
    P = Prog(nc)
    K = Common(P, nc, io["ident"])
    phase_A(P, K, io, tile_T)
    P.emit()
    P.close()
    return nc


def host_inputs_A(inp):
    x = np.asarray(inp["x"], np.float32)
    meta = np.asarray(inp["meta_tokens"], np.float32)
    conv = np.asarray(inp["sc_conv_w"], np.float32)[0]
    vecA = np.concatenate([_fm(inp["mixer_norm"][0]), _fm(inp["ffn_norm"][0]),
                           _fm(conv[0]), _fm(conv[1]), _fm(conv[2]), _fm(inp["mixer_norm"][1])], axis=1)
    shared = {
        "ident": np.eye(128, dtype=np.float32),
        "vecA": np.ascontiguousarray(vecA),
        "sc_w_in": np.ascontiguousarray(np.asarray(inp["sc_w_in"], np.float32)[0]),
        "sc_w_out": np.ascontiguousarray(np.asarray(inp["sc_w_out"], np.float32)[0]),
        "wg": np.ascontiguousarray(np.asarray(inp["ffn_w_gate"], np.float32)[0]),
        "wu": np.ascontiguousarray(np.asarray(inp["ffn_w_up"], np.float32)[0]),
        "wd": np.ascontiguousarray(np.asarray(inp["ffn_w_down"], np.float32)[0]),
    }
    maps = []
    for c in range(8):
        b, r = c // 2, c % 2
        if r == 0:
            xa = np.concatenate([meta, x[b, 0:2048]], axis=0)
        else:
            xa = x[b, 2032:4096]
        m = dict(shared)
        m["xa"] = np.ascontiguousarray(xa)
        maps.append(m)
    return maps


def build_C(tile_T=512):
    nc = bass.Bass("TRN2", target_bir_lowering=False)
    io = {}

    def inp(name, shape, dt=F32):
        io[name] = nc.dram_tensor(name, list(shape), dt, kind="ExternalInput").ap()

    inp("h1c", [NF, 128, TC])
    inp("ogc", [32, 128, TC], BF16)
    inp("ident", [128, 128])
    inp("vecC", [128, 32])
    inp("gdn_w_out", [2 * D, D])
    inp("wg", [D, DFF])
    inp("wu", [D, DFF])
    inp("wd", [DFF, D])
    io["out"] = nc.dram_tensor("out", [TC, D], F32, kind="ExternalOutput").ap()
    P = Prog(nc)
    K = Common(P, nc, io["ident"])
    phase_C(P, K, io, tile_T)
    P.emit()
    P.close()
    return nc


NEG = -30000.0
DBG = ""
DBGSTEP = 99
POOLENG = "dve"
EXPF = AF.Exp
TB = 320
NCH = TB // 64
DK = 128


def phase_B(P, K, io, ntile=None, xch=None):
    hn1g, gw_in, og = io.get("hn1g"), io["gw_in"], io.get("og")
    T = TB
    ntile = ntile or TG // T
    nc = P.nc
    vec = P.sb("vecB_sb", [128, 129], F32)
    vec_b = Buf("vecB")
    P.dma("sp", K.cl, vec[:], io["vecB"], writes=[vec_b])
    row = P.sb("rowB_sb", [128, 32], F32)
    row_b = Buf("rowB")
    P.dma("sp", K.cl, row[:], io["rowB"], writes=[row_b])
    cB = P.sb("cB_sb", [64, 192], F32)
    cB_b = Buf("cB")
    P.dma("sp", K.cl, cB[:], io["cB"], writes=[cB_b])
    Umat = cB[:, 0:64]
    NEGI = cB[:, 64:128]
    NEGS = cB[:, 128:192]
    identb = P.sb("identb", [128, 128], BF16)
    identb_b = Buf("identb")
    P.op("dve", E("tensor_copy", out=identb[:], in_=K.ident[:]), reads=[K.ident_b], writes=[identb_b])
    onec = P.sb("onec", [128, 1], F32)
    onec_b = Buf("onec")
    P.op("dve", E("memset", ap=onec[:], constant=1.0), writes=[onec_b])
    nea = P.sb("nea", [64, 16], F32)
    nea_b = Buf("nea")
    P.op("act", E("activation", out=nea[:], in_=row[0:64, 0:16], func=AF.Exp), reads=[row_b], writes=[nea_b])
    P.op("dve", E("tensor_scalar", out=nea[:], in0=nea[:], scalar1=-1.0, scalar2=None, op0=ALU.mult),
         reads=[nea_b], writes=[nea_b])
    wba = P.sb("wba", [128, NF, 32], BF16)
    wba_b = Buf("wba")
    gw_v = gw_in.rearrange("(k p) m -> p k m", p=128)
    P.dma("pool", P.lane("wba"), wba[:], gw_v[:, :, 6144:6176], writes=[wba_b])
    hnr = Ring(P, "hnB", 2, [128, NF, T], BF16, lanes=True)
    qf = P.sb("qf", [128, 8, T], BF16)
    kf = P.sb("kf", [128, 8, T], BF16)
    vf = P.sb("vf", [128, 16, T], BF16)
    zs = P.sb("zs", [128, 16, T], BF16)
    qf_b, kf_b, vf_b, zs_b = Buf("qf"), Buf("kf"), Buf("vf"), Buf("zs")
    ogr = Ring(P, "ogB", 2, [128, 16, T], BF16, lanes=True)
    if xch is None:
        for sl in ogr.slots:
            P.out_lanes.append(sl[2])
    pcr = Ring(P, "pc", 2, [128, T + 3], F32)
    accr = Ring(P, "accB", 2, [128, T], F32)
    silr = Ring(P, "sil", 2, [128, T], F32)
    rinr = Ring(P, "rin", 2, [128, T], F32)
    halo = P.sb("haloB", [128, 32, 3], F32)
    halo_b = [Buf("haloB%d" % i) for i in range(32)]
    P.op("dve", E("memset", ap=halo[:], constant=0.0), writes=halo_b)
    beta = P.sb("beta", [64, NCH, 16], F32)
    lb = P.sb("lb", [64, NCH, 16], F32)
    gtm = P.sb("gtm", [64, NCH, 16], F32)
    sm_b = Buf("small")
    S = P.sb("S", [128, 16, 128], F32)
    S_b = [Buf("S0"), Buf("S1")]
    Sb = P.sb("Sbf", [128, 16, 128], BF16)
    Sb_b = [Buf("Sb0"), Buf("Sb1")]
    P.op("dve", E("memset", ap=S[:], constant=0.0), writes=S_b)
    P.op("dve", E("memset", ap=Sb[:], constant=0.0), writes=Sb_b)
    def mk(name, shape, dt):
        return [(P.sb("%s%d" % (name, i), shape, dt), Buf("%s%d" % (name, i))) for i in range(2)]
    gcs = mk("gcs", [64, 16], F32)
    gbs = mk("gbs", [64, 16], F32)
    egs = mk("egs", [64, 16], F32)
    begs = mk("begs", [64, 16], F32)
    ekts = mk("ekts", [64, 16], F32)
    gtot = mk("gtot", [128, 16], F32)
    gls = mk("gls", [128, 16], F32)
    Dg1 = mk("Dg1", [64, 8, 64], F32)
    Dg2 = mk("Dg2", [64, 8, 64], F32)
    t1, t2, EI, ES = Dg1, Dg2, Dg1, Dg2
    EG = mk("EG", [128, 8, 64], F32)
    qg = mk("qg", [128, 8, 64], BF16)
    qkT = mk("qkT", [64, 8, 64], BF16)
    Pm = [mk("Pm%d" % j, [64, 8, 64], BF16) for j in range(2)]
    PT = [mk("PT%d" % j, [64, 8, 64], BF16) for j in range(2)]
    Rf = mk("Rf", [64, 8, 64], F32)
    Rb = mk("Rb", [64, 8, 64], BF16)
    vb = mk("vb", [64, 8, 128], BF16)
    kbg = mk("kbg", [64, 8, 128], BF16)
    kt = mk("kt", [64, 8, 128], BF16)
    nwT = mk("nwT", [128, 8, 64], BF16)
    vn = mk("vn", [64, 8, 128], BF16)
    osb = mk("osb", [128, 8, 64], F32)
    osq = mk("osq", [128, 8, 64], F32)
    orst = osq
    bank = [(K.psum.slots[i][0], K.psum.slots[i][1]) for i in range(8)]
    identI = K.ident[0:64, 0:64]

    def bc(ap, shape):
        return ap.to_broadcast(list(shape))

    for ti in range(ntile):
        t0 = ti * T
        hn, hn_b, hl = hnr.next()
        if xch is None:
            P.dma("sp", hl, hn[:, :, :], hn1g.rearrange("f p t -> p f t")[:, :, t0:t0 + T], writes=[hn_b])
        else:
            pieces = []
            s0 = t0
            while s0 < t0 + T:
                if s0 < 48:
                    s1 = min(48, t0 + T)
                    P.op("dve", E("memset", ap=hn[:, :, s0 - t0:s1 - t0], constant=0.0), writes=[hn_b])
                    s0 = s1
                    continue
                rk, j = (0, s0 - 48) if s0 < 2112 else (1, s0 - 2112 + 16)
                lim = 2112 if s0 < 2112 else TG
                q, cq = j // TA_TILE, j % TA_TILE
                part, pc0, plim = (0, cq, 512) if cq < 512 else (1, cq - 512, TA_TILE - 512)
                n = min(t0 + T - s0, lim - s0, plim - pc0)
                xi = 2 * q + part
                src = xch["hn1_all"][xi].rearrange("(r f p) t -> r p f t", r=2, p=128)[rk][:, :, pc0:pc0 + n]
                pieces.append((hn[:, :, s0 - t0:s0 - t0 + n], src, xch["hn1_all_b"][xi]))
                s0 += n
            P.dma_group("sp", hl, [(d_, s_) for d_, s_, _ in pieces], reads=[b_ for _, _, b_ in pieces], writes=[hn_b])
        for blk in range(24):
            wt, wb, wl = K.wring.next()
            wv = wt[:, 0:4096].rearrange("p (k m) -> p k m", k=NF)
            P.dma("pool", wl, wv, gw_v[:, :, blk * 256:(blk + 1) * 256], writes=[wb])
            for mi in range(2):
                mc = blk * 2 + mi
                pb, pbb, _ = K.psum.next()
                mm_group(P, pb[:, :T], pbb, [(wv[:, k, mi * 128:(mi + 1) * 128], hn[:, k, :]) for k in range(NF)],
                         [wb, hn_b])
                if mc >= 32:
                    h = mc - 32
                    P.op("act", E("activation", out=zs[:, h, :], in_=pb[:, :T], func=AF.Silu), reads=[pbb], writes=[zs_b])
                    continue
                pc, pcb, _ = pcr.next()
                ac, acb, _ = accr.next()
                P.op("act", E("activation", out=pc[:, 3:3 + T], in_=pb[:, :T], func=AF.Copy), reads=[pbb], writes=[pcb])
                P.op("act", E("activation", out=pc[:, 0:3], in_=halo[:, mc, :], func=AF.Copy),
                     reads=[halo_b[mc]], writes=[pcb])
                P.op("dve", E("tensor_scalar", out=ac[:, :], in0=pc[:, 3:3 + T], scalar1=vec[:, 96 + mc:97 + mc],
                              scalar2=None, op0=ALU.mult), reads=[pcb, vec_b], writes=[acb])
                for j in range(3):
                    P.op("dve", E("scalar_tensor_tensor", out=ac[:, :], in0=pc[:, j:j + T],
                                  scalar=vec[:, j * 32 + mc:j * 32 + mc + 1], in1=ac[:, :], op0=ALU.mult, op1=ALU.add),
                         reads=[pcb, vec_b, acb], writes=[acb])
                P.op("act", E("activation", out=halo[:, mc, :], in_=pc[:, T:T + 3], func=AF.Copy),
                     reads=[pcb], writes=[halo_b[mc]])
                if mc >= 16:
                    h = mc - 16
                    P.op("act", E("activation", out=vf[:, h, :], in_=ac[:, :], func=AF.Silu), reads=[acb], writes=[vf_b])
                    continue
                sl, slb, _ = silr.next()
                P.op("act", E("activation", out=sl[:, :], in_=ac[:, :], func=AF.Silu), reads=[acb], writes=[slb])
                sq, sqb, _ = K.sq.next()
                P.op("act", E("activation", out=sq[:, :T], in_=sl[:, :], func=AF.Square), reads=[slb], writes=[sqb])
                p2, p2b, _ = K.psum.next()
                P.op("pe", E("matmul", out=p2[:, :T], lhsT=K.ones[:], rhs=sq[:, :T], start=True, stop=True),
                     reads=[sqb, K.ones_b], writes=[p2b])
                ri, rib, _ = rinr.next()
                rsqrt_op(P, K, ri[:, :], rib, p2[:, :T], p2b, 1.0, EPS)
                if mc < 8:
                    P.op("dve", E("scalar_tensor_tensor", out=qf[:, mc, :], in0=sl[:, :], scalar=DK ** -0.5, in1=ri[:, :],
                                  op0=ALU.mult, op1=ALU.mult), reads=[slb, rib], writes=[qf_b])
                else:
                    P.op("dve", E("tensor_tensor", out=kf[:, mc - 8, :], in0=sl[:, :], in1=ri[:, :], op=ALU.mult),
                         reads=[slb, rib], writes=[kf_b])
        if DBG == "s2":
            ogt, og_b, ol = ogr.next()
            P.op("dve", E("tensor_copy", out=ogt[:, :, :], in_=vf[:, :, :]), reads=[vf_b, qf_b, kf_b, zs_b], writes=[og_b])
            P.dma("sp", ol, og.rearrange("f p t -> p f t")[:, :, t0:t0 + T], ogt[:, :, :], reads=[og_b])
            continue
        pba, pba_b = bank[0]
        for c in range(NCH):
            mm_group(P, pba[0:64, c * 32:(c + 1) * 32], pba_b,
                     [(hn[:, k, c * 64:(c + 1) * 64], wba[:, k, :]) for k in range(NF)], [hn_b, wba_b])
        pv = pba[0:64, 0:NCH * 32].rearrange("p (c j) -> p c j", j=32)
        P.op("act", E("activation", out=beta[:, :, :], in_=pv[:, :, 0:16], func=AF.Sigmoid), reads=[pba_b], writes=[sm_b])
        P.op("act", E("activation", out=lb[:, :, :], in_=beta[:, :, :], func=AF.Ln), reads=[sm_b], writes=[sm_b])
        P.op("dve", E("tensor_tensor", out=gtm[:, :, :], in0=pv[:, :, 16:32],
                      in1=bc(row[0:64, 16:32].unsqueeze(1), [64, NCH, 16]), op=ALU.add),
             reads=[pba_b, row_b], writes=[sm_b])
        P.op("act", E("activation", out=gtm[:, :, :], in_=gtm[:, :, :], func=AF.Exp), reads=[sm_b], writes=[sm_b])
        P.op("act", E("activation", out=gtm[:, :, :], in_=gtm[:, :, :], func=AF.Ln, bias=onec[0:64, 0:1]),
             reads=[sm_b, onec_b], writes=[sm_b])
        P.op("dve", E("tensor_tensor", out=gtm[:, :, :], in0=gtm[:, :, :],
                      in1=bc(nea[:, :].unsqueeze(1), [64, NCH, 16]), op=ALU.mult), reads=[sm_b, nea_b], writes=[sm_b])
        ogt, og_b, ol = ogr.next()
        if DBG == "s2b":
            P.op("dve", E("tensor_copy", out=ogt[:, :, :], in_=vf[:, :, :]), reads=[vf_b, qf_b, kf_b, zs_b, sm_b], writes=[og_b])
            P.dma("sp", ol, og.rearrange("f p t -> p f t")[:, :, t0:t0 + T], ogt[:, :, :], reads=[og_b])
            continue
        for c in range(NCH):
            cs = slice(c * 64, (c + 1) * 64)
            pm, pm_b = bank[0]
            P.op("pe", E("matmul", out=pm[0:64, 256:272], lhsT=Umat, rhs=gtm[:, c, :], start=True, stop=True),
                 reads=[cB_b, sm_b], writes=[pm_b])
            P.op("pe", E("matmul", out=pm[:, 272:288], lhsT=K.ones[0:64, :], rhs=gtm[:, c, :], start=True, stop=True),
                 reads=[K.ones_b, sm_b], writes=[pm_b])
            gc, gc_b = gcs[0]
            gb, gb_b = gbs[0]
            eg, eg_b = egs[0]
            beg, beg_b = begs[0]
            ekt, ekt_b = ekts[0]
            gt, gt_b = gtot[0]
            gl, gl_b = gls[0]
            P.op("dve", E("tensor_copy", out=gc[:, :], in_=pm[0:64, 256:272]), reads=[pm_b], writes=[gc_b])
            P.op("act", E("activation", out=gt[:, :], in_=pm[:, 272:288], func=AF.Copy), reads=[pm_b], writes=[gt_b])
            P.op("dve", E("tensor_tensor", out=gb[:, :], in0=gc[:, :], in1=lb[:, c, :], op=ALU.add),
                 reads=[gc_b, sm_b], writes=[gb_b])
            P.op("act", E("activation", out=eg[:, :], in_=gc[:, :], func=AF.Exp), reads=[gc_b], writes=[eg_b])
            P.op("dve", E("tensor_tensor", out=beg[:, :], in0=eg[:, :], in1=beta[:, c, :], op=ALU.mult),
                 reads=[eg_b, sm_b], writes=[beg_b])
            P.op("dve", E("tensor_tensor", out=ekt[:, :], in0=gt[0:64, :], in1=gc[:, :], op=ALU.subtract),
                 reads=[gt_b, gc_b], writes=[ekt_b])
            P.op("act", E("activation", out=ekt[:, :], in_=ekt[:, :], func=AF.Exp), reads=[ekt_b], writes=[ekt_b])
            P.op("act", E("activation", out=gl[:, :], in_=gt[:, :], func=AF.Exp), reads=[gt_b], writes=[gl_b])
            for hh in range(2):
                h0 = hh * 8
                k0 = hh * 4
                hs = slice(h0, h0 + 8)
                if DBGSTEP < 2:
                    continue
                d1, d1_b = Dg1[hh]
                d2, d2_b = Dg2[hh]
                P.op("dve", E("tensor_tensor", out=d1[:, :, :], in0=bc(identI.unsqueeze(1), [64, 8, 64]),
                              in1=bc(gc[:, hs].unsqueeze(2), [64, 8, 64]), op=ALU.mult),
                     reads=[K.ident_b, gc_b], writes=[d1_b])
                P.op("dve", E("tensor_tensor", out=d2[:, :, :], in0=bc(identI.unsqueeze(1), [64, 8, 64]),
                              in1=bc(gb[:, hs].unsqueeze(2), [64, 8, 64]), op=ALU.mult),
                     reads=[K.ident_b, gb_b], writes=[d2_b])
                if DBGSTEP < 2.2:
                    continue
                pG, pG_b = bank[1]
                pGb, pGb_b = bank[2]
                P.op("pe", E("matmul", out=pG[:, :], lhsT=K.ones[0:64, :], rhs=d1[:, :, :].rearrange("p h c -> p (h c)"),
                             start=True, stop=True), reads=[K.ones_b, d1_b], writes=[pG_b])
                P.op("pe", E("matmul", out=pGb[0:64, :], lhsT=K.ones[0:64, 0:64], rhs=d2[:, :, :].rearrange("p h c -> p (h c)"),
                             start=True, stop=True), reads=[K.ones_b, d2_b], writes=[pGb_b])
                if DBGSTEP < 2.4:
                    continue
                pG3 = pG[:, :].rearrange("p (h c) -> p h c", c=64)
                pGb3 = pGb[:, :].rearrange("p (h c) -> p h c", c=64)
                a1, a1_b = t1[hh]
                a2, a2_b = t2[hh]
                ei, ei_b = EI[hh]
                es, es_b = ES[hh]
                egr, egr_b = EG[hh]
                gcb = bc(gc[:, hs].unsqueeze(2), [64, 8, 64])
                P.op("dve", E("tensor_tensor", out=a1[:, :, :], in0=pG3[0:64], in1=gcb, op=ALU.subtract),
                     reads=[pG_b, gc_b], writes=[a1_b])
                P.op("dve", E("tensor_tensor", out=a2[:, :, :], in0=pGb3[0:64], in1=gcb, op=ALU.subtract),
                     reads=[pGb_b, gc_b], writes=[a2_b])
                if DBGSTEP < 2.5:
                    continue
                P.op("act", E("activation", out=egr[:, :, :], in_=pG3, func=AF.Exp), reads=[pG_b], writes=[egr_b])
                if DBGSTEP < 2.6:
                    continue
                P.op(POOLENG, E("tensor_tensor", out=a1[:, :, :], in0=a1[:, :, :], in1=bc(NEGI.unsqueeze(1), [64, 8, 64]),
                               op=ALU.add), reads=[a1_b, cB_b], writes=[a1_b])
                P.op(POOLENG, E("tensor_tensor", out=a2[:, :, :], in0=a2[:, :, :], in1=bc(NEGS.unsqueeze(1), [64, 8, 64]),
                               op=ALU.add), reads=[a2_b, cB_b], writes=[a2_b])
                if DBGSTEP < 2.7:
                    continue
                P.op("act", E("activation", out=ei[:, :, :], in_=a1[:, :, :], func=EXPF), reads=[a1_b], writes=[ei_b])
                P.op("act", E("activation", out=es[:, :, :], in_=a2[:, :, :], func=EXPF), reads=[a2_b], writes=[es_b])
                if DBGSTEP < 3:
                    continue
                qgt, qg_b = qg[hh]
                P.op("dve", E("tensor_tensor", out=qgt[:, :, :].rearrange("p (k two) c -> p k two c", two=2),
                              in0=bc(qf[:, k0:k0 + 4, cs].unsqueeze(2), [128, 4, 2, 64]),
                              in1=egr[:, :, :].rearrange("p (k two) c -> p k two c", two=2), op=ALU.mult),
                     reads=[qf_b, egr_b], writes=[qg_b])
                pK, pK_b = bank[3]
                for kh in range(4):
                    P.op("pe", E("matmul", out=pK[0:64, kh * 128:kh * 128 + 64], lhsT=kf[:, k0 + kh, cs], rhs=kf[:, k0 + kh, cs],
                                 start=True, stop=True), reads=[kf_b], writes=[pK_b])
                    P.op("pe", E("matmul", out=pK[0:64, kh * 128 + 64:kh * 128 + 128], lhsT=kf[:, k0 + kh, cs],
                                 rhs=qf[:, k0 + kh, cs], start=True, stop=True), reads=[kf_b, qf_b], writes=[pK_b])
                pK4 = pK[0:64, :].rearrange("p (k j c) -> p k j c", j=2, c=64)
                p0, p0_b = Pm[0][hh]
                qk, qk_b = qkT[hh]
                P.op("dve", E("tensor_tensor", out=p0[:, :, :].rearrange("p (k two) c -> p k two c", two=2),
                              in0=bc(pK4[:, :, 0, :].unsqueeze(2), [64, 4, 2, 64]),
                              in1=es[:, :, :].rearrange("p (k two) c -> p k two c", two=2), op=ALU.mult),
                     reads=[pK_b, es_b], writes=[p0_b])
                P.op("dve", E("tensor_tensor", out=qk[:, :, :].rearrange("p (k two) c -> p k two c", two=2),
                              in0=bc(pK4[:, :, 1, :].unsqueeze(2), [64, 4, 2, 64]),
                              in1=ei[:, :, :].rearrange("p (k two) c -> p k two c", two=2), op=ALU.mult),
                     reads=[pK_b, ei_b], writes=[qk_b])
                P.op(POOLENG, E("tensor_scalar", out=p0[:, :, :], in0=p0[:, :, :], scalar1=-1.0, scalar2=None, op0=ALU.mult),
                     reads=[p0_b], writes=[p0_b])
                if DBGSTEP < 4:
                    continue
                pT, pT_b = bank[7]
                pTv = pT[:, :].bitcast(BF16)
                for h in range(8):
                    P.op("pe", E("transpose", out=pTv[0:64, h * 128:(h + 1) * 128], in_=vf[:, h0 + h, cs], identity=identb[:, :]),
                         reads=[vf_b, identb_b], writes=[pT_b])
                vbt, vb_b = vb[hh]
                P.op("dve", E("tensor_tensor", out=vbt[:, :, :], in0=pTv[0:64, :].rearrange("p (h v) -> p h v", v=128),
                              in1=bc(beta[:, c, hs].unsqueeze(2), [64, 8, 128]), op=ALU.mult),
                     reads=[pT_b, sm_b], writes=[vb_b])
                pT2, pT2_b = bank[6]
                pT2v = pT2[:, :].bitcast(BF16)
                for kh in range(4):
                    P.op("pe", E("transpose", out=pT2v[0:64, kh * 128:(kh + 1) * 128], in_=kf[:, k0 + kh, cs], identity=identb[:, :]),
                         reads=[kf_b, identb_b], writes=[pT2_b])
                ktm4 = bc(pT2v[0:64, 0:512].rearrange("p (k d) -> p k d", d=128).unsqueeze(2), [64, 4, 2, 128])
                kbt, kb_b = kbg[hh]
                ktt, kt_b = kt[hh]
                P.op("dve", E("tensor_tensor", out=kbt[:, :, :].rearrange("p (k two) d -> p k two d", two=2), in0=ktm4,
                              in1=bc(beg[:, hs].unsqueeze(2), [64, 8, 128]).rearrange("p (k two) d -> p k two d", two=2),
                              op=ALU.mult), reads=[pT2_b, beg_b], writes=[kb_b])
                P.op("dve", E("tensor_tensor", out=ktt[:, :, :].rearrange("p (k two) d -> p k two d", two=2), in0=ktm4,
                              in1=bc(ekt[:, hs].unsqueeze(2), [64, 8, 128]).rearrange("p (k two) d -> p k two d", two=2),
                              op=ALU.mult), reads=[pT2_b, ekt_b], writes=[kt_b])
                if DBGSTEP < 5:
                    continue
                pA, pA_b = bank[4]
                pB, pB_b = bank[5]
                pC, pC_b = bank[6]
                pAv = pA[:, :].bitcast(BF16)
                for h in range(8):
                    P.op("pe", E("transpose", out=pAv[0:64, h * 64:(h + 1) * 64], in_=p0[:, h, :], identity=identb[0:64, 0:64]),
                         reads=[p0_b, identb_b], writes=[pA_b])
                q0, q0_b = PT[0][hh]
                P.op("act", E("activation", out=q0[:, :, :], in_=pAv[0:64, 0:512].rearrange("p (h c) -> p h c", c=64), func=AF.Copy),
                     reads=[pA_b], writes=[q0_b])
                rf, rf_b = Rf[hh]
                rb, rb_b = Rb[hh]
                P.op("dve", E("tensor_tensor", out=rf[:, :, :], in0=p0[:, :, :], in1=bc(identI.unsqueeze(1), [64, 8, 64]),
                              op=ALU.add), reads=[p0_b, K.ident_b], writes=[rf_b])
                P.op("act", E("activation", out=rb[:, :, :], in_=rf[:, :, :], func=AF.Copy), reads=[rf_b], writes=[rb_b])
                cur = 0
                for lvl in range(6):
                    pc_, pc_b = Pm[cur][hh]
                    pt_, pt_b = PT[cur][hh]
                    pn_, pn_b = Pm[1 - cur][hh]
                    ptn_, ptn_b = PT[1 - cur][hh]
                    if lvl >= 1:
                        for h in range(8):
                            P.op("pe", E("matmul", out=pC[0:64, h * 64:(h + 1) * 64], lhsT=pt_[:, h, :], rhs=rb[:, h, :],
                                         start=True, stop=True), reads=[pt_b, rb_b], writes=[pC_b])
                    if lvl <= 4:
                        for h in range(8):
                            P.op("pe", E("matmul", out=pA[0:64, h * 64:(h + 1) * 64], lhsT=pt_[:, h, :], rhs=pc_[:, h, :],
                                         start=True, stop=True), reads=[pt_b, pc_b], writes=[pA_b])
                        for h in range(8):
                            P.op("pe", E("matmul", out=pB[0:64, h * 64:(h + 1) * 64], lhsT=pc_[:, h, :], rhs=pt_[:, h, :],
                                         start=True, stop=True), reads=[pt_b, pc_b], writes=[pB_b])
                    if lvl >= 1:
                        P.op("dve", E("tensor_tensor", out=rf[:, :, :], in0=pC[0:64, :].rearrange("p (h c) -> p h c", c=64),
                                      in1=rf[:, :, :], op=ALU.add), reads=[pC_b, rf_b], writes=[rf_b])
                        P.op("act", E("activation", out=rb[:, :, :], in_=rf[:, :, :], func=AF.Copy), reads=[rf_b], writes=[rb_b])
                    if lvl <= 4:
                        P.op("act", E("activation", out=pn_[:, :, :], in_=pA[0:64, :].rearrange("p (h c) -> p h c", c=64),
                                      func=AF.Copy), reads=[pA_b], writes=[pn_b])
                        P.op("dve", E("tensor_copy", out=ptn_[:, :, :], in_=pB[0:64, :].rearrange("p (h c) -> p h c", c=64)),
                             reads=[pB_b], writes=[ptn_b])
                        cur = 1 - cur
                if DBGSTEP < 6:
                    continue
                pW, pW_b = bank[2]
                for h in range(8):
                    P.op("pe", E("matmul", out=pW[:, h * 64:(h + 1) * 64], lhsT=kbt[:, h, :], rhs=rb[:, h, :], start=True, stop=True),
                         reads=[kb_b, rb_b], writes=[pW_b])
                nw, nw_b = nwT[hh]
                P.op("act", E("activation", out=nw[:, :, :], in_=pW[:, :].rearrange("p (h c) -> p h c", c=64), func=AF.Copy,
                              scale=-1.0), reads=[pW_b], writes=[nw_b])
                if DBGSTEP < 7:
                    continue
                vnt, vn_b = vn[hh]
                for q in range(2):
                    pV, pV_b = bank[4 + q]
                    for h4 in range(4):
                        h = q * 4 + h4
                        P.op("pe", E("matmul", out=pV[0:64, h4 * 128:(h4 + 1) * 128], lhsT=rb[:, h, :], rhs=vbt[:, h, :],
                                     start=True, stop=False), reads=[rb_b, vb_b], writes=[pV_b])
                        P.op("pe", E("matmul", out=pV[0:64, h4 * 128:(h4 + 1) * 128], lhsT=nw[:, h, :], rhs=Sb[:, h0 + h, :],
                                     start=False, stop=True), reads=[nw_b, Sb_b[hh]], writes=[pV_b])
                    if q == 0:
                        P.op("act", E("activation", out=vnt[:, 0:4, :], in_=pV[0:64, :].rearrange("p (h v) -> p h v", v=128),
                                      func=AF.Copy), reads=[pV_b], writes=[vn_b])
                    else:
                        P.op("dve", E("tensor_copy", out=vnt[:, 4:8, :], in_=pV[0:64, :].rearrange("p (h v) -> p h v", v=128)),
                             reads=[pV_b], writes=[vn_b])
                if DBGSTEP < 8:
                    continue
                pO, pO_b = bank[1]
                for h in range(8):
                    P.op("pe", E("matmul", out=pO[:, h * 64:(h + 1) * 64], lhsT=Sb[:, h0 + h, :], rhs=qgt[:, h, :],
                                 start=True, stop=False), reads=[Sb_b[hh], qg_b], writes=[pO_b])
                    P.op("pe", E("matmul", out=pO[:, h * 64:(h + 1) * 64], lhsT=vnt[:, h, :], rhs=qk[:, h, :],
                                 start=False, stop=True), reads=[vn_b, qk_b], writes=[pO_b])
                if DBGSTEP < 9:
                    continue
                for q in range(2):
                    pS, pS_b = bank[4 + q]
                    for h4 in range(4):
                        h = q * 4 + h4
                        P.op("pe", E("matmul", out=pS[:, h4 * 128:(h4 + 1) * 128], lhsT=ktt[:, h, :], rhs=vnt[:, h, :],
                                     start=True, stop=True), reads=[kt_b, vn_b], writes=[pS_b])
                    sv = S[:, h0 + q * 4:h0 + q * 4 + 4, :]
                    P.op("dve", E("tensor_tensor", out=sv, in0=sv, in1=bc(gl[:, h0 + q * 4:h0 + q * 4 + 4].unsqueeze(2), [128, 4, 128]),
                                  op=ALU.mult), reads=[S_b[hh], gl_b], writes=[S_b[hh]])
                    P.op("dve", E("tensor_tensor", out=sv, in0=pS[:, :].rearrange("p (h v) -> p h v", v=128), in1=sv, op=ALU.add),
                         reads=[pS_b, S_b[hh]], writes=[S_b[hh]])
                    P.op("act", E("activation", out=Sb[:, h0 + q * 4:h0 + q * 4 + 4, :], in_=sv, func=AF.Copy),
                         reads=[S_b[hh]], writes=[Sb_b[hh]])
                if DBGSTEP < 10:
                    continue
                ob, ob_b = osb[hh]
                oq, oq_b = osq[hh]
                orr, or_b = orst[hh]
                pO3 = pO[:, :].rearrange("p (h c) -> p h c", c=64)
                P.op("dve", E("tensor_copy", out=ob[:, :, :], in_=pO3), reads=[pO_b], writes=[ob_b])
                P.op("act", E("activation", out=oq[:, :, :], in_=pO3, func=AF.Square), reads=[pO_b], writes=[oq_b])
                pQ, pQ_b = bank[3]
                P.op("pe", E("matmul", out=pQ[:, :], lhsT=K.ones[:, :], rhs=oq[:, :, :].rearrange("p h c -> p (h c)"),
                             start=True, stop=True), reads=[K.ones_b, oq_b], writes=[pQ_b])
                rsqrt_op(P, K, orr[:, :, :].rearrange("p h c -> p (h c)"), or_b, pQ[:, :], pQ_b, 1.0 / 128, EPS)
                P.op("dve", E("scalar_tensor_tensor", out=ob[:, :, :].rearrange("p h c -> p (h c)"),
                              in0=ob[:, :, :].rearrange("p h c -> p (h c)"), scalar=vec[:, 128:129],
                              in1=orr[:, :, :].rearrange("p h c -> p (h c)"), op0=ALU.mult, op1=ALU.mult),
                     reads=[ob_b, or_b, vec_b], writes=[ob_b])
                P.op("dve", E("tensor_tensor", out=ogt[:, hs, cs], in0=ob[:, :, :], in1=zs[:, hs, cs], op=ALU.mult),
                     reads=[ob_b, zs_b], writes=[og_b])
        if DBGSTEP < 10:
            P.op("dve", E("tensor_copy", out=ogt[:, :, :], in_=vf[:, :, :]), reads=[vf_b], writes=[og_b])
        if xch is None:
            P.dma("sp", ol, og.rearrange("f p t -> p f t")[:, :, t0:t0 + T], ogt[:, :, :], reads=[og_b])
        else:
            P.dma("sp", ol, xch["og_loc"][ti].rearrange("(f p) t -> p f t", p=128), ogt[:, :, :],
                  reads=[og_b], writes=[xch["og_loc_b"][ti]])
            P.cc_allgather(xch["og_loc"][ti], xch["og_loc_b"][ti], xch["og_all"][ti], xch["og_all_b"][ti], PAIRS)


def build_B(ntile=None):
    nc = bass.Bass("TRN2", target_bir_lowering=False)
    io = {}

    def inp(name, shape, dt=F32):
        io[name] = nc.dram_tensor(name, list(shape), dt, kind="ExternalInput").ap()

    inp("hn1g", [NF, 128, TG], BF16)
    inp("ident", [128, 128])
    inp("vecB", [128, 129])
    inp("rowB", [128, 32])
    inp("cB", [64, 192])
    inp("gw_in", [D, 6176])
    io["og"] = nc.dram_tensor("og", [NF, 128, TG], BF16, kind="ExternalOutput").ap()
    P = Prog(nc)
    K = Common(P, nc, io["ident"])
    phase_B(P, K, io, ntile)
    P.emit()
    P.close()
    return nc


def const_cB():
    t = np.arange(64)
    U = (t[:, None] <= t[None, :]).astype(np.float32)
    negi = np.where(t[None, :] >= t[:, None], 0.0, NEG).astype(np.float32)
    negs = np.where(t[None, :] > t[:, None], 0.0, NEG).astype(np.float32)
    return np.ascontiguousarray(np.concatenate([U, negi, negs], axis=1))


def host_weights_B(inp, r):
    w = np.asarray(inp["gdn_w_in"], np.float32)[0]
    cw = np.asarray(inp["gdn_conv_w"], np.float32)[0]
    qs = slice(r * 1024, (r + 1) * 1024)
    ks = slice(2048 + r * 1024, 2048 + (r + 1) * 1024)
    vs = slice(4096 + r * 2048, 4096 + (r + 1) * 2048)
    zs = slice(8192 + r * 2048, 8192 + (r + 1) * 2048)
    bs = slice(12288 + r * 16, 12288 + (r + 1) * 16)
    as_ = slice(12320 + r * 16, 12320 + (r + 1) * 16)
    gw = np.ascontiguousarray(np.concatenate([w[:, qs], w[:, ks], w[:, vs], w[:, zs], w[:, bs], w[:, as_]], axis=1))
    cwl = np.concatenate([cw[:, qs], cw[:, ks], cw[:, vs]], axis=1)
    vec = np.concatenate([_fm(cwl[j]) for j in range(4)] + [np.asarray(inp["gdn_norm_w"], np.float32)[0].reshape(128, 1)], axis=1)
    al = np.asarray(inp["gdn_a_log"], np.float32)[0][r * 16:(r + 1) * 16]
    dtb = np.asarray(inp["gdn_dt_bias"], np.float32)[0][r * 16:(r + 1) * 16]
    row = np.ascontiguousarray(np.broadcast_to(np.concatenate([al, dtb])[None, :], (128, 32)))
    return {"gw_in": gw, "vecB": np.ascontiguousarray(vec.astype(np.float32)), "rowB": row.astype(np.float32),
            "cB": const_cB(), "ident": np.eye(128, dtype=np.float32)}


def _run(nc, maps):
    res = run_bass_kernel_spmd(nc, maps, core_ids=list(range(8)))
    return res.results


def _kernel_unfused(**inputs):
    resA = _run(build_A(), host_inputs_A(inputs))
    wB = [host_weights_B(inputs, r) for r in range(2)]
    mapsB = []
    for c in range(8):
        b, r = c // 2, c % 2
        h0 = np.asarray(resA[2 * b]["hn1"])
        h1_ = np.asarray(resA[2 * b + 1]["hn1"])
        seq = np.concatenate([np.zeros((NF, 128, 48), dtype=h0.dtype), h0, h1_[:, :, 16:]], axis=2)
        m = dict(wB[r])
        m["hn1g"] = np.ascontiguousarray(seq)
        mapsB.append(m)
    resB = _run(build_B(), mapsB)
    shared = {
        "ident": np.eye(128, dtype=np.float32),
        "vecC": np.ascontiguousarray(np.concatenate([_fm(inputs["ffn_norm"][1]), _fm(inputs["final_norm"])], axis=1)),
        "gdn_w_out": np.ascontiguousarray(np.asarray(inputs["gdn_w_out"], np.float32)[0]),
        "wg": np.ascontiguousarray(np.asarray(inputs["ffn_w_gate"], np.float32)[1]),
        "wu": np.ascontiguousarray(np.asarray(inputs["ffn_w_up"], np.float32)[1]),
        "wd": np.ascontiguousarray(np.asarray(inputs["ffn_w_down"], np.float32)[1]),
    }
    mapsC = []
    for c in range(8):
        b, r = c // 2, c % 2
        lo = 64 + r * 2048
        ogc = np.concatenate([np.asarray(resB[2 * b]["og"])[:, :, lo:lo + 2048],
                              np.asarray(resB[2 * b + 1]["og"])[:, :, lo:lo + 2048]], axis=0)
        m = dict(shared)
        m["ogc"] = np.ascontiguousarray(ogc)
        m["h1c"] = np.ascontiguousarray(np.asarray(resA[c]["h1"])[:, :, 16:])
        mapsC.append(m)
    resC = _run(build_C(), mapsC)
    out = np.empty((BATCH, SEQ, D), np.float32)
    for c in range(8):
        b, r = c // 2, c % 2
        out[b, r * 2048:(r + 1) * 2048] = np.asarray(resC[c]["out"])
    return out


def build_fused():
    nc = bass.Bass("TRN2", target_bir_lowering=False)
    io = {}

    def inp(name, shape, dt=F32):
        io[name] = nc.dram_tensor(name, list(shape), dt, kind="ExternalInput").ap()

    inp("xa", [TA, D])
    inp("ident", [128, 128])
    inp("vecA", [128, 96])
    inp("sc_w_in", [D, 3 * D])
    inp("sc_w_out", [D, D])
    inp("wg0", [D, DFF])
    inp("wu0", [D, DFF])
    inp("wd0", [DFF, D])
    inp("vecB", [128, 129])
    inp("rowB", [128, 32])
    inp("cB", [64, 192])
    inp("gw_in", [D, 6176])
    inp("vecC", [128, 32])
    inp("mk", [128, 2])
    inp("gdn_w_out", [2 * D, D])
    inp("wg1", [D, DFF])
    inp("wu1", [D, DFF])
    inp("wd1", [DFF, D])
    io["out"] = nc.dram_tensor("out", [TC, D], F32, kind="ExternalOutput").ap()
    nA = TA // TA_TILE
    nB = TG // TB
    h1 = nc.dram_tensor("h1_loc", [NF, 128, TA], F32).ap()
    xch = {
        "h1_b": Buf("h1"),
        "hn1_loc": [nc.dram_tensor("hn1_loc%d" % i, [D, XWS[i % 2]], BF16).ap() for i in range(2 * nA)],
        "hn1_all": [nc.dram_tensor("hn1_all%d" % i, [2 * D, XWS[i % 2]], BF16).ap() for i in range(2 * nA)],
        "og_loc": [nc.dram_tensor("og_loc%d" % i, [D, TB], BF16).ap() for i in range(nB)],
        "og_all": [nc.dram_tensor("og_all%d" % i, [2 * D, TB], BF16).ap() for i in range(nB)],
    }
    for k in ("hn1_loc", "hn1_all", "og_loc", "og_all"):
        xch[k + "_b"] = [Buf("%s%d" % (k, i)) for i in range(len(xch[k]))]
    P = Prog(nc)
    K = Common(P, nc, io["ident"])
    ioA = {"xa": io["xa"], "vecA": io["vecA"], "sc_w_in": io["sc_w_in"], "sc_w_out": io["sc_w_out"],
           "wg": io["wg0"], "wu": io["wu0"], "wd": io["wd0"], "h1": h1}
    phase_A(P, K, ioA, TA_TILE, xch)
    P.emit(final=False)
    P.close()
    K = Common(P, nc, io["ident"])
    ioB = {"vecB": io["vecB"], "rowB": io["rowB"], "cB": io["cB"], "gw_in": io["gw_in"]}
    phase_B(P, K, ioB, None, xch)
    P.emit(final=False)
    P.close()
    K = Common(P, nc, io["ident"])
    ioC = {"h1c": h1, "vecC": io["vecC"], "mk": io["mk"], "gdn_w_out": io["gdn_w_out"],
           "wg": io["wg1"], "wu": io["wu1"], "wd": io["wd1"], "out": io["out"]}
    phase_C(P, K, ioC, 512, xch)
    P.emit(final=True)
    P.finish()
    return nc


def host_inputs_fused(inp):
    mapsA = host_inputs_A(inp)
    wB = [host_weights_B(inp, r) for r in range(2)]
    shared = {
        "vecC": np.ascontiguousarray(np.concatenate([_fm(inp["ffn_norm"][1]), _fm(inp["final_norm"])], axis=1)),
        "gdn_w_out": np.ascontiguousarray(np.asarray(inp["gdn_w_out"], np.float32)[0]),
        "wg1": np.ascontiguousarray(np.asarray(inp["ffn_w_gate"], np.float32)[1]),
        "wu1": np.ascontiguousarray(np.asarray(inp["ffn_w_up"], np.float32)[1]),
        "wd1": np.ascontiguousarray(np.asarray(inp["ffn_w_down"], np.float32)[1]),
    }
    maps = []
    for c in range(8):
        r = c % 2
        a = mapsA[c]
        m = {"xa": a["xa"], "ident": a["ident"], "vecA": a["vecA"], "sc_w_in": a["sc_w_in"], "sc_w_out": a["sc_w_out"],
             "wg0": a["wg"], "wu0": a["wu"], "wd0": a["wd"]}
        for k in ("vecB", "rowB", "cB", "gw_in"):
            m[k] = wB[r][k]
        m.update(shared)
        mk = np.zeros((128, 2), np.float32)
        mk[:, r] = 1.0
        m["mk"] = mk
        maps.append(m)
    return maps


def kernel_unfused(**inputs):
    return _kernel_unfused(**inputs)


def kernel(**inputs):
    nc = build_fused()
    res = run_bass_kernel_spmd(nc, host_inputs_fused(inputs), core_ids=list(range(8))).results
    out = np.empty((BATCH, SEQ, D), np.float32)
    for c in range(8):
        b, r = c // 2, c % 2
        out[b, r * 2048:(r + 1) * 2048] = np.asarray(res[c]["out"])
    return out
```

```python
import numpy as np
import ml_dtypes
import concourse.bass as bass
import concourse.mybir as mybir
from concourse.bass_utils import run_bass_kernel_spmd

F32 = mybir.dt.float32
BF16 = mybir.dt.bfloat16
AF = mybir.ActivationFunctionType
ALU = mybir.AluOpType
AX = mybir.AxisListType

D = 2048
NF = 16
DFF = 5632
NCF = 44
SEQ = 4096
BATCH = 4
N_META = 16
EPS = 1e-6
TA = 2064
TC = 2048
TG = 4160
PAIRS = [[0, 1], [2, 3], [4, 5], [6, 7]]
TA_TILE = 516
XWS = (512, 64)

ENGS = ("pe", "act", "dve", "pool", "sp")


class Buf:
    __slots__ = ("name", "last_w", "readers", "excl")

    def __init__(self, name="", excl=False):
        self.name = name
        self.last_w = None
        self.readers = []
        self.excl = excl


class Ins:
    __slots__ = ("eng", "emit", "deps", "inc", "ordinal", "lane", "lane_val", "is_dma", "cc_inc", "phase")

    def __init__(self, eng, emit):
        self.eng = eng
        self.emit = emit
        self.deps = []
        self.inc = False
        self.ordinal = None
        self.lane = None
        self.lane_val = None
        self.is_dma = False
        self.cc_inc = 16
        self.phase = 0


class Lane:
    def __init__(self, name):
        self.name = name
        self.sem = None
        self.count = 0
        self.last = None


class Prog:
    def __init__(self, nc):
        self.nc = nc
        self.streams = {e: [] for e in ENGS}
        self.lanes = []
        self._ctx = []
        self.out_lanes = []
        self.phase = 0
        self.sems = None
        self.ord = {e: 0 for e in ENGS}
        self._semctx = []

    def sb(self, name, shape, dt):
        g = self.nc.sbuf_tensor("%s_p%d" % (name, self.phase), list(shape), dt)
        t = g.__enter__()
        self._ctx.append(g)
        return t

    def ps(self, name, shape, dt=F32):
        g = self.nc.psum_tensor("%s_p%d" % (name, self.phase), list(shape), dt)
        t = g.__enter__()
        self._ctx.append(g)
        return t

    def lane(self, name, out=False):
        l = Lane("%s_p%d" % (name, self.phase))
        self.lanes.append(l)
        if out:
            self.out_lanes.append(l)
        return l

    def _track(self, ins, reads, writes):
        deps = ins.deps
        for b in reads:
            if b.last_w is not None:
                deps.append(b.last_w)
            if b.excl:
                for r in b.readers:
                    if r.eng != ins.eng:
                        deps.append(r)
        for b in writes:
            if b.last_w is not None:
                deps.append(b.last_w)
            deps.extend(b.readers)
        for b in reads:
            if not ins.is_dma:
                b.readers = [r for r in b.readers if r.is_dma or r.eng != ins.eng]
            b.readers.append(ins)
        for b in writes:
            b.last_w = ins
            b.readers = []

    def op(self, eng, emit, reads=(), writes=()):
        ins = Ins(eng, emit)
        ins.phase = self.phase
        self._track(ins, reads, writes)
        self.streams[eng].append(ins)
        return ins

    def dma(self, eng, lane, out, in_, reads=(), writes=()):
        return self.dma_group(eng, lane, [(out, in_)], reads, writes)

    def dma_group(self, eng, lane, pairs, reads=(), writes=()):
        pairs = list(pairs)
        ins = Ins(eng, lambda e: [e.dma_start(out=o, in_=i) for (o, i) in pairs])
        ins.is_dma = True
        ins.phase = self.phase
        ins.lane = lane
        if lane.last is not None:
            ins.deps.append(lane.last)
        lane.count += 16 * len(pairs)
        ins.lane_val = lane.count
        lane.last = ins
        self._track(ins, reads, writes)
        self.streams[eng].append(ins)
        return ins

    def cc_allgather(self, in_ap, in_buf, out_ap, out_buf, groups):
        lane = self.lane("cc%d" % len(self.lanes))
        ins = Ins("pool", lambda e: [e.collective_compute("AllGather", ALU.bypass, replica_groups=groups,
                                                          ins=[in_ap], outs=[out_ap])])
        ins.is_dma = True
        ins.phase = self.phase
        ins.lane = lane
        ins.cc_inc = 1
        lane.count += 1
        ins.lane_val = lane.count
        lane.last = ins
        self._track(ins, [in_buf], [out_buf])
        self.streams["pool"].append(ins)
        return ins

    def emit(self, final=True):
        nc = self.nc
        cur = self.phase
        for e in ENGS:
            for ins in self.streams[e]:
                for d in ins.deps:
                    if not d.is_dma and d.phase == cur:
                        d.inc = True
        for e in ENGS:
            c = self.ord[e]
            for ins in self.streams[e]:
                if ins.inc and not ins.is_dma:
                    c += 1
                    ins.ordinal = c
            self.ord[e] = c
        if self.sems is None:
            self.sems = {}
            for e in ENGS:
                g = nc.semaphore("sem_" + e)
                self.sems[e] = g.__enter__()
                self._semctx.append(g)
        sems = self.sems
        for l in self.lanes:
            if l.sem is None:
                g = nc.semaphore("lane_" + l.name)
                l.sem = g.__enter__()
                self._semctx.append(g)
        out_lanes = self.out_lanes if final else []

        def run(ename, eng):
            seen = {}
            for ins in self.streams[ename]:
                need = {}
                for d in ins.deps:
                    if d.is_dma:
                        key = ("l", id(d.lane))
                        sem = d.lane.sem
                        val = d.lane_val
                    else:
                        if d.phase != cur:
                            continue
                        if d.eng == ename and ename == "pe":
                            continue
                        key = ("e", d.eng)
                        sem = sems[d.eng]
                        val = d.ordinal
                    if seen.get(key, 0) >= val:
                        continue
                    if key not in need or need[key][1] < val:
                        need[key] = (sem, val)
                for key, (sem, val) in need.items():
                    eng.wait_ge(sem, val)
                    seen[key] = val
                bi = ins.emit(eng)
                if ins.is_dma:
                    for b1 in bi:
                        b1.then_inc(ins.lane.sem, getattr(ins, "cc_inc", 16))
                elif ins.inc:
                    bi.then_inc(sems[ename], 1)
            if ename == "sp":
                for l in out_lanes:
                    if l.count:
                        eng.wait_ge(l.sem, l.count)

        with nc.Block() as block:
            @block.tensor
            def _(t):
                run("pe", t)

            @block.scalar
            def _(t):
                run("act", t)

            @block.vector
            def _(t):
                run("dve", t)

            @block.gpsimd
            def _(t):
                run("pool", t)

            @block.sync
            def _(t):
                run("sp", t)
        self.streams = {e: [] for e in ENGS}
        self.phase += 1

    def close(self):
        for g in reversed(self._ctx):
            g.__exit__(None, None, None)
        self._ctx = []

    def finish(self):
        self.close()
        for g in reversed(self._semctx):
            g.__exit__(None, None, None)
        self._semctx = []


class Ring:
    def __init__(self, P, name, n, shape, dt, lanes=False, psum=False):
        self.slots = []
        for i in range(n):
            t = (P.ps if psum else P.sb)("%s%d" % (name, i), shape, dt)
            self.slots.append((t, Buf("%s%d" % (name, i), excl=psum), P.lane("%s%d" % (name, i)) if lanes else None))
        self.i = 0

    def next(self):
        s = self.slots[self.i % len(self.slots)]
        self.i += 1
        return s


def E(method, **kw):
    return lambda e: getattr(e, method)(**kw)


def mm_group(P, out_ap, out_buf, pairs, reads):
    n = len(pairs)
    for i, (l, r) in enumerate(pairs):
        P.op("pe", E("matmul", out=out_ap, lhsT=l, rhs=r, start=(i == 0), stop=(i == n - 1)),
             reads=reads, writes=[out_buf])


def make_segs(T, maxn=512):
    nseg = (T + maxn - 1) // maxn
    base = (T + nseg - 1) // nseg
    segs = []
    a = 0
    while a < T:
        b = min(T, a + base)
        segs.append((a, b))
        a = b
    return segs


class Common:
    def __init__(self, P, nc, ident_dram, wslots=3, wsize=6144):
        self.P = P
        self.psum = Ring(P, "bank", 8, [128, 512], F32, psum=True)
        self.wring = Ring(P, "wr", wslots, [128, wsize], BF16, lanes=True)
        self.ident = P.sb("ident_sb", [128, 128], F32)
        self.ident_b = Buf("ident")
        self.ones = P.sb("ones_sb", [128, 128], F32)
        self.ones_b = Buf("ones")
        self.cl = P.lane("const")
        P.dma("sp", self.cl, self.ident[:], ident_dram, writes=[self.ident_b])
        P.op("dve", E("memset", ap=self.ones[:], constant=1.0), writes=[self.ones_b])
        self.epsc = P.sb("epsc", [128, 2], F32)
        self.eps_b = Buf("epsc")
        self.eps_col = {EPS: 0}
        P.op("dve", E("memset", ap=self.epsc[:], constant=EPS), writes=[self.eps_b])
        self.sq = Ring(P, "sq", 3, [128, 512], F32)
        self.tmp = Ring(P, "tmp", 4, [128, 512], F32)


def rsqrt_op(P, K, out, out_b, in_, in_b, scale, eps):
    P.op("act", E("activation", out=out, in_=in_, func=AF.Sqrt, bias=K.epsc[:in_.shape[0], K.eps_col[eps]:K.eps_col[eps] + 1],
                  scale=scale), reads=[in_b, K.eps_b], writes=[out_b])
    P.op("dve", E("reciprocal", out=out, in_=out), reads=[out_b], writes=[out_b])


def rmsnorm_fm(P, K, src, src_b, dst, dst_b, segs, vec, vec_b, wcol, rstd, rstd_b):
    for (a, b) in segs:
        n = b - a
        pb, pbb, _ = K.psum.next()
        for f in range(NF):
            sq, sqb, _ = K.sq.next()
            P.op("act", E("activation", out=sq[:, :n], in_=src[:, f, a:b], func=AF.Square),
                 reads=[src_b], writes=[sqb])
            P.op("pe", E("matmul", out=pb[:, :n], lhsT=K.ones[:], rhs=sq[:, :n], start=(f == 0), stop=(f == NF - 1)),
                 reads=[sqb, K.ones_b], writes=[pbb])
        rsqrt_op(P, K, rstd[:, a:b], rstd_b, pb[:, :n], pbb, 1.0 / D, EPS)
        for f in range(NF):
            P.op("dve", E("scalar_tensor_tensor", out=dst[:, f, a:b], in0=src[:, f, a:b],
                          scalar=vec[:, wcol + f:wcol + f + 1], in1=rstd[:, a:b], op0=ALU.mult, op1=ALU.mult),
                 reads=[src_b, rstd_b, vec_b], writes=[dst_b])


def ffn_fm(P, K, hT, hT_b, hn, hn_b, act, act_b, segs, wg, wu, wd):
    wgv = wg.rearrange("(k p) m -> p k m", p=128)
    wuv = wu.rearrange("(k p) m -> p k m", p=128)
    wdv = wd.rearrange("(k p) m -> p k m", p=128)
    for c in range(NCF):
        wt, wb, wl = K.wring.next()
        wv = wt[:, 0:4096].rearrange("p (k j m) -> p k j m", k=NF, j=2)
        P.dma_group("pool", wl, [(wv[:, :, 0, :], wgv[:, :, c * 128:(c + 1) * 128]),
                                 (wv[:, :, 1, :], wuv[:, :, c * 128:(c + 1) * 128])], writes=[wb])
        for (a, b) in segs:
            n = b - a
            pg, pgb, _ = K.psum.next()
            pu, pub, _ = K.psum.next()
            mm_group(P, pg[:, :n], pgb, [(wv[:, k, 0, :], hn[:, k, a:b]) for k in range(NF)], [wb, hn_b])
            mm_group(P, pu[:, :n], pub, [(wv[:, k, 1, :], hn[:, k, a:b]) for k in range(NF)], [wb, hn_b])
            st, stb, _ = K.tmp.next()
            P.op("act", E("activation", out=st[:, :n], in_=pg[:, :n], func=AF.Silu), reads=[pgb], writes=[stb])
            P.op("dve", E("tensor_tensor", out=act[:, c, a:b], in0=pu[:, :n], in1=st[:, :n], op=ALU.mult),
                 reads=[pub, stb], writes=[act_b])
    for m in range(NF):
        wt, wb, wl = K.wring.next()
        wv = wt[:, 0:NCF * 128].rearrange("p (k m) -> p k m", k=NCF)
        P.dma("pool", wl, wv, wdv[:, :, m * 128:(m + 1) * 128], writes=[wb])
        for (a, b) in segs:
            n = b - a
            pb, pbb, _ = K.psum.next()
            mm_group(P, pb[:, :n], pbb, [(wv[:, k, :], act[:, k, a:b]) for k in range(NCF)], [wb, act_b])
            P.op("dve", E("tensor_tensor", out=hT[:, m, a:b], in0=pb[:, :n], in1=hT[:, m, a:b], op=ALU.add),
                 reads=[pbb, hT_b], writes=[hT_b])


def phase_A(P, K, io, tile_T=516, xch=None):
    xa, w_in, w_out = io["xa"], io["sc_w_in"], io["sc_w_out"]
    h1, hn1 = io["h1"], io.get("hn1")
    T = tile_T
    ntile = TA // T
    assert ntile * T == TA
    segs = make_segs(T)
    vec = P.sb("vecA_sb", [128, 96], F32)
    vec_b = Buf("vecA")
    P.dma("sp", K.cl, vec[:], io["vecA"], writes=[vec_b])
    hT = P.sb("hT", [128, NF, T], F32)
    hT_b = Buf("hT")
    hn = P.sb("hn", [128, NF, T], BF16)
    hn_b = Buf("hn")
    act = P.sb("act", [128, NCF, T], BF16)
    act_b = Buf("act")
    y = act[:, 0:NF, :]
    rstd = P.sb("rstd", [128, T], F32)
    rstd_b = Buf("rstd")
    xs = Ring(P, "xs", 2, [128, D], F32, lanes=True)
    cur = Ring(P, "cu", 2, [128, T + 2], F32)
    bsb = Ring(P, "bsb", 2, [128, T], F32)
    acc = Ring(P, "acc", 2, [128, T], F32)
    halo = P.sb("halo", [128, NF, 2], F32)
    halo_b = [Buf("halo%d" % f) for f in range(NF)]
    P.op("dve", E("memset", ap=halo[:], constant=0.0), writes=halo_b)
    st_lane = P.lane("stA", out=(xch is None))
    w_in_v = w_in.rearrange("(k p) m -> p k m", p=128)
    w_out_v = w_out.rearrange("(k p) m -> p k m", p=128)
    h1_v = h1.rearrange("f p t -> p f t")
    hn1_v = hn1.rearrange("f p t -> p f t") if hn1 is not None else None

    for ti in range(ntile):
        t0 = ti * T
        for g0 in range(0, T, 128):
            gs = min(128, T - g0)
            xt, xb, xl = xs.next()
            P.dma("sp", xl, xt[:gs, :], xa[t0 + g0:t0 + g0 + gs, :], writes=[xb])
            for fq in range(4):
                pb, pbb, _ = K.psum.next()
                for j in range(4):
                    f = fq * 4 + j
                    P.op("pe", E("transpose", out=pb[:, j * 128:j * 128 + gs], in_=xt[:gs, f * 128:(f + 1) * 128],
                                 identity=K.ident[:gs, :gs]), reads=[xb, K.ident_b], writes=[pbb])
                P.op("act", E("activation", out=hT[:, fq * 4:(fq + 1) * 4, g0:g0 + gs],
                              in_=pb[:, :].rearrange("p (j t) -> p j t", t=128)[:, :, :gs], func=AF.Copy),
                     reads=[pbb], writes=[hT_b])
        rmsnorm_fm(P, K, hT, hT_b, hn, hn_b, segs, vec, vec_b, 0, rstd, rstd_b)
        for f in range(NF):
            wt, wb, wl = K.wring.next()
            wv = wt[:, 0:6144].rearrange("p (k j m) -> p k j m", k=NF, j=3)
            P.dma_group("pool", wl, [(wv[:, :, j, :], w_in_v[:, :, j * D + f * 128:j * D + (f + 1) * 128])
                                     for j in range(3)], writes=[wb])
            cu, cub, _ = cur.next()
            bs, bsbb, _ = bsb.next()
            ac, acb, _ = acc.next()
            P.op("act", E("activation", out=cu[:, 0:2], in_=halo[:, f, :], func=AF.Copy),
                 reads=[halo_b[f]], writes=[cub])
            for (a, b) in segs:
                n = b - a
                pbk = [K.psum.next() for _ in range(3)]
                for j in range(3):
                    mm_group(P, pbk[j][0][:, :n], pbk[j][1],
                             [(wv[:, k, j, :], hn[:, k, a:b]) for k in range(NF)], [wb, hn_b])
                ut, utb, _ = K.tmp.next()
                P.op("act", E("activation", out=ut[:, :n], in_=pbk[2][0][:, :n], func=AF.Copy),
                     reads=[pbk[2][1]], writes=[utb])
                P.op("dve", E("tensor_tensor", out=cu[:, 2 + a:2 + b], in0=pbk[1][0][:, :n], in1=ut[:, :n], op=ALU.mult),
                     reads=[pbk[1][1], utb], writes=[cub])
                P.op("act", E("activation", out=bs[:, a:b], in_=pbk[0][0][:, :n], func=AF.Copy),
                     reads=[pbk[0][1]], writes=[bsbb])
            c0, c1, c2 = 32 + f, 48 + f, 64 + f
            P.op("dve", E("tensor_scalar", out=ac[:, :], in0=cu[:, 2:2 + T], scalar1=vec[:, c2:c2 + 1], scalar2=None,
                          op0=ALU.mult), reads=[cub, vec_b], writes=[acb])
            P.op("dve", E("scalar_tensor_tensor", out=ac[:, :], in0=cu[:, 1:1 + T], scalar=vec[:, c1:c1 + 1],
                          in1=ac[:, :], op0=ALU.mult, op1=ALU.add), reads=[cub, vec_b, acb], writes=[acb])
            P.op("dve", E("scalar_tensor_tensor", out=ac[:, :], in0=cu[:, 0:T], scalar=vec[:, c0:c0 + 1],
                          in1=ac[:, :], op0=ALU.mult, op1=ALU.add), reads=[cub, vec_b, acb], writes=[acb])
            P.op("dve", E("tensor_tensor", out=y[:, f, :], in0=ac[:, :], in1=bs[:, :], op=ALU.mult),
                 reads=[acb, bsbb], writes=[act_b])
            P.op("act", E("activation", out=halo[:, f, :], in_=cu[:, T:T + 2], func=AF.Copy),
                 reads=[cub], writes=[halo_b[f]])
        for mp in range(NF // 2):
            wt, wb, wl = K.wring.next()
            wv = wt[:, 0:4096].rearrange("p (k m) -> p k m", k=NF)
            P.dma("pool", wl, wv, w_out_v[:, :, mp * 256:(mp + 1) * 256], writes=[wb])
            for mi in range(2):
                m = mp * 2 + mi
                for (a, b) in segs:
                    n = b - a
                    pb, pbb, _ = K.psum.next()
                    mm_group(P, pb[:, :n], pbb, [(wv[:, k, mi * 128:(mi + 1) * 128], y[:, k, a:b]) for k in range(NF)],
                             [wb, act_b])
                    P.op("dve", E("tensor_tensor", out=hT[:, m, a:b], in0=pb[:, :n], in1=hT[:, m, a:b], op=ALU.add),
                         reads=[pbb, hT_b], writes=[hT_b])
        rmsnorm_fm(P, K, hT, hT_b, hn, hn_b, segs, vec, vec_b, 16, rstd, rstd_b)
        ffn_fm(P, K, hT, hT_b, hn, hn_b, act, act_b, segs, io["wg"], io["wu"], io["wd"])
        if xch is None:
            P.dma("sp", st_lane, h1_v[:, :, t0:t0 + T], hT[:, :, :], reads=[hT_b])
            rmsnorm_fm(P, K, hT, hT_b, hn, hn_b, segs, vec, vec_b, 80, rstd, rstd_b)
            P.dma("sp", st_lane, hn1_v[:, :, t0:t0 + T], hn[:, :, :], reads=[hn_b])
        else:
            P.dma("sp", st_lane, h1_v[:, :, t0:t0 + T], hT[:, :, :], reads=[hT_b], writes=[xch["h1_b"]])
            rmsnorm_fm(P, K, hT, hT_b, hn, hn_b, segs, vec, vec_b, 80, rstd, rstd_b)
            for part, (c0, c1) in enumerate(((0, 512), (512, T))):
                xi = 2 * ti + part
                P.dma("sp", st_lane, xch["hn1_loc"][xi].rearrange("(f p) t -> p f t", p=128)[:, :, 0:c1 - c0],
                      hn[:, :, c0:c1], reads=[hn_b], writes=[xch["hn1_loc_b"][xi]])
                P.cc_allgather(xch["hn1_loc"][xi], xch["hn1_loc_b"][xi], xch["hn1_all"][xi], xch["hn1_all_b"][xi], PAIRS)
    return st_lane


def phase_C(P, K, io, tile_T=512, xch=None):
    h1c, og, gw_out, out = io["h1c"], io.get("ogc"), io["gdn_w_out"], io["out"]
    T = tile_T
    ntile = TC // T
    segs = make_segs(T)
    vec = P.sb("vecC_sb", [128, 32], F32)
    vec_b = Buf("vecC")
    P.dma("sp", K.cl, vec[:], io["vecC"], writes=[vec_b])
    hT = P.sb("hTc", [128, NF, T], F32)
    hT_b = Buf("hTc")
    hn = P.sb("hnc", [128, NF, T], BF16)
    hn_b = Buf("hnc")
    act = P.sb("actc", [128, NCF, T], BF16)
    act_b = Buf("actc")
    ogt = act[:, 0:32, :]
    rstd = P.sb("rstdc", [128, T], F32)
    rstd_b = Buf("rstdc")
    ot = Ring(P, "ot", 2, [128, D], F32, lanes=True)
    ld = P.lane("ldC")
    h1_v = h1c.rearrange("f p t -> p f t")
    og_v = og.rearrange("f p t -> p f t") if og is not None else None
    wo_v = gw_out.rearrange("(k p) m -> p k m", p=128)
    hoff = 0 if xch is None else 16
    if xch is not None:
        mk = P.sb("mk_sb", [128, 2], F32)
        mk_b = Buf("mk")
        P.dma("sp", K.cl, mk[:], io["mk"], writes=[mk_b])
    for l in ot.slots:
        P.out_lanes.append(l[2])
    for ti in range(ntile):
        t0 = ti * T
        if xch is None:
            P.dma("sp", ld, hT[:, :, :], h1_v[:, :, t0:t0 + T], writes=[hT_b])
            P.dma_group("sp", ld, [(ogt[:, 0:16, :], og_v[:, 0:16, t0:t0 + T]),
                                   (ogt[:, 16:32, :], og_v[:, 16:32, t0:t0 + T])], writes=[act_b])
        else:
            P.dma("sp", ld, hT[:, :, :], h1_v[:, :, hoff + t0:hoff + t0 + T], reads=[xch["h1_b"]], writes=[hT_b])
            for hf in range(2):
                for cand, dst, dst_b in ((0, ogt, act_b), (1, hn, hn_b)):
                    pieces = []
                    s0 = 64 + cand * 2048 + t0
                    end = s0 + T
                    while s0 < end:
                        bt, cq = s0 // TB, s0 % TB
                        n = min(end - s0, TB - cq)
                        src = xch["og_all"][bt].rearrange("(rf p) t -> p rf t", p=128)[:, hf * 16:(hf + 1) * 16, cq:cq + n]
                        d0 = s0 - (64 + cand * 2048 + t0)
                        dd = dst[:, hf * 16:(hf + 1) * 16, d0:d0 + n] if cand == 0 else dst[:, :, d0:d0 + n]
                        pieces.append((dd, src, xch["og_all_b"][bt]))
                        s0 += n
                    P.dma_group("sp", ld, [(d_, s_) for d_, s_, _ in pieces], reads=[b_ for _, _, b_ in pieces],
                                writes=[dst_b])
                oh = ogt[:, hf * 16:(hf + 1) * 16, :]
                P.op("dve", E("tensor_scalar", out=oh, in0=oh, scalar1=mk[:, 0:1], scalar2=None, op0=ALU.mult),
                     reads=[act_b, mk_b], writes=[act_b])
                P.op("dve", E("scalar_tensor_tensor", out=oh, in0=hn[:, :, :], scalar=mk[:, 1:2], in1=oh,
                              op0=ALU.mult, op1=ALU.add), reads=[hn_b, act_b, mk_b], writes=[act_b])
        for m in range(NF):
            wt, wb, wl = K.wring.next()
            wv = wt[:, 0:4096].rearrange("p (k m) -> p k m", k=32)
            P.dma("pool", wl, wv, wo_v[:, :, m * 128:(m + 1) * 128], writes=[wb])
            for (a, b) in segs:
                n = b - a
                pb, pbb, _ = K.psum.next()
                mm_group(P, pb[:, :n], pbb, [(wv[:, k, :], ogt[:, k, a:b]) for k in range(32)], [wb, act_b])
                P.op("dve", E("tensor_tensor", out=hT[:, m, a:b], in0=pb[:, :n], in1=hT[:, m, a:b], op=ALU.add),
                     reads=[pbb, hT_b], writes=[hT_b])
        rmsnorm_fm(P, K, hT, hT_b, hn, hn_b, segs, vec, vec_b, 0, rstd, rstd_b)
        ffn_fm(P, K, hT, hT_b, hn, hn_b, act, act_b, segs, io["wg"], io["wu"], io["wd"])
        rmsnorm_fm(P, K, hT, hT_b, hT, hT_b, segs, vec, vec_b, 16, rstd, rstd_b)
        for g0 in range(0, T, 128):
            o_t, ob, ol = ot.next()
            for fq in range(4):
                pb, pbb, _ = K.psum.next()
                for j in range(4):
                    f = fq * 4 + j
                    P.op("pe", E("transpose", out=pb[:, j * 128:(j + 1) * 128], in_=hT[:, f, g0:g0 + 128],
                                 identity=K.ident[:, :]), reads=[hT_b, K.ident_b], writes=[pbb])
                if fq % 2:
                    P.op("act", E("activation", out=o_t[:, fq * 512:(fq + 1) * 512], in_=pb[:, :], func=AF.Copy),
                         reads=[pbb], writes=[ob])
                else:
                    P.op("dve", E("tensor_copy", out=o_t[:, fq * 512:(fq + 1) * 512], in_=pb[:, :]),
                         reads=[pbb], writes=[ob])
            P.dma("sp", ol, out[t0 + g0:t0 + g0 + 128, :], o_t[:, :], reads=[ob])


def _fm(v):
    v = np.asarray(v, np.float32)
    return np.ascontiguousarray(v.reshape(-1, 128).T)


def build_A(tile_T=516):
    nc = bass.Bass("TRN2", target_bir_lowering=False)
    io = {}

    def inp(name, shape, dt=F32):
        io[name] = nc.dram_tensor(name, list(shape), dt, kind="ExternalInput").ap()

    inp("xa", [TA, D])
    inp("ident", [128, 128])
    inp("vecA", [128, 96])
    inp("sc_w_in", [D, 3 * D])
    inp("sc_w_out", [D, D])
    inp("wg", [D, DFF])
    inp("wu", [D, DFF])
    inp("wd", [DFF, D])
    io["h1"] = nc.dram_tensor("h1", [NF, 128, TA], F32, kind="ExternalOutput").ap()
    io["hn1"] = nc.dram_tensor("hn1", [NF, 128, TA], BF16, kind="ExternalOutput").ap()
    P = Prog(nc)
    K = Common(P, nc, io["ident"])
    phase_A(P, K, io, tile_T)
    P.emit()
    P.close()
    return nc


def host_inputs_A(inp):
    x = np.asarray(inp["x"], np.float32)
    meta = np.asarray(inp["meta_tokens"], np.float32)
    conv = np.asarray(inp["sc_conv_w"], np.float32)[0]
    vecA = np.concatenate([_fm(inp["mixer_norm"][0]), _fm(inp["ffn_norm"][0]),
                           _fm(conv[0]), _fm(conv[1]), _fm(conv[2]), _fm(inp["mixer_norm"][1])], axis=1)
    shared = {
        "ident": np.eye(128, dtype=np.float32),
        "vecA": np.ascontiguousarray(vecA),
        "sc_w_in": np.ascontiguousarray(np.asarray(inp["sc_w_in"], np.float32)[0]),
        "sc_w_out": np.ascontiguousarray(np.asarray(inp["sc_w_out"], np.float32)[0]),
        "wg": np.ascontiguousarray(np.asarray(inp["ffn_w_gate"], np.float32)[0]),
        "wu": np.ascontiguousarray(np.asarray(inp["ffn_w_up"], np.float32)[0]),
        "wd": np.ascontiguousarray(np.asarray(inp["ffn_w_down"], np.float32)[0]),
    }
    maps = []
    for c in range(8):
        b, r = c // 2, c % 2
        if r == 0:
            xa = np.concatenate([meta, x[b, 0:2048]], axis=0)
        else:
            xa = x[b, 2032:4096]
        m = dict(shared)
        m["xa"] = np.ascontiguousarray(xa)
        maps.append(m)
    return maps


def build_C(tile_T=512):
    nc = bass.Bass("TRN2", target_bir_lowering=False)
    io = {}

    def inp(name, shape, dt=F32):
        io[name] = nc.dram_tensor(name, list(shape), dt, kind="ExternalInput").ap()

    inp("h1c", [NF, 128, TC])
    inp("ogc", [32, 128, TC], BF16)
    inp("ident", [128, 128])
    inp("vecC", [128, 32])
    inp("gdn_w_out", [2 * D, D])
    inp("wg", [D, DFF])
    inp("wu", [D, DFF])
    inp("wd", [DFF, D])
    io["out"] = nc.dram_tensor("out", [TC, D], F32, kind="ExternalOutput").ap()
    P = Prog(nc)
    K = Common(P, nc, io["ident"])
    phase_C(P, K, io, tile_T)
    P.emit()
    P.close()
    return nc


NEG = -30000.0
DBG = ""
DBGSTEP = 99
POOLENG = "dve"
STACKED = True
EXPF = AF.Exp
TB = 320
NCH = TB // 64
DK = 128


def phase_B(P, K, io, ntile=None, xch=None):
    hn1g, gw_in, og = io.get("hn1g"), io["gw_in"], io.get("og")
    T = TB
    ntile = ntile or TG // T
    nc = P.nc
    vec = P.sb("vecB_sb", [128, 129], F32)
    vec_b = Buf("vecB")
    P.dma("sp", K.cl, vec[:], io["vecB"], writes=[vec_b])
    row = P.sb("rowB_sb", [128, 32], F32)
    row_b = Buf("rowB")
    P.dma("sp", K.cl, row[:], io["rowB"], writes=[row_b])
    cB = P.sb("cB_sb", [64, 192], F32)
    cB_b = Buf("cB")
    P.dma("sp", K.cl, cB[:], io["cB"], writes=[cB_b])
    Umat = cB[:, 0:64]
    NEGI = cB[:, 64:128]
    NEGS = cB[:, 128:192]
    identb = P.sb("identb", [128, 128], BF16)
    identb_b = Buf("identb")
    P.op("dve", E("tensor_copy", out=identb[:], in_=K.ident[:]), reads=[K.ident_b], writes=[identb_b])
    onec = P.sb("onec", [128, 1], F32)
    onec_b = Buf("onec")
    P.op("dve", E("memset", ap=onec[:], constant=1.0), writes=[onec_b])
    nea = P.sb("nea", [64, 16], F32)
    nea_b = Buf("nea")
    P.op("act", E("activation", out=nea[:], in_=row[0:64, 0:16], func=AF.Exp), reads=[row_b], writes=[nea_b])
    P.op("dve", E("tensor_scalar", out=nea[:], in0=nea[:], scalar1=-1.0, scalar2=None, op0=ALU.mult),
         reads=[nea_b], writes=[nea_b])
    wba = P.sb("wba", [128, NF, 32], BF16)
    wba_b = Buf("wba")
    gw_v = gw_in.rearrange("(k p) m -> p k m", p=128)
    P.dma("pool", P.lane("wba"), wba[:], gw_v[:, :, 6144:6176], writes=[wba_b])
    hnr = Ring(P, "hnB", 2, [128, NF, T], BF16, lanes=True)
    qf = P.sb("qf", [128, 8, T], BF16)
    kf = P.sb("kf", [128, 8, T], BF16)
    vf = P.sb("vf", [128, 16, T], BF16)
    zs = P.sb("zs", [128, 16, T], BF16)
    qf_b, kf_b, vf_b, zs_b = Buf("qf"), Buf("kf"), Buf("vf"), Buf("zs")
    ogr = Ring(P, "ogB", 2, [128, 16, T], BF16, lanes=True)
    if xch is None:
        for sl in ogr.slots:
            P.out_lanes.append(sl[2])
    pcr = Ring(P, "pc", 2, [128, T + 3], F32)
    accr = Ring(P, "accB", 2, [128, T], F32)
    silr = Ring(P, "sil", 2, [128, T], F32)
    rinr = Ring(P, "rin", 2, [128, T], F32)
    halo = P.sb("haloB", [128, 32, 3], F32)
    halo_b = [Buf("haloB%d" % i) for i in range(32)]
    P.op("dve", E("memset", ap=halo[:], constant=0.0), writes=halo_b)
    beta = P.sb("beta", [64, NCH, 16], F32)
    lb = P.sb("lb", [64, NCH, 16], F32)
    gtm = P.sb("gtm", [64, NCH, 16], F32)
    sm_b = Buf("small")
    S = P.sb("S", [128, 16, 128], F32)
    S_b = [Buf("S0"), Buf("S1")]
    Sb = P.sb("Sbf", [128, 16, 128], BF16)
    Sb_b = [Buf("Sb0"), Buf("Sb1")]
    P.op("dve", E("memset", ap=S[:], constant=0.0), writes=S_b)
    P.op("dve", E("memset", ap=Sb[:], constant=0.0), writes=Sb_b)
    def mk(name, shape, dt):
        if STACKED:
            shape = [shape[0]] + [1] * (len(shape) - 1)
        return [(P.sb("%s%d" % (name, i), shape, dt), Buf("%s%d" % (name, i))) for i in range(2)]
    gcs = mk("gcs", [64, 16], F32)
    gbs = mk("gbs", [64, 16], F32)
    egs = mk("egs", [64, 16], F32)
    begs = mk("begs", [64, 16], F32)
    ekts = mk("ekts", [64, 16], F32)
    gtot = mk("gtot", [128, 16], F32)
    gls = mk("gls", [128, 16], F32)
    Dg1 = mk("Dg1", [64, 8, 64], F32)
    Dg2 = mk("Dg2", [64, 8, 64], F32)
    t1, t2, EI, ES = Dg1, Dg2, Dg1, Dg2
    EG = mk("EG", [128, 8, 64], F32)
    qg = mk("qg", [128, 8, 64], BF16)
    qkT = mk("qkT", [64, 8, 64], BF16)
    Pm = [mk("Pm%d" % j, [64, 8, 64], BF16) for j in range(2)]
    PT = [mk("PT%d" % j, [64, 8, 64], BF16) for j in range(2)]
    Rf = mk("Rf", [64, 8, 64], F32)
    Rb = mk("Rb", [64, 8, 64], BF16)
    vb = mk("vb", [64, 8, 128], BF16)
    kbg = mk("kbg", [64, 8, 128], BF16)
    kt = mk("kt", [64, 8, 128], BF16)
    nwT = mk("nwT", [128, 8, 64], BF16)
    vn = mk("vn", [64, 8, 128], BF16)
    osb = mk("osb", [128, 8, 64], F32)
    osq = mk("osq", [128, 8, 64], F32)
    orst = osq
    bank = [(K.psum.slots[i][0], K.psum.slots[i][1]) for i in range(8)]
    identI = K.ident[0:64, 0:64]

    def bc(ap, shape):
        return ap.to_broadcast(list(shape))


    if STACKED:
        cS = P.sb("cBst_sb", [128, 256], F32)
        cS_b = Buf("cBst")
        P.dma("sp", K.cl, cS[:], io["cB2"], writes=[cS_b])
        U_st, NEGI_st, NEGS_st, I_st = cS[:, 0:64], cS[:, 64:128], cS[:, 128:192], cS[:, 192:256]
        blk1m = P.sb("blk1", [128, 128], F32)
        sel = [P.sb("sel%d" % i, [128, 128], F32) for i in range(2)]
        cm_b = Buf("cmats")
        P.op("dve", E("memset", ap=blk1m[:], constant=0.0), writes=[cm_b])
        P.op("dve", E("memset", ap=blk1m[0:64, 0:64], constant=1.0), writes=[cm_b])
        P.op("dve", E("memset", ap=blk1m[64:128, 64:128], constant=1.0), writes=[cm_b])
        for i in range(2):
            P.op("dve", E("memset", ap=sel[i][:], constant=0.0), writes=[cm_b])
            P.op("dve", E("memset", ap=sel[i][i * 64:(i + 1) * 64, :], constant=1.0), writes=[cm_b])
        dtb_st = P.sb("dtb_st", [128, 8], F32)
        nea_st = P.sb("nea_st", [128, 8], F32)
        st_b = Buf("stconst")
        for i in range(2):
            ps_ = slice(i * 64, (i + 1) * 64)
            P.op("act", E("activation", out=dtb_st[ps_, :], in_=row[ps_, 16 + i * 8:24 + i * 8], func=AF.Copy),
                 reads=[row_b], writes=[st_b])
            P.op("act", E("activation", out=nea_st[ps_, :], in_=row[ps_, i * 8:(i + 1) * 8], func=AF.Exp),
                 reads=[row_b], writes=[st_b])
        P.op("dve", E("tensor_scalar", out=nea_st[:], in0=nea_st[:], scalar1=-1.0, scalar2=None, op0=ALU.mult),
             reads=[st_b], writes=[st_b])
        betaS = P.sb("betaS", [128, NCH, 8], F32)
        lbS = P.sb("lbS", [128, NCH, 8], F32)
        gS = P.sb("gS", [128, NCH, 8], F32)
        smS_b = Buf("smallS")

        def one(name, shape, dt):
            return P.sb(name, shape, dt), Buf(name)
        gcS, gcS_b = one("gcS", [128, 8], F32)
        gbS, gbS_b = one("gbS", [128, 8], F32)
        egS, egS_b = one("egS", [128, 8], F32)
        begS, begS_b = one("begS", [128, 8], F32)
        ektS, ektS_b = one("ektS", [128, 8], F32)
        gtS, gtS_b = one("gtS", [128, 8], F32)
        glS, glS_b = one("glS", [128, 16], F32)
        D1, D1_b = one("D1", [128, 8, 64], F32)
        D2, D2_b = one("D2", [128, 8, 64], F32)
        EGs = [one("EGs%d" % i, [128, 8, 64], F32) for i in range(2)]
        QGs = [one("QGs%d" % i, [128, 8, 64], BF16) for i in range(2)]
        QK, QK_b = one("QKs", [128, 8, 64], BF16)
        PmS = [one("PmS%d" % i, [128, 8, 64], BF16) for i in range(2)]
        PTS = [one("PTS%d" % i, [128, 8, 64], BF16) for i in range(2)]
        RfS, RfS_b = one("RfS", [128, 8, 64], F32)
        RbS, RbS_b = one("RbS", [128, 8, 64], BF16)
        VB, VB_b = one("VBs", [128, 8, 128], BF16)
        KBG, KBG_b = one("KBGs", [128, 8, 128], BF16)
        KT, KT_b = one("KTs", [128, 8, 128], BF16)
        NW = [one("NWs%d" % i, [128, 8, 64], BF16) for i in range(2)]
        VN, VN_b = one("VNs", [128, 8, 128], BF16)
        OB = [one("OBs%d" % i, [128, 8, 64], F32) for i in range(2)]
        OQ = [one("OQs%d" % i, [128, 8, 64], F32) for i in range(2)]

    def stacked_tail(ti, t0, hn, hn_b):
        HS = (slice(0, 64), slice(64, 128))
        pba, pba_b = bank[0]
        wba4 = wba[:, :, :].rearrange("p k (two h) -> p k two h", two=2)
        for c in range(NCH):
            for hh in range(2):
                mm_group(P, pba[HS[hh], c * 16:(c + 1) * 16], pba_b,
                         [(hn[:, k, c * 64:(c + 1) * 64], wba4[:, k, :, hh * 8:(hh + 1) * 8]) for k in range(NF)],
                         [hn_b, wba_b])
        pv = pba[:, 0:NCH * 16].rearrange("p (c two j) -> p c two j", two=2, j=8)
        P.op("act", E("activation", out=betaS[:, :, :], in_=pv[:, :, 0, :], func=AF.Sigmoid), reads=[pba_b], writes=[smS_b])
        P.op("act", E("activation", out=lbS[:, :, :], in_=betaS[:, :, :], func=AF.Ln), reads=[smS_b], writes=[smS_b])
        P.op("dve", E("tensor_tensor", out=gS[:, :, :], in0=pv[:, :, 1, :], in1=bc(dtb_st[:, :].unsqueeze(1), [128, NCH, 8]),
                      op=ALU.add), reads=[pba_b, st_b], writes=[smS_b])
        P.op("act", E("activation", out=gS[:, :, :], in_=gS[:, :, :], func=AF.Exp), reads=[smS_b], writes=[smS_b])
        P.op("act", E("activation", out=gS[:, :, :], in_=gS[:, :, :], func=AF.Ln, bias=onec[:, 0:1]),
             reads=[smS_b, onec_b], writes=[smS_b])
        P.op("dve", E("tensor_tensor", out=gS[:, :, :], in0=gS[:, :, :], in1=bc(nea_st[:, :].unsqueeze(1), [128, NCH, 8]),
                      op=ALU.mult), reads=[smS_b, st_b], writes=[smS_b])
        ogt, og_b, ol = ogr.next()
        for c in range(NCH):
            cs = slice(c * 64, (c + 1) * 64)
            pm, pm_b = bank[0]
            for hh in range(2):
                P.op("pe", E("matmul", out=pm[HS[hh], 256:264], lhsT=U_st[HS[hh], :], rhs=gS[HS[hh], c, :], start=True, stop=True),
                     reads=[cS_b, smS_b], writes=[pm_b])
            P.op("pe", E("matmul", out=pm[:, 264:272], lhsT=blk1m[:, :], rhs=gS[:, c, :], start=True, stop=True),
                 reads=[cm_b, smS_b], writes=[pm_b])
            for hh in range(2):
                P.op("pe", E("matmul", out=pm[:, 272 + hh * 8:280 + hh * 8], lhsT=sel[hh][:, :], rhs=gS[:, c, :], start=True, stop=True),
                     reads=[cm_b, smS_b], writes=[pm_b])
            P.op("dve", E("tensor_copy", out=gcS[:, :], in_=pm[:, 256:264]), reads=[pm_b], writes=[gcS_b])
            P.op("dve", E("tensor_copy", out=gtS[:, :], in_=pm[:, 264:272]), reads=[pm_b], writes=[gtS_b])
            P.op("act", E("activation", out=glS[:, :], in_=pm[:, 272:288], func=AF.Exp), reads=[pm_b], writes=[glS_b])
            P.op("dve", E("tensor_tensor", out=gbS[:, :], in0=gcS[:, :], in1=lbS[:, c, :], op=ALU.add),
                 reads=[gcS_b, smS_b], writes=[gbS_b])
            P.op("act", E("activation", out=egS[:, :], in_=gcS[:, :], func=AF.Exp), reads=[gcS_b], writes=[egS_b])
            P.op("dve", E("tensor_tensor", out=begS[:, :], in0=egS[:, :], in1=betaS[:, c, :], op=ALU.mult),
                 reads=[egS_b, smS_b], writes=[begS_b])
            P.op("dve", E("tensor_tensor", out=ektS[:, :], in0=gtS[:, :], in1=gcS[:, :], op=ALU.subtract),
                 reads=[gtS_b, gcS_b], writes=[ektS_b])
            P.op("act", E("activation", out=ektS[:, :], in_=ektS[:, :], func=AF.Exp), reads=[ektS_b], writes=[ektS_b])
            P.op("dve", E("tensor_tensor", out=D1[:, :, :], in0=bc(I_st.unsqueeze(1), [128, 8, 64]),
                          in1=bc(gcS[:, :].unsqueeze(2), [128, 8, 64]), op=ALU.mult), reads=[cS_b, gcS_b], writes=[D1_b])
            P.op("pool", E("tensor_tensor", out=D2[:, :, :], in0=bc(I_st.unsqueeze(1), [128, 8, 64]),
                           in1=bc(gbS[:, :].unsqueeze(2), [128, 8, 64]), op=ALU.mult), reads=[cS_b, gbS_b], writes=[D2_b])
            d1f = D1[:, :, :].rearrange("p h c -> p (h c)")
            d2f = D2[:, :, :].rearrange("p h c -> p (h c)")
            pG, pG_b = bank[1]
            pGb, pGb_b = bank[2]
            P.op("pe", E("matmul", out=pG[:, :], lhsT=blk1m[:, :], rhs=d1f, start=True, stop=True), reads=[cm_b, D1_b], writes=[pG_b])
            P.op("pe", E("matmul", out=pGb[:, :], lhsT=blk1m[:, :], rhs=d2f, start=True, stop=True), reads=[cm_b, D2_b], writes=[pGb_b])
            pR = [bank[3], bank[4]]
            for hh in range(2):
                P.op("pe", E("matmul", out=pR[hh][0][:, :], lhsT=sel[hh][:, :], rhs=d1f, start=True, stop=True),
                     reads=[cm_b, D1_b], writes=[pR[hh][1]])
            gcb = bc(gcS[:, :].unsqueeze(2), [128, 8, 64])
            v3 = lambda t: t[:, :].rearrange("p (h c) -> p h c", c=64)
            P.op("dve", E("tensor_tensor", out=D1[:, :, :], in0=v3(pG), in1=gcb, op=ALU.subtract), reads=[pG_b, gcS_b], writes=[D1_b])
            P.op("dve", E("tensor_tensor", out=D2[:, :, :], in0=v3(pGb), in1=gcb, op=ALU.subtract), reads=[pGb_b, gcS_b], writes=[D2_b])
            for hh in range(2):
                P.op("act", E("activation", out=EGs[hh][0][:, :, :], in_=v3(pR[hh][0]), func=AF.Exp),
                     reads=[pR[hh][1]], writes=[EGs[hh][1]])
            P.op("pool", E("tensor_tensor", out=D1[:, :, :], in0=D1[:, :, :], in1=bc(NEGI_st.unsqueeze(1), [128, 8, 64]), op=ALU.add),
                 reads=[D1_b, cS_b], writes=[D1_b])
            P.op("pool", E("tensor_tensor", out=D2[:, :, :], in0=D2[:, :, :], in1=bc(NEGS_st.unsqueeze(1), [128, 8, 64]), op=ALU.add),
                 reads=[D2_b, cS_b], writes=[D2_b])
            P.op("act", E("activation", out=D1[:, :, :], in_=D1[:, :, :], func=AF.Exp), reads=[D1_b], writes=[D1_b])
            P.op("act", E("activation", out=D2[:, :, :], in_=D2[:, :, :], func=AF.Exp), reads=[D2_b], writes=[D2_b])
            pair = lambda t: t[:, :, :].rearrange("p (k two) c -> p k two c", two=2)
            for hh in range(2):
                P.op("dve" if hh == 0 else "pool",
                     E("tensor_tensor", out=pair(QGs[hh][0]), in0=bc(qf[:, hh * 4:hh * 4 + 4, cs].unsqueeze(2), [128, 4, 2, 64]),
                       in1=pair(EGs[hh][0]), op=ALU.mult), reads=[qf_b, EGs[hh][1]], writes=[QGs[hh][1]])
            pK, pK_b = bank[5]
            for hh in range(2):
                for kh in range(4):
                    kg = hh * 4 + kh
                    P.op("pe", E("matmul", out=pK[HS[hh], kh * 128:kh * 128 + 64], lhsT=kf[:, kg, cs], rhs=kf[:, kg, cs],
                                 start=True, stop=True), reads=[kf_b], writes=[pK_b])
                    P.op("pe", E("matmul", out=pK[HS[hh], kh * 128 + 64:kh * 128 + 128], lhsT=kf[:, kg, cs], rhs=qf[:, kg, cs],
                                 start=True, stop=True), reads=[kf_b, qf_b], writes=[pK_b])
            pK4 = pK[:, :].rearrange("p (k j c) -> p k j c", j=2, c=64)
            p0, p0_b = PmS[0]
            P.op("dve", E("tensor_tensor", out=pair(p0), in0=bc(pK4[:, :, 0, :].unsqueeze(2), [128, 4, 2, 64]), in1=pair(D2), op=ALU.mult),
                 reads=[pK_b, D2_b], writes=[p0_b])
            P.op("dve", E("tensor_tensor", out=pair(QK), in0=bc(pK4[:, :, 1, :].unsqueeze(2), [128, 4, 2, 64]), in1=pair(D1), op=ALU.mult),
                 reads=[pK_b, D1_b], writes=[QK_b])
            P.op("act", E("activation", out=p0[:, :, :], in_=p0[:, :, :], func=AF.Copy, scale=-1.0), reads=[p0_b], writes=[p0_b])
            pT, pT_b = bank[6]
            pTv = pT[:, :].bitcast(BF16)
            for hh in range(2):
                for j in range(8):
                    P.op("pe", E("transpose", out=pTv[HS[hh], j * 128:(j + 1) * 128], in_=vf[:, hh * 8 + j, cs], identity=identb[:, :]),
                         reads=[vf_b, identb_b], writes=[pT_b])
            P.op("dve", E("tensor_tensor", out=VB[:, :, :], in0=pTv[:, :].rearrange("p (h v) -> p h v", v=128),
                          in1=bc(betaS[:, c, :].unsqueeze(2), [128, 8, 128]), op=ALU.mult), reads=[pT_b, smS_b], writes=[VB_b])
            pT2, pT2_b = bank[7]
            pT2v = pT2[:, :].bitcast(BF16)
            for hh in range(2):
                for kh in range(4):
                    P.op("pe", E("transpose", out=pT2v[HS[hh], kh * 128:(kh + 1) * 128], in_=kf[:, hh * 4 + kh, cs], identity=identb[:, :]),
                         reads=[kf_b, identb_b], writes=[pT2_b])
            ktm4 = bc(pT2v[:, 0:512].rearrange("p (k d) -> p k d", d=128).unsqueeze(2), [128, 4, 2, 128])
            pair2 = lambda t: t.rearrange("p (k two) d -> p k two d", two=2)
            P.op("dve", E("tensor_tensor", out=pair2(KBG[:, :, :]), in0=ktm4,
                          in1=pair2(bc(begS[:, :].unsqueeze(2), [128, 8, 128])), op=ALU.mult), reads=[pT2_b, begS_b], writes=[KBG_b])
            P.op("pool" if False else "dve", E("tensor_tensor", out=pair2(KT[:, :, :]), in0=ktm4,
                          in1=pair2(bc(ektS[:, :].unsqueeze(2), [128, 8, 128])), op=ALU.mult), reads=[pT2_b, ektS_b], writes=[KT_b])
            pA, pA_b = bank[1]
            pB, pB_b = bank[2]
            pC, pC_b = bank[3]
            pAv = pA[:, :].bitcast(BF16)
            for hh in range(2):
                for j in range(8):
                    P.op("pe", E("transpose", out=pAv[HS[hh], j * 64:(j + 1) * 64], in_=p0[HS[hh], j, :],
                                 identity=identb[HS[hh], hh * 64:(hh + 1) * 64]), reads=[p0_b, identb_b], writes=[pA_b])
            q0, q0_b = PTS[0]
            P.op("act", E("activation", out=q0[:, :, :], in_=pAv[:, 0:512].rearrange("p (h c) -> p h c", c=64), func=AF.Copy),
                 reads=[pA_b], writes=[q0_b])
            P.op("dve", E("tensor_tensor", out=RfS[:, :, :], in0=p0[:, :, :], in1=bc(I_st.unsqueeze(1), [128, 8, 64]), op=ALU.add),
                 reads=[p0_b, cS_b], writes=[RfS_b])
            P.op("act", E("activation", out=RbS[:, :, :], in_=RfS[:, :, :], func=AF.Copy), reads=[RfS_b], writes=[RbS_b])
            cur = 0
            for lvl in range(6):
                pc_, pc_b = PmS[cur]
                pt_, pt_b = PTS[cur]
                pn_, pn_b = PmS[1 - cur]
                ptn_, ptn_b = PTS[1 - cur]
                if lvl >= 1:
                    for hh in range(2):
                        for j in range(8):
                            P.op("pe", E("matmul", out=pC[HS[hh], j * 64:(j + 1) * 64], lhsT=pt_[HS[hh], j, :], rhs=RbS[HS[hh], j, :],
                                         start=True, stop=True), reads=[pt_b, RbS_b], writes=[pC_b])
                if lvl <= 4:
                    for hh in range(2):
                        for j in range(8):
                            P.op("pe", E("matmul", out=pA[HS[hh], j * 64:(j + 1) * 64], lhsT=pt_[HS[hh], j, :], rhs=pc_[HS[hh], j, :],
                                         start=True, stop=True), reads=[pt_b, pc_b], writes=[pA_b])
                    for hh in range(2):
                        for j in range(8):
                            P.op("pe", E("matmul", out=pB[HS[hh], j * 64:(j + 1) * 64], lhsT=pc_[HS[hh], j, :], rhs=pt_[HS[hh], j, :],
                                         start=True, stop=True), reads=[pt_b, pc_b], writes=[pB_b])
                if lvl >= 1:
                    P.op("dve", E("tensor_tensor", out=RfS[:, :, :], in0=v3(pC), in1=RfS[:, :, :], op=ALU.add),
                         reads=[pC_b, RfS_b], writes=[RfS_b])
                    P.op("act", E("activation", out=RbS[:, :, :], in_=RfS[:, :, :], func=AF.Copy), reads=[RfS_b], writes=[RbS_b])
                if lvl <= 4:
                    P.op("act", E("activation", out=pn_[:, :, :], in_=v3(pA), func=AF.Copy), reads=[pA_b], writes=[pn_b])
                    P.op("dve", E("tensor_copy", out=ptn_[:, :, :], in_=v3(pB)), reads=[pB_b], writes=[ptn_b])
                    cur = 1 - cur
            for hh in range(2):
                pW, pW_b = bank[4 + hh]
                for j in range(8):
                    P.op("pe", E("matmul", out=pW[:, j * 64:(j + 1) * 64], lhsT=KBG[HS[hh], j, :], rhs=RbS[HS[hh], j, :],
                                 start=True, stop=True), reads=[KBG_b, RbS_b], writes=[pW_b])
                P.op("act" if hh == 0 else "dve",
                     E("activation", out=NW[hh][0][:, :, :], in_=v3(pW), func=AF.Copy, scale=-1.0) if hh == 0 else
                     E("tensor_scalar", out=NW[hh][0][:, :, :], in0=v3(pW), scalar1=-1.0, scalar2=None, op0=ALU.mult),
                     reads=[pW_b], writes=[NW[hh][1]])
            for q in range(2):
                pV, pV_b = bank[6 + q]
                for hh in range(2):
                    for j4 in range(4):
                        j = q * 4 + j4
                        P.op("pe", E("matmul", out=pV[HS[hh], j4 * 128:(j4 + 1) * 128], lhsT=RbS[HS[hh], j, :], rhs=VB[HS[hh], j, :],
                                     start=True, stop=False), reads=[RbS_b, VB_b], writes=[pV_b])
                        P.op("pe", E("matmul", out=pV[HS[hh], j4 * 128:(j4 + 1) * 128], lhsT=NW[hh][0][:, j, :], rhs=Sb[:, hh * 8 + j, :],
                                     start=False, stop=True), reads=[NW[hh][1], Sb_b[hh]], writes=[pV_b])
                if q == 0:
                    P.op("act", E("activation", out=VN[:, 0:4, :], in_=pV[:, :].rearrange("p (h v) -> p h v", v=128), func=AF.Copy),
                         reads=[pV_b], writes=[VN_b])
                else:
                    P.op("dve", E("tensor_copy", out=VN[:, 4:8, :], in_=pV[:, :].rearrange("p (h v) -> p h v", v=128)),
                         reads=[pV_b], writes=[VN_b])
            pO = [bank[1], bank[2]]
            for hh in range(2):
                for j in range(8):
                    P.op("pe", E("matmul", out=pO[hh][0][:, j * 64:(j + 1) * 64], lhsT=Sb[:, hh * 8 + j, :], rhs=QGs[hh][0][:, j, :],
                                 start=True, stop=False), reads=[Sb_b[hh], QGs[hh][1]], writes=[pO[hh][1]])
                    P.op("pe", E("matmul", out=pO[hh][0][:, j * 64:(j + 1) * 64], lhsT=VN[HS[hh], j, :], rhs=QK[HS[hh], j, :],
                                 start=False, stop=True), reads=[VN_b, QK_b], writes=[pO[hh][1]])
            for hh in range(2):
                for q in range(2):
                    pS, pS_b = bank[4 + q] if hh == 0 else bank[6 + q]
                    for j4 in range(4):
                        j = q * 4 + j4
                        P.op("pe", E("matmul", out=pS[:, j4 * 128:(j4 + 1) * 128], lhsT=KT[HS[hh], j, :], rhs=VN[HS[hh], j, :],
                                     start=True, stop=True), reads=[KT_b, VN_b], writes=[pS_b])
                    h0 = hh * 8 + q * 4
                    sv = S[:, h0:h0 + 4, :]
                    e1 = "dve" if q == 0 else "pool"
                    P.op(e1, E("tensor_tensor", out=sv, in0=sv, in1=bc(glS[:, h0:h0 + 4].unsqueeze(2), [128, 4, 128]), op=ALU.mult),
                         reads=[S_b[hh], glS_b], writes=[S_b[hh]])
                    P.op("dve", E("tensor_tensor", out=sv, in0=pS[:, :].rearrange("p (h v) -> p h v", v=128), in1=sv, op=ALU.add),
                         reads=[pS_b, S_b[hh]], writes=[S_b[hh]])
                    P.op("act", E("activation", out=Sb[:, h0:h0 + 4, :], in_=sv, func=AF.Copy), reads=[S_b[hh]], writes=[Sb_b[hh]])
            for hh in range(2):
                ob, ob_b = OB[hh]
                oq, oq_b = OQ[hh]
                pO3 = v3(pO[hh][0])
                hs = slice(hh * 8, (hh + 1) * 8)
                P.op("dve", E("tensor_copy", out=ob[:, :, :], in_=pO3), reads=[pO[hh][1]], writes=[ob_b])
                P.op("act", E("activation", out=oq[:, :, :], in_=pO3, func=AF.Square), reads=[pO[hh][1]], writes=[oq_b])
                pQ, pQ_b = bank[3] if hh == 0 else bank[0]
                P.op("pe", E("matmul", out=pQ[:, :], lhsT=K.ones[:, :], rhs=oq[:, :, :].rearrange("p h c -> p (h c)"),
                             start=True, stop=True), reads=[K.ones_b, oq_b], writes=[pQ_b])
                rsqrt_op(P, K, oq[:, :, :].rearrange("p h c -> p (h c)"), oq_b, pQ[:, :], pQ_b, 1.0 / 128, EPS)
                P.op("dve", E("scalar_tensor_tensor", out=ob[:, :, :].rearrange("p h c -> p (h c)"),
                              in0=ob[:, :, :].rearrange("p h c -> p (h c)"), scalar=vec[:, 128:129],
                              in1=oq[:, :, :].rearrange("p h c -> p (h c)"), op0=ALU.mult, op1=ALU.mult),
                     reads=[ob_b, oq_b, vec_b], writes=[ob_b])
                P.op("pool", E("tensor_tensor", out=ogt[:, hs, cs], in0=ob[:, :, :], in1=zs[:, hs, cs], op=ALU.mult),
                     reads=[ob_b, zs_b], writes=[og_b])
        if xch is None:
            P.dma("sp", ol, og.rearrange("f p t -> p f t")[:, :, t0:t0 + T], ogt[:, :, :], reads=[og_b])
        else:
            P.dma("sp", ol, xch["og_loc"][ti].rearrange("(f p) t -> p f t", p=128), ogt[:, :, :],
                  reads=[og_b], writes=[xch["og_loc_b"][ti]])
            P.cc_allgather(xch["og_loc"][ti], xch["og_loc_b"][ti], xch["og_all"][ti], xch["og_all_b"][ti], PAIRS)

    for ti in range(ntile):
        t0 = ti * T
        hn, hn_b, hl = hnr.next()
        if xch is None:
            P.dma("sp", hl, hn[:, :, :], hn1g.rearrange("f p t -> p f t")[:, :, t0:t0 + T], writes=[hn_b])
        else:
            pieces = []
            s0 = t0
            while s0 < t0 + T:
                if s0 < 48:
                    s1 = min(48, t0 + T)
                    P.op("dve", E("memset", ap=hn[:, :, s0 - t0:s1 - t0], constant=0.0), writes=[hn_b])
                    s0 = s1
                    continue
                rk, j = (0, s0 - 48) if s0 < 2112 else (1, s0 - 2112 + 16)
                lim = 2112 if s0 < 2112 else TG
                q, cq = j // TA_TILE, j % TA_TILE
                part, pc0, plim = (0, cq, 512) if cq < 512 else (1, cq - 512, TA_TILE - 512)
                n = min(t0 + T - s0, lim - s0, plim - pc0)
                xi = 2 * q + part
                src = xch["hn1_all"][xi].rearrange("(r f p) t -> r p f t", r=2, p=128)[rk][:, :, pc0:pc0 + n]
                pieces.append((hn[:, :, s0 - t0:s0 - t0 + n], src, xch["hn1_all_b"][xi]))
                s0 += n
            P.dma_group("sp", hl, [(d_, s_) for d_, s_, _ in pieces], reads=[b_ for _, _, b_ in pieces], writes=[hn_b])
        for blk in range(24):
            wt, wb, wl = K.wring.next()
            wv = wt[:, 0:4096].rearrange("p (k m) -> p k m", k=NF)
            P.dma("pool", wl, wv, gw_v[:, :, blk * 256:(blk + 1) * 256], writes=[wb])
            for mi in range(2):
                mc = blk * 2 + mi
                pb, pbb, _ = K.psum.next()
                mm_group(P, pb[:, :T], pbb, [(wv[:, k, mi * 128:(mi + 1) * 128], hn[:, k, :]) for k in range(NF)],
                         [wb, hn_b])
                if mc >= 32:
                    h = mc - 32
                    P.op("act", E("activation", out=zs[:, h, :], in_=pb[:, :T], func=AF.Silu), reads=[pbb], writes=[zs_b])
                    continue
                pc, pcb, _ = pcr.next()
                ac, acb, _ = accr.next()
                P.op("act", E("activation", out=pc[:, 3:3 + T], in_=pb[:, :T], func=AF.Copy), reads=[pbb], writes=[pcb])
                P.op("act", E("activation", out=pc[:, 0:3], in_=halo[:, mc, :], func=AF.Copy),
                     reads=[halo_b[mc]], writes=[pcb])
                P.op("dve", E("tensor_scalar", out=ac[:, :], in0=pc[:, 3:3 + T], scalar1=vec[:, 96 + mc:97 + mc],
                              scalar2=None, op0=ALU.mult), reads=[pcb, vec_b], writes=[acb])
                for j in range(3):
                    P.op("dve", E("scalar_tensor_tensor", out=ac[:, :], in0=pc[:, j:j + T],
                                  scalar=vec[:, j * 32 + mc:j * 32 + mc + 1], in1=ac[:, :], op0=ALU.mult, op1=ALU.add),
                         reads=[pcb, vec_b, acb], writes=[acb])
                P.op("act", E("activation", out=halo[:, mc, :], in_=pc[:, T:T + 3], func=AF.Copy),
                     reads=[pcb], writes=[halo_b[mc]])
                if mc >= 16:
                    h = mc - 16
                    P.op("act", E("activation", out=vf[:, h, :], in_=ac[:, :], func=AF.Silu), reads=[acb], writes=[vf_b])
                    continue
                sl, slb, _ = silr.next()
                P.op("act", E("activation", out=sl[:, :], in_=ac[:, :], func=AF.Silu), reads=[acb], writes=[slb])
                sq, sqb, _ = K.sq.next()
                P.op("act", E("activation", out=sq[:, :T], in_=sl[:, :], func=AF.Square), reads=[slb], writes=[sqb])
                p2, p2b, _ = K.psum.next()
                P.op("pe", E("matmul", out=p2[:, :T], lhsT=K.ones[:], rhs=sq[:, :T], start=True, stop=True),
                     reads=[sqb, K.ones_b], writes=[p2b])
                ri, rib, _ = rinr.next()
                rsqrt_op(P, K, ri[:, :], rib, p2[:, :T], p2b, 1.0, EPS)
                if mc < 8:
                    P.op("dve", E("scalar_tensor_tensor", out=qf[:, mc, :], in0=sl[:, :], scalar=DK ** -0.5, in1=ri[:, :],
                                  op0=ALU.mult, op1=ALU.mult), reads=[slb, rib], writes=[qf_b])
                else:
                    P.op("dve", E("tensor_tensor", out=kf[:, mc - 8, :], in0=sl[:, :], in1=ri[:, :], op=ALU.mult),
                         reads=[slb, rib], writes=[kf_b])
        if STACKED:
            stacked_tail(ti, t0, hn, hn_b)
            continue
        if DBG == "s2":
            ogt, og_b, ol = ogr.next()
            P.op("dve", E("tensor_copy", out=ogt[:, :, :], in_=vf[:, :, :]), reads=[vf_b, qf_b, kf_b, zs_b], writes=[og_b])
            P.dma("sp", ol, og.rearrange("f p t -> p f t")[:, :, t0:t0 + T], ogt[:, :, :], reads=[og_b])
            continue
        pba, pba_b = bank[0]
        for c in range(NCH):
            mm_group(P, pba[0:64, c * 32:(c + 1) * 32], pba_b,
                     [(hn[:, k, c * 64:(c + 1) * 64], wba[:, k, :]) for k in range(NF)], [hn_b, wba_b])
        pv = pba[0:64, 0:NCH * 32].rearrange("p (c j) -> p c j", j=32)
        P.op("act", E("activation", out=beta[:, :, :], in_=pv[:, :, 0:16], func=AF.Sigmoid), reads=[pba_b], writes=[sm_b])
        P.op("act", E("activation", out=lb[:, :, :], in_=beta[:, :, :], func=AF.Ln), reads=[sm_b], writes=[sm_b])
        P.op("dve", E("tensor_tensor", out=gtm[:, :, :], in0=pv[:, :, 16:32],
                      in1=bc(row[0:64, 16:32].unsqueeze(1), [64, NCH, 16]), op=ALU.add),
             reads=[pba_b, row_b], writes=[sm_b])
        P.op("act", E("activation", out=gtm[:, :, :], in_=gtm[:, :, :], func=AF.Exp), reads=[sm_b], writes=[sm_b])
        P.op("act", E("activation", out=gtm[:, :, :], in_=gtm[:, :, :], func=AF.Ln, bias=onec[0:64, 0:1]),
             reads=[sm_b, onec_b], writes=[sm_b])
        P.op("dve", E("tensor_tensor", out=gtm[:, :, :], in0=gtm[:, :, :],
                      in1=bc(nea[:, :].unsqueeze(1), [64, NCH, 16]), op=ALU.mult), reads=[sm_b, nea_b], writes=[sm_b])
        ogt, og_b, ol = ogr.next()
        if DBG == "s2b":
            P.op("dve", E("tensor_copy", out=ogt[:, :, :], in_=vf[:, :, :]), reads=[vf_b, qf_b, kf_b, zs_b, sm_b], writes=[og_b])
            P.dma("sp", ol, og.rearrange("f p t -> p f t")[:, :, t0:t0 + T], ogt[:, :, :], reads=[og_b])
            continue
        for c in range(NCH):
            cs = slice(c * 64, (c + 1) * 64)
            pm, pm_b = bank[0]
            P.op("pe", E("matmul", out=pm[0:64, 256:272], lhsT=Umat, rhs=gtm[:, c, :], start=True, stop=True),
                 reads=[cB_b, sm_b], writes=[pm_b])
            P.op("pe", E("matmul", out=pm[:, 272:288], lhsT=K.ones[0:64, :], rhs=gtm[:, c, :], start=True, stop=True),
                 reads=[K.ones_b, sm_b], writes=[pm_b])
            gc, gc_b = gcs[0]
            gb, gb_b = gbs[0]
            eg, eg_b = egs[0]
            beg, beg_b = begs[0]
            ekt, ekt_b = ekts[0]
            gt, gt_b = gtot[0]
            gl, gl_b = gls[0]
            P.op("dve", E("tensor_copy", out=gc[:, :], in_=pm[0:64, 256:272]), reads=[pm_b], writes=[gc_b])
            P.op("act", E("activation", out=gt[:, :], in_=pm[:, 272:288], func=AF.Copy), reads=[pm_b], writes=[gt_b])
            P.op("dve", E("tensor_tensor", out=gb[:, :], in0=gc[:, :], in1=lb[:, c, :], op=ALU.add),
                 reads=[gc_b, sm_b], writes=[gb_b])
            P.op("act", E("activation", out=eg[:, :], in_=gc[:, :], func=AF.Exp), reads=[gc_b], writes=[eg_b])
            P.op("dve", E("tensor_tensor", out=beg[:, :], in0=eg[:, :], in1=beta[:, c, :], op=ALU.mult),
                 reads=[eg_b, sm_b], writes=[beg_b])
            P.op("dve", E("tensor_tensor", out=ekt[:, :], in0=gt[0:64, :], in1=gc[:, :], op=ALU.subtract),
                 reads=[gt_b, gc_b], writes=[ekt_b])
            P.op("act", E("activation", out=ekt[:, :], in_=ekt[:, :], func=AF.Exp), reads=[ekt_b], writes=[ekt_b])
            P.op("act", E("activation", out=gl[:, :], in_=gt[:, :], func=AF.Exp), reads=[gt_b], writes=[gl_b])
            for hh in range(2):
                h0 = hh * 8
                k0 = hh * 4
                hs = slice(h0, h0 + 8)
                if DBGSTEP < 2:
                    continue
                d1, d1_b = Dg1[hh]
                d2, d2_b = Dg2[hh]
                P.op("dve", E("tensor_tensor", out=d1[:, :, :], in0=bc(identI.unsqueeze(1), [64, 8, 64]),
                              in1=bc(gc[:, hs].unsqueeze(2), [64, 8, 64]), op=ALU.mult),
                     reads=[K.ident_b, gc_b], writes=[d1_b])
                P.op("dve", E("tensor_tensor", out=d2[:, :, :], in0=bc(identI.unsqueeze(1), [64, 8, 64]),
                              in1=bc(gb[:, hs].unsqueeze(2), [64, 8, 64]), op=ALU.mult),
                     reads=[K.ident_b, gb_b], writes=[d2_b])
                if DBGSTEP < 2.2:
                    continue
                pG, pG_b = bank[1]
                pGb, pGb_b = bank[2]
                P.op("pe", E("matmul", out=pG[:, :], lhsT=K.ones[0:64, :], rhs=d1[:, :, :].rearrange("p h c -> p (h c)"),
                             start=True, stop=True), reads=[K.ones_b, d1_b], writes=[pG_b])
                P.op("pe", E("matmul", out=pGb[0:64, :], lhsT=K.ones[0:64, 0:64], rhs=d2[:, :, :].rearrange("p h c -> p (h c)"),
                             start=True, stop=True), reads=[K.ones_b, d2_b], writes=[pGb_b])
                if DBGSTEP < 2.4:
                    continue
                pG3 = pG[:, :].rearrange("p (h c) -> p h c", c=64)
                pGb3 = pGb[:, :].rearrange("p (h c) -> p h c", c=64)
                a1, a1_b = t1[hh]
                a2, a2_b = t2[hh]
                ei, ei_b = EI[hh]
                es, es_b = ES[hh]
                egr, egr_b = EG[hh]
                gcb = bc(gc[:, hs].unsqueeze(2), [64, 8, 64])
                P.op("dve", E("tensor_tensor", out=a1[:, :, :], in0=pG3[0:64], in1=gcb, op=ALU.subtract),
                     reads=[pG_b, gc_b], writes=[a1_b])
                P.op("dve", E("tensor_tensor", out=a2[:, :, :], in0=pGb3[0:64], in1=gcb, op=ALU.subtract),
                     reads=[pGb_b, gc_b], writes=[a2_b])
                if DBGSTEP < 2.5:
                    continue
                P.op("act", E("activation", out=egr[:, :, :], in_=pG3, func=AF.Exp), reads=[pG_b], writes=[egr_b])
                if DBGSTEP < 2.6:
                    continue
                P.op(POOLENG, E("tensor_tensor", out=a1[:, :, :], in0=a1[:, :, :], in1=bc(NEGI.unsqueeze(1), [64, 8, 64]),
                               op=ALU.add), reads=[a1_b, cB_b], writes=[a1_b])
                P.op(POOLENG, E("tensor_tensor", out=a2[:, :, :], in0=a2[:, :, :], in1=bc(NEGS.unsqueeze(1), [64, 8, 64]),
                               op=ALU.add), reads=[a2_b, cB_b], writes=[a2_b])
                if DBGSTEP < 2.7:
                    continue
                P.op("act", E("activation", out=ei[:, :, :], in_=a1[:, :, :], func=EXPF), reads=[a1_b], writes=[ei_b])
                P.op("act", E("activation", out=es[:, :, :], in_=a2[:, :, :], func=EXPF), reads=[a2_b], writes=[es_b])
                if DBGSTEP < 3:
                    continue
                qgt, qg_b = qg[hh]
                P.op("dve", E("tensor_tensor", out=qgt[:, :, :].rearrange("p (k two) c -> p k two c", two=2),
                              in0=bc(qf[:, k0:k0 + 4, cs].unsqueeze(2), [128, 4, 2, 64]),
                              in1=egr[:, :, :].rearrange("p (k two) c -> p k two c", two=2), op=ALU.mult),
                     reads=[qf_b, egr_b], writes=[qg_b])
                pK, pK_b = bank[3]
                for kh in range(4):
                    P.op("pe", E("matmul", out=pK[0:64, kh * 128:kh * 128 + 64], lhsT=kf[:, k0 + kh, cs], rhs=kf[:, k0 + kh, cs],
                                 start=True, stop=True), reads=[kf_b], writes=[pK_b])
                    P.op("pe", E("matmul", out=pK[0:64, kh * 128 + 64:kh * 128 + 128], lhsT=kf[:, k0 + kh, cs],
                                 rhs=qf[:, k0 + kh, cs], start=True, stop=True), reads=[kf_b, qf_b], writes=[pK_b])
                pK4 = pK[0:64, :].rearrange("p (k j c) -> p k j c", j=2, c=64)
                p0, p0_b = Pm[0][hh]
                qk, qk_b = qkT[hh]
                P.op("dve", E("tensor_tensor", out=p0[:, :, :].rearrange("p (k two) c -> p k two c", two=2),
                              in0=bc(pK4[:, :, 0, :].unsqueeze(2), [64, 4, 2, 64]),
                              in1=es[:, :, :].rearrange("p (k two) c -> p k two c", two=2), op=ALU.mult),
                     reads=[pK_b, es_b], writes=[p0_b])
                P.op("dve", E("tensor_tensor", out=qk[:, :, :].rearrange("p (k two) c -> p k two c", two=2),
                              in0=bc(pK4[:, :, 1, :].unsqueeze(2), [64, 4, 2, 64]),
                              in1=ei[:, :, :].rearrange("p (k two) c -> p k two c", two=2), op=ALU.mult),
                     reads=[pK_b, ei_b], writes=[qk_b])
                P.op(POOLENG, E("tensor_scalar", out=p0[:, :, :], in0=p0[:, :, :], scalar1=-1.0, scalar2=None, op0=ALU.mult),
                     reads=[p0_b], writes=[p0_b])
                if DBGSTEP < 4:
                    continue
                pT, pT_b = bank[7]
                pTv = pT[:, :].bitcast(BF16)
                for h in range(8):
                    P.op("pe", E("transpose", out=pTv[0:64, h * 128:(h + 1) * 128], in_=vf[:, h0 + h, cs], identity=identb[:, :]),
                         reads=[vf_b, identb_b], writes=[pT_b])
                vbt, vb_b = vb[hh]
                P.op("dve", E("tensor_tensor", out=vbt[:, :, :], in0=pTv[0:64, :].rearrange("p (h v) -> p h v", v=128),
                              in1=bc(beta[:, c, hs].unsqueeze(2), [64, 8, 128]), op=ALU.mult),
                     reads=[pT_b, sm_b], writes=[vb_b])
                pT2, pT2_b = bank[6]
                pT2v = pT2[:, :].bitcast(BF16)
                for kh in range(4):
                    P.op("pe", E("transpose", out=pT2v[0:64, kh * 128:(kh + 1) * 128], in_=kf[:, k0 + kh, cs], identity=identb[:, :]),
                         reads=[kf_b, identb_b], writes=[pT2_b])
                ktm4 = bc(pT2v[0:64, 0:512].rearrange("p (k d) -> p k d", d=128).unsqueeze(2), [64, 4, 2, 128])
                kbt, kb_b = kbg[hh]
                ktt, kt_b = kt[hh]
                P.op("dve", E("tensor_tensor", out=kbt[:, :, :].rearrange("p (k two) d -> p k two d", two=2), in0=ktm4,
                              in1=bc(beg[:, hs].unsqueeze(2), [64, 8, 128]).rearrange("p (k two) d -> p k two d", two=2),
                              op=ALU.mult), reads=[pT2_b, beg_b], writes=[kb_b])
                P.op("dve", E("tensor_tensor", out=ktt[:, :, :].rearrange("p (k two) d -> p k two d", two=2), in0=ktm4,
                              in1=bc(ekt[:, hs].unsqueeze(2), [64, 8, 128]).rearrange("p (k two) d -> p k two d", two=2),
                              op=ALU.mult), reads=[pT2_b, ekt_b], writes=[kt_b])
                if DBGSTEP < 5:
                    continue
                pA, pA_b = bank[4]
                pB, pB_b = bank[5]
                pC, pC_b = bank[6]
                pAv = pA[:, :].bitcast(BF16)
                for h in range(8):
                    P.op("pe", E("transpose", out=pAv[0:64, h * 64:(h + 1) * 64], in_=p0[:, h, :], identity=identb[0:64, 0:64]),
                         reads=[p0_b, identb_b], writes=[pA_b])
                q0, q0_b = PT[0][hh]
                P.op("act", E("activation", out=q0[:, :, :], in_=pAv[0:64, 0:512].rearrange("p (h c) -> p h c", c=64), func=AF.Copy),
                     reads=[pA_b], writes=[q0_b])
                rf, rf_b = Rf[hh]
                rb, rb_b = Rb[hh]
                P.op("dve", E("tensor_tensor", out=rf[:, :, :], in0=p0[:, :, :], in1=bc(identI.unsqueeze(1), [64, 8, 64]),
                              op=ALU.add), reads=[p0_b, K.ident_b], writes=[rf_b])
                P.op("act", E("activation", out=rb[:, :, :], in_=rf[:, :, :], func=AF.Copy), reads=[rf_b], writes=[rb_b])
                cur = 0
                for lvl in range(6):
                    pc_, pc_b = Pm[cur][hh]
                    pt_, pt_b = PT[cur][hh]
                    pn_, pn_b = Pm[1 - cur][hh]
                    ptn_, ptn_b = PT[1 - cur][hh]
                    if lvl >= 1:
                        for h in range(8):
                            P.op("pe", E("matmul", out=pC[0:64, h * 64:(h + 1) * 64], lhsT=pt_[:, h, :], rhs=rb[:, h, :],
                                         start=True, stop=True), reads=[pt_b, rb_b], writes=[pC_b])
                    if lvl <= 4:
                        for h in range(8):
                            P.op("pe", E("matmul", out=pA[0:64, h * 64:(h + 1) * 64], lhsT=pt_[:, h, :], rhs=pc_[:, h, :],
                                         start=True, stop=True), reads=[pt_b, pc_b], writes=[pA_b])
                        for h in range(8):
                            P.op("pe", E("matmul", out=pB[0:64, h * 64:(h + 1) * 64], lhsT=pc_[:, h, :], rhs=pt_[:, h, :],
                                         start=True, stop=True), reads=[pt_b, pc_b], writes=[pB_b])
                    if lvl >= 1:
                        P.op("dve", E("tensor_tensor", out=rf[:, :, :], in0=pC[0:64, :].rearrange("p (h c) -> p h c", c=64),
                                      in1=rf[:, :, :], op=ALU.add), reads=[pC_b, rf_b], writes=[rf_b])
                        P.op("act", E("activation", out=rb[:, :, :], in_=rf[:, :, :], func=AF.Copy), reads=[rf_b], writes=[rb_b])
                    if lvl <= 4:
                        P.op("act", E("activation", out=pn_[:, :, :], in_=pA[0:64, :].rearrange("p (h c) -> p h c", c=64),
                                      func=AF.Copy), reads=[pA_b], writes=[pn_b])
                        P.op("dve", E("tensor_copy", out=ptn_[:, :, :], in_=pB[0:64, :].rearrange("p (h c) -> p h c", c=64)),
                             reads=[pB_b], writes=[ptn_b])
                        cur = 1 - cur
                if DBGSTEP < 6:
                    continue
                pW, pW_b = bank[2]
                for h in range(8):
                    P.op("pe", E("matmul", out=pW[:, h * 64:(h + 1) * 64], lhsT=kbt[:, h, :], rhs=rb[:, h, :], start=True, stop=True),
                         reads=[kb_b, rb_b], writes=[pW_b])
                nw, nw_b = nwT[hh]
                P.op("act", E("activation", out=nw[:, :, :], in_=pW[:, :].rearrange("p (h c) -> p h c", c=64), func=AF.Copy,
                              scale=-1.0), reads=[pW_b], writes=[nw_b])
                if DBGSTEP < 7:
                    continue
                vnt, vn_b = vn[hh]
                for q in range(2):
                    pV, pV_b = bank[4 + q]
                    for h4 in range(4):
                        h = q * 4 + h4
                        P.op("pe", E("matmul", out=pV[0:64, h4 * 128:(h4 + 1) * 128], lhsT=rb[:, h, :], rhs=vbt[:, h, :],
                                     start=True, stop=False), reads=[rb_b, vb_b], writes=[pV_b])
                        P.op("pe", E("matmul", out=pV[0:64, h4 * 128:(h4 + 1) * 128], lhsT=nw[:, h, :], rhs=Sb[:, h0 + h, :],
                                     start=False, stop=True), reads=[nw_b, Sb_b[hh]], writes=[pV_b])
                    if q == 0:
                        P.op("act", E("activation", out=vnt[:, 0:4, :], in_=pV[0:64, :].rearrange("p (h v) -> p h v", v=128),
                                      func=AF.Copy), reads=[pV_b], writes=[vn_b])
                    else:
                        P.op("dve", E("tensor_copy", out=vnt[:, 4:8, :], in_=pV[0:64, :].rearrange("p (h v) -> p h v", v=128)),
                             reads=[pV_b], writes=[vn_b])
                if DBGSTEP < 8:
                    continue
                pO, pO_b = bank[1]
                for h in range(8):
                    P.op("pe", E("matmul", out=pO[:, h * 64:(h + 1) * 64], lhsT=Sb[:, h0 + h, :], rhs=qgt[:, h, :],
                                 start=True, stop=False), reads=[Sb_b[hh], qg_b], writes=[pO_b])
                    P.op("pe", E("matmul", out=pO[:, h * 64:(h + 1) * 64], lhsT=vnt[:, h, :], rhs=qk[:, h, :],
                                 start=False, stop=True), reads=[vn_b, qk_b], writes=[pO_b])
                if DBGSTEP < 9:
                    continue
                for q in range(2):
                    pS, pS_b = bank[4 + q]
                    for h4 in range(4):
                        h = q * 4 + h4
                        P.op("pe", E("matmul", out=pS[:, h4 * 128:(h4 + 1) * 128], lhsT=ktt[:, h, :], rhs=vnt[:, h, :],
                                     start=True, stop=True), reads=[kt_b, vn_b], writes=[pS_b])
                    sv = S[:, h0 + q * 4:h0 + q * 4 + 4, :]
                    P.op("dve", E("tensor_tensor", out=sv, in0=sv, in1=bc(gl[:, h0 + q * 4:h0 + q * 4 + 4].unsqueeze(2), [128, 4, 128]),
                                  op=ALU.mult), reads=[S_b[hh], gl_b], writes=[S_b[hh]])
                    P.op("dve", E("tensor_tensor", out=sv, in0=pS[:, :].rearrange("p (h v) -> p h v", v=128), in1=sv, op=ALU.add),
                         reads=[pS_b, S_b[hh]], writes=[S_b[hh]])
                    P.op("act", E("activation", out=Sb[:, h0 + q * 4:h0 + q * 4 + 4, :], in_=sv, func=AF.Copy),
                         reads=[S_b[hh]], writes=[Sb_b[hh]])
                if DBGSTEP < 10:
                    continue
                ob, ob_b = osb[hh]
                oq, oq_b = osq[hh]
                orr, or_b = orst[hh]
                pO3 = pO[:, :].rearrange("p (h c) -> p h c", c=64)
                P.op("dve", E("tensor_copy", out=ob[:, :, :], in_=pO3), reads=[pO_b], writes=[ob_b])
                P.op("act", E("activation", out=oq[:, :, :], in_=pO3, func=AF.Square), reads=[pO_b], writes=[oq_b])
                pQ, pQ_b = bank[3]
                P.op("pe", E("matmul", out=pQ[:, :], lhsT=K.ones[:, :], rhs=oq[:, :, :].rearrange("p h c -> p (h c)"),
                             start=True, stop=True), reads=[K.ones_b, oq_b], writes=[pQ_b])
                rsqrt_op(P, K, orr[:, :, :].rearrange("p h c -> p (h c)"), or_b, pQ[:, :], pQ_b, 1.0 / 128, EPS)
                P.op("dve", E("scalar_tensor_tensor", out=ob[:, :, :].rearrange("p h c -> p (h c)"),
                              in0=ob[:, :, :].rearrange("p h c -> p (h c)"), scalar=vec[:, 128:129],
                              in1=orr[:, :, :].rearrange("p h c -> p (h c)"), op0=ALU.mult, op1=ALU.mult),
                     reads=[ob_b, or_b, vec_b], writes=[ob_b])
                P.op("dve", E("tensor_tensor", out=ogt[:, hs, cs], in0=ob[:, :, :], in1=zs[:, hs, cs], op=ALU.mult),
                     reads=[ob_b, zs_b], writes=[og_b])
        if DBGSTEP < 10:
            P.op("dve", E("tensor_copy", out=ogt[:, :, :], in_=vf[:, :, :]), reads=[vf_b], writes=[og_b])
        if xch is None:
            P.dma("sp", ol, og.rearrange("f p t -> p f t")[:, :, t0:t0 + T], ogt[:, :, :], reads=[og_b])
        else:
            P.dma("sp", ol, xch["og_loc"][ti].rearrange("(f p) t -> p f t", p=128), ogt[:, :, :],
                  reads=[og_b], writes=[xch["og_loc_b"][ti]])
            P.cc_allgather(xch["og_loc"][ti], xch["og_loc_b"][ti], xch["og_all"][ti], xch["og_all_b"][ti], PAIRS)


def build_B(ntile=None):
    nc = bass.Bass("TRN2", target_bir_lowering=False)
    io = {}

    def inp(name, shape, dt=F32):
        io[name] = nc.dram_tensor(name, list(shape), dt, kind="ExternalInput").ap()

    inp("hn1g", [NF, 128, TG], BF16)
    inp("ident", [128, 128])
    inp("vecB", [128, 129])
    inp("rowB", [128, 32])
    inp("cB", [64, 192])
    inp("cB2", [128, 256])
    inp("gw_in", [D, 6176])
    io["og"] = nc.dram_tensor("og", [NF, 128, TG], BF16, kind="ExternalOutput").ap()
    P = Prog(nc)
    K = Common(P, nc, io["ident"])
    phase_B(P, K, io, ntile)
    P.emit()
    P.close()
    return nc


def const_cB():
    t = np.arange(64)
    U = (t[:, None] <= t[None, :]).astype(np.float32)
    negi = np.where(t[None, :] >= t[:, None], 0.0, NEG).astype(np.float32)
    negs = np.where(t[None, :] > t[:, None], 0.0, NEG).astype(np.float32)
    return np.ascontiguousarray(np.concatenate([U, negi, negs], axis=1))


def const_cB2():
    c = const_cB()
    c = np.concatenate([c, np.eye(64, dtype=np.float32)], axis=1)
    return np.ascontiguousarray(np.concatenate([c, c], axis=0))


def host_weights_B(inp, r):
    w = np.asarray(inp["gdn_w_in"], np.float32)[0]
    cw = np.asarray(inp["gdn_conv_w"], np.float32)[0]
    qs = slice(r * 1024, (r + 1) * 1024)
    ks = slice(2048 + r * 1024, 2048 + (r + 1) * 1024)
    vs = slice(4096 + r * 2048, 4096 + (r + 1) * 2048)
    zs = slice(8192 + r * 2048, 8192 + (r + 1) * 2048)
    bs = slice(12288 + r * 16, 12288 + (r + 1) * 16)
    as_ = slice(12320 + r * 16, 12320 + (r + 1) * 16)
    gw = np.ascontiguousarray(np.concatenate([w[:, qs], w[:, ks], w[:, vs], w[:, zs], w[:, bs], w[:, as_]], axis=1))
    cwl = np.concatenate([cw[:, qs], cw[:, ks], cw[:, vs]], axis=1)
    vec = np.concatenate([_fm(cwl[j]) for j in range(4)] + [np.asarray(inp["gdn_norm_w"], np.float32)[0].reshape(128, 1)], axis=1)
    al = np.asarray(inp["gdn_a_log"], np.float32)[0][r * 16:(r + 1) * 16]
    dtb = np.asarray(inp["gdn_dt_bias"], np.float32)[0][r * 16:(r + 1) * 16]
    row = np.ascontiguousarray(np.broadcast_to(np.concatenate([al, dtb])[None, :], (128, 32)))
    return {"gw_in": gw, "vecB": np.ascontiguousarray(vec.astype(np.float32)), "rowB": row.astype(np.float32),
            "cB": const_cB(), "cB2": const_cB2(), "ident": np.eye(128, dtype=np.float32)}


def _run(nc, maps):
    res = run_bass_kernel_spmd(nc, maps, core_ids=list(range(8)))
    return res.results


def _kernel_unfused(**inputs):
    resA = _run(build_A(), host_inputs_A(inputs))
    wB = [host_weights_B(inputs, r) for r in range(2)]
    mapsB = []
    for c in range(8):
        b, r = c // 2, c % 2
        h0 = np.asarray(resA[2 * b]["hn1"])
        h1_ = np.asarray(resA[2 * b + 1]["hn1"])
        seq = np.concatenate([np.zeros((NF, 128, 48), dtype=h0.dtype), h0, h1_[:, :, 16:]], axis=2)
        m = dict(wB[r])
        m["hn1g"] = np.ascontiguousarray(seq)
        mapsB.append(m)
    resB = _run(build_B(), mapsB)
    shared = {
        "ident": np.eye(128, dtype=np.float32),
        "vecC": np.ascontiguousarray(np.concatenate([_fm(inputs["ffn_norm"][1]), _fm(inputs["final_norm"])], axis=1)),
        "gdn_w_out": np.ascontiguousarray(np.asarray(inputs["gdn_w_out"], np.float32)[0]),
        "wg": np.ascontiguousarray(np.asarray(inputs["ffn_w_gate"], np.float32)[1]),
        "wu": np.ascontiguousarray(np.asarray(inputs["ffn_w_up"], np.float32)[1]),
        "wd": np.ascontiguousarray(np.asarray(inputs["ffn_w_down"], np.float32)[1]),
    }
    mapsC = []
    for c in range(8):
        b, r = c // 2, c % 2
        lo = 64 + r * 2048
        ogc = np.concatenate([np.asarray(resB[2 * b]["og"])[:, :, lo:lo + 2048],
                              np.asarray(resB[2 * b + 1]["og"])[:, :, lo:lo + 2048]], axis=0)
        m = dict(shared)
        m["ogc"] = np.ascontiguousarray(ogc)
        m["h1c"] = np.ascontiguousarray(np.asarray(resA[c]["h1"])[:, :, 16:])
        mapsC.append(m)
    resC = _run(build_C(), mapsC)
    out = np.empty((BATCH, SEQ, D), np.float32)
    for c in range(8):
        b, r = c // 2, c % 2
        out[b, r * 2048:(r + 1) * 2048] = np.asarray(resC[c]["out"])
    return out


def build_fused():
    nc = bass.Bass("TRN2", target_bir_lowering=False)
    io = {}

    def inp(name, shape, dt=F32):
        io[name] = nc.dram_tensor(name, list(shape), dt, kind="ExternalInput").ap()

    inp("xa", [TA, D])
    inp("ident", [128, 128])
    inp("vecA", [128, 96])
    inp("sc_w_in", [D, 3 * D])
    inp("sc_w_out", [D, D])
    inp("wg0", [D, DFF])
    inp("wu0", [D, DFF])
    inp("wd0", [DFF, D])
    inp("vecB", [128, 129])
    inp("rowB", [128, 32])
    inp("cB", [64, 192])
    inp("cB2", [128, 256])
    inp("gw_in", [D, 6176])
    inp("vecC", [128, 32])
    inp("mk", [128, 2])
    inp("gdn_w_out", [2 * D, D])
    inp("wg1", [D, DFF])
    inp("wu1", [D, DFF])
    inp("wd1", [DFF, D])
    io["out"] = nc.dram_tensor("out", [TC, D], F32, kind="ExternalOutput").ap()
    nA = TA // TA_TILE
    nB = TG // TB
    h1 = nc.dram_tensor("h1_loc", [NF, 128, TA], F32).ap()
    xch = {
        "h1_b": Buf("h1"),
        "hn1_loc": [nc.dram_tensor("hn1_loc%d" % i, [D, XWS[i % 2]], BF16).ap() for i in range(2 * nA)],
        "hn1_all": [nc.dram_tensor("hn1_all%d" % i, [2 * D, XWS[i % 2]], BF16).ap() for i in range(2 * nA)],
        "og_loc": [nc.dram_tensor("og_loc%d" % i, [D, TB], BF16).ap() for i in range(nB)],
        "og_all": [nc.dram_tensor("og_all%d" % i, [2 * D, TB], BF16).ap() for i in range(nB)],
    }
    for k in ("hn1_loc", "hn1_all", "og_loc", "og_all"):
        xch[k + "_b"] = [Buf("%s%d" % (k, i)) for i in range(len(xch[k]))]
    P = Prog(nc)
    K = Common(P, nc, io["ident"])
    ioA = {"xa": io["xa"], "vecA": io["vecA"], "sc_w_in": io["sc_w_in"], "sc_w_out": io["sc_w_out"],
           "wg": io["wg0"], "wu": io["wu0"], "wd": io["wd0"], "h1": h1}
    phase_A(P, K, ioA, TA_TILE, xch)
    P.emit(final=False)
    P.close()
    K = Common(P, nc, io["ident"])
    ioB = {"vecB": io["vecB"], "rowB": io["rowB"], "cB": io["cB"], "cB2": io["cB2"], "gw_in": io["gw_in"]}
    phase_B(P, K, ioB, None, xch)
    P.emit(final=False)
    P.close()
    K = Common(P, nc, io["ident"])
    ioC = {"h1c": h1, "vecC": io["vecC"], "mk": io["mk"], "gdn_w_out": io["gdn_w_out"],
           "wg": io["wg1"], "wu": io["wu1"], "wd": io["wd1"], "out": io["out"]}
    phase_C(P, K, ioC, 512, xch)
    P.emit(final=True)
    P.finish()
    return nc


def host_inputs_fused(inp):
    mapsA = host_inputs_A(inp)
    wB = [host_weights_B(inp, r) for r in range(2)]
    shared = {
        "vecC": np.ascontiguousarray(np.concatenate([_fm(inp["ffn_norm"][1]), _fm(inp["final_norm"])], axis=1)),
        "gdn_w_out": np.ascontiguousarray(np.asarray(inp["gdn_w_out"], np.float32)[0]),
        "wg1": np.ascontiguousarray(np.asarray(inp["ffn_w_gate"], np.float32)[1]),
        "wu1": np.ascontiguousarray(np.asarray(inp["ffn_w_up"], np.float32)[1]),
        "wd1": np.ascontiguousarray(np.asarray(inp["ffn_w_down"], np.float32)[1]),
    }
    maps = []
    for c in range(8):
        r = c % 2
        a = mapsA[c]
        m = {"xa": a["xa"], "ident": a["ident"], "vecA": a["vecA"], "sc_w_in": a["sc_w_in"], "sc_w_out": a["sc_w_out"],
             "wg0": a["wg"], "wu0": a["wu"], "wd0": a["wd"]}
        for k in ("vecB", "rowB", "cB", "cB2", "gw_in"):
            m[k] = wB[r][k]
        m.update(shared)
        mk = np.zeros((128, 2), np.float32)
        mk[:, r] = 1.0
        m["mk"] = mk
        maps.append(m)
    return maps


def kernel_unfused(**inputs):
    return _kernel_unfused(**inputs)


def kernel(**inputs):
    nc = build_fused()
    res = run_bass_kernel_spmd(nc, host_inputs_fused(inputs), core_ids=list(range(8))).results
    out = np.empty((BATCH, SEQ, D), np.float32)
    for c in range(8):
        b, r = c // 2, c % 2
        out[b, r * 2048:(r + 1) * 2048] = np.asarray(res[c]["out"])
    return out
```

```python
import numpy as np
import ml_dtypes
import concourse.bass as bass
import concourse.mybir as mybir
from concourse.bass_utils import run_bass_kernel_spmd

F32 = mybir.dt.float32
BF16 = mybir.dt.bfloat16
AF = mybir.ActivationFunctionType
ALU = mybir.AluOpType
AX = mybir.AxisListType

D = 2048
NF = 16
DFF = 5632
NCF = 44
SEQ = 4096
BATCH = 4
N_META = 16
EPS = 1e-6
TA = 2064
TC = 2048
TG = 4160
PAIRS = [[0, 1], [2, 3], [4, 5], [6, 7]]
TA_TILES = (528, 512, 512, 512)
XWS = (512, 64)

ENGS = ("pe", "act", "dve", "pool", "sp")


class Buf:
    __slots__ = ("name", "last_w", "readers", "excl")

    def __init__(self, name="", excl=False):
        self.name = name
        self.last_w = None
        self.readers = []
        self.excl = excl


class Ins:
    __slots__ = ("eng", "emit", "deps", "inc", "ordinal", "lane", "lane_val", "is_dma", "cc_inc", "phase")

    def __init__(self, eng, emit):
        self.eng = eng
        self.emit = emit
        self.deps = []
        self.inc = False
        self.ordinal = None
        self.lane = None
        self.lane_val = None
        self.is_dma = False
        self.cc_inc = 16
        self.phase = 0


class Lane:
    def __init__(self, name):
        self.name = name
        self.sem = None
        self.count = 0
        self.last = None


class Prog:
    def __init__(self, nc):
        self.nc = nc
        self.streams = {e: [] for e in ENGS}
        self.lanes = []
        self._ctx = []
        self.out_lanes = []
        self.phase = 0
        self.sems = None
        self.ord = {e: 0 for e in ENGS}
        self._semctx = []

    def sb(self, name, shape, dt):
        g = self.nc.sbuf_tensor("%s_p%d" % (name, self.phase), list(shape), dt)
        t = g.__enter__()
        self._ctx.append(g)
        return t

    def ps(self, name, shape, dt=F32):
        g = self.nc.psum_tensor("%s_p%d" % (name, self.phase), list(shape), dt)
        t = g.__enter__()
        self._ctx.append(g)
        return t

    def lane(self, name, out=False):
        l = Lane("%s_p%d" % (name, self.phase))
        self.lanes.append(l)
        if out:
            self.out_lanes.append(l)
        return l

    def _track(self, ins, reads, writes):
        deps = ins.deps
        for b in reads:
            if b.last_w is not None:
                deps.append(b.last_w)
            if b.excl:
                for r in b.readers:
                    if r.eng != ins.eng:
                        deps.append(r)
        for b in writes:
            if b.last_w is not None:
                deps.append(b.last_w)
            deps.extend(b.readers)
        for b in reads:
            if not ins.is_dma:
                b.readers = [r for r in b.readers if r.is_dma or r.eng != ins.eng]
            b.readers.append(ins)
        for b in writes:
            b.last_w = ins
            b.readers = []

    def op(self, eng, emit, reads=(), writes=()):
        ins = Ins(eng, emit)
        ins.phase = self.phase
        self._track(ins, reads, writes)
        self.streams[eng].append(ins)
        return ins

    def dma(self, eng, lane, out, in_, reads=(), writes=()):
        return self.dma_group(eng, lane, [(out, in_)], reads, writes)

    def dma_group(self, eng, lane, pairs, reads=(), writes=()):
        pairs = list(pairs)
        ins = Ins(eng, lambda e: [e.dma_start(out=o, in_=i) for (o, i) in pairs])
        ins.is_dma = True
        ins.phase = self.phase
        ins.lane = lane
        if lane.last is not None:
            ins.deps.append(lane.last)
        lane.count += 16 * len(pairs)
        ins.lane_val = lane.count
        lane.last = ins
        self._track(ins, reads, writes)
        self.streams[eng].append(ins)
        return ins

    def cc_allgather(self, in_ap, in_buf, out_ap, out_buf, groups):
        lane = self.lane("cc%d" % len(self.lanes))
        ins = Ins("pool", lambda e: [e.collective_compute("AllGather", ALU.bypass, replica_groups=groups,
                                                          ins=[in_ap], outs=[out_ap])])
        ins.is_dma = True
        ins.phase = self.phase
        ins.lane = lane
        ins.cc_inc = 1
        lane.count += 1
        ins.lane_val = lane.count
        lane.last = ins
        self._track(ins, [in_buf], [out_buf])
        self.streams["pool"].append(ins)
        return ins

    def emit(self, final=True):
        nc = self.nc
        cur = self.phase
        for e in ENGS:
            for ins in self.streams[e]:
                for d in ins.deps:
                    if not d.is_dma and d.phase == cur:
                        d.inc = True
        for e in ENGS:
            c = self.ord[e]
            for ins in self.streams[e]:
                if ins.inc and not ins.is_dma:
                    c += 1
                    ins.ordinal = c
            self.ord[e] = c
        if self.sems is None:
            self.sems = {}
            for e in ENGS:
                g = nc.semaphore("sem_" + e)
                self.sems[e] = g.__enter__()
                self._semctx.append(g)
        sems = self.sems
        for l in self.lanes:
            if l.sem is None:
                g = nc.semaphore("lane_" + l.name)
                l.sem = g.__enter__()
                self._semctx.append(g)
        out_lanes = self.out_lanes if final else []

        def run(ename, eng):
            seen = {}
            for ins in self.streams[ename]:
                need = {}
                for d in ins.deps:
                    if d.is_dma:
                        key = ("l", id(d.lane))
                        sem = d.lane.sem
                        val = d.lane_val
                    else:
                        if d.phase != cur:
                            continue
                        if d.eng == ename and ename == "pe":
                            continue
                        key = ("e", d.eng)
                        sem = sems[d.eng]
                        val = d.ordinal
                    if seen.get(key, 0) >= val:
                        continue
                    if key not in need or need[key][1] < val:
                        need[key] = (sem, val)
                for key, (sem, val) in need.items():
                    eng.wait_ge(sem, val)
                    seen[key] = val
                bi = ins.emit(eng)
                if ins.is_dma:
                    for b1 in bi:
                        b1.then_inc(ins.lane.sem, getattr(ins, "cc_inc", 16))
                elif ins.inc:
                    bi.then_inc(sems[ename], 1)
            if ename == "sp":
                for l in out_lanes:
                    if l.count:
                        eng.wait_ge(l.sem, l.count)

        with nc.Block() as block:
            @block.tensor
            def _(t):
                run("pe", t)

            @block.scalar
            def _(t):
                run("act", t)

            @block.vector
            def _(t):
                run("dve", t)

            @block.gpsimd
            def _(t):
                run("pool", t)

            @block.sync
            def _(t):
                run("sp", t)
        self.streams = {e: [] for e in ENGS}
        self.phase += 1

    def close(self):
        for g in reversed(self._ctx):
            g.__exit__(None, None, None)
        self._ctx = []

    def finish(self):
        self.close()
        for g in reversed(self._semctx):
            g.__exit__(None, None, None)
        self._semctx = []


class Ring:
    def __init__(self, P, name, n, shape, dt, lanes=False, psum=False):
        self.slots = []
        for i in range(n):
            t = (P.ps if psum else P.sb)("%s%d" % (name, i), shape, dt)
            self.slots.append((t, Buf("%s%d" % (name, i), excl=psum), P.lane("%s%d" % (name, i)) if lanes else None))
        self.i = 0

    def next(self):
        s = self.slots[self.i % len(self.slots)]
        self.i += 1
        return s


def E(method, **kw):
    return lambda e: getattr(e, method)(**kw)


def mm_group(P, out_ap, out_buf, pairs, reads):
    n = len(pairs)
    for i, (l, r) in enumerate(pairs):
        P.op("pe", E("matmul", out=out_ap, lhsT=l, rhs=r, start=(i == 0), stop=(i == n - 1)),
             reads=reads, writes=[out_buf])


def make_segs(T, maxn=512):
    nseg = (T + maxn - 1) // maxn
    base = (T + nseg - 1) // nseg
    segs = []
    a = 0
    while a < T:
        b = min(T, a + base)
        segs.append((a, b))
        a = b
    return segs


class Common:
    def __init__(self, P, nc, ident_dram, wslots=3, wsize=6144):
        self.P = P
        self.psum = Ring(P, "bank", 8, [128, 512], F32, psum=True)
        self.wring = Ring(P, "wr", wslots, [128, wsize], BF16, lanes=True)
        self.ident = P.sb("ident_sb", [128, 128], F32)
        self.ident_b = Buf("ident")
        self.ones = P.sb("ones_sb", [128, 128], F32)
        self.ones_b = Buf("ones")
        self.cl = P.lane("const")
        P.dma("sp", self.cl, self.ident[:], ident_dram, writes=[self.ident_b])
        P.op("dve", E("memset", ap=self.ones[:], constant=1.0), writes=[self.ones_b])
        self.epsc = P.sb("epsc", [128, 2], F32)
        self.eps_b = Buf("epsc")
        self.eps_col = {EPS: 0}
        P.op("dve", E("memset", ap=self.epsc[:], constant=EPS), writes=[self.eps_b])
        self.sq = Ring(P, "sq", 3, [128, 512], F32)
        self.tmp = Ring(P, "tmp", 4, [128, 512], F32)


def rsqrt_op(P, K, out, out_b, in_, in_b, scale, eps):
    P.op("act", E("activation", out=out, in_=in_, func=AF.Ln, bias=K.epsc[:in_.shape[0], K.eps_col[eps]:K.eps_col[eps] + 1],
                  scale=scale), reads=[in_b, K.eps_b], writes=[out_b])
    P.op("act", E("activation", out=out, in_=out, func=AF.Exp, scale=-0.5), reads=[out_b], writes=[out_b])


def rmsnorm_fm(P, K, src, src_b, dst, dst_b, segs, vec, vec_b, wcol, rstd, rstd_b):
    for (a, b) in segs:
        n = b - a
        pb, pbb, _ = K.psum.next()
        for f in range(NF):
            sq, sqb, _ = K.sq.next()
            P.op("act", E("activation", out=sq[:, :n], in_=src[:, f, a:b], func=AF.Square),
                 reads=[src_b], writes=[sqb])
            P.op("pe", E("matmul", out=pb[:, :n], lhsT=K.ones[:], rhs=sq[:, :n], start=(f == 0), stop=(f == NF - 1)),
                 reads=[sqb, K.ones_b], writes=[pbb])
        rsqrt_op(P, K, rstd[:, a:b], rstd_b, pb[:, :n], pbb, 1.0 / D, EPS)
        for f in range(NF):
            P.op("dve", E("scalar_tensor_tensor", out=dst[:, f, a:b], in0=src[:, f, a:b],
                          scalar=vec[:, wcol + f:wcol + f + 1], in1=rstd[:, a:b], op0=ALU.mult, op1=ALU.mult),
                 reads=[src_b, rstd_b, vec_b], writes=[dst_b])


def ffn_fm(P, K, hT, hT_b, hn, hn_b, act, act_b, segs, wg, wu, wd):
    wgv = wg.rearrange("(k p) m -> p k m", p=128)
    wuv = wu.rearrange("(k p) m -> p k m", p=128)
    wdv = wd.rearrange("(k p) m -> p k m", p=128)
    for c in range(NCF):
        wt, wb, wl = K.wring.next()
        wv = wt[:, 0:4096].rearrange("p (k j m) -> p k j m", k=NF, j=2)
        P.dma_group("pool", wl, [(wv[:, :, 0, :], wgv[:, :, c * 128:(c + 1) * 128]),
                                 (wv[:, :, 1, :], wuv[:, :, c * 128:(c + 1) * 128])], writes=[wb])
        for (a, b) in segs:
            n = b - a
            pg, pgb, _ = K.psum.next()
            pu, pub, _ = K.psum.next()
            mm_group(P, pg[:, :n], pgb, [(wv[:, k, 0, :], hn[:, k, a:b]) for k in range(NF)], [wb, hn_b])
            mm_group(P, pu[:, :n], pub, [(wv[:, k, 1, :], hn[:, k, a:b]) for k in range(NF)], [wb, hn_b])
            st, stb, _ = K.tmp.next()
            P.op("act", E("activation", out=st[:, :n], in_=pg[:, :n], func=AF.Silu), reads=[pgb], writes=[stb])
            P.op("dve", E("tensor_tensor", out=act[:, c, a:b], in0=pu[:, :n], in1=st[:, :n], op=ALU.mult),
                 reads=[pub, stb], writes=[act_b])
    for m in range(NF):
        wt, wb, wl = K.wring.next()
        wv = wt[:, 0:NCF * 128].rearrange("p (k m) -> p k m", k=NCF)
        P.dma("pool", wl, wv, wdv[:, :, m * 128:(m + 1) * 128], writes=[wb])
        for (a, b) in segs:
            n = b - a
            pb, pbb, _ = K.psum.next()
            mm_group(P, pb[:, :n], pbb, [(wv[:, k, :], act[:, k, a:b]) for k in range(NCF)], [wb, act_b])
            P.op("dve", E("tensor_tensor", out=hT[:, m, a:b], in0=pb[:, :n], in1=hT[:, m, a:b], op=ALU.add),
                 reads=[pbb, hT_b], writes=[hT_b])


def phase_A(P, K, io, tile_T=516, xch=None):
    xa, w_in, w_out = io["xa"], io["sc_w_in"], io["sc_w_out"]
    h1, hn1 = io["h1"], io.get("hn1")
    T = max(TA_TILES)
    ntile = len(TA_TILES)
    vec = P.sb("vecA_sb", [128, 96], F32)
    vec_b = Buf("vecA")
    P.dma("sp", K.cl, vec[:], io["vecA"], writes=[vec_b])
    hT_full = P.sb("hT", [128, NF, T], F32)
    hT_b = Buf("hT")
    hn_full = P.sb("hn", [128, NF, T], BF16)
    hn_b = Buf("hn")
    act_full = P.sb("act", [128, NCF, T], BF16)
    act_b = Buf("act")
    rstd_full = P.sb("rstd", [128, T], F32)
    rstd_b = Buf("rstd")
    xs = Ring(P, "xs", 2, [128, D], F32, lanes=True)
    cur = Ring(P, "cu", 2, [128, T + 2], F32)
    bsb = Ring(P, "bsb", 2, [128, T], F32)
    acc = Ring(P, "acc", 2, [128, T], F32)
    halo = P.sb("halo", [128, NF, 2], F32)
    halo_b = [Buf("halo%d" % f) for f in range(NF)]
    P.op("dve", E("memset", ap=halo[:], constant=0.0), writes=halo_b)
    st_lane = P.lane("stA", out=(xch is None))
    w_in_v = w_in.rearrange("(k p) m -> p k m", p=128)
    w_out_v = w_out.rearrange("(k p) m -> p k m", p=128)
    h1_v = h1.rearrange("f p t -> p f t")
    hn1_v = hn1.rearrange("f p t -> p f t") if hn1 is not None else None

    for ti in range(ntile):
        T = TA_TILES[ti]
        t0 = sum(TA_TILES[:ti])
        segs = [(0, T)] if T <= 512 else [(0, T - 512), (T - 512, T)]
        hT = hT_full[:, :, 0:T]
        hn = hn_full[:, :, 0:T]
        act = act_full[:, :, 0:T]
        y = act[:, 0:NF, :]
        rstd = rstd_full[:, 0:T]
        for g0 in range(0, T, 128):
            gs = min(128, T - g0)
            xt, xb, xl = xs.next()
            P.dma("sp", xl, xt[:gs, :], xa[t0 + g0:t0 + g0 + gs, :], writes=[xb])
            for fq in range(4):
                pb, pbb, _ = K.psum.next()
                for j in range(4):
                    f = fq * 4 + j
                    P.op("pe", E("transpose", out=pb[:, j * 128:j * 128 + gs], in_=xt[:gs, f * 128:(f + 1) * 128],
                                 identity=K.ident[:gs, :gs]), reads=[xb, K.ident_b], writes=[pbb])
                P.op("act", E("activation", out=hT[:, fq * 4:(fq + 1) * 4, g0:g0 + gs],
                              in_=pb[:, :].rearrange("p (j t) -> p j t", t=128)[:, :, :gs], func=AF.Copy),
                     reads=[pbb], writes=[hT_b])
        rmsnorm_fm(P, K, hT, hT_b, hn, hn_b, segs, vec, vec_b, 0, rstd, rstd_b)
        for f in range(NF):
            wt, wb, wl = K.wring.next()
            wv = wt[:, 0:6144].rearrange("p (k j m) -> p k j m", k=NF, j=3)
            P.dma_group("pool", wl, [(wv[:, :, j, :], w_in_v[:, :, j * D + f * 128:j * D + (f + 1) * 128])
                                     for j in range(3)], writes=[wb])
            cu, cub, _ = cur.next()
            bs, bsbb, _ = bsb.next()
            ac, acb, _ = acc.next()
            cu, bs, ac = cu[:, 0:T + 2], bs[:, 0:T], ac[:, 0:T]
            P.op("act", E("activation", out=cu[:, 0:2], in_=halo[:, f, :], func=AF.Copy),
                 reads=[halo_b[f]], writes=[cub])
            for (a, b) in segs:
                n = b - a
                pbk = [K.psum.next() for _ in range(3)]
                for j in range(3):
                    mm_group(P, pbk[j][0][:, :n], pbk[j][1],
                             [(wv[:, k, j, :], hn[:, k, a:b]) for k in range(NF)], [wb, hn_b])
                ut, utb, _ = K.tmp.next()
                P.op("act", E("activation", out=ut[:, :n], in_=pbk[2][0][:, :n], func=AF.Copy),
                     reads=[pbk[2][1]], writes=[utb])
                P.op("dve", E("tensor_tensor", out=cu[:, 2 + a:2 + b], in0=pbk[1][0][:, :n], in1=ut[:, :n], op=ALU.mult),
                     reads=[pbk[1][1], utb], writes=[cub])
                P.op("act", E("activation", out=bs[:, a:b], in_=pbk[0][0][:, :n], func=AF.Copy),
                     reads=[pbk[0][1]], writes=[bsbb])
            c0, c1, c2 = 32 + f, 48 + f, 64 + f
            P.op("dve", E("tensor_scalar", out=ac[:, :], in0=cu[:, 2:2 + T], scalar1=vec[:, c2:c2 + 1], scalar2=None,
                          op0=ALU.mult), reads=[cub, vec_b], writes=[acb])
            P.op("dve", E("scalar_tensor_tensor", out=ac[:, :], in0=cu[:, 1:1 + T], scalar=vec[:, c1:c1 + 1],
                          in1=ac[:, :], op0=ALU.mult, op1=ALU.add), reads=[cub, vec_b, acb], writes=[acb])
            P.op("dve", E("scalar_tensor_tensor", out=ac[:, :], in0=cu[:, 0:T], scalar=vec[:, c0:c0 + 1],
                          in1=ac[:, :], op0=ALU.mult, op1=ALU.add), reads=[cub, vec_b, acb], writes=[acb])
            P.op("dve", E("tensor_tensor", out=y[:, f, :], in0=ac[:, :], in1=bs[:, :], op=ALU.mult),
                 reads=[acb, bsbb], writes=[act_b])
            P.op("act", E("activation", out=halo[:, f, :], in_=cu[:, T:T + 2], func=AF.Copy),
                 reads=[cub], writes=[halo_b[f]])
        for mp in range(NF // 2):
            wt, wb, wl = K.wring.next()
            wv = wt[:, 0:4096].rearrange("p (k m) -> p k m", k=NF)
            P.dma("pool", wl, wv, w_out_v[:, :, mp * 256:(mp + 1) * 256], writes=[wb])
            for mi in range(2):
                m = mp * 2 + mi
                for (a, b) in segs:
                    n = b - a
                    pb, pbb, _ = K.psum.next()
                    mm_group(P, pb[:, :n], pbb, [(wv[:, k, mi * 128:(mi + 1) * 128], y[:, k, a:b]) for k in range(NF)],
                             [wb, act_b])
                    P.op("dve", E("tensor_tensor", out=hT[:, m, a:b], in0=pb[:, :n], in1=hT[:, m, a:b], op=ALU.add),
                         reads=[pbb, hT_b], writes=[hT_b])
        rmsnorm_fm(P, K, hT, hT_b, hn, hn_b, segs, vec, vec_b, 16, rstd, rstd_b)
        ffn_fm(P, K, hT, hT_b, hn, hn_b, act, act_b, segs, io["wg"], io["wu"], io["wd"])
        if xch is None:
            P.dma("sp", st_lane, h1_v[:, :, t0:t0 + T], hT[:, :, :], reads=[hT_b])
            rmsnorm_fm(P, K, hT, hT_b, hn, hn_b, segs, vec, vec_b, 80, rstd, rstd_b)
            P.dma("sp", st_lane, hn1_v[:, :, t0:t0 + T], hn[:, :, :], reads=[hn_b])
        else:
            P.dma("sp", st_lane, h1_v[:, :, t0:t0 + T], hT[:, :, :], reads=[hT_b], writes=[xch["h1_b"]])
            rmsnorm_fm(P, K, hT, hT_b, hn, hn_b, segs, vec, vec_b, 80, rstd, rstd_b)
            for part, (c0, c1) in enumerate(((0, 512), (512, T))):
                if c1 <= c0:
                    continue
                xi = 2 * ti + part
                P.dma("sp", st_lane, xch["hn1_loc"][xi].rearrange("(f p) t -> p f t", p=128)[:, :, 0:c1 - c0],
                      hn[:, :, c0:c1], reads=[hn_b], writes=[xch["hn1_loc_b"][xi]])
                P.cc_allgather(xch["hn1_loc"][xi], xch["hn1_loc_b"][xi], xch["hn1_all"][xi], xch["hn1_all_b"][xi], PAIRS)
    return st_lane


def phase_C(P, K, io, tile_T=512, xch=None):
    h1c, og, gw_out, out = io["h1c"], io.get("ogc"), io["gdn_w_out"], io["out"]
    T = tile_T
    ntile = TC // T
    segs = make_segs(T)
    vec = P.sb("vecC_sb", [128, 32], F32)
    vec_b = Buf("vecC")
    P.dma("sp", K.cl, vec[:], io["vecC"], writes=[vec_b])
    hT = P.sb("hTc", [128, NF, T], F32)
    hT_b = Buf("hTc")
    hn = P.sb("hnc", [128, NF, T], BF16)
    hn_b = Buf("hnc")
    act = P.sb("actc", [128, NCF, T], BF16)
    act_b = Buf("actc")
    ogt = act[:, 0:32, :]
    rstd = P.sb("rstdc", [128, T], F32)
    rstd_b = Buf("rstdc")
    ot = Ring(P, "ot", 2, [128, D], F32, lanes=True)
    ld = P.lane("ldC")
    h1_v = h1c.rearrange("f p t -> p f t")
    og_v = og.rearrange("f p t -> p f t") if og is not None else None
    wo_v = gw_out.rearrange("(k p) m -> p k m", p=128)
    hoff = 0 if xch is None else 16
    if xch is not None:
        mk = P.sb("mk_sb", [128, 2], F32)
        mk_b = Buf("mk")
        P.dma("sp", K.cl, mk[:], io["mk"], writes=[mk_b])
    for l in ot.slots:
        P.out_lanes.append(l[2])
    for ti in range(ntile):
        t0 = ti * T
        if xch is None:
            P.dma("sp", ld, hT[:, :, :], h1_v[:, :, t0:t0 + T], writes=[hT_b])
            P.dma_group("sp", ld, [(ogt[:, 0:16, :], og_v[:, 0:16, t0:t0 + T]),
                                   (ogt[:, 16:32, :], og_v[:, 16:32, t0:t0 + T])], writes=[act_b])
        else:
            P.dma("sp", ld, hT[:, :, :], h1_v[:, :, hoff + t0:hoff + t0 + T], reads=[xch["h1_b"]], writes=[hT_b])
            for hf in range(2):
                for cand, dst, dst_b in ((0, ogt, act_b), (1, hn, hn_b)):
                    pieces = []
                    s0 = 64 + cand * 2048 + t0
                    end = s0 + T
                    while s0 < end:
                        bt, cq = s0 // TB, s0 % TB
                        n = min(end - s0, TB - cq)
                        src = xch["og_all"][bt].rearrange("(rf p) t -> p rf t", p=128)[:, hf * 16:(hf + 1) * 16, cq:cq + n]
                        d0 = s0 - (64 + cand * 2048 + t0)
                        dd = dst[:, hf * 16:(hf + 1) * 16, d0:d0 + n] if cand == 0 else dst[:, :, d0:d0 + n]
                        pieces.append((dd, src, xch["og_all_b"][bt]))
                        s0 += n
                    P.dma_group("sp", ld, [(d_, s_) for d_, s_, _ in pieces], reads=[b_ for _, _, b_ in pieces],
                                writes=[dst_b])
                oh = ogt[:, hf * 16:(hf + 1) * 16, :]
                P.op("dve", E("tensor_scalar", out=oh, in0=oh, scalar1=mk[:, 0:1], scalar2=None, op0=ALU.mult),
                     reads=[act_b, mk_b], writes=[act_b])
                P.op("dve", E("scalar_tensor_tensor", out=oh, in0=hn[:, :, :], scalar=mk[:, 1:2], in1=oh,
                              op0=ALU.mult, op1=ALU.add), reads=[hn_b, act_b, mk_b], writes=[act_b])
        for m in range(NF):
            wt, wb, wl = K.wring.next()
            wv = wt[:, 0:4096].rearrange("p (k m) -> p k m", k=32)
            P.dma("pool", wl, wv, wo_v[:, :, m * 128:(m + 1) * 128], writes=[wb])
            for (a, b) in segs:
                n = b - a
                pb, pbb, _ = K.psum.next()
                mm_group(P, pb[:, :n], pbb, [(wv[:, k, :], ogt[:, k, a:b]) for k in range(32)], [wb, act_b])
                P.op("dve", E("tensor_tensor", out=hT[:, m, a:b], in0=pb[:, :n], in1=hT[:, m, a:b], op=ALU.add),
                     reads=[pbb, hT_b], writes=[hT_b])
        rmsnorm_fm(P, K, hT, hT_b, hn, hn_b, segs, vec, vec_b, 0, rstd, rstd_b)
        ffn_fm(P, K, hT, hT_b, hn, hn_b, act, act_b, segs, io["wg"], io["wu"], io["wd"])
        rmsnorm_fm(P, K, hT, hT_b, hT, hT_b, segs, vec, vec_b, 16, rstd, rstd_b)
        for g0 in range(0, T, 128):
            o_t, ob, ol = ot.next()
            for fq in range(4):
                pb, pbb, _ = K.psum.next()
                for j in range(4):
                    f = fq * 4 + j
                    P.op("pe", E("transpose", out=pb[:, j * 128:(j + 1) * 128], in_=hT[:, f, g0:g0 + 128],
                                 identity=K.ident[:, :]), reads=[hT_b, K.ident_b], writes=[pbb])
                if fq % 2:
                    P.op("act", E("activation", out=o_t[:, fq * 512:(fq + 1) * 512], in_=pb[:, :], func=AF.Copy),
                         reads=[pbb], writes=[ob])
                else:
                    P.op("dve", E("tensor_copy", out=o_t[:, fq * 512:(fq + 1) * 512], in_=pb[:, :]),
                         reads=[pbb], writes=[ob])
            P.dma("sp", ol, out[t0 + g0:t0 + g0 + 128, :], o_t[:, :], reads=[ob])


def _fm(v):
    v = np.asarray(v, np.float32)
    return np.ascontiguousarray(v.reshape(-1, 128).T)


def build_A(tile_T=None):
    nc = bass.Bass("TRN2", target_bir_lowering=False)
    io = {}

    def inp(name, shape, dt=F32):
        io[name] = nc.dram_tensor(name, list(shape), dt, kind="ExternalInput").ap()

    inp("xa", [TA, D])
    inp("ident", [128, 128])
    inp("vecA", [128, 96])
    inp("sc_w_in", [D, 3 * D])
    inp("sc_w_out", [D, D])
    inp("wg", [D, DFF])
    inp("wu", [D, DFF])
    inp("wd", [DFF, D])
    io["h1"] = nc.dram_tensor("h1", [NF, 128, TA], F32, kind="ExternalOutput").ap()
    io["hn1"] = nc.dram_tensor("hn1", [NF, 128, TA], BF16, kind="ExternalOutput").ap()
    P = Prog(nc)
    K = Common(P, nc, io["ident"])
    phase_A(P, K, io, tile_T)
    P.emit()
    P.close()
    return nc


def host_inputs_A(inp):
    x = np.asarray(inp["x"], np.float32)
    meta = np.asarray(inp["meta_tokens"], np.float32)
    conv = np.asarray(inp["sc_conv_w"], np.float32)[0]
    vecA = np.concatenate([_fm(inp["mixer_norm"][0]), _fm(inp["ffn_norm"][0]),
                           _fm(conv[0]), _fm(conv[1]), _fm(conv[2]), _fm(inp["mixer_norm"][1])], axis=1)
    shared = {
        "ident": np.eye(128, dtype=np.float32),
        "vecA": np.ascontiguousarray(vecA),
        "sc_w_in": np.ascontiguousarray(np.asarray(inp["sc_w_in"], np.float32)[0]),
        "sc_w_out": np.ascontiguousarray(np.asarray(inp["sc_w_out"], np.float32)[0]),
        "wg": np.ascontiguousarray(np.asarray(inp["ffn_w_gate"], np.float32)[0]),
        "wu": np.ascontiguousarray(np.asarray(inp["ffn_w_up"], np.float32)[0]),
        "wd": np.ascontiguousarray(np.asarray(inp["ffn_w_down"], np.float32)[0]),
    }
    maps = []
    for c in range(8):
        b, r = c // 2, c % 2
        if r == 0:
            xa = np.concatenate([meta, x[b, 0:2048]], axis=0)
        else:
            xa = x[b, 2032:4096]
        m = dict(shared)
        m["xa"] = np.ascontiguousarray(xa)
        maps.append(m)
    return maps


def build_C(tile_T=512):
    nc = bass.Bass("TRN2", target_bir_lowering=False)
    io = {}

    def inp(name, shape, dt=F32):
        io[name] = nc.dram_tensor(name, list(shape), dt, kind="ExternalInput").ap()

    inp("h1c", [NF, 128, TC])
    inp("ogc", [32, 128, TC], BF16)
    inp("ident", [128, 128])
    inp("vecC", [128, 32])
    inp("gdn_w_out", [2 * D, D])
    inp("wg", [D, DFF])
    inp("wu", [D, DFF])
    inp("wd", [DFF, D])
    io["out"] = nc.dram_tensor("out", [TC, D], F32, kind="ExternalOutput").ap()
    P = Prog(nc)
    K = Common(P, nc, io["ident"])
    phase_C(P, K, io, tile_T)
    P.emit()
    P.close()
    return nc


NEG = -30000.0
DBG = ""
DBGSTEP = 99
POOLENG = "dve"
STACKED = True
EXPF = AF.Exp
TB = 320
NCH = TB // 64
DK = 128


def phase_B(P, K, io, ntile=None, xch=None):
    hn1g, gw_in, og = io.get("hn1g"), io["gw_in"], io.get("og")
    T = TB
    ntile = ntile or TG // T
    nc = P.nc
    vec = P.sb("vecB_sb", [128, 129], F32)
    vec_b = Buf("vecB")
    P.dma("sp", K.cl, vec[:], io["vecB"], writes=[vec_b])
    row = P.sb("rowB_sb", [128, 32], F32)
    row_b = Buf("rowB")
    P.dma("sp", K.cl, row[:], io["rowB"], writes=[row_b])
    cB = P.sb("cB_sb", [64, 192], F32)
    cB_b = Buf("cB")
    P.dma("sp", K.cl, cB[:], io["cB"], writes=[cB_b])
    Umat = cB[:, 0:64]
    NEGI = cB[:, 64:128]
    NEGS = cB[:, 128:192]
    identb = P.sb("identb", [128, 128], BF16)
    identb_b = Buf("identb")
    P.op("dve", E("tensor_copy", out=identb[:], in_=K.ident[:]), reads=[K.ident_b], writes=[identb_b])
    onec = P.sb("onec", [128, 1], F32)
    onec_b = Buf("onec")
    P.op("dve", E("memset", ap=onec[:], constant=1.0), writes=[onec_b])
    nea = P.sb("nea", [64, 16], F32)
    nea_b = Buf("nea")
    P.op("act", E("activation", out=nea[:], in_=row[0:64, 0:16], func=AF.Exp), reads=[row_b], writes=[nea_b])
    P.op("dve", E("tensor_scalar", out=nea[:], in0=nea[:], scalar1=-1.0, scalar2=None, op0=ALU.mult),
         reads=[nea_b], writes=[nea_b])
    wba = P.sb("wba", [128, NF, 32], BF16)
    wba_b = Buf("wba")
    gw_v = gw_in.rearrange("(k p) m -> p k m", p=128)
    P.dma("pool", P.lane("wba"), wba[:], gw_v[:, :, 6144:6176], writes=[wba_b])
    hnr = Ring(P, "hnB", 2, [128, NF, T], BF16, lanes=True)
    qf = P.sb("qf", [128, 8, T], BF16)
    kf = P.sb("kf", [128, 8, T], BF16)
    vf = P.sb("vf", [128, 16, T], BF16)
    zs = P.sb("zs", [128, 16, T], BF16)
    qf_b, kf_b, vf_b, zs_b = Buf("qf"), Buf("kf"), Buf("vf"), Buf("zs")
    ogr = Ring(P, "ogB", 2, [128, 16, T], BF16, lanes=True)
    if xch is None:
        for sl in ogr.slots:
            P.out_lanes.append(sl[2])
    pcr = Ring(P, "pc", 2, [128, T + 3], F32)
    accr = Ring(P, "accB", 2, [128, T], F32)
    silr = Ring(P, "sil", 2, [128, T], F32)
    rinr = Ring(P, "rin", 2, [128, T], F32)
    halo = P.sb("haloB", [128, 32, 3], F32)
    halo_b = [Buf("haloB%d" % i) for i in range(32)]
    P.op("dve", E("memset", ap=halo[:], constant=0.0), writes=halo_b)
    beta = P.sb("beta", [64, NCH, 16], F32)
    lb = P.sb("lb", [64, NCH, 16], F32)
    gtm = P.sb("gtm", [64, NCH, 16], F32)
    sm_b = Buf("small")
    S = P.sb("S", [128, 16, 128], F32)
    S_b = [Buf("S0"), Buf("S1")]
    Sb = P.sb("Sbf", [128, 16, 128], BF16)
    Sb_b = [Buf("Sb0"), Buf("Sb1")]
    P.op("dve", E("memset", ap=S[:], constant=0.0), writes=S_b)
    P.op("dve", E("memset", ap=Sb[:], constant=0.0), writes=Sb_b)
    def mk(name, shape, dt):
        if STACKED:
            shape = [shape[0]] + [1] * (len(shape) - 1)
        return [(P.sb("%s%d" % (name, i), shape, dt), Buf("%s%d" % (name, i))) for i in range(2)]
    gcs = mk("gcs", [64, 16], F32)
    gbs = mk("gbs", [64, 16], F32)
    egs = mk("egs", [64, 16], F32)
    begs = mk("begs", [64, 16], F32)
    ekts = mk("ekts", [64, 16], F32)
    gtot = mk("gtot", [128, 16], F32)
    gls = mk("gls", [128, 16], F32)
    Dg1 = mk("Dg1", [64, 8, 64], F32)
    Dg2 = mk("Dg2", [64, 8, 64], F32)
    t1, t2, EI, ES = Dg1, Dg2, Dg1, Dg2
    EG = mk("EG", [128, 8, 64], F32)
    qg = mk("qg", [128, 8, 64], BF16)
    qkT = mk("qkT", [64, 8, 64], BF16)
    Pm = [mk("Pm%d" % j, [64, 8, 64], BF16) for j in range(2)]
    PT = [mk("PT%d" % j, [64, 8, 64], BF16) for j in range(2)]
    Rf = mk("Rf", [64, 8, 64], F32)
    Rb = mk("Rb", [64, 8, 64], BF16)
    vb = mk("vb", [64, 8, 128], BF16)
    kbg = mk("kbg", [64, 8, 128], BF16)
    kt = mk("kt", [64, 8, 128], BF16)
    nwT = mk("nwT", [128, 8, 64], BF16)
    vn = mk("vn", [64, 8, 128], BF16)
    osb = mk("osb", [128, 8, 64], F32)
    osq = mk("osq", [128, 8, 64], F32)
    orst = osq
    bank = [(K.psum.slots[i][0], K.psum.slots[i][1]) for i in range(8)]
    identI = K.ident[0:64, 0:64]

    def bc(ap, shape):
        return ap.to_broadcast(list(shape))


    if STACKED:
        cS = P.sb("cBst_sb", [128, 256], F32)
        cS_b = Buf("cBst")
        P.dma("sp", K.cl, cS[:], io["cB2"], writes=[cS_b])
        U_st, NEGI_st, NEGS_st, I_st = cS[:, 0:64], cS[:, 64:128], cS[:, 128:192], cS[:, 192:256]
        blk1m = P.sb("blk1", [128, 128], F32)
        sel = [P.sb("sel%d" % i, [128, 128], F32) for i in range(2)]
        cm_b = Buf("cmats")
        P.op("dve", E("memset", ap=blk1m[:], constant=0.0), writes=[cm_b])
        P.op("dve", E("memset", ap=blk1m[0:64, 0:64], constant=1.0), writes=[cm_b])
        P.op("dve", E("memset", ap=blk1m[64:128, 64:128], constant=1.0), writes=[cm_b])
        for i in range(2):
            P.op("dve", E("memset", ap=sel[i][:], constant=0.0), writes=[cm_b])
            P.op("dve", E("memset", ap=sel[i][i * 64:(i + 1) * 64, :], constant=1.0), writes=[cm_b])
        dtb_st = P.sb("dtb_st", [128, 8], F32)
        nea_st = P.sb("nea_st", [128, 8], F32)
        st_b = Buf("stconst")
        for i in range(2):
            ps_ = slice(i * 64, (i + 1) * 64)
            P.op("act", E("activation", out=dtb_st[ps_, :], in_=row[ps_, 16 + i * 8:24 + i * 8], func=AF.Copy),
                 reads=[row_b], writes=[st_b])
            P.op("act", E("activation", out=nea_st[ps_, :], in_=row[ps_, i * 8:(i + 1) * 8], func=AF.Exp),
                 reads=[row_b], writes=[st_b])
        P.op("dve", E("tensor_scalar", out=nea_st[:], in0=nea_st[:], scalar1=-1.0, scalar2=None, op0=ALU.mult),
             reads=[st_b], writes=[st_b])
        betaS = P.sb("betaS", [128, NCH, 8], F32)
        lbS = P.sb("lbS", [128, NCH, 8], F32)
        gS = P.sb("gS", [128, NCH, 8], F32)
        smS_b = Buf("smallS")

        def one(name, shape, dt):
            return P.sb(name, shape, dt), Buf(name)
        gcS, gcS_b = one("gcS", [128, 8], F32)
        gbS, gbS_b = one("gbS", [128, 8], F32)
        egS, egS_b = one("egS", [128, 8], F32)
        BEG = [one("begS%d" % i, [128, 8], F32) for i in range(2)]
        EKT = [one("ektS%d" % i, [128, 8], F32) for i in range(2)]
        GLB = [one("glS%d" % i, [128, 16], F32) for i in range(2)]
        QKB = [one("QKs%d" % i, [128, 8, 64], BF16) for i in range(2)]
        P0B = [one("P0s%d" % i, [128, 8, 64], BF16) for i in range(2)]
        QGS = [[one("QGs%d_%d" % (i, h), [128, 8, 64], BF16) for h in range(2)] for i in range(2)]
        gtS, gtS_b = one("gtS", [128, 8], F32)
        D1, D1_b = one("D1", [128, 8, 64], F32)
        D2, D2_b = one("D2", [128, 8, 64], F32)
        EGs = [one("EGs%d" % i, [128, 8, 64], F32) for i in range(2)]
        PmS = [one("PmS%d" % i, [128, 8, 64], BF16) for i in range(2)]
        PTS = [one("PTS%d" % i, [128, 8, 64], BF16) for i in range(2)]
        RfS, RfS_b = one("RfS", [128, 8, 64], F32)
        RbS, RbS_b = one("RbS", [128, 8, 64], BF16)
        VB, VB_b = one("VBs", [128, 8, 128], BF16)
        KBG, KBG_b = one("KBGs", [128, 8, 128], BF16)
        KT, KT_b = one("KTs", [128, 8, 128], BF16)
        NW = [one("NWs%d" % i, [128, 8, 64], BF16) for i in range(2)]
        VN, VN_b = one("VNs", [128, 8, 128], BF16)
        OB = [one("OBs%d" % i, [128, 8, 64], F32) for i in range(2)]
        OQ = [one("OQs%d" % i, [128, 8, 64], F32) for i in range(2)]

    def stacked_tail(ti, t0, hn, hn_b):
        HS = (slice(0, 64), slice(64, 128))
        pba, pba_b = bank[0]
        wba4 = wba[:, :, :].rearrange("p k (two h) -> p k two h", two=2)
        for c in range(NCH):
            for hh in range(2):
                mm_group(P, pba[HS[hh], c * 16:(c + 1) * 16], pba_b,
                         [(hn[:, k, c * 64:(c + 1) * 64], wba4[:, k, :, hh * 8:(hh + 1) * 8]) for k in range(NF)],
                         [hn_b, wba_b])
        pv = pba[:, 0:NCH * 16].rearrange("p (c two j) -> p c two j", two=2, j=8)
        P.op("act", E("activation", out=betaS[:, :, :], in_=pv[:, :, 0, :], func=AF.Sigmoid), reads=[pba_b], writes=[smS_b])
        P.op("act", E("activation", out=lbS[:, :, :], in_=betaS[:, :, :], func=AF.Ln), reads=[smS_b], writes=[smS_b])
        P.op("dve", E("tensor_tensor", out=gS[:, :, :], in0=pv[:, :, 1, :], in1=bc(dtb_st[:, :].unsqueeze(1), [128, NCH, 8]),
                      op=ALU.add), reads=[pba_b, st_b], writes=[smS_b])
        P.op("act", E("activation", out=gS[:, :, :], in_=gS[:, :, :], func=AF.Exp), reads=[smS_b], writes=[smS_b])
        P.op("act", E("activation", out=gS[:, :, :], in_=gS[:, :, :], func=AF.Ln, bias=onec[:, 0:1]),
             reads=[smS_b, onec_b], writes=[smS_b])
        P.op("dve", E("tensor_tensor", out=gS[:, :, :], in0=gS[:, :, :], in1=bc(nea_st[:, :].unsqueeze(1), [128, NCH, 8]),
                      op=ALU.mult), reads=[smS_b, st_b], writes=[smS_b])
        ogt, og_b, ol = ogr.next()
        v3 = lambda t: t[:, :].rearrange("p (h c) -> p h c", c=64)
        pair = lambda t: t[:, :, :].rearrange("p (k two) c -> p k two c", two=2)
        pair2 = lambda t: t.rearrange("p (k two) d -> p k two d", two=2)

        def pro(c):
            cp = c % 2
            cs = slice(c * 64, (c + 1) * 64)
            begS, begS_b = BEG[cp]
            ektS, ektS_b = EKT[cp]
            glS, glS_b = GLB[cp]
            QK, QK_b = QKB[cp]
            p0, p0_b = P0B[cp]
            pm, pm_b = bank[0]
            for hh in range(2):
                P.op("pe", E("matmul", out=pm[HS[hh], 256:264], lhsT=U_st[HS[hh], :], rhs=gS[HS[hh], c, :], start=True, stop=True),
                     reads=[cS_b, smS_b], writes=[pm_b])
            P.op("pe", E("matmul", out=pm[:, 264:272], lhsT=blk1m[:, :], rhs=gS[:, c, :], start=True, stop=True),
                 reads=[cm_b, smS_b], writes=[pm_b])
            for hh in range(2):
                P.op("pe", E("matmul", out=pm[:, 272 + hh * 8:280 + hh * 8], lhsT=sel[hh][:, :], rhs=gS[:, c, :], start=True, stop=True),
                     reads=[cm_b, smS_b], writes=[pm_b])
            yield
            P.op("dve", E("tensor_copy", out=gcS[:, :], in_=pm[:, 256:264]), reads=[pm_b], writes=[gcS_b])
            P.op("dve", E("tensor_copy", out=gtS[:, :], in_=pm[:, 264:272]), reads=[pm_b], writes=[gtS_b])
            P.op("act", E("activation", out=glS[:, :], in_=pm[:, 272:288], func=AF.Exp), reads=[pm_b], writes=[glS_b])
            P.op("dve", E("tensor_tensor", out=gbS[:, :], in0=gcS[:, :], in1=lbS[:, c, :], op=ALU.add),
                 reads=[gcS_b, smS_b], writes=[gbS_b])
            P.op("act", E("activation", out=egS[:, :], in_=gcS[:, :], func=AF.Exp), reads=[gcS_b], writes=[egS_b])
            P.op("dve", E("tensor_tensor", out=begS[:, :], in0=egS[:, :], in1=betaS[:, c, :], op=ALU.mult),
                 reads=[egS_b, smS_b], writes=[begS_b])
            P.op("dve", E("tensor_tensor", out=ektS[:, :], in0=gtS[:, :], in1=gcS[:, :], op=ALU.subtract),
                 reads=[gtS_b, gcS_b], writes=[ektS_b])
            P.op("act", E("activation", out=ektS[:, :], in_=ektS[:, :], func=AF.Exp), reads=[ektS_b], writes=[ektS_b])
            P.op("dve", E("tensor_tensor", out=D1[:, :, :], in0=bc(I_st.unsqueeze(1), [128, 8, 64]),
                          in1=bc(gcS[:, :].unsqueeze(2), [128, 8, 64]), op=ALU.mult), reads=[cS_b, gcS_b], writes=[D1_b])
            P.op("pool", E("tensor_tensor", out=D2[:, :, :], in0=bc(I_st.unsqueeze(1), [128, 8, 64]),
                           in1=bc(gbS[:, :].unsqueeze(2), [128, 8, 64]), op=ALU.mult), reads=[cS_b, gbS_b], writes=[D2_b])
            yield
            d1f = D1[:, :, :].rearrange("p h c -> p (h c)")
            d2f = D2[:, :, :].rearrange("p h c -> p (h c)")
            pG, pG_b = bank[1]
            pGb, pGb_b = bank[2]
            pR, pR_b = bank[3]
            P.op("pe", E("matmul", out=pR[:, :], lhsT=sel[0][:, :], rhs=d1f, start=True, stop=True), reads=[cm_b, D1_b], writes=[pR_b])
            P.op("pe", E("matmul", out=pG[:, :], lhsT=blk1m[:, :], rhs=d1f, start=True, stop=True), reads=[cm_b, D1_b], writes=[pG_b])
            P.op("pe", E("matmul", out=pGb[:, :], lhsT=blk1m[:, :], rhs=d2f, start=True, stop=True), reads=[cm_b, D2_b], writes=[pGb_b])
            yield
            P.op("act", E("activation", out=EGs[0][0][:, :, :], in_=v3(pR), func=AF.Exp), reads=[pR_b], writes=[EGs[0][1]])
            P.op("pe", E("matmul", out=pR[:, :], lhsT=sel[1][:, :], rhs=d1f, start=True, stop=True), reads=[cm_b, D1_b], writes=[pR_b])
            gcb = bc(gcS[:, :].unsqueeze(2), [128, 8, 64])
            P.op("dve", E("tensor_tensor", out=D1[:, :, :], in0=v3(pG), in1=gcb, op=ALU.subtract), reads=[pG_b, gcS_b], writes=[D1_b])
            P.op("dve", E("tensor_tensor", out=D2[:, :, :], in0=v3(pGb), in1=gcb, op=ALU.subtract), reads=[pGb_b, gcS_b], writes=[D2_b])
            P.op("act", E("activation", out=EGs[1][0][:, :, :], in_=v3(pR), func=AF.Exp), reads=[pR_b], writes=[EGs[1][1]])
            yield
            P.op("pool", E("tensor_tensor", out=D1[:, :, :], in0=D1[:, :, :], in1=bc(NEGI_st.unsqueeze(1), [128, 8, 64]), op=ALU.add),
                 reads=[D1_b, cS_b], writes=[D1_b])
            P.op("pool", E("tensor_tensor", out=D2[:, :, :], in0=D2[:, :, :], in1=bc(NEGS_st.unsqueeze(1), [128, 8, 64]), op=ALU.add),
                 reads=[D2_b, cS_b], writes=[D2_b])
            P.op("act", E("activation", out=D1[:, :, :], in_=D1[:, :, :], func=AF.Exp), reads=[D1_b], writes=[D1_b])
            P.op("act", E("activation", out=D2[:, :, :], in_=D2[:, :, :], func=AF.Exp), reads=[D2_b], writes=[D2_b])
            for hh in range(2):
                qg_, qg_b = QGS[cp][hh]
                P.op("dve" if hh == 0 else "pool",
                     E("tensor_tensor", out=pair(qg_), in0=bc(qf[:, hh * 4:hh * 4 + 4, cs].unsqueeze(2), [128, 4, 2, 64]),
                       in1=pair(EGs[hh][0]), op=ALU.mult), reads=[qf_b, EGs[hh][1]], writes=[qg_b])
            yield
            pK, pK_b = bank[0]
            for hh in range(2):
                for kh in range(4):
                    kg = hh * 4 + kh
                    P.op("pe", E("matmul", out=pK[HS[hh], kh * 128:kh * 128 + 64], lhsT=kf[:, kg, cs], rhs=kf[:, kg, cs],
                                 start=True, stop=True), reads=[kf_b], writes=[pK_b])
                    P.op("pe", E("matmul", out=pK[HS[hh], kh * 128 + 64:kh * 128 + 128], lhsT=kf[:, kg, cs], rhs=qf[:, kg, cs],
                                 start=True, stop=True), reads=[kf_b, qf_b], writes=[pK_b])
            yield
            pK4 = pK[:, :].rearrange("p (k j c) -> p k j c", j=2, c=64)
            P.op("dve", E("tensor_tensor", out=pair(p0), in0=bc(pK4[:, :, 0, :].unsqueeze(2), [128, 4, 2, 64]), in1=pair(D2), op=ALU.mult),
                 reads=[pK_b, D2_b], writes=[p0_b])
            P.op("dve", E("tensor_tensor", out=pair(QK), in0=bc(pK4[:, :, 1, :].unsqueeze(2), [128, 4, 2, 64]), in1=pair(D1), op=ALU.mult),
                 reads=[pK_b, D1_b], writes=[QK_b])
            P.op("act", E("activation", out=p0[:, :, :], in_=p0[:, :, :], func=AF.Copy, scale=-1.0), reads=[p0_b], writes=[p0_b])
            yield

        def inv(c, filler):
            cp = c % 2
            cs = slice(c * 64, (c + 1) * 64)
            begS, begS_b = BEG[cp]
            ektS, ektS_b = EKT[cp]
            p0, p0_b = P0B[cp]
            pA, pA_b = bank[4]
            pB, pB_b = bank[5]
            pC, pC_b = bank[6]
            pT, pT_b = bank[7]
            pAv = pA[:, :].bitcast(BF16)
            for hh in range(2):
                for j in range(8):
                    P.op("pe", E("transpose", out=pAv[HS[hh], j * 64:(j + 1) * 64], in_=p0[HS[hh], j, :],
                                 identity=identb[HS[hh], hh * 64:(hh + 1) * 64]), reads=[p0_b, identb_b], writes=[pA_b])
            q0, q0_b = PTS[0]
            P.op("act", E("activation", out=q0[:, :, :], in_=pAv[:, 0:512].rearrange("p (h c) -> p h c", c=64), func=AF.Copy),
                 reads=[pA_b], writes=[q0_b])
            P.op("dve", E("tensor_tensor", out=RfS[:, :, :], in0=p0[:, :, :], in1=bc(I_st.unsqueeze(1), [128, 8, 64]), op=ALU.add),
                 reads=[p0_b, cS_b], writes=[RfS_b])
            P.op("act", E("activation", out=RbS[:, :, :], in_=RfS[:, :, :], func=AF.Copy), reads=[RfS_b], writes=[RbS_b])
            pTv = pT[:, :].bitcast(BF16)
            for hh in range(2):
                for j in range(8):
                    P.op("pe", E("transpose", out=pTv[HS[hh], j * 128:(j + 1) * 128], in_=vf[:, hh * 8 + j, cs], identity=identb[:, :]),
                         reads=[vf_b, identb_b], writes=[pT_b])
            P.op("dve", E("tensor_tensor", out=VB[:, :, :], in0=pTv[:, :].rearrange("p (h v) -> p h v", v=128),
                          in1=bc(betaS[:, c, :].unsqueeze(2), [128, 8, 128]), op=ALU.mult), reads=[pT_b, smS_b], writes=[VB_b])
            next(filler, None)
            pcur, pcur_b = p0, p0_b
            pt_, pt_b = PTS[0]
            nxt = 0
            for lvl in range(6):
                pn_, pn_b = PmS[nxt]
                ptn_, ptn_b = PTS[1 - (lvl % 2)]
                if lvl >= 1:
                    for hh in range(2):
                        for j in range(8):
                            P.op("pe", E("matmul", out=pC[HS[hh], j * 64:(j + 1) * 64], lhsT=pt_[HS[hh], j, :], rhs=RbS[HS[hh], j, :],
                                         start=True, stop=True), reads=[pt_b, RbS_b], writes=[pC_b])
                if lvl <= 4:
                    for hh in range(2):
                        for j in range(8):
                            P.op("pe", E("matmul", out=pA[HS[hh], j * 64:(j + 1) * 64], lhsT=pt_[HS[hh], j, :], rhs=pcur[HS[hh], j, :],
                                         start=True, stop=True), reads=[pt_b, pcur_b], writes=[pA_b])
                    for hh in range(2):
                        for j in range(8):
                            P.op("pe", E("matmul", out=pB[HS[hh], j * 64:(j + 1) * 64], lhsT=pcur[HS[hh], j, :], rhs=pt_[HS[hh], j, :],
                                         start=True, stop=True), reads=[pt_b, pcur_b], writes=[pB_b])
                if lvl == 1:
                    pT2v = pT[:, :].bitcast(BF16)
                    for hh in range(2):
                        for kh in range(4):
                            P.op("pe", E("transpose", out=pT2v[HS[hh], kh * 128:(kh + 1) * 128], in_=kf[:, hh * 4 + kh, cs], identity=identb[:, :]),
                                 reads=[kf_b, identb_b], writes=[pT_b])
                    ktm4 = bc(pT2v[:, 0:512].rearrange("p (k d) -> p k d", d=128).unsqueeze(2), [128, 4, 2, 128])
                    P.op("dve", E("tensor_tensor", out=pair2(KBG[:, :, :]), in0=ktm4,
                                  in1=pair2(bc(begS[:, :].unsqueeze(2), [128, 8, 128])), op=ALU.mult), reads=[pT_b, begS_b], writes=[KBG_b])
                    P.op("dve", E("tensor_tensor", out=pair2(KT[:, :, :]), in0=ktm4,
                                  in1=pair2(bc(ektS[:, :].unsqueeze(2), [128, 8, 128])), op=ALU.mult), reads=[pT_b, ektS_b], writes=[KT_b])
                next(filler, None)
                if lvl >= 1:
                    P.op("pool", E("tensor_tensor", out=RbS[:, :, :], in0=RfS[:, :, :], in1=RfS[:, :, :], op=ALU.bypass),
                         reads=[RfS_b], writes=[RbS_b]) if False else None
                    P.op("dve", E("tensor_tensor", out=RfS[:, :, :], in0=v3(pC), in1=RfS[:, :, :], op=ALU.add),
                         reads=[pC_b, RfS_b], writes=[RfS_b])
                    P.op("act", E("activation", out=RbS[:, :, :], in_=RfS[:, :, :], func=AF.Copy), reads=[RfS_b], writes=[RbS_b])
                if lvl <= 4:
                    P.op("act", E("activation", out=pn_[:, :, :], in_=v3(pA), func=AF.Copy), reads=[pA_b], writes=[pn_b])
                    P.op("dve", E("tensor_copy", out=ptn_[:, :, :], in_=v3(pB)), reads=[pB_b], writes=[ptn_b])
                    pcur, pcur_b = pn_, pn_b
                    pt_, pt_b = ptn_, ptn_b
                    nxt = 1 - nxt
                next(filler, None)

        def tail(c):
            cp = c % 2
            cs = slice(c * 64, (c + 1) * 64)
            glS, glS_b = GLB[cp]
            QK, QK_b = QKB[cp]
            for hh in range(2):
                pW, pW_b = bank[4 + hh]
                for j in range(8):
                    P.op("pe", E("matmul", out=pW[:, j * 64:(j + 1) * 64], lhsT=KBG[HS[hh], j, :], rhs=RbS[HS[hh], j, :],
                                 start=True, stop=True), reads=[KBG_b, RbS_b], writes=[pW_b])
                if hh == 0:
                    P.op("act", E("activation", out=NW[hh][0][:, :, :], in_=v3(pW), func=AF.Copy, scale=-1.0),
                         reads=[pW_b], writes=[NW[hh][1]])
                else:
                    P.op("dve", E("tensor_scalar", out=NW[hh][0][:, :, :], in0=v3(pW), scalar1=-1.0, scalar2=None, op0=ALU.mult),
                         reads=[pW_b], writes=[NW[hh][1]])
            for q in range(2):
                pV, pV_b = bank[6 + q]
                for hh in range(2):
                    for j4 in range(4):
                        j = q * 4 + j4
                        P.op("pe", E("matmul", out=pV[HS[hh], j4 * 128:(j4 + 1) * 128], lhsT=RbS[HS[hh], j, :], rhs=VB[HS[hh], j, :],
                                     start=True, stop=False), reads=[RbS_b, VB_b], writes=[pV_b])
                        P.op("pe", E("matmul", out=pV[HS[hh], j4 * 128:(j4 + 1) * 128], lhsT=NW[hh][0][:, j, :], rhs=Sb[:, hh * 8 + j, :],
                                     start=False, stop=True), reads=[NW[hh][1], Sb_b[hh]], writes=[pV_b])
                if q == 0:
                    P.op("act", E("activation", out=VN[:, 0:4, :], in_=pV[:, :].rearrange("p (h v) -> p h v", v=128), func=AF.Copy),
                         reads=[pV_b], writes=[VN_b])
                else:
                    P.op("dve", E("tensor_copy", out=VN[:, 4:8, :], in_=pV[:, :].rearrange("p (h v) -> p h v", v=128)),
                         reads=[pV_b], writes=[VN_b])
            pO = [bank[0], bank[1]]
            for hh in range(2):
                qg_, qg_b = QGS[cp][hh]
                for j in range(8):
                    P.op("pe", E("matmul", out=pO[hh][0][:, j * 64:(j + 1) * 64], lhsT=Sb[:, hh * 8 + j, :], rhs=qg_[:, j, :],
                                 start=True, stop=False), reads=[Sb_b[hh], qg_b], writes=[pO[hh][1]])
                    P.op("pe", E("matmul", out=pO[hh][0][:, j * 64:(j + 1) * 64], lhsT=VN[HS[hh], j, :], rhs=QK[HS[hh], j, :],
                                 start=False, stop=True), reads=[VN_b, QK_b], writes=[pO[hh][1]])
            for hh in range(2):
                for q in range(2):
                    pS, pS_b = bank[2 + q] if hh == 0 else bank[4 + q]
                    for j4 in range(4):
                        j = q * 4 + j4
                        P.op("pe", E("matmul", out=pS[:, j4 * 128:(j4 + 1) * 128], lhsT=KT[HS[hh], j, :], rhs=VN[HS[hh], j, :],
                                     start=True, stop=True), reads=[KT_b, VN_b], writes=[pS_b])
                    h0 = hh * 8 + q * 4
                    sv = S[:, h0:h0 + 4, :]
                    e1 = "dve" if q == 0 else "pool"
                    P.op(e1, E("tensor_tensor", out=sv, in0=sv, in1=bc(glS[:, h0:h0 + 4].unsqueeze(2), [128, 4, 128]), op=ALU.mult),
                         reads=[S_b[hh], glS_b], writes=[S_b[hh]])
                    P.op("dve", E("tensor_tensor", out=sv, in0=pS[:, :].rearrange("p (h v) -> p h v", v=128), in1=sv, op=ALU.add),
                         reads=[pS_b, S_b[hh]], writes=[S_b[hh]])
                    P.op("act", E("activation", out=Sb[:, h0:h0 + 4, :], in_=sv, func=AF.Copy), reads=[S_b[hh]], writes=[Sb_b[hh]])
            for hh in range(2):
                ob, ob_b = OB[hh]
                oq, oq_b = OQ[hh]
                pO3 = v3(pO[hh][0])
                hs = slice(hh * 8, (hh + 1) * 8)
                P.op("dve", E("tensor_copy", out=ob[:, :, :], in_=pO3), reads=[pO[hh][1]], writes=[ob_b])
                P.op("act", E("activation", out=oq[:, :, :], in_=pO3, func=AF.Square), reads=[pO[hh][1]], writes=[oq_b])
                pQ, pQ_b = bank[6 + hh]
                P.op("pe", E("matmul", out=pQ[:, :], lhsT=K.ones[:, :], rhs=oq[:, :, :].rearrange("p h c -> p (h c)"),
                             start=True, stop=True), reads=[K.ones_b, oq_b], writes=[pQ_b])
                rsqrt_op(P, K, oq[:, :, :].rearrange("p h c -> p (h c)"), oq_b, pQ[:, :], pQ_b, 1.0 / 128, EPS)
                P.op("dve", E("scalar_tensor_tensor", out=ob[:, :, :].rearrange("p h c -> p (h c)"),
                              in0=ob[:, :, :].rearrange("p h c -> p (h c)"), scalar=vec[:, 128:129],
                              in1=oq[:, :, :].rearrange("p h c -> p (h c)"), op0=ALU.mult, op1=ALU.mult),
                     reads=[ob_b, oq_b, vec_b], writes=[ob_b])
                P.op("pool", E("tensor_tensor", out=ogt[:, hs, cs], in0=ob[:, :, :], in1=zs[:, hs, cs], op=ALU.mult),
                     reads=[ob_b, zs_b], writes=[og_b])

        for _ in pro(0):
            pass
        for c in range(NCH):
            filler = pro(c + 1) if c + 1 < NCH else iter(())
            inv(c, filler)
            for _ in filler:
                pass
            tail(c)
        if xch is None:
            P.dma("sp", ol, og.rearrange("f p t -> p f t")[:, :, t0:t0 + T], ogt[:, :, :], reads=[og_b])
        else:
            P.dma("sp", ol, xch["og_loc"][ti].rearrange("(f p) t -> p f t", p=128), ogt[:, :, :],
                  reads=[og_b], writes=[xch["og_loc_b"][ti]])
            P.cc_allgather(xch["og_loc"][ti], xch["og_loc_b"][ti], xch["og_all"][ti], xch["og_all_b"][ti], PAIRS)

    for ti in range(ntile):
        t0 = ti * T
        hn, hn_b, hl = hnr.next()
        if xch is None:
            P.dma("sp", hl, hn[:, :, :], hn1g.rearrange("f p t -> p f t")[:, :, t0:t0 + T], writes=[hn_b])
        else:
            pieces = []
            s0 = t0
            while s0 < t0 + T:
                if s0 < 48:
                    s1 = min(48, t0 + T)
                    P.op("dve", E("memset", ap=hn[:, :, s0 - t0:s1 - t0], constant=0.0), writes=[hn_b])
                    s0 = s1
                    continue
                rk, j = (0, s0 - 48) if s0 < 2112 else (1, s0 - 2112 + 16)
                lim = 2112 if s0 < 2112 else TG
                q = max(i for i in range(len(TA_TILES)) if sum(TA_TILES[:i]) <= j)
                cq = j - sum(TA_TILES[:q])
                part, pc0, plim = (0, cq, 512) if cq < 512 else (1, cq - 512, TA_TILES[q] - 512)
                n = min(t0 + T - s0, lim - s0, plim - pc0)
                xi = 2 * q + part
                src = xch["hn1_all"][xi].rearrange("(r f p) t -> r p f t", r=2, p=128)[rk][:, :, pc0:pc0 + n]
                pieces.append((hn[:, :, s0 - t0:s0 - t0 + n], src, xch["hn1_all_b"][xi]))
                s0 += n
            P.dma_group("sp", hl, [(d_, s_) for d_, s_, _ in pieces], reads=[b_ for _, _, b_ in pieces], writes=[hn_b])
        for blk in range(24):
            wt, wb, wl = K.wring.next()
            wv = wt[:, 0:4096].rearrange("p (k m) -> p k m", k=NF)
            P.dma("pool", wl, wv, gw_v[:, :, blk * 256:(blk + 1) * 256], writes=[wb])
            for mi in range(2):
                mc = blk * 2 + mi
                pb, pbb, _ = K.psum.next()
                mm_group(P, pb[:, :T], pbb, [(wv[:, k, mi * 128:(mi + 1) * 128], hn[:, k, :]) for k in range(NF)],
                         [wb, hn_b])
                if mc >= 32:
                    h = mc - 32
                    P.op("act", E("activation", out=zs[:, h, :], in_=pb[:, :T], func=AF.Silu), reads=[pbb], writes=[zs_b])
                    continue
                pc, pcb, _ = pcr.next()
                ac, acb, _ = accr.next()
                P.op("act", E("activation", out=pc[:, 3:3 + T], in_=pb[:, :T], func=AF.Copy), reads=[pbb], writes=[pcb])
                P.op("act", E("activation", out=pc[:, 0:3], in_=halo[:, mc, :], func=AF.Copy),
                     reads=[halo_b[mc]], writes=[pcb])
                P.op("dve", E("tensor_scalar", out=ac[:, :], in0=pc[:, 3:3 + T], scalar1=vec[:, 96 + mc:97 + mc],
                              scalar2=None, op0=ALU.mult), reads=[pcb, vec_b], writes=[acb])
                for j in range(3):
                    P.op("dve", E("scalar_tensor_tensor", out=ac[:, :], in0=pc[:, j:j + T],
                                  scalar=vec[:, j * 32 + mc:j * 32 + mc + 1], in1=ac[:, :], op0=ALU.mult, op1=ALU.add),
                         reads=[pcb, vec_b, acb], writes=[acb])
                P.op("act", E("activation", out=halo[:, mc, :], in_=pc[:, T:T + 3], func=AF.Copy),
                     reads=[pcb], writes=[halo_b[mc]])
                if mc >= 16:
                    h = mc - 16
                    P.op("act", E("activation", out=vf[:, h, :], in_=ac[:, :], func=AF.Silu), reads=[acb], writes=[vf_b])
                    continue
                sl, slb, _ = silr.next()
                P.op("act", E("activation", out=sl[:, :], in_=ac[:, :], func=AF.Silu), reads=[acb], writes=[slb])
                sq, sqb, _ = K.sq.next()
                P.op("act", E("activation", out=sq[:, :T], in_=sl[:, :], func=AF.Square), reads=[slb], writes=[sqb])
                p2, p2b, _ = K.psum.next()
                P.op("pe", E("matmul", out=p2[:, :T], lhsT=K.ones[:], rhs=sq[:, :T], start=True, stop=True),
                     reads=[sqb, K.ones_b], writes=[p2b])
                ri, rib, _ = rinr.next()
                rsqrt_op(P, K, ri[:, :], rib, p2[:, :T], p2b, 1.0, EPS)
                if mc < 8:
                    P.op("dve", E("scalar_tensor_tensor", out=qf[:, mc, :], in0=sl[:, :], scalar=DK ** -0.5, in1=ri[:, :],
                                  op0=ALU.mult, op1=ALU.mult), reads=[slb, rib], writes=[qf_b])
                else:
                    P.op("dve", E("tensor_tensor", out=kf[:, mc - 8, :], in0=sl[:, :], in1=ri[:, :], op=ALU.mult),
                         reads=[slb, rib], writes=[kf_b])
        if STACKED:
            stacked_tail(ti, t0, hn, hn_b)
            continue
        if DBG == "s2":
            ogt, og_b, ol = ogr.next()
            P.op("dve", E("tensor_copy", out=ogt[:, :, :], in_=vf[:, :, :]), reads=[vf_b, qf_b, kf_b, zs_b], writes=[og_b])
            P.dma("sp", ol, og.rearrange("f p t -> p f t")[:, :, t0:t0 + T], ogt[:, :, :], reads=[og_b])
            continue
        pba, pba_b = bank[0]
        for c in range(NCH):
            mm_group(P, pba[0:64, c * 32:(c + 1) * 32], pba_b,
                     [(hn[:, k, c * 64:(c + 1) * 64], wba[:, k, :]) for k in range(NF)], [hn_b, wba_b])
        pv = pba[0:64, 0:NCH * 32].rearrange("p (c j) -> p c j", j=32)
        P.op("act", E("activation", out=beta[:, :, :], in_=pv[:, :, 0:16], func=AF.Sigmoid), reads=[pba_b], writes=[sm_b])
        P.op("act", E("activation", out=lb[:, :, :], in_=beta[:, :, :], func=AF.Ln), reads=[sm_b], writes=[sm_b])
        P.op("dve", E("tensor_tensor", out=gtm[:, :, :], in0=pv[:, :, 16:32],
                      in1=bc(row[0:64, 16:32].unsqueeze(1), [64, NCH, 16]), op=ALU.add),
             reads=[pba_b, row_b], writes=[sm_b])
        P.op("act", E("activation", out=gtm[:, :, :], in_=gtm[:, :, :], func=AF.Exp), reads=[sm_b], writes=[sm_b])
        P.op("act", E("activation", out=gtm[:, :, :], in_=gtm[:, :, :], func=AF.Ln, bias=onec[0:64, 0:1]),
             reads=[sm_b, onec_b], writes=[sm_b])
        P.op("dve", E("tensor_tensor", out=gtm[:, :, :], in0=gtm[:, :, :],
                      in1=bc(nea[:, :].unsqueeze(1), [64, NCH, 16]), op=ALU.mult), reads=[sm_b, nea_b], writes=[sm_b])
        ogt, og_b, ol = ogr.next()
        if DBG == "s2b":
            P.op("dve", E("tensor_copy", out=ogt[:, :, :], in_=vf[:, :, :]), reads=[vf_b, qf_b, kf_b, zs_b, sm_b], writes=[og_b])
            P.dma("sp", ol, og.rearrange("f p t -> p f t")[:, :, t0:t0 + T], ogt[:, :, :], reads=[og_b])
            continue
        for c in range(NCH):
            cs = slice(c * 64, (c + 1) * 64)
            pm, pm_b = bank[0]
            P.op("pe", E("matmul", out=pm[0:64, 256:272], lhsT=Umat, rhs=gtm[:, c, :], start=True, stop=True),
                 reads=[cB_b, sm_b], writes=[pm_b])
            P.op("pe", E("matmul", out=pm[:, 272:288], lhsT=K.ones[0:64, :], rhs=gtm[:, c, :], start=True, stop=True),
                 reads=[K.ones_b, sm_b], writes=[pm_b])
            gc, gc_b = gcs[0]
            gb, gb_b = gbs[0]
            eg, eg_b = egs[0]
            beg, beg_b = begs[0]
            ekt, ekt_b = ekts[0]
            gt, gt_b = gtot[0]
            gl, gl_b = gls[0]
            P.op("dve", E("tensor_copy", out=gc[:, :], in_=pm[0:64, 256:272]), reads=[pm_b], writes=[gc_b])
            P.op("act", E("activation", out=gt[:, :], in_=pm[:, 272:288], func=AF.Copy), reads=[pm_b], writes=[gt_b])
            P.op("dve", E("tensor_tensor", out=gb[:, :], in0=gc[:, :], in1=lb[:, c, :], op=ALU.add),
                 reads=[gc_b, sm_b], writes=[gb_b])
            P.op("act", E("activation", out=eg[:, :], in_=gc[:, :], func=AF.Exp), reads=[gc_b], writes=[eg_b])
            P.op("dve", E("tensor_tensor", out=beg[:, :], in0=eg[:, :], in1=beta[:, c, :], op=ALU.mult),
                 reads=[eg_b, sm_b], writes=[beg_b])
            P.op("dve", E("tensor_tensor", out=ekt[:, :], in0=gt[0:64, :], in1=gc[:, :], op=ALU.subtract),
                 reads=[gt_b, gc_b], writes=[ekt_b])
            P.op("act", E("activation", out=ekt[:, :], in_=ekt[:, :], func=AF.Exp), reads=[ekt_b], writes=[ekt_b])
            P.op("act", E("activation", out=gl[:, :], in_=gt[:, :], func=AF.Exp), reads=[gt_b], writes=[gl_b])
            for hh in range(2):
                h0 = hh * 8
                k0 = hh * 4
                hs = slice(h0, h0 + 8)
                if DBGSTEP < 2:
                    continue
                d1, d1_b = Dg1[hh]
                d2, d2_b = Dg2[hh]
                P.op("dve", E("tensor_tensor", out=d1[:, :, :], in0=bc(identI.unsqueeze(1), [64, 8, 64]),
                              in1=bc(gc[:, hs].unsqueeze(2), [64, 8, 64]), op=ALU.mult),
                     reads=[K.ident_b, gc_b], writes=[d1_b])
                P.op("dve", E("tensor_tensor", out=d2[:, :, :], in0=bc(identI.unsqueeze(1), [64, 8, 64]),
                              in1=bc(gb[:, hs].unsqueeze(2), [64, 8, 64]), op=ALU.mult),
                     reads=[K.ident_b, gb_b], writes=[d2_b])
                if DBGSTEP < 2.2:
                    continue
                pG, pG_b = bank[1]
                pGb, pGb_b = bank[2]
                P.op("pe", E("matmul", out=pG[:, :], lhsT=K.ones[0:64, :], rhs=d1[:, :, :].rearrange("p h c -> p (h c)"),
                             start=True, stop=True), reads=[K.ones_b, d1_b], writes=[pG_b])
                P.op("pe", E("matmul", out=pGb[0:64, :], lhsT=K.ones[0:64, 0:64], rhs=d2[:, :, :].rearrange("p h c -> p (h c)"),
                             start=True, stop=True), reads=[K.ones_b, d2_b], writes=[pGb_b])
                if DBGSTEP < 2.4:
                    continue
                pG3 = pG[:, :].rearrange("p (h c) -> p h c", c=64)
                pGb3 = pGb[:, :].rearrange("p (h c) -> p h c", c=64)
                a1, a1_b = t1[hh]
                a2, a2_b = t2[hh]
                ei, ei_b = EI[hh]
                es, es_b = ES[hh]
                egr, egr_b = EG[hh]
                gcb = bc(gc[:, hs].unsqueeze(2), [64, 8, 64])
                P.op("dve", E("tensor_tensor", out=a1[:, :, :], in0=pG3[0:64], in1=gcb, op=ALU.subtract),
                     reads=[pG_b, gc_b], writes=[a1_b])
                P.op("dve", E("tensor_tensor", out=a2[:, :, :], in0=pGb3[0:64], in1=gcb, op=ALU.subtract),
                     reads=[pGb_b, gc_b], writes=[a2_b])
                if DBGSTEP < 2.5:
                    continue
                P.op("act", E("activation", out=egr[:, :, :], in_=pG3, func=AF.Exp), reads=[pG_b], writes=[egr_b])
                if DBGSTEP < 2.6:
                    continue
                P.op(POOLENG, E("tensor_tensor", out=a1[:, :, :], in0=a1[:, :, :], in1=bc(NEGI.unsqueeze(1), [64, 8, 64]),
                               op=ALU.add), reads=[a1_b, cB_b], writes=[a1_b])
                P.op(POOLENG, E("tensor_tensor", out=a2[:, :, :], in0=a2[:, :, :], in1=bc(NEGS.unsqueeze(1), [64, 8, 64]),
                               op=ALU.add), reads=[a2_b, cB_b], writes=[a2_b])
                if DBGSTEP < 2.7:
                    continue
                P.op("act", E("activation", out=ei[:, :, :], in_=a1[:, :, :], func=EXPF), reads=[a1_b], writes=[ei_b])
                P.op("act", E("activation", out=es[:, :, :], in_=a2[:, :, :], func=EXPF), reads=[a2_b], writes=[es_b])
                if DBGSTEP < 3:
                    continue
                qgt, qg_b = qg[hh]
                P.op("dve", E("tensor_tensor", out=qgt[:, :, :].rearrange("p (k two) c -> p k two c", two=2),
                              in0=bc(qf[:, k0:k0 + 4, cs].unsqueeze(2), [128, 4, 2, 64]),
                              in1=egr[:, :, :].rearrange("p (k two) c -> p k two c", two=2), op=ALU.mult),
                     reads=[qf_b, egr_b], writes=[qg_b])
                pK, pK_b = bank[3]
                for kh in range(4):
                    P.op("pe", E("matmul", out=pK[0:64, kh * 128:kh * 128 + 64], lhsT=kf[:, k0 + kh, cs], rhs=kf[:, k0 + kh, cs],
                                 start=True, stop=True), reads=[kf_b], writes=[pK_b])
                    P.op("pe", E("matmul", out=pK[0:64, kh * 128 + 64:kh * 128 + 128], lhsT=kf[:, k0 + kh, cs],
                                 rhs=qf[:, k0 + kh, cs], start=True, stop=True), reads=[kf_b, qf_b], writes=[pK_b])
                pK4 = pK[0:64, :].rearrange("p (k j c) -> p k j c", j=2, c=64)
                p0, p0_b = Pm[0][hh]
                qk, qk_b = qkT[hh]
                P.op("dve", E("tensor_tensor", out=p0[:, :, :].rearrange("p (k two) c -> p k two c", two=2),
                              in0=bc(pK4[:, :, 0, :].unsqueeze(2), [64, 4, 2, 64]),
                              in1=es[:, :, :].rearrange("p (k two) c -> p k two c", two=2), op=ALU.mult),
                     reads=[pK_b, es_b], writes=[p0_b])
                P.op("dve", E("tensor_tensor", out=qk[:, :, :].rearrange("p (k two) c -> p k two c", two=2),
                              in0=bc(pK4[:, :, 1, :].unsqueeze(2), [64, 4, 2, 64]),
                              in1=ei[:, :, :].rearrange("p (k two) c -> p k two c", two=2), op=ALU.mult),
                     reads=[pK_b, ei_b], writes=[qk_b])
                P.op(POOLENG, E("tensor_scalar", out=p0[:, :, :], in0=p0[:, :, :], scalar1=-1.0, scalar2=None, op0=ALU.mult),
                     reads=[p0_b], writes=[p0_b])
                if DBGSTEP < 4:
                    continue
                pT, pT_b = bank[7]
                pTv = pT[:, :].bitcast(BF16)
                for h in range(8):
                    P.op("pe", E("transpose", out=pTv[0:64, h * 128:(h + 1) * 128], in_=vf[:, h0 + h, cs], identity=identb[:, :]),
                         reads=[vf_b, identb_b], writes=[pT_b])
                vbt, vb_b = vb[hh]
                P.op("dve", E("tensor_tensor", out=vbt[:, :, :], in0=pTv[0:64, :].rearrange("p (h v) -> p h v", v=128),
                              in1=bc(beta[:, c, hs].unsqueeze(2), [64, 8, 128]), op=ALU.mult),
                     reads=[pT_b, sm_b], writes=[vb_b])
                pT2, pT2_b = bank[6]
                pT2v = pT2[:, :].bitcast(BF16)
                for kh in range(4):
                    P.op("pe", E("transpose", out=pT2v[0:64, kh * 128:(kh + 1) * 128], in_=kf[:, k0 + kh, cs], identity=identb[:, :]),
                         reads=[kf_b, identb_b], writes=[pT2_b])
                ktm4 = bc(pT2v[0:64, 0:512].rearrange("p (k d) -> p k d", d=128).unsqueeze(2), [64, 4, 2, 128])
                kbt, kb_b = kbg[hh]
                ktt, kt_b = kt[hh]
                P.op("dve", E("tensor_tensor", out=kbt[:, :, :].rearrange("p (k two) d -> p k two d", two=2), in0=ktm4,
                              in1=bc(beg[:, hs].unsqueeze(2), [64, 8, 128]).rearrange("p (k two) d -> p k two d", two=2),
                              op=ALU.mult), reads=[pT2_b, beg_b], writes=[kb_b])
                P.op("dve", E("tensor_tensor", out=ktt[:, :, :].rearrange("p (k two) d -> p k two d", two=2), in0=ktm4,
                              in1=bc(ekt[:, hs].unsqueeze(2), [64, 8, 128]).rearrange("p (k two) d -> p k two d", two=2),
                              op=ALU.mult), reads=[pT2_b, ekt_b], writes=[kt_b])
                if DBGSTEP < 5:
                    continue
                pA, pA_b = bank[4]
                pB, pB_b = bank[5]
                pC, pC_b = bank[6]
                pAv = pA[:, :].bitcast(BF16)
                for h in range(8):
                    P.op("pe", E("transpose", out=pAv[0:64, h * 64:(h + 1) * 64], in_=p0[:, h, :], identity=identb[0:64, 0:64]),
                         reads=[p0_b, identb_b], writes=[pA_b])
                q0, q0_b = PT[0][hh]
                P.op("act", E("activation", out=q0[:, :, :], in_=pAv[0:64, 0:512].rearrange("p (h c) -> p h c", c=64), func=AF.Copy),
                     reads=[pA_b], writes=[q0_b])
                rf, rf_b = Rf[hh]
                rb, rb_b = Rb[hh]
                P.op("dve", E("tensor_tensor", out=rf[:, :, :], in0=p0[:, :, :], in1=bc(identI.unsqueeze(1), [64, 8, 64]),
                              op=ALU.add), reads=[p0_b, K.ident_b], writes=[rf_b])
                P.op("act", E("activation", out=rb[:, :, :], in_=rf[:, :, :], func=AF.Copy), reads=[rf_b], writes=[rb_b])
                cur = 0
                for lvl in range(6):
                    pc_, pc_b = Pm[cur][hh]
                    pt_, pt_b = PT[cur][hh]
                    pn_, pn_b = Pm[1 - cur][hh]
                    ptn_, ptn_b = PT[1 - cur][hh]
                    if lvl >= 1:
                        for h in range(8):
                            P.op("pe", E("matmul", out=pC[0:64, h * 64:(h + 1) * 64], lhsT=pt_[:, h, :], rhs=rb[:, h, :],
                                         start=True, stop=True), reads=[pt_b, rb_b], writes=[pC_b])
                    if lvl <= 4:
                        for h in range(8):
                            P.op("pe", E("matmul", out=pA[0:64, h * 64:(h + 1) * 64], lhsT=pt_[:, h, :], rhs=pc_[:, h, :],
                                         start=True, stop=True), reads=[pt_b, pc_b], writes=[pA_b])
                        for h in range(8):
                            P.op("pe", E("matmul", out=pB[0:64, h * 64:(h + 1) * 64], lhsT=pc_[:, h, :], rhs=pt_[:, h, :],
                                         start=True, stop=True), reads=[pt_b, pc_b], writes=[pB_b])
                    if lvl >= 1:
                        P.op("dve", E("tensor_tensor", out=rf[:, :, :], in0=pC[0:64, :].rearrange("p (h c) -> p h c", c=64),
                                      in1=rf[:, :, :], op=ALU.add), reads=[pC_b, rf_b], writes=[rf_b])
                        P.op("act", E("activation", out=rb[:, :, :], in_=rf[:, :, :], func=AF.Copy), reads=[rf_b], writes=[rb_b])
                    if lvl <= 4:
                        P.op("act", E("activation", out=pn_[:, :, :], in_=pA[0:64, :].rearrange("p (h c) -> p h c", c=64),
                                      func=AF.Copy), reads=[pA_b], writes=[pn_b])
                        P.op("dve", E("tensor_copy", out=ptn_[:, :, :], in_=pB[0:64, :].rearrange("p (h c) -> p h c", c=64)),
                             reads=[pB_b], writes=[ptn_b])
                        cur = 1 - cur
                if DBGSTEP < 6:
                    continue
                pW, pW_b = bank[2]
                for h in range(8):
                    P.op("pe", E("matmul", out=pW[:, h * 64:(h + 1) * 64], lhsT=kbt[:, h, :], rhs=rb[:, h, :], start=True, stop=True),
                         reads=[kb_b, rb_b], writes=[pW_b])
                nw, nw_b = nwT[hh]
                P.op("act", E("activation", out=nw[:, :, :], in_=pW[:, :].rearrange("p (h c) -> p h c", c=64), func=AF.Copy,
                              scale=-1.0), reads=[pW_b], writes=[nw_b])
                if DBGSTEP < 7:
                    continue
                vnt, vn_b = vn[hh]
                for q in range(2):
                    pV, pV_b = bank[4 + q]
                    for h4 in range(4):
                        h = q * 4 + h4
                        P.op("pe", E("matmul", out=pV[0:64, h4 * 128:(h4 + 1) * 128], lhsT=rb[:, h, :], rhs=vbt[:, h, :],
                                     start=True, stop=False), reads=[rb_b, vb_b], writes=[pV_b])
                        P.op("pe", E("matmul", out=pV[0:64, h4 * 128:(h4 + 1) * 128], lhsT=nw[:, h, :], rhs=Sb[:, h0 + h, :],
                                     start=False, stop=True), reads=[nw_b, Sb_b[hh]], writes=[pV_b])
                    if q == 0:
                        P.op("act", E("activation", out=vnt[:, 0:4, :], in_=pV[0:64, :].rearrange("p (h v) -> p h v", v=128),
                                      func=AF.Copy), reads=[pV_b], writes=[vn_b])
                    else:
                        P.op("dve", E("tensor_copy", out=vnt[:, 4:8, :], in_=pV[0:64, :].rearrange("p (h v) -> p h v", v=128)),
                             reads=[pV_b], writes=[vn_b])
                if DBGSTEP < 8:
                    continue
                pO, pO_b = bank[1]
                for h in range(8):
                    P.op("pe", E("matmul", out=pO[:, h * 64:(h + 1) * 64], lhsT=Sb[:, h0 + h, :], rhs=qgt[:, h, :],
                                 start=True, stop=False), reads=[Sb_b[hh], qg_b], writes=[pO_b])
                    P.op("pe", E("matmul", out=pO[:, h * 64:(h + 1) * 64], lhsT=vnt[:, h, :], rhs=qk[:, h, :],
                                 start=False, stop=True), reads=[vn_b, qk_b], writes=[pO_b])
                if DBGSTEP < 9:
                    continue
                for q in range(2):
                    pS, pS_b = bank[4 + q]
                    for h4 in range(4):
                        h = q * 4 + h4
                        P.op("pe", E("matmul", out=pS[:, h4 * 128:(h4 + 1) * 128], lhsT=ktt[:, h, :], rhs=vnt[:, h, :],
                                     start=True, stop=True), reads=[kt_b, vn_b], writes=[pS_b])
                    sv = S[:, h0 + q * 4:h0 + q * 4 + 4, :]
                    P.op("dve", E("tensor_tensor", out=sv, in0=sv, in1=bc(gl[:, h0 + q * 4:h0 + q * 4 + 4].unsqueeze(2), [128, 4, 128]),
                                  op=ALU.mult), reads=[S_b[hh], gl_b], writes=[S_b[hh]])
                    P.op("dve", E("tensor_tensor", out=sv, in0=pS[:, :].rearrange("p (h v) -> p h v", v=128), in1=sv, op=ALU.add),
                         reads=[pS_b, S_b[hh]], writes=[S_b[hh]])
                    P.op("act", E("activation", out=Sb[:, h0 + q * 4:h0 + q * 4 + 4, :], in_=sv, func=AF.Copy),
                         reads=[S_b[hh]], writes=[Sb_b[hh]])
                if DBGSTEP < 10:
                    continue
                ob, ob_b = osb[hh]
                oq, oq_b = osq[hh]
                orr, or_b = orst[hh]
                pO3 = pO[:, :].rearrange("p (h c) -> p h c", c=64)
                P.op("dve", E("tensor_copy", out=ob[:, :, :], in_=pO3), reads=[pO_b], writes=[ob_b])
                P.op("act", E("activation", out=oq[:, :, :], in_=pO3, func=AF.Square), reads=[pO_b], writes=[oq_b])
                pQ, pQ_b = bank[3]
                P.op("pe", E("matmul", out=pQ[:, :], lhsT=K.ones[:, :], rhs=oq[:, :, :].rearrange("p h c -> p (h c)"),
                             start=True, stop=True), reads=[K.ones_b, oq_b], writes=[pQ_b])
                rsqrt_op(P, K, orr[:, :, :].rearrange("p h c -> p (h c)"), or_b, pQ[:, :], pQ_b, 1.0 / 128, EPS)
                P.op("dve", E("scalar_tensor_tensor", out=ob[:, :, :].rearrange("p h c -> p (h c)"),
                              in0=ob[:, :, :].rearrange("p h c -> p (h c)"), scalar=vec[:, 128:129],
                              in1=orr[:, :, :].rearrange("p h c -> p (h c)"), op0=ALU.mult, op1=ALU.mult),
                     reads=[ob_b, or_b, vec_b], writes=[ob_b])
                P.op("dve", E("tensor_tensor", out=ogt[:, hs, cs], in0=ob[:, :, :], in1=zs[:, hs, cs], op=ALU.mult),
                     reads=[ob_b, zs_b], writes=[og_b])
        if DBGSTEP < 10:
            P.op("dve", E("tensor_copy", out=ogt[:, :, :], in_=vf[:, :, :]), reads=[vf_b], writes=[og_b])
        if xch is None:
            P.dma("sp", ol, og.rearrange("f p t -> p f t")[:, :, t0:t0 + T], ogt[:, :, :], reads=[og_b])
        else:
            P.dma("sp", ol, xch["og_loc"][ti].rearrange("(f p) t -> p f t", p=128), ogt[:, :, :],
                  reads=[og_b], writes=[xch["og_loc_b"][ti]])
            P.cc_allgather(xch["og_loc"][ti], xch["og_loc_b"][ti], xch["og_all"][ti], xch["og_all_b"][ti], PAIRS)


def build_B(ntile=None):
    nc = bass.Bass("TRN2", target_bir_lowering=False)
    io = {}

    def inp(name, shape, dt=F32):
        io[name] = nc.dram_tensor(name, list(shape), dt, kind="ExternalInput").ap()

    inp("hn1g", [NF, 128, TG], BF16)
    inp("ident", [128, 128])
    inp("vecB", [128, 129])
    inp("rowB", [128, 32])
    inp("cB", [64, 192])
    inp("cB2", [128, 256])
    inp("gw_in", [D, 6176])
    io["og"] = nc.dram_tensor("og", [NF, 128, TG], BF16, kind="ExternalOutput").ap()
    P = Prog(nc)
    K = Common(P, nc, io["ident"])
    phase_B(P, K, io, ntile)
    P.emit()
    P.close()
    return nc


def const_cB():
    t = np.arange(64)
    U = (t[:, None] <= t[None, :]).astype(np.float32)
    negi = np.where(t[None, :] >= t[:, None], 0.0, NEG).astype(np.float32)
    negs = np.where(t[None, :] > t[:, None], 0.0, NEG).astype(np.float32)
    return np.ascontiguousarray(np.concatenate([U, negi, negs], axis=1))


def const_cB2():
    c = const_cB()
    c = np.concatenate([c, np.eye(64, dtype=np.float32)], axis=1)
    return np.ascontiguousarray(np.concatenate([c, c], axis=0))


def host_weights_B(inp, r):
    w = np.asarray(inp["gdn_w_in"], np.float32)[0]
    cw = np.asarray(inp["gdn_conv_w"], np.float32)[0]
    qs = slice(r * 1024, (r + 1) * 1024)
    ks = slice(2048 + r * 1024, 2048 + (r + 1) * 1024)
    vs = slice(4096 + r * 2048, 4096 + (r + 1) * 2048)
    zs = slice(8192 + r * 2048, 8192 + (r + 1) * 2048)
    bs = slice(12288 + r * 16, 12288 + (r + 1) * 16)
    as_ = slice(12320 + r * 16, 12320 + (r + 1) * 16)
    gw = np.ascontiguousarray(np.concatenate([w[:, qs], w[:, ks], w[:, vs], w[:, zs], w[:, bs], w[:, as_]], axis=1))
    cwl = np.concatenate([cw[:, qs], cw[:, ks], cw[:, vs]], axis=1)
    vec = np.concatenate([_fm(cwl[j]) for j in range(4)] + [np.asarray(inp["gdn_norm_w"], np.float32)[0].reshape(128, 1)], axis=1)
    al = np.asarray(inp["gdn_a_log"], np.float32)[0][r * 16:(r + 1) * 16]
    dtb = np.asarray(inp["gdn_dt_bias"], np.float32)[0][r * 16:(r + 1) * 16]
    row = np.ascontiguousarray(np.broadcast_to(np.concatenate([al, dtb])[None, :], (128, 32)))
    return {"gw_in": gw, "vecB": np.ascontiguousarray(vec.astype(np.float32)), "rowB": row.astype(np.float32),
            "cB": const_cB(), "cB2": const_cB2(), "ident": np.eye(128, dtype=np.float32)}


def _run(nc, maps):
    res = run_bass_kernel_spmd(nc, maps, core_ids=list(range(8)))
    return res.results


def _kernel_unfused(**inputs):
    resA = _run(build_A(), host_inputs_A(inputs))
    wB = [host_weights_B(inputs, r) for r in range(2)]
    mapsB = []
    for c in range(8):
        b, r = c // 2, c % 2
        h0 = np.asarray(resA[2 * b]["hn1"])
        h1_ = np.asarray(resA[2 * b + 1]["hn1"])
        seq = np.concatenate([np.zeros((NF, 128, 48), dtype=h0.dtype), h0, h1_[:, :, 16:]], axis=2)
        m = dict(wB[r])
        m["hn1g"] = np.ascontiguousarray(seq)
        mapsB.append(m)
    resB = _run(build_B(), mapsB)
    shared = {
        "ident": np.eye(128, dtype=np.float32),
        "vecC": np.ascontiguousarray(np.concatenate([_fm(inputs["ffn_norm"][1]), _fm(inputs["final_norm"])], axis=1)),
        "gdn_w_out": np.ascontiguousarray(np.asarray(inputs["gdn_w_out"], np.float32)[0]),
        "wg": np.ascontiguousarray(np.asarray(inputs["ffn_w_gate"], np.float32)[1]),
        "wu": np.ascontiguousarray(np.asarray(inputs["ffn_w_up"], np.float32)[1]),
        "wd": np.ascontiguousarray(np.asarray(inputs["ffn_w_down"], np.float32)[1]),
    }
    mapsC = []
    for c in range(8):
        b, r = c // 2, c % 2
        lo = 64 + r * 2048
        ogc = np.concatenate([np.asarray(resB[2 * b]["og"])[:, :, lo:lo + 2048],
                              np.asarray(resB[2 * b + 1]["og"])[:, :, lo:lo + 2048]], axis=0)
        m = dict(shared)
        m["ogc"] = np.ascontiguousarray(ogc)
        m["h1c"] = np.ascontiguousarray(np.asarray(resA[c]["h1"])[:, :, 16:])
        mapsC.append(m)
    resC = _run(build_C(), mapsC)
    out = np.empty((BATCH, SEQ, D), np.float32)
    for c in range(8):
        b, r = c // 2, c % 2
        out[b, r * 2048:(r + 1) * 2048] = np.asarray(resC[c]["out"])
    return out


def build_fused():
    nc = bass.Bass("TRN2", target_bir_lowering=False)
    io = {}

    def inp(name, shape, dt=F32):
        io[name] = nc.dram_tensor(name, list(shape), dt, kind="ExternalInput").ap()

    inp("xa", [TA, D])
    inp("ident", [128, 128])
    inp("vecA", [128, 96])
    inp("sc_w_in", [D, 3 * D])
    inp("sc_w_out", [D, D])
    inp("wg0", [D, DFF])
    inp("wu0", [D, DFF])
    inp("wd0", [DFF, D])
    inp("vecB", [128, 129])
    inp("rowB", [128, 32])
    inp("cB", [64, 192])
    inp("cB2", [128, 256])
    inp("gw_in", [D, 6176])
    inp("vecC", [128, 32])
    inp("mk", [128, 2])
    inp("gdn_w_out", [2 * D, D])
    inp("wg1", [D, DFF])
    inp("wu1", [D, DFF])
    inp("wd1", [DFF, D])
    io["out"] = nc.dram_tensor("out", [TC, D], F32, kind="ExternalOutput").ap()
    nA = len(TA_TILES)
    nB = TG // TB
    h1 = nc.dram_tensor("h1_loc", [NF, 128, TA], F32).ap()
    xch = {
        "h1_b": Buf("h1"),
        "hn1_loc": [nc.dram_tensor("hn1_loc%d" % i, [D, XWS[i % 2]], BF16).ap() for i in range(2 * nA)],
        "hn1_all": [nc.dram_tensor("hn1_all%d" % i, [2 * D, XWS[i % 2]], BF16).ap() for i in range(2 * nA)],
        "og_loc": [nc.dram_tensor("og_loc%d" % i, [D, TB], BF16).ap() for i in range(nB)],
        "og_all": [nc.dram_tensor("og_all%d" % i, [2 * D, TB], BF16).ap() for i in range(nB)],
    }
    for k in ("hn1_loc", "hn1_all", "og_loc", "og_all"):
        xch[k + "_b"] = [Buf("%s%d" % (k, i)) for i in range(len(xch[k]))]
    P = Prog(nc)
    K = Common(P, nc, io["ident"])
    ioA = {"xa": io["xa"], "vecA": io["vecA"], "sc_w_in": io["sc_w_in"], "sc_w_out": io["sc_w_out"],
           "wg": io["wg0"], "wu": io["wu0"], "wd": io["wd0"], "h1": h1}
    phase_A(P, K, ioA, None, xch)
    P.emit(final=False)
    P.close()
    K = Common(P, nc, io["ident"])
    ioB = {"vecB": io["vecB"], "rowB": io["rowB"], "cB": io["cB"], "cB2": io["cB2"], "gw_in": io["gw_in"]}
    phase_B(P, K, ioB, None, xch)
    P.emit(final=False)
    P.close()
    K = Common(P, nc, io["ident"])
    ioC = {"h1c": h1, "vecC": io["vecC"], "mk": io["mk"], "gdn_w_out": io["gdn_w_out"],
           "wg": io["wg1"], "wu": io["wu1"], "wd": io["wd1"], "out": io["out"]}
    phase_C(P, K, ioC, 512, xch)
    P.emit(final=True)
    P.finish()
    return nc


def host_inputs_fused(inp):
    mapsA = host_inputs_A(inp)
    wB = [host_weights_B(inp, r) for r in range(2)]
    shared = {
        "vecC": np.ascontiguousarray(np.concatenate([_fm(inp["ffn_norm"][1]), _fm(inp["final_norm"])], axis=1)),
        "gdn_w_out": np.ascontiguousarray(np.asarray(inp["gdn_w_out"], np.float32)[0]),
        "wg1": np.ascontiguousarray(np.asarray(inp["ffn_w_gate"], np.float32)[1]),
        "wu1": np.ascontiguousarray(np.asarray(inp["ffn_w_up"], np.float32)[1]),
        "wd1": np.ascontiguousarray(np.asarray(inp["ffn_w_down"], np.float32)[1]),
    }
    maps = []
    for c in range(8):
        r = c % 2
        a = mapsA[c]
        m = {"xa": a["xa"], "ident": a["ident"], "vecA": a["vecA"], "sc_w_in": a["sc_w_in"], "sc_w_out": a["sc_w_out"],
             "wg0": a["wg"], "wu0": a["wu"], "wd0": a["wd"]}
        for k in ("vecB", "rowB", "cB", "cB2", "gw_in"):
            m[k] = wB[r][k]
        m.update(shared)
        mk = np.zeros((128, 2), np.float32)
        mk[:, r] = 1.0
        m["mk"] = mk
        maps.append(m)
    return maps


def kernel_unfused(**inputs):
    return _kernel_unfused(**inputs)


def kernel(**inputs):
    nc = build_fused()
    res = run_bass_kernel_spmd(nc, host_inputs_fused(inputs), core_ids=list(range(8))).results
    out = np.empty((BATCH, SEQ, D), np.float32)
    for c in range(8):
        b, r = c // 2, c % 2
        out[b, r * 2048:(r + 1) * 2048] = np.asarray(res[c]["out"])
    return out
```

```python
import numpy as np
import ml_dtypes
import concourse.bass as bass
import concourse.mybir as mybir
from concourse.bass_utils import run_bass_kernel_spmd

F32 = mybir.dt.float32
BF16 = mybir.dt.bfloat16
AF = mybir.ActivationFunctionType
ALU = mybir.AluOpType
AX = mybir.AxisListType

D = 2048
NF = 16
DFF = 5632
NCF = 44
SEQ = 4096
BATCH = 4
N_META = 16
EPS = 1e-6
TA = 2064
TC = 2048
TG = 4160
PAIRS = [[0, 1], [2, 3], [4, 5], [6, 7]]
TA_TILES = (528, 512, 512, 512)
XWS = (512, 64)

ENGS = ("pe", "act", "dve", "pool", "sp")


class Buf:
    __slots__ = ("name", "last_w", "readers", "excl")

    def __init__(self, name="", excl=False):
        self.name = name
        self.last_w = None
        self.readers = []
        self.excl = excl


class Ins:
    __slots__ = ("eng", "emit", "deps", "inc", "ordinal", "lane", "lane_val", "is_dma", "cc_inc", "phase")

    def __init__(self, eng, emit):
        self.eng = eng
        self.emit = emit
        self.deps = []
        self.inc = False
        self.ordinal = None
        self.lane = None
        self.lane_val = None
        self.is_dma = False
        self.cc_inc = 16
        self.phase = 0


class Lane:
    def __init__(self, name):
        self.name = name
        self.sem = None
        self.count = 0
        self.last = None


class Prog:
    def __init__(self, nc):
        self.nc = nc
        self.streams = {e: [] for e in ENGS}
        self.lanes = []
        self._ctx = []
        self.out_lanes = []
        self.phase = 0
        self.sems = None
        self.ord = {e: 0 for e in ENGS}
        self._semctx = []

    def sb(self, name, shape, dt):
        g = self.nc.sbuf_tensor("%s_p%d" % (name, self.phase), list(shape), dt)
        t = g.__enter__()
        self._ctx.append(g)
        return t

    def ps(self, name, shape, dt=F32):
        g = self.nc.psum_tensor("%s_p%d" % (name, self.phase), list(shape), dt)
        t = g.__enter__()
        self._ctx.append(g)
        return t

    def lane(self, name, out=False):
        l = Lane("%s_p%d" % (name, self.phase))
        self.lanes.append(l)
        if out:
            self.out_lanes.append(l)
        return l

    def _track(self, ins, reads, writes):
        deps = ins.deps
        for b in reads:
            if b.last_w is not None:
                deps.append(b.last_w)
            if b.excl:
                for r in b.readers:
                    if r.eng != ins.eng:
                        deps.append(r)
        for b in writes:
            if b.last_w is not None:
                deps.append(b.last_w)
            deps.extend(b.readers)
        for b in reads:
            if not ins.is_dma:
                b.readers = [r for r in b.readers if r.is_dma or r.eng != ins.eng]
            b.readers.append(ins)
        for b in writes:
            b.last_w = ins
            b.readers = []

    def op(self, eng, emit, reads=(), writes=()):
        ins = Ins(eng, emit)
        ins.phase = self.phase
        self._track(ins, reads, writes)
        self.streams[eng].append(ins)
        return ins

    def dma(self, eng, lane, out, in_, reads=(), writes=()):
        return self.dma_group(eng, lane, [(out, in_)], reads, writes)

    def dma_group(self, eng, lane, pairs, reads=(), writes=()):
        pairs = list(pairs)
        ins = Ins(eng, lambda e: [e.dma_start(out=o, in_=i) for (o, i) in pairs])
        ins.is_dma = True
        ins.phase = self.phase
        ins.lane = lane
        if lane.last is not None:
            ins.deps.append(lane.last)
        lane.count += 16 * len(pairs)
        ins.lane_val = lane.count
        lane.last = ins
        self._track(ins, reads, writes)
        self.streams[eng].append(ins)
        return ins

    def cc_allgather(self, in_ap, in_buf, out_ap, out_buf, groups):
        lane = self.lane("cc%d" % len(self.lanes))
        ins = Ins("pool", lambda e: [e.collective_compute("AllGather", ALU.bypass, replica_groups=groups,
                                                          ins=[in_ap], outs=[out_ap])])
        ins.is_dma = True
        ins.phase = self.phase
        ins.lane = lane
        ins.cc_inc = 1
        lane.count += 1
        ins.lane_val = lane.count
        lane.last = ins
        self._track(ins, [in_buf], [out_buf])
        self.streams["pool"].append(ins)
        return ins

    def emit(self, final=True):
        nc = self.nc
        cur = self.phase
        for e in ENGS:
            for ins in self.streams[e]:
                for d in ins.deps:
                    if not d.is_dma and d.phase == cur:
                        d.inc = True
        for e in ENGS:
            c = self.ord[e]
            for ins in self.streams[e]:
                if ins.inc and not ins.is_dma:
                    c += 1
                    ins.ordinal = c
            self.ord[e] = c
        if self.sems is None:
            self.sems = {}
            for e in ENGS:
                g = nc.semaphore("sem_" + e)
                self.sems[e] = g.__enter__()
                self._semctx.append(g)
        sems = self.sems
        for l in self.lanes:
            if l.sem is None:
                g = nc.semaphore("lane_" + l.name)
                l.sem = g.__enter__()
                self._semctx.append(g)
        out_lanes = self.out_lanes if final else []

        def run(ename, eng):
            seen = {}
            for ins in self.streams[ename]:
                need = {}
                for d in ins.deps:
                    if d.is_dma:
                        key = ("l", id(d.lane))
                        sem = d.lane.sem
                        val = d.lane_val
                    else:
                        if d.phase != cur:
                            continue
                        if d.eng == ename and ename == "pe":
                            continue
                        key = ("e", d.eng)
                        sem = sems[d.eng]
                        val = d.ordinal
                    if seen.get(key, 0) >= val:
                        continue
                    if key not in need or need[key][1] < val:
                        need[key] = (sem, val)
                for key, (sem, val) in need.items():
                    eng.wait_ge(sem, val)
                    seen[key] = val
                bi = ins.emit(eng)
                if ins.is_dma:
                    for b1 in bi:
                        b1.then_inc(ins.lane.sem, getattr(ins, "cc_inc", 16))
                elif ins.inc:
                    bi.then_inc(sems[ename], 1)
            if ename == "sp":
                for l in out_lanes:
                    if l.count:
                        eng.wait_ge(l.sem, l.count)

        with nc.Block() as block:
            @block.tensor
            def _(t):
                run("pe", t)

            @block.scalar
            def _(t):
                run("act", t)

            @block.vector
            def _(t):
                run("dve", t)

            @block.gpsimd
            def _(t):
                run("pool", t)

            @block.sync
            def _(t):
                run("sp", t)
        self.streams = {e: [] for e in ENGS}
        self.phase += 1

    def close(self):
        for g in reversed(self._ctx):
            g.__exit__(None, None, None)
        self._ctx = []

    def finish(self):
        self.close()
        for g in reversed(self._semctx):
            g.__exit__(None, None, None)
        self._semctx = []


class Ring:
    def __init__(self, P, name, n, shape, dt, lanes=False, psum=False):
        self.slots = []
        for i in range(n):
            t = (P.ps if psum else P.sb)("%s%d" % (name, i), shape, dt)
            self.slots.append((t, Buf("%s%d" % (name, i), excl=psum), P.lane("%s%d" % (name, i)) if lanes else None))
        self.i = 0

    def next(self):
        s = self.slots[self.i % len(self.slots)]
        self.i += 1
        return s


def E(method, **kw):
    return lambda e: getattr(e, method)(**kw)


def mm_group(P, out_ap, out_buf, pairs, reads):
    n = len(pairs)
    for i, (l, r) in enumerate(pairs):
        P.op("pe", E("matmul", out=out_ap, lhsT=l, rhs=r, start=(i == 0), stop=(i == n - 1)),
             reads=reads, writes=[out_buf])


def make_segs(T, maxn=512):
    nseg = (T + maxn - 1) // maxn
    base = (T + nseg - 1) // nseg
    segs = []
    a = 0
    while a < T:
        b = min(T, a + base)
        segs.append((a, b))
        a = b
    return segs


class Common:
    def __init__(self, P, nc, ident_dram, wslots=3, wsize=6144):
        self.P = P
        self.psum = Ring(P, "bank", 8, [128, 512], F32, psum=True)
        self.wring = Ring(P, "wr", wslots, [128, wsize], BF16, lanes=True)
        self.ident = P.sb("ident_sb", [128, 128], F32)
        self.ident_b = Buf("ident")
        self.ones = P.sb("ones_sb", [128, 128], F32)
        self.ones_b = Buf("ones")
        self.cl = P.lane("const")
        P.dma("sp", self.cl, self.ident[:], ident_dram, writes=[self.ident_b])
        P.op("dve", E("memset", ap=self.ones[:], constant=1.0), writes=[self.ones_b])
        self.epsc = P.sb("epsc", [128, 2], F32)
        self.eps_b = Buf("epsc")
        self.eps_col = {EPS: 0}
        P.op("dve", E("memset", ap=self.epsc[:], constant=EPS), writes=[self.eps_b])
        self.sq = Ring(P, "sq", 3, [128, 512], F32)
        self.tmp = Ring(P, "tmp", 4, [128, 512], F32)


def rsqrt_op(P, K, out, out_b, in_, in_b, scale, eps):
    P.op("act", E("activation", out=out, in_=in_, func=AF.Ln, bias=K.epsc[:in_.shape[0], K.eps_col[eps]:K.eps_col[eps] + 1],
                  scale=scale), reads=[in_b, K.eps_b], writes=[out_b])
    P.op("act", E("activation", out=out, in_=out, func=AF.Exp, scale=-0.5), reads=[out_b], writes=[out_b])


def rmsnorm_fm(P, K, src, src_b, dst, dst_b, segs, vec, vec_b, wcol, rstd, rstd_b):
    for (a, b) in segs:
        n = b - a
        pb, pbb, _ = K.psum.next()
        for f in range(NF):
            sq, sqb, _ = K.sq.next()
            P.op("act", E("activation", out=sq[:, :n], in_=src[:, f, a:b], func=AF.Square),
                 reads=[src_b], writes=[sqb])
            P.op("pe", E("matmul", out=pb[:, :n], lhsT=K.ones[:], rhs=sq[:, :n], start=(f == 0), stop=(f == NF - 1)),
                 reads=[sqb, K.ones_b], writes=[pbb])
        rsqrt_op(P, K, rstd[:, a:b], rstd_b, pb[:, :n], pbb, 1.0 / D, EPS)
        for f in range(NF):
            P.op("dve", E("scalar_tensor_tensor", out=dst[:, f, a:b], in0=src[:, f, a:b],
                          scalar=vec[:, wcol + f:wcol + f + 1], in1=rstd[:, a:b], op0=ALU.mult, op1=ALU.mult),
                 reads=[src_b, rstd_b, vec_b], writes=[dst_b])


def ffn_fm(P, K, hT, hT_b, hn, hn_b, act, act_b, segs, wg, wu, wd):
    wgv = wg.rearrange("(k p) m -> p k m", p=128)
    wuv = wu.rearrange("(k p) m -> p k m", p=128)
    wdv = wd.rearrange("(k p) m -> p k m", p=128)
    for c in range(NCF):
        wt, wb, wl = K.wring.next()
        wv = wt[:, 0:4096].rearrange("p (k j m) -> p k j m", k=NF, j=2)
        P.dma_group("pool", wl, [(wv[:, :, 0, :], wgv[:, :, c * 128:(c + 1) * 128]),
                                 (wv[:, :, 1, :], wuv[:, :, c * 128:(c + 1) * 128])], writes=[wb])
        for (a, b) in segs:
            n = b - a
            pg, pgb, _ = K.psum.next()
            pu, pub, _ = K.psum.next()
            mm_group(P, pg[:, :n], pgb, [(wv[:, k, 0, :], hn[:, k, a:b]) for k in range(NF)], [wb, hn_b])
            mm_group(P, pu[:, :n], pub, [(wv[:, k, 1, :], hn[:, k, a:b]) for k in range(NF)], [wb, hn_b])
            st, stb, _ = K.tmp.next()
            P.op("act", E("activation", out=st[:, :n], in_=pg[:, :n], func=AF.Silu), reads=[pgb], writes=[stb])
            P.op("dve", E("tensor_tensor", out=act[:, c, a:b], in0=pu[:, :n], in1=st[:, :n], op=ALU.mult),
                 reads=[pub, stb], writes=[act_b])
    for m in range(NF):
        wt, wb, wl = K.wring.next()
        wv = wt[:, 0:NCF * 128].rearrange("p (k m) -> p k m", k=NCF)
        P.dma("pool", wl, wv, wdv[:, :, m * 128:(m + 1) * 128], writes=[wb])
        for (a, b) in segs:
            n = b - a
            pb, pbb, _ = K.psum.next()
            mm_group(P, pb[:, :n], pbb, [(wv[:, k, :], act[:, k, a:b]) for k in range(NCF)], [wb, act_b])
            P.op("dve", E("tensor_tensor", out=hT[:, m, a:b], in0=pb[:, :n], in1=hT[:, m, a:b], op=ALU.add),
                 reads=[pbb, hT_b], writes=[hT_b])


def phase_A(P, K, io, tile_T=516, xch=None):
    xa, w_in, w_out = io["xa"], io["sc_w_in"], io["sc_w_out"]
    h1, hn1 = io["h1"], io.get("hn1")
    T = max(TA_TILES)
    ntile = len(TA_TILES)
    vec = P.sb("vecA_sb", [128, 96], F32)
    vec_b = Buf("vecA")
    P.dma("sp", K.cl, vec[:], io["vecA"], writes=[vec_b])
    hT_full = P.sb("hT", [128, NF, T], F32)
    hT_b = Buf("hT")
    hn_full = P.sb("hn", [128, NF, T], BF16)
    hn_b = Buf("hn")
    act_full = P.sb("act", [128, NCF, T], BF16)
    act_b = Buf("act")
    rstd_full = P.sb("rstd", [128, T], F32)
    rstd_b = Buf("rstd")
    xs = Ring(P, "xs", 2, [128, D], F32, lanes=True)
    cur = Ring(P, "cu", 2, [128, T + 2], F32)
    bsb = Ring(P, "bsb", 2, [128, T], F32)
    acc = Ring(P, "acc", 2, [128, T], F32)
    halo = P.sb("halo", [128, NF, 2], F32)
    halo_b = [Buf("halo%d" % f) for f in range(NF)]
    P.op("dve", E("memset", ap=halo[:], constant=0.0), writes=halo_b)
    st_lane = P.lane("stA", out=(xch is None))
    w_in_v = w_in.rearrange("(k p) m -> p k m", p=128)
    w_out_v = w_out.rearrange("(k p) m -> p k m", p=128)
    h1_v = h1.rearrange("f p t -> p f t")
    hn1_v = hn1.rearrange("f p t -> p f t") if hn1 is not None else None

    for ti in range(ntile):
        T = TA_TILES[ti]
        t0 = sum(TA_TILES[:ti])
        segs = [(0, T)] if T <= 512 else [(0, T - 512), (T - 512, T)]
        hT = hT_full[:, :, 0:T]
        hn = hn_full[:, :, 0:T]
        act = act_full[:, :, 0:T]
        y = act[:, 0:NF, :]
        rstd = rstd_full[:, 0:T]
        for g0 in range(0, T, 128):
            gs = min(128, T - g0)
            xt, xb, xl = xs.next()
            P.dma("sp", xl, xt[:gs, :], xa[t0 + g0:t0 + g0 + gs, :], writes=[xb])
            for fq in range(4):
                pb, pbb, _ = K.psum.next()
                for j in range(4):
                    f = fq * 4 + j
                    P.op("pe", E("transpose", out=pb[:, j * 128:j * 128 + gs], in_=xt[:gs, f * 128:(f + 1) * 128],
                                 identity=K.ident[:gs, :gs]), reads=[xb, K.ident_b], writes=[pbb])
                P.op("act", E("activation", out=hT[:, fq * 4:(fq + 1) * 4, g0:g0 + gs],
                              in_=pb[:, :].rearrange("p (j t) -> p j t", t=128)[:, :, :gs], func=AF.Copy),
                     reads=[pbb], writes=[hT_b])
        rmsnorm_fm(P, K, hT, hT_b, hn, hn_b, segs, vec, vec_b, 0, rstd, rstd_b)
        for f in range(NF):
            wt, wb, wl = K.wring.next()
            wv = wt[:, 0:6144].rearrange("p (k j m) -> p k j m", k=NF, j=3)
            P.dma_group("pool", wl, [(wv[:, :, j, :], w_in_v[:, :, j * D + f * 128:j * D + (f + 1) * 128])
                                     for j in range(3)], writes=[wb])
            cu, cub, _ = cur.next()
            bs, bsbb, _ = bsb.next()
            ac, acb, _ = acc.next()
            cu, bs, ac = cu[:, 0:T + 2], bs[:, 0:T], ac[:, 0:T]
            P.op("act", E("activation", out=cu[:, 0:2], in_=halo[:, f, :], func=AF.Copy),
                 reads=[halo_b[f]], writes=[cub])
            for (a, b) in segs:
                n = b - a
                pbk = [K.psum.next() for _ in range(3)]
                for j in range(3):
                    mm_group(P, pbk[j][0][:, :n], pbk[j][1],
                             [(wv[:, k, j, :], hn[:, k, a:b]) for k in range(NF)], [wb, hn_b])
                ut, utb, _ = K.tmp.next()
                P.op("act", E("activation", out=ut[:, :n], in_=pbk[2][0][:, :n], func=AF.Copy),
                     reads=[pbk[2][1]], writes=[utb])
                P.op("dve", E("tensor_tensor", out=cu[:, 2 + a:2 + b], in0=pbk[1][0][:, :n], in1=ut[:, :n], op=ALU.mult),
                     reads=[pbk[1][1], utb], writes=[cub])
                P.op("act", E("activation", out=bs[:, a:b], in_=pbk[0][0][:, :n], func=AF.Copy),
                     reads=[pbk[0][1]], writes=[bsbb])
            c0, c1, c2 = 32 + f, 48 + f, 64 + f
            P.op("dve", E("tensor_scalar", out=ac[:, :], in0=cu[:, 2:2 + T], scalar1=vec[:, c2:c2 + 1], scalar2=None,
                          op0=ALU.mult), reads=[cub, vec_b], writes=[acb])
            P.op("dve", E("scalar_tensor_tensor", out=ac[:, :], in0=cu[:, 1:1 + T], scalar=vec[:, c1:c1 + 1],
                          in1=ac[:, :], op0=ALU.mult, op1=ALU.add), reads=[cub, vec_b, acb], writes=[acb])
            P.op("dve", E("scalar_tensor_tensor", out=ac[:, :], in0=cu[:, 0:T], scalar=vec[:, c0:c0 + 1],
                          in1=ac[:, :], op0=ALU.mult, op1=ALU.add), reads=[cub, vec_b, acb], writes=[acb])
            P.op("dve", E("tensor_tensor", out=y[:, f, :], in0=ac[:, :], in1=bs[:, :], op=ALU.mult),
                 reads=[acb, bsbb], writes=[act_b])
            P.op("act", E("activation", out=halo[:, f, :], in_=cu[:, T:T + 2], func=AF.Copy),
                 reads=[cub], writes=[halo_b[f]])
        for mp in range(NF // 2):
            wt, wb, wl = K.wring.next()
            wv = wt[:, 0:4096].rearrange("p (k m) -> p k m", k=NF)
            P.dma("pool", wl, wv, w_out_v[:, :, mp * 256:(mp + 1) * 256], writes=[wb])
            for mi in range(2):
                m = mp * 2 + mi
                for (a, b) in segs:
                    n = b - a
                    pb, pbb, _ = K.psum.next()
                    mm_group(P, pb[:, :n], pbb, [(wv[:, k, mi * 128:(mi + 1) * 128], y[:, k, a:b]) for k in range(NF)],
                             [wb, act_b])
                    P.op("dve", E("tensor_tensor", out=hT[:, m, a:b], in0=pb[:, :n], in1=hT[:, m, a:b], op=ALU.add),
                         reads=[pbb, hT_b], writes=[hT_b])
        rmsnorm_fm(P, K, hT, hT_b, hn, hn_b, segs, vec, vec_b, 16, rstd, rstd_b)
        ffn_fm(P, K, hT, hT_b, hn, hn_b, act, act_b, segs, io["wg"], io["wu"], io["wd"])
        if xch is None:
            P.dma("sp", st_lane, h1_v[:, :, t0:t0 + T], hT[:, :, :], reads=[hT_b])
            rmsnorm_fm(P, K, hT, hT_b, hn, hn_b, segs, vec, vec_b, 80, rstd, rstd_b)
            P.dma("sp", st_lane, hn1_v[:, :, t0:t0 + T], hn[:, :, :], reads=[hn_b])
        else:
            P.dma("sp", st_lane, h1_v[:, :, t0:t0 + T], hT[:, :, :], reads=[hT_b], writes=[xch["h1_b"]])
            rmsnorm_fm(P, K, hT, hT_b, hn, hn_b, segs, vec, vec_b, 80, rstd, rstd_b)
            for part, (c0, c1) in enumerate(((0, 512), (T - 64, T))):
                if part == 1 and T <= 512:
                    continue
                xi = 2 * ti + part
                P.dma("sp", st_lane, xch["hn1_loc"][xi].rearrange("(f p) t -> p f t", p=128)[:, :, 0:c1 - c0],
                      hn[:, :, c0:c1], reads=[hn_b], writes=[xch["hn1_loc_b"][xi]])
                P.cc_allgather(xch["hn1_loc"][xi], xch["hn1_loc_b"][xi], xch["hn1_all"][xi], xch["hn1_all_b"][xi], PAIRS)
    return st_lane


def phase_C(P, K, io, tile_T=512, xch=None):
    h1c, og, gw_out, out = io["h1c"], io.get("ogc"), io["gdn_w_out"], io["out"]
    T = tile_T
    ntile = TC // T
    segs = make_segs(T)
    vec = P.sb("vecC_sb", [128, 32], F32)
    vec_b = Buf("vecC")
    P.dma("sp", K.cl, vec[:], io["vecC"], writes=[vec_b])
    hT = P.sb("hTc", [128, NF, T], F32)
    hT_b = Buf("hTc")
    hn = P.sb("hnc", [128, NF, T], BF16)
    hn_b = Buf("hnc")
    act = P.sb("actc", [128, NCF, T], BF16)
    act_b = Buf("actc")
    ogt = act[:, 0:32, :]
    rstd = P.sb("rstdc", [128, T], F32)
    rstd_b = Buf("rstdc")
    ot = Ring(P, "ot", 2, [128, D], F32, lanes=True)
    ld = P.lane("ldC")
    h1_v = h1c.rearrange("f p t -> p f t")
    og_v = og.rearrange("f p t -> p f t") if og is not None else None
    wo_v = gw_out.rearrange("(k p) m -> p k m", p=128)
    hoff = 0 if xch is None else 16
    if xch is not None:
        mk = P.sb("mk_sb", [128, 2], F32)
        mk_b = Buf("mk")
        P.dma("sp", K.cl, mk[:], io["mk"], writes=[mk_b])
    for l in ot.slots:
        P.out_lanes.append(l[2])
    for ti in range(ntile):
        t0 = ti * T
        if xch is None:
            P.dma("sp", ld, hT[:, :, :], h1_v[:, :, t0:t0 + T], writes=[hT_b])
            P.dma_group("sp", ld, [(ogt[:, 0:16, :], og_v[:, 0:16, t0:t0 + T]),
                                   (ogt[:, 16:32, :], og_v[:, 16:32, t0:t0 + T])], writes=[act_b])
        else:
            P.dma("sp", ld, hT[:, :, :], h1_v[:, :, hoff + t0:hoff + t0 + T], reads=[xch["h1_b"]], writes=[hT_b])
            for hf in range(2):
                for cand, dst, dst_b in ((0, ogt, act_b), (1, hn, hn_b)):
                    pieces = []
                    s0 = 64 + cand * 2048 + t0
                    end = s0 + T
                    while s0 < end:
                        bt, cq = s0 // TB, s0 % TB
                        n = min(end - s0, TB - cq)
                        src = xch["og_all"][bt].rearrange("(rf p) t -> p rf t", p=128)[:, hf * 16:(hf + 1) * 16, cq:cq + n]
                        d0 = s0 - (64 + cand * 2048 + t0)
                        dd = dst[:, hf * 16:(hf + 1) * 16, d0:d0 + n] if cand == 0 else dst[:, :, d0:d0 + n]
                        pieces.append((dd, src, xch["og_all_b"][bt]))
                        s0 += n
                    P.dma_group("sp", ld, [(d_, s_) for d_, s_, _ in pieces], reads=[b_ for _, _, b_ in pieces],
                                writes=[dst_b])
                oh = ogt[:, hf * 16:(hf + 1) * 16, :]
                P.op("dve", E("tensor_scalar", out=oh, in0=oh, scalar1=mk[:, 0:1], scalar2=None, op0=ALU.mult),
                     reads=[act_b, mk_b], writes=[act_b])
                P.op("dve", E("scalar_tensor_tensor", out=oh, in0=hn[:, :, :], scalar=mk[:, 1:2], in1=oh,
                              op0=ALU.mult, op1=ALU.add), reads=[hn_b, act_b, mk_b], writes=[act_b])
        for m in range(NF):
            wt, wb, wl = K.wring.next()
            wv = wt[:, 0:4096].rearrange("p (k m) -> p k m", k=32)
            P.dma("pool", wl, wv, wo_v[:, :, m * 128:(m + 1) * 128], writes=[wb])
            for (a, b) in segs:
                n = b - a
                pb, pbb, _ = K.psum.next()
                mm_group(P, pb[:, :n], pbb, [(wv[:, k, :], ogt[:, k, a:b]) for k in range(32)], [wb, act_b])
                P.op("dve", E("tensor_tensor", out=hT[:, m, a:b], in0=pb[:, :n], in1=hT[:, m, a:b], op=ALU.add),
                     reads=[pbb, hT_b], writes=[hT_b])
        rmsnorm_fm(P, K, hT, hT_b, hn, hn_b, segs, vec, vec_b, 0, rstd, rstd_b)
        ffn_fm(P, K, hT, hT_b, hn, hn_b, act, act_b, segs, io["wg"], io["wu"], io["wd"])
        rmsnorm_fm(P, K, hT, hT_b, hT, hT_b, segs, vec, vec_b, 16, rstd, rstd_b)
        for g0 in range(0, T, 128):
            o_t, ob, ol = ot.next()
            for fq in range(4):
                pb, pbb, _ = K.psum.next()
                for j in range(4):
                    f = fq * 4 + j
                    P.op("pe", E("transpose", out=pb[:, j * 128:(j + 1) * 128], in_=hT[:, f, g0:g0 + 128],
                                 identity=K.ident[:, :]), reads=[hT_b, K.ident_b], writes=[pbb])
                if fq % 2:
                    P.op("act", E("activation", out=o_t[:, fq * 512:(fq + 1) * 512], in_=pb[:, :], func=AF.Copy),
                         reads=[pbb], writes=[ob])
                else:
                    P.op("dve", E("tensor_copy", out=o_t[:, fq * 512:(fq + 1) * 512], in_=pb[:, :]),
                         reads=[pbb], writes=[ob])
            P.dma("sp", ol, out[t0 + g0:t0 + g0 + 128, :], o_t[:, :], reads=[ob])


def _fm(v):
    v = np.asarray(v, np.float32)
    return np.ascontiguousarray(v.reshape(-1, 128).T)


def build_A(tile_T=None):
    nc = bass.Bass("TRN2", target_bir_lowering=False)
    io = {}

    def inp(name, shape, dt=F32):
        io[name] = nc.dram_tensor(name, list(shape), dt, kind="ExternalInput").ap()

    inp("xa", [TA, D])
    inp("ident", [128, 128])
    inp("vecA", [128, 96])
    inp("sc_w_in", [D, 3 * D])
    inp("sc_w_out", [D, D])
    inp("wg", [D, DFF])
    inp("wu", [D, DFF])
    inp("wd", [DFF, D])
    io["h1"] = nc.dram_tensor("h1", [NF, 128, TA], F32, kind="ExternalOutput").ap()
    io["hn1"] = nc.dram_tensor("hn1", [NF, 128, TA], BF16, kind="ExternalOutput").ap()
    P = Prog(nc)
    K = Common(P, nc, io["ident"])
    phase_A(P, K, io, tile_T)
    P.emit()
    P.close()
    return nc


def host_inputs_A(inp):
    x = np.asarray(inp["x"], np.float32)
    meta = np.asarray(inp["meta_tokens"], np.float32)
    conv = np.asarray(inp["sc_conv_w"], np.float32)[0]
    vecA = np.concatenate([_fm(inp["mixer_norm"][0]), _fm(inp["ffn_norm"][0]),
                           _fm(conv[0]), _fm(conv[1]), _fm(conv[2]), _fm(inp["mixer_norm"][1])], axis=1)
    shared = {
        "ident": np.eye(128, dtype=np.float32),
        "vecA": np.ascontiguousarray(vecA),
        "sc_w_in": np.ascontiguousarray(np.asarray(inp["sc_w_in"], np.float32)[0]),
        "sc_w_out": np.ascontiguousarray(np.asarray(inp["sc_w_out"], np.float32)[0]),
        "wg": np.ascontiguousarray(np.asarray(inp["ffn_w_gate"], np.float32)[0]),
        "wu": np.ascontiguousarray(np.asarray(inp["ffn_w_up"], np.float32)[0]),
        "wd": np.ascontiguousarray(np.asarray(inp["ffn_w_down"], np.float32)[0]),
    }
    maps = []
    for c in range(8):
        b, r = c // 2, c % 2
        if r == 0:
            xa = np.concatenate([meta, x[b, 0:2048]], axis=0)
        else:
            xa = x[b, 2032:4096]
        m = dict(shared)
        m["xa"] = np.ascontiguousarray(xa)
        maps.append(m)
    return maps


def build_C(tile_T=512):
    nc = bass.Bass("TRN2", target_bir_lowering=False)
    io = {}

    def inp(name, shape, dt=F32):
        io[name] = nc.dram_tensor(name, list(shape), dt, kind="ExternalInput").ap()

    inp("h1c", [NF, 128, TC])
    inp("ogc", [32, 128, TC], BF16)
    inp("ident", [128, 128])
    inp("vecC", [128, 32])
    inp("gdn_w_out", [2 * D, D])
    inp("wg", [D, DFF])
    inp("wu", [D, DFF])
    inp("wd", [DFF, D])
    io["out"] = nc.dram_tensor("out", [TC, D], F32, kind="ExternalOutput").ap()
    P = Prog(nc)
    K = Common(P, nc, io["ident"])
    phase_C(P, K, io, tile_T)
    P.emit()
    P.close()
    return nc


NEG = -30000.0
DBG = ""
DBGSTEP = 99
POOLENG = "dve"
STACKED = True
EXPF = AF.Exp
TB = 320
NCH = TB // 64
DK = 128


def phase_B(P, K, io, ntile=None, xch=None):
    hn1g, gw_in, og = io.get("hn1g"), io["gw_in"], io.get("og")
    T = TB
    ntile = ntile or TG // T
    nc = P.nc
    vec = P.sb("vecB_sb", [128, 129], F32)
    vec_b = Buf("vecB")
    P.dma("sp", K.cl, vec[:], io["vecB"], writes=[vec_b])
    row = P.sb("rowB_sb", [128, 32], F32)
    row_b = Buf("rowB")
    P.dma("sp", K.cl, row[:], io["rowB"], writes=[row_b])
    cB = P.sb("cB_sb", [64, 192], F32)
    cB_b = Buf("cB")
    P.dma("sp", K.cl, cB[:], io["cB"], writes=[cB_b])
    Umat = cB[:, 0:64]
    NEGI = cB[:, 64:128]
    NEGS = cB[:, 128:192]
    identb = P.sb("identb", [128, 128], BF16)
    identb_b = Buf("identb")
    P.op("dve", E("tensor_copy", out=identb[:], in_=K.ident[:]), reads=[K.ident_b], writes=[identb_b])
    onec = P.sb("onec", [128, 1], F32)
    onec_b = Buf("onec")
    P.op("dve", E("memset", ap=onec[:], constant=1.0), writes=[onec_b])
    nea = P.sb("nea", [64, 16], F32)
    nea_b = Buf("nea")
    P.op("act", E("activation", out=nea[:], in_=row[0:64, 0:16], func=AF.Exp), reads=[row_b], writes=[nea_b])
    P.op("dve", E("tensor_scalar", out=nea[:], in0=nea[:], scalar1=-1.0, scalar2=None, op0=ALU.mult),
         reads=[nea_b], writes=[nea_b])
    wba = P.sb("wba", [128, NF, 32], BF16)
    wba_b = Buf("wba")
    gw_v = gw_in.rearrange("(k p) m -> p k m", p=128)
    P.dma("pool", P.lane("wba"), wba[:], gw_v[:, :, 6144:6176], writes=[wba_b])
    hnr = Ring(P, "hnB", 2, [128, NF, T], BF16, lanes=True)
    qf = P.sb("qf", [128, 8, T], BF16)
    kf = P.sb("kf", [128, 8, T], BF16)
    vf = P.sb("vf", [128, 16, T], BF16)
    zs = P.sb("zs", [128, 16, T], BF16)
    qf_b, kf_b, vf_b, zs_b = Buf("qf"), Buf("kf"), Buf("vf"), Buf("zs")
    ogr = Ring(P, "ogB", 2, [128, 16, T], BF16, lanes=True)
    if xch is None:
        for sl in ogr.slots:
            P.out_lanes.append(sl[2])
    pcr = Ring(P, "pc", 2, [128, T + 3], F32)
    accr = Ring(P, "accB", 2, [128, T], F32)
    silr = Ring(P, "sil", 2, [128, T], F32)
    rinr = Ring(P, "rin", 2, [128, T], F32)
    halo = P.sb("haloB", [128, 32, 3], F32)
    halo_b = [Buf("haloB%d" % i) for i in range(32)]
    P.op("dve", E("memset", ap=halo[:], constant=0.0), writes=halo_b)
    beta = P.sb("beta", [64, NCH, 16], F32)
    lb = P.sb("lb", [64, NCH, 16], F32)
    gtm = P.sb("gtm", [64, NCH, 16], F32)
    sm_b = Buf("small")
    S = P.sb("S", [128, 16, 128], F32)
    S_b = [Buf("S0"), Buf("S1")]
    Sb = P.sb("Sbf", [128, 16, 128], BF16)
    Sb_b = [Buf("Sb0"), Buf("Sb1")]
    P.op("dve", E("memset", ap=S[:], constant=0.0), writes=S_b)
    P.op("dve", E("memset", ap=Sb[:], constant=0.0), writes=Sb_b)
    def mk(name, shape, dt):
        if STACKED:
            shape = [shape[0]] + [1] * (len(shape) - 1)
        return [(P.sb("%s%d" % (name, i), shape, dt), Buf("%s%d" % (name, i))) for i in range(2)]
    gcs = mk("gcs", [64, 16], F32)
    gbs = mk("gbs", [64, 16], F32)
    egs = mk("egs", [64, 16], F32)
    begs = mk("begs", [64, 16], F32)
    ekts = mk("ekts", [64, 16], F32)
    gtot = mk("gtot", [128, 16], F32)
    gls = mk("gls", [128, 16], F32)
    Dg1 = mk("Dg1", [64, 8, 64], F32)
    Dg2 = mk("Dg2", [64, 8, 64], F32)
    t1, t2, EI, ES = Dg1, Dg2, Dg1, Dg2
    EG = mk("EG", [128, 8, 64], F32)
    qg = mk("qg", [128, 8, 64], BF16)
    qkT = mk("qkT", [64, 8, 64], BF16)
    Pm = [mk("Pm%d" % j, [64, 8, 64], BF16) for j in range(2)]
    PT = [mk("PT%d" % j, [64, 8, 64], BF16) for j in range(2)]
    Rf = mk("Rf", [64, 8, 64], F32)
    Rb = mk("Rb", [64, 8, 64], BF16)
    vb = mk("vb", [64, 8, 128], BF16)
    kbg = mk("kbg", [64, 8, 128], BF16)
    kt = mk("kt", [64, 8, 128], BF16)
    nwT = mk("nwT", [128, 8, 64], BF16)
    vn = mk("vn", [64, 8, 128], BF16)
    osb = mk("osb", [128, 8, 64], F32)
    osq = mk("osq", [128, 8, 64], F32)
    orst = osq
    bank = [(K.psum.slots[i][0], K.psum.slots[i][1]) for i in range(8)]
    identI = K.ident[0:64, 0:64]

    def bc(ap, shape):
        return ap.to_broadcast(list(shape))


    if STACKED:
        cS = P.sb("cBst_sb", [128, 256], F32)
        cS_b = Buf("cBst")
        P.dma("sp", K.cl, cS[:], io["cB2"], writes=[cS_b])
        U_st, NEGI_st, NEGS_st, I_st = cS[:, 0:64], cS[:, 64:128], cS[:, 128:192], cS[:, 192:256]
        blk1m = P.sb("blk1", [128, 128], F32)
        sel = [P.sb("sel%d" % i, [128, 128], F32) for i in range(2)]
        cm_b = Buf("cmats")
        P.op("dve", E("memset", ap=blk1m[:], constant=0.0), writes=[cm_b])
        P.op("dve", E("memset", ap=blk1m[0:64, 0:64], constant=1.0), writes=[cm_b])
        P.op("dve", E("memset", ap=blk1m[64:128, 64:128], constant=1.0), writes=[cm_b])
        for i in range(2):
            P.op("dve", E("memset", ap=sel[i][:], constant=0.0), writes=[cm_b])
            P.op("dve", E("memset", ap=sel[i][i * 64:(i + 1) * 64, :], constant=1.0), writes=[cm_b])
        dtb_st = P.sb("dtb_st", [128, 8], F32)
        nea_st = P.sb("nea_st", [128, 8], F32)
        st_b = Buf("stconst")
        for i in range(2):
            ps_ = slice(i * 64, (i + 1) * 64)
            P.op("act", E("activation", out=dtb_st[ps_, :], in_=row[ps_, 16 + i * 8:24 + i * 8], func=AF.Copy),
                 reads=[row_b], writes=[st_b])
            P.op("act", E("activation", out=nea_st[ps_, :], in_=row[ps_, i * 8:(i + 1) * 8], func=AF.Exp),
                 reads=[row_b], writes=[st_b])
        P.op("dve", E("tensor_scalar", out=nea_st[:], in0=nea_st[:], scalar1=-1.0, scalar2=None, op0=ALU.mult),
             reads=[st_b], writes=[st_b])
        betaS = P.sb("betaS", [128, NCH, 8], F32)
        lbS = P.sb("lbS", [128, NCH, 8], F32)
        gS = P.sb("gS", [128, NCH, 8], F32)
        smS_b = Buf("smallS")

        def one(name, shape, dt):
            return P.sb(name, shape, dt), Buf(name)
        gcS, gcS_b = one("gcS", [128, 8], F32)
        gbS, gbS_b = one("gbS", [128, 8], F32)
        egS, egS_b = one("egS", [128, 8], F32)
        BEG = [one("begS%d" % i, [128, 8], F32) for i in range(2)]
        EKT = [one("ektS%d" % i, [128, 8], F32) for i in range(2)]
        GLB = [one("glS%d" % i, [128, 16], F32) for i in range(2)]
        QKB = [one("QKs%d" % i, [128, 8, 64], BF16) for i in range(2)]
        P0B = [one("P0s%d" % i, [128, 8, 64], BF16) for i in range(2)]
        QGS = [[one("QGs%d_%d" % (i, h), [128, 8, 64], BF16) for h in range(2)] for i in range(2)]
        gtS, gtS_b = one("gtS", [128, 8], F32)
        D1, D1_b = one("D1", [128, 8, 64], F32)
        D2, D2_b = one("D2", [128, 8, 64], F32)
        EGs = [one("EGs%d" % i, [128, 8, 64], F32) for i in range(2)]
        PmS = [one("PmS%d" % i, [128, 8, 64], BF16) for i in range(2)]
        PTS = [one("PTS%d" % i, [128, 8, 64], BF16) for i in range(2)]
        RfS, RfS_b = one("RfS", [128, 8, 64], F32)
        RbS, RbS_b = one("RbS", [128, 8, 64], BF16)
        VB, VB_b = one("VBs", [128, 8, 128], BF16)
        KBG, KBG_b = one("KBGs", [128, 8, 128], BF16)
        KT, KT_b = one("KTs", [128, 8, 128], BF16)
        NW = [one("NWs%d" % i, [128, 8, 64], BF16) for i in range(2)]
        VN, VN_b = one("VNs", [128, 8, 128], BF16)
        OB = [one("OBs%d" % i, [128, 8, 64], F32) for i in range(2)]
        OQ = [one("OQs%d" % i, [128, 8, 64], F32) for i in range(2)]

    def stacked_tail(ti, t0, hn, hn_b):
        HS = (slice(0, 64), slice(64, 128))
        pba, pba_b = bank[0]
        wba4 = wba[:, :, :].rearrange("p k (two h) -> p k two h", two=2)
        for c in range(NCH):
            for hh in range(2):
                mm_group(P, pba[HS[hh], c * 16:(c + 1) * 16], pba_b,
                         [(hn[:, k, c * 64:(c + 1) * 64], wba4[:, k, :, hh * 8:(hh + 1) * 8]) for k in range(NF)],
                         [hn_b, wba_b])
        pv = pba[:, 0:NCH * 16].rearrange("p (c two j) -> p c two j", two=2, j=8)
        P.op("act", E("activation", out=betaS[:, :, :], in_=pv[:, :, 0, :], func=AF.Sigmoid), reads=[pba_b], writes=[smS_b])
        P.op("act", E("activation", out=lbS[:, :, :], in_=betaS[:, :, :], func=AF.Ln), reads=[smS_b], writes=[smS_b])
        P.op("dve", E("tensor_tensor", out=gS[:, :, :], in0=pv[:, :, 1, :], in1=bc(dtb_st[:, :].unsqueeze(1), [128, NCH, 8]),
                      op=ALU.add), reads=[pba_b, st_b], writes=[smS_b])
        P.op("act", E("activation", out=gS[:, :, :], in_=gS[:, :, :], func=AF.Exp), reads=[smS_b], writes=[smS_b])
        P.op("act", E("activation", out=gS[:, :, :], in_=gS[:, :, :], func=AF.Ln, bias=onec[:, 0:1]),
             reads=[smS_b, onec_b], writes=[smS_b])
        P.op("dve", E("tensor_tensor", out=gS[:, :, :], in0=gS[:, :, :], in1=bc(nea_st[:, :].unsqueeze(1), [128, NCH, 8]),
                      op=ALU.mult), reads=[smS_b, st_b], writes=[smS_b])
        ogt, og_b, ol = ogr.next()
        v3 = lambda t: t[:, :].rearrange("p (h c) -> p h c", c=64)
        pair = lambda t: t[:, :, :].rearrange("p (k two) c -> p k two c", two=2)
        pair2 = lambda t: t.rearrange("p (k two) d -> p k two d", two=2)

        def pro(c):
            cp = c % 2
            cs = slice(c * 64, (c + 1) * 64)
            begS, begS_b = BEG[cp]
            ektS, ektS_b = EKT[cp]
            glS, glS_b = GLB[cp]
            QK, QK_b = QKB[cp]
            p0, p0_b = P0B[cp]
            pm, pm_b = bank[0]
            for hh in range(2):
                P.op("pe", E("matmul", out=pm[HS[hh], 256:264], lhsT=U_st[HS[hh], :], rhs=gS[HS[hh], c, :], start=True, stop=True),
                     reads=[cS_b, smS_b], writes=[pm_b])
            P.op("pe", E("matmul", out=pm[:, 264:272], lhsT=blk1m[:, :], rhs=gS[:, c, :], start=True, stop=True),
                 reads=[cm_b, smS_b], writes=[pm_b])
            for hh in range(2):
                P.op("pe", E("matmul", out=pm[:, 272 + hh * 8:280 + hh * 8], lhsT=sel[hh][:, :], rhs=gS[:, c, :], start=True, stop=True),
                     reads=[cm_b, smS_b], writes=[pm_b])
            yield
            P.op("dve", E("tensor_copy", out=gcS[:, :], in_=pm[:, 256:264]), reads=[pm_b], writes=[gcS_b])
            P.op("dve", E("tensor_copy", out=gtS[:, :], in_=pm[:, 264:272]), reads=[pm_b], writes=[gtS_b])
            P.op("act", E("activation", out=glS[:, :], in_=pm[:, 272:288], func=AF.Exp), reads=[pm_b], writes=[glS_b])
            P.op("dve", E("tensor_tensor", out=gbS[:, :], in0=gcS[:, :], in1=lbS[:, c, :], op=ALU.add),
                 reads=[gcS_b, smS_b], writes=[gbS_b])
            P.op("act", E("activation", out=egS[:, :], in_=gcS[:, :], func=AF.Exp), reads=[gcS_b], writes=[egS_b])
            P.op("dve", E("tensor_tensor", out=begS[:, :], in0=egS[:, :], in1=betaS[:, c, :], op=ALU.mult),
                 reads=[egS_b, smS_b], writes=[begS_b])
            P.op("dve", E("tensor_tensor", out=ektS[:, :], in0=gtS[:, :], in1=gcS[:, :], op=ALU.subtract),
                 reads=[gtS_b, gcS_b], writes=[ektS_b])
            P.op("act", E("activation", out=ektS[:, :], in_=ektS[:, :], func=AF.Exp), reads=[ektS_b], writes=[ektS_b])
            P.op("dve", E("tensor_tensor", out=D1[:, :, :], in0=bc(I_st.unsqueeze(1), [128, 8, 64]),
                          in1=bc(gcS[:, :].unsqueeze(2), [128, 8, 64]), op=ALU.mult), reads=[cS_b, gcS_b], writes=[D1_b])
            P.op("pool", E("tensor_tensor", out=D2[:, :, :], in0=bc(I_st.unsqueeze(1), [128, 8, 64]),
                           in1=bc(gbS[:, :].unsqueeze(2), [128, 8, 64]), op=ALU.mult), reads=[cS_b, gbS_b], writes=[D2_b])
            yield
            d1f = D1[:, :, :].rearrange("p h c -> p (h c)")
            d2f = D2[:, :, :].rearrange("p h c -> p (h c)")
            pG, pG_b = bank[1]
            pGb, pGb_b = bank[2]
            pR, pR_b = bank[3]
            P.op("pe", E("matmul", out=pR[:, :], lhsT=sel[0][:, :], rhs=d1f, start=True, stop=True), reads=[cm_b, D1_b], writes=[pR_b])
            P.op("pe", E("matmul", out=pG[:, :], lhsT=blk1m[:, :], rhs=d1f, start=True, stop=True), reads=[cm_b, D1_b], writes=[pG_b])
            P.op("pe", E("matmul", out=pGb[:, :], lhsT=blk1m[:, :], rhs=d2f, start=True, stop=True), reads=[cm_b, D2_b], writes=[pGb_b])
            yield
            P.op("act", E("activation", out=EGs[0][0][:, :, :], in_=v3(pR), func=AF.Exp), reads=[pR_b], writes=[EGs[0][1]])
            P.op("pe", E("matmul", out=pR[:, :], lhsT=sel[1][:, :], rhs=d1f, start=True, stop=True), reads=[cm_b, D1_b], writes=[pR_b])
            gcb = bc(gcS[:, :].unsqueeze(2), [128, 8, 64])
            P.op("dve", E("tensor_tensor", out=D1[:, :, :], in0=v3(pG), in1=gcb, op=ALU.subtract), reads=[pG_b, gcS_b], writes=[D1_b])
            P.op("dve", E("tensor_tensor", out=D2[:, :, :], in0=v3(pGb), in1=gcb, op=ALU.subtract), reads=[pGb_b, gcS_b], writes=[D2_b])
            P.op("act", E("activation", out=EGs[1][0][:, :, :], in_=v3(pR), func=AF.Exp), reads=[pR_b], writes=[EGs[1][1]])
            yield
            P.op("pool", E("tensor_tensor", out=D1[:, :, :], in0=D1[:, :, :], in1=bc(NEGI_st.unsqueeze(1), [128, 8, 64]), op=ALU.add),
                 reads=[D1_b, cS_b], writes=[D1_b])
            P.op("pool", E("tensor_tensor", out=D2[:, :, :], in0=D2[:, :, :], in1=bc(NEGS_st.unsqueeze(1), [128, 8, 64]), op=ALU.add),
                 reads=[D2_b, cS_b], writes=[D2_b])
            P.op("act", E("activation", out=D1[:, :, :], in_=D1[:, :, :], func=AF.Exp), reads=[D1_b], writes=[D1_b])
            P.op("act", E("activation", out=D2[:, :, :], in_=D2[:, :, :], func=AF.Exp), reads=[D2_b], writes=[D2_b])
            for hh in range(2):
                qg_, qg_b = QGS[cp][hh]
                P.op("dve" if hh == 0 else "pool",
                     E("tensor_tensor", out=pair(qg_), in0=bc(qf[:, hh * 4:hh * 4 + 4, cs].unsqueeze(2), [128, 4, 2, 64]),
                       in1=pair(EGs[hh][0]), op=ALU.mult), reads=[qf_b, EGs[hh][1]], writes=[qg_b])
            yield
            pK, pK_b = bank[0]
            for hh in range(2):
                for kh in range(4):
                    kg = hh * 4 + kh
                    P.op("pe", E("matmul", out=pK[HS[hh], kh * 128:kh * 128 + 64], lhsT=kf[:, kg, cs], rhs=kf[:, kg, cs],
                                 start=True, stop=True), reads=[kf_b], writes=[pK_b])
                    P.op("pe", E("matmul", out=pK[HS[hh], kh * 128 + 64:kh * 128 + 128], lhsT=kf[:, kg, cs], rhs=qf[:, kg, cs],
                                 start=True, stop=True), reads=[kf_b, qf_b], writes=[pK_b])
            yield
            pK4 = pK[:, :].rearrange("p (k j c) -> p k j c", j=2, c=64)
            P.op("dve", E("tensor_tensor", out=pair(p0), in0=bc(pK4[:, :, 0, :].unsqueeze(2), [128, 4, 2, 64]), in1=pair(D2), op=ALU.mult),
                 reads=[pK_b, D2_b], writes=[p0_b])
            P.op("dve", E("tensor_tensor", out=pair(QK), in0=bc(pK4[:, :, 1, :].unsqueeze(2), [128, 4, 2, 64]), in1=pair(D1), op=ALU.mult),
                 reads=[pK_b, D1_b], writes=[QK_b])
            P.op("act", E("activation", out=p0[:, :, :], in_=p0[:, :, :], func=AF.Copy, scale=-1.0), reads=[p0_b], writes=[p0_b])
            yield

        def inv(c, filler):
            cp = c % 2
            cs = slice(c * 64, (c + 1) * 64)
            begS, begS_b = BEG[cp]
            ektS, ektS_b = EKT[cp]
            p0, p0_b = P0B[cp]
            pA, pA_b = bank[4]
            pB, pB_b = bank[5]
            pC, pC_b = bank[6]
            pT, pT_b = bank[7]
            pAv = pA[:, :].bitcast(BF16)
            for hh in range(2):
                for j in range(8):
                    P.op("pe", E("transpose", out=pAv[HS[hh], j * 64:(j + 1) * 64], in_=p0[HS[hh], j, :],
                                 identity=identb[HS[hh], hh * 64:(hh + 1) * 64]), reads=[p0_b, identb_b], writes=[pA_b])
            q0, q0_b = PTS[0]
            P.op("act", E("activation", out=q0[:, :, :], in_=pAv[:, 0:512].rearrange("p (h c) -> p h c", c=64), func=AF.Copy),
                 reads=[pA_b], writes=[q0_b])
            P.op("dve", E("tensor_tensor", out=RfS[:, :, :], in0=p0[:, :, :], in1=bc(I_st.unsqueeze(1), [128, 8, 64]), op=ALU.add),
                 reads=[p0_b, cS_b], writes=[RfS_b])
            P.op("act", E("activation", out=RbS[:, :, :], in_=RfS[:, :, :], func=AF.Copy), reads=[RfS_b], writes=[RbS_b])
            pTv = pT[:, :].bitcast(BF16)
            for hh in range(2):
                for j in range(8):
                    P.op("pe", E("transpose", out=pTv[HS[hh], j * 128:(j + 1) * 128], in_=vf[:, hh * 8 + j, cs], identity=identb[:, :]),
                         reads=[vf_b, identb_b], writes=[pT_b])
            P.op("dve", E("tensor_tensor", out=VB[:, :, :], in0=pTv[:, :].rearrange("p (h v) -> p h v", v=128),
                          in1=bc(betaS[:, c, :].unsqueeze(2), [128, 8, 128]), op=ALU.mult), reads=[pT_b, smS_b], writes=[VB_b])
            next(filler, None)
            pcur, pcur_b = p0, p0_b
            pt_, pt_b = PTS[0]
            nxt = 0
            for lvl in range(6):
                pn_, pn_b = PmS[nxt]
                ptn_, ptn_b = PTS[1 - (lvl % 2)]
                if lvl >= 1:
                    for hh in range(2):
                        for j in range(8):
                            P.op("pe", E("matmul", out=pC[HS[hh], j * 64:(j + 1) * 64], lhsT=pt_[HS[hh], j, :], rhs=RbS[HS[hh], j, :],
                                         start=True, stop=True), reads=[pt_b, RbS_b], writes=[pC_b])
                if lvl <= 4:
                    for hh in range(2):
                        for j in range(8):
                            P.op("pe", E("matmul", out=pA[HS[hh], j * 64:(j + 1) * 64], lhsT=pt_[HS[hh], j, :], rhs=pcur[HS[hh], j, :],
                                         start=True, stop=True), reads=[pt_b, pcur_b], writes=[pA_b])
                    for hh in range(2):
                        for j in range(8):
                            P.op("pe", E("matmul", out=pB[HS[hh], j * 64:(j + 1) * 64], lhsT=pcur[HS[hh], j, :], rhs=pt_[HS[hh], j, :],
                                         start=True, stop=True), reads=[pt_b, pcur_b], writes=[pB_b])
                if lvl == 1:
                    pT2v = pT[:, :].bitcast(BF16)
                    for hh in range(2):
                        for kh in range(4):
                            P.op("pe", E("transpose", out=pT2v[HS[hh], kh * 128:(kh + 1) * 128], in_=kf[:, hh * 4 + kh, cs], identity=identb[:, :]),
                                 reads=[kf_b, identb_b], writes=[pT_b])
                    ktm4 = bc(pT2v[:, 0:512].rearrange("p (k d) -> p k d", d=128).unsqueeze(2), [128, 4, 2, 128])
                    P.op("dve", E("tensor_tensor", out=pair2(KBG[:, :, :]), in0=ktm4,
                                  in1=pair2(bc(begS[:, :].unsqueeze(2), [128, 8, 128])), op=ALU.mult), reads=[pT_b, begS_b], writes=[KBG_b])
                    P.op("dve", E("tensor_tensor", out=pair2(KT[:, :, :]), in0=ktm4,
                                  in1=pair2(bc(ektS[:, :].unsqueeze(2), [128, 8, 128])), op=ALU.mult), reads=[pT_b, ektS_b], writes=[KT_b])
                next(filler, None)
                if lvl >= 1:
                    P.op("pool", E("tensor_tensor", out=RbS[:, :, :], in0=RfS[:, :, :], in1=RfS[:, :, :], op=ALU.bypass),
                         reads=[RfS_b], writes=[RbS_b]) if False else None
                    P.op("dve", E("tensor_tensor", out=RfS[:, :, :], in0=v3(pC), in1=RfS[:, :, :], op=ALU.add),
                         reads=[pC_b, RfS_b], writes=[RfS_b])
                    P.op("act", E("activation", out=RbS[:, :, :], in_=RfS[:, :, :], func=AF.Copy), reads=[RfS_b], writes=[RbS_b])
                if lvl <= 4:
                    P.op("act", E("activation", out=pn_[:, :, :], in_=v3(pA), func=AF.Copy), reads=[pA_b], writes=[pn_b])
                    P.op("dve", E("tensor_copy", out=ptn_[:, :, :], in_=v3(pB)), reads=[pB_b], writes=[ptn_b])
                    pcur, pcur_b = pn_, pn_b
                    pt_, pt_b = ptn_, ptn_b
                    nxt = 1 - nxt
                next(filler, None)

        def tail(c):
            cp = c % 2
            cs = slice(c * 64, (c + 1) * 64)
            glS, glS_b = GLB[cp]
            QK, QK_b = QKB[cp]
            for hh in range(2):
                pW, pW_b = bank[4 + hh]
                for j in range(8):
                    P.op("pe", E("matmul", out=pW[:, j * 64:(j + 1) * 64], lhsT=KBG[HS[hh], j, :], rhs=RbS[HS[hh], j, :],
                                 start=True, stop=True), reads=[KBG_b, RbS_b], writes=[pW_b])
                if hh == 0:
                    P.op("act", E("activation", out=NW[hh][0][:, :, :], in_=v3(pW), func=AF.Copy, scale=-1.0),
                         reads=[pW_b], writes=[NW[hh][1]])
                else:
                    P.op("dve", E("tensor_scalar", out=NW[hh][0][:, :, :], in0=v3(pW), scalar1=-1.0, scalar2=None, op0=ALU.mult),
                         reads=[pW_b], writes=[NW[hh][1]])
            for q in range(2):
                pV, pV_b = bank[6 + q]
                for hh in range(2):
                    for j4 in range(4):
                        j = q * 4 + j4
                        P.op("pe", E("matmul", out=pV[HS[hh], j4 * 128:(j4 + 1) * 128], lhsT=RbS[HS[hh], j, :], rhs=VB[HS[hh], j, :],
                                     start=True, stop=False), reads=[RbS_b, VB_b], writes=[pV_b])
                        P.op("pe", E("matmul", out=pV[HS[hh], j4 * 128:(j4 + 1) * 128], lhsT=NW[hh][0][:, j, :], rhs=Sb[:, hh * 8 + j, :],
                                     start=False, stop=True), reads=[NW[hh][1], Sb_b[hh]], writes=[pV_b])
                if q == 0:
                    P.op("act", E("activation", out=VN[:, 0:4, :], in_=pV[:, :].rearrange("p (h v) -> p h v", v=128), func=AF.Copy),
                         reads=[pV_b], writes=[VN_b])
                else:
                    P.op("dve", E("tensor_copy", out=VN[:, 4:8, :], in_=pV[:, :].rearrange("p (h v) -> p h v", v=128)),
                         reads=[pV_b], writes=[VN_b])
            pO = [bank[0], bank[1]]
            for hh in range(2):
                qg_, qg_b = QGS[cp][hh]
                for j in range(8):
                    P.op("pe", E("matmul", out=pO[hh][0][:, j * 64:(j + 1) * 64], lhsT=Sb[:, hh * 8 + j, :], rhs=qg_[:, j, :],
                                 start=True, stop=False), reads=[Sb_b[hh], qg_b], writes=[pO[hh][1]])
                    P.op("pe", E("matmul", out=pO[hh][0][:, j * 64:(j + 1) * 64], lhsT=VN[HS[hh], j, :], rhs=QK[HS[hh], j, :],
                                 start=False, stop=True), reads=[VN_b, QK_b], writes=[pO[hh][1]])
            for hh in range(2):
                for q in range(2):
                    pS, pS_b = bank[2 + q] if hh == 0 else bank[4 + q]
                    for j4 in range(4):
                        j = q * 4 + j4
                        P.op("pe", E("matmul", out=pS[:, j4 * 128:(j4 + 1) * 128], lhsT=KT[HS[hh], j, :], rhs=VN[HS[hh], j, :],
                                     start=True, stop=True), reads=[KT_b, VN_b], writes=[pS_b])
                    h0 = hh * 8 + q * 4
                    sv = S[:, h0:h0 + 4, :]
                    e1 = "dve" if q == 0 else "pool"
                    P.op(e1, E("tensor_tensor", out=sv, in0=sv, in1=bc(glS[:, h0:h0 + 4].unsqueeze(2), [128, 4, 128]), op=ALU.mult),
                         reads=[S_b[hh], glS_b], writes=[S_b[hh]])
                    P.op("dve", E("tensor_tensor", out=sv, in0=pS[:, :].rearrange("p (h v) -> p h v", v=128), in1=sv, op=ALU.add),
                         reads=[pS_b, S_b[hh]], writes=[S_b[hh]])
                    P.op("act", E("activation", out=Sb[:, h0:h0 + 4, :], in_=sv, func=AF.Copy), reads=[S_b[hh]], writes=[Sb_b[hh]])
            for hh in range(2):
                ob, ob_b = OB[hh]
                oq, oq_b = OQ[hh]
                pO3 = v3(pO[hh][0])
                hs = slice(hh * 8, (hh + 1) * 8)
                P.op("dve", E("tensor_copy", out=ob[:, :, :], in_=pO3), reads=[pO[hh][1]], writes=[ob_b])
                P.op("act", E("activation", out=oq[:, :, :], in_=pO3, func=AF.Square), reads=[pO[hh][1]], writes=[oq_b])
                pQ, pQ_b = bank[6 + hh]
                P.op("pe", E("matmul", out=pQ[:, :], lhsT=K.ones[:, :], rhs=oq[:, :, :].rearrange("p h c -> p (h c)"),
                             start=True, stop=True), reads=[K.ones_b, oq_b], writes=[pQ_b])
                rsqrt_op(P, K, oq[:, :, :].rearrange("p h c -> p (h c)"), oq_b, pQ[:, :], pQ_b, 1.0 / 128, EPS)
                P.op("dve", E("scalar_tensor_tensor", out=ob[:, :, :].rearrange("p h c -> p (h c)"),
                              in0=ob[:, :, :].rearrange("p h c -> p (h c)"), scalar=vec[:, 128:129],
                              in1=oq[:, :, :].rearrange("p h c -> p (h c)"), op0=ALU.mult, op1=ALU.mult),
                     reads=[ob_b, oq_b, vec_b], writes=[ob_b])
                P.op("pool", E("tensor_tensor", out=ogt[:, hs, cs], in0=ob[:, :, :], in1=zs[:, hs, cs], op=ALU.mult),
                     reads=[ob_b, zs_b], writes=[og_b])

        for _ in pro(0):
            pass
        for c in range(NCH):
            filler = pro(c + 1) if c + 1 < NCH else iter(())
            inv(c, filler)
            for _ in filler:
                pass
            tail(c)
        if xch is None:
            P.dma("sp", ol, og.rearrange("f p t -> p f t")[:, :, t0:t0 + T], ogt[:, :, :], reads=[og_b])
        else:
            P.dma("sp", ol, xch["og_loc"][ti].rearrange("(f p) t -> p f t", p=128), ogt[:, :, :],
                  reads=[og_b], writes=[xch["og_loc_b"][ti]])
            P.cc_allgather(xch["og_loc"][ti], xch["og_loc_b"][ti], xch["og_all"][ti], xch["og_all_b"][ti], PAIRS)

    for ti in range(ntile):
        t0 = ti * T
        hn, hn_b, hl = hnr.next()
        if xch is None:
            P.dma("sp", hl, hn[:, :, :], hn1g.rearrange("f p t -> p f t")[:, :, t0:t0 + T], writes=[hn_b])
        else:
            pieces = []
            s0 = t0
            while s0 < t0 + T:
                if s0 < 48:
                    s1 = min(48, t0 + T)
                    P.op("dve", E("memset", ap=hn[:, :, s0 - t0:s1 - t0], constant=0.0), writes=[hn_b])
                    s0 = s1
                    continue
                rk, j = (0, s0 - 48) if s0 < 2112 else (1, s0 - 2112 + 16)
                lim = 2112 if s0 < 2112 else TG
                q = max(i for i in range(len(TA_TILES)) if sum(TA_TILES[:i]) <= j)
                cq = j - sum(TA_TILES[:q])
                part, pc0, plim = (0, cq, 512) if cq < 512 else (1, cq - (TA_TILES[q] - 64), 64)
                n = min(t0 + T - s0, lim - s0, plim - pc0)
                xi = 2 * q + part
                src = xch["hn1_all"][xi].rearrange("(r f p) t -> r p f t", r=2, p=128)[rk][:, :, pc0:pc0 + n]
                pieces.append((hn[:, :, s0 - t0:s0 - t0 + n], src, xch["hn1_all_b"][xi]))
                s0 += n
            P.dma_group("sp", hl, [(d_, s_) for d_, s_, _ in pieces], reads=[b_ for _, _, b_ in pieces], writes=[hn_b])
        for blk in range(24):
            wt, wb, wl = K.wring.next()
            wv = wt[:, 0:4096].rearrange("p (k m) -> p k m", k=NF)
            P.dma("pool", wl, wv, gw_v[:, :, blk * 256:(blk + 1) * 256], writes=[wb])
            for mi in range(2):
                mc = blk * 2 + mi
                pb, pbb, _ = K.psum.next()
                mm_group(P, pb[:, :T], pbb, [(wv[:, k, mi * 128:(mi + 1) * 128], hn[:, k, :]) for k in range(NF)],
                         [wb, hn_b])
                if mc >= 32:
                    h = mc - 32
                    P.op("act", E("activation", out=zs[:, h, :], in_=pb[:, :T], func=AF.Silu), reads=[pbb], writes=[zs_b])
                    continue
                pc, pcb, _ = pcr.next()
                ac, acb, _ = accr.next()
                P.op("act", E("activation", out=pc[:, 3:3 + T], in_=pb[:, :T], func=AF.Copy), reads=[pbb], writes=[pcb])
                P.op("act", E("activation", out=pc[:, 0:3], in_=halo[:, mc, :], func=AF.Copy),
                     reads=[halo_b[mc]], writes=[pcb])
                P.op("dve", E("tensor_scalar", out=ac[:, :], in0=pc[:, 3:3 + T], scalar1=vec[:, 96 + mc:97 + mc],
                              scalar2=None, op0=ALU.mult), reads=[pcb, vec_b], writes=[acb])
                for j in range(3):
                    P.op("dve", E("scalar_tensor_tensor", out=ac[:, :], in0=pc[:, j:j + T],
                                  scalar=vec[:, j * 32 + mc:j * 32 + mc + 1], in1=ac[:, :], op0=ALU.mult, op1=ALU.add),
                         reads=[pcb, vec_b, acb], writes=[acb])
                P.op("act", E("activation", out=halo[:, mc, :], in_=pc[:, T:T + 3], func=AF.Copy),
                     reads=[pcb], writes=[halo_b[mc]])
                if mc >= 16:
                    h = mc - 16
                    P.op("act", E("activation", out=vf[:, h, :], in_=ac[:, :], func=AF.Silu), reads=[acb], writes=[vf_b])
                    continue
                sl, slb, _ = silr.next()
                P.op("act", E("activation", out=sl[:, :], in_=ac[:, :], func=AF.Silu), reads=[acb], writes=[slb])
                sq, sqb, _ = K.sq.next()
                P.op("act", E("activation", out=sq[:, :T], in_=sl[:, :], func=AF.Square), reads=[slb], writes=[sqb])
                p2, p2b, _ = K.psum.next()
                P.op("pe", E("matmul", out=p2[:, :T], lhsT=K.ones[:], rhs=sq[:, :T], start=True, stop=True),
                     reads=[sqb, K.ones_b], writes=[p2b])
                ri, rib, _ = rinr.next()
                rsqrt_op(P, K, ri[:, :], rib, p2[:, :T], p2b, 1.0, EPS)
                if mc < 8:
                    P.op("dve", E("scalar_tensor_tensor", out=qf[:, mc, :], in0=sl[:, :], scalar=DK ** -0.5, in1=ri[:, :],
                                  op0=ALU.mult, op1=ALU.mult), reads=[slb, rib], writes=[qf_b])
                else:
                    P.op("dve", E("tensor_tensor", out=kf[:, mc - 8, :], in0=sl[:, :], in1=ri[:, :], op=ALU.mult),
                         reads=[slb, rib], writes=[kf_b])
        if STACKED:
            stacked_tail(ti, t0, hn, hn_b)
            continue
        if DBG == "s2":
            ogt, og_b, ol = ogr.next()
            P.op("dve", E("tensor_copy", out=ogt[:, :, :], in_=vf[:, :, :]), reads=[vf_b, qf_b, kf_b, zs_b], writes=[og_b])
            P.dma("sp", ol, og.rearrange("f p t -> p f t")[:, :, t0:t0 + T], ogt[:, :, :], reads=[og_b])
            continue
        pba, pba_b = bank[0]
        for c in range(NCH):
            mm_group(P, pba[0:64, c * 32:(c + 1) * 32], pba_b,
                     [(hn[:, k, c * 64:(c + 1) * 64], wba[:, k, :]) for k in range(NF)], [hn_b, wba_b])
        pv = pba[0:64, 0:NCH * 32].rearrange("p (c j) -> p c j", j=32)
        P.op("act", E("activation", out=beta[:, :, :], in_=pv[:, :, 0:16], func=AF.Sigmoid), reads=[pba_b], writes=[sm_b])
        P.op("act", E("activation", out=lb[:, :, :], in_=beta[:, :, :], func=AF.Ln), reads=[sm_b], writes=[sm_b])
        P.op("dve", E("tensor_tensor", out=gtm[:, :, :], in0=pv[:, :, 16:32],
                      in1=bc(row[0:64, 16:32].unsqueeze(1), [64, NCH, 16]), op=ALU.add),
             reads=[pba_b, row_b], writes=[sm_b])
        P.op("act", E("activation", out=gtm[:, :, :], in_=gtm[:, :, :], func=AF.Exp), reads=[sm_b], writes=[sm_b])
        P.op("act", E("activation", out=gtm[:, :, :], in_=gtm[:, :, :], func=AF.Ln, bias=onec[0:64, 0:1]),
             reads=[sm_b, onec_b], writes=[sm_b])
        P.op("dve", E("tensor_tensor", out=gtm[:, :, :], in0=gtm[:, :, :],
                      in1=bc(nea[:, :].unsqueeze(1), [64, NCH, 16]), op=ALU.mult), reads=[sm_b, nea_b], writes=[sm_b])
        ogt, og_b, ol = ogr.next()
        if DBG == "s2b":
            P.op("dve", E("tensor_copy", out=ogt[:, :, :], in_=vf[:, :, :]), reads=[vf_b, qf_b, kf_b, zs_b, sm_b], writes=[og_b])
            P.dma("sp", ol, og.rearrange("f p t -> p f t")[:, :, t0:t0 + T], ogt[:, :, :], reads=[og_b])
            continue
        for c in range(NCH):
            cs = slice(c * 64, (c + 1) * 64)
            pm, pm_b = bank[0]
            P.op("pe", E("matmul", out=pm[0:64, 256:272], lhsT=Umat, rhs=gtm[:, c, :], start=True, stop=True),
                 reads=[cB_b, sm_b], writes=[pm_b])
            P.op("pe", E("matmul", out=pm[:, 272:288], lhsT=K.ones[0:64, :], rhs=gtm[:, c, :], start=True, stop=True),
                 reads=[K.ones_b, sm_b], writes=[pm_b])
            gc, gc_b = gcs[0]
            gb, gb_b = gbs[0]
            eg, eg_b = egs[0]
            beg, beg_b = begs[0]
            ekt, ekt_b = ekts[0]
            gt, gt_b = gtot[0]
            gl, gl_b = gls[0]
            P.op("dve", E("tensor_copy", out=gc[:, :], in_=pm[0:64, 256:272]), reads=[pm_b], writes=[gc_b])
            P.op("act", E("activation", out=gt[:, :], in_=pm[:, 272:288], func=AF.Copy), reads=[pm_b], writes=[gt_b])
            P.op("dve", E("tensor_tensor", out=gb[:, :], in0=gc[:, :], in1=lb[:, c, :], op=ALU.add),
                 reads=[gc_b, sm_b], writes=[gb_b])
            P.op("act", E("activation", out=eg[:, :], in_=gc[:, :], func=AF.Exp), reads=[gc_b], writes=[eg_b])
            P.op("dve", E("tensor_tensor", out=beg[:, :], in0=eg[:, :], in1=beta[:, c, :], op=ALU.mult),
                 reads=[eg_b, sm_b], writes=[beg_b])
            P.op("dve", E("tensor_tensor", out=ekt[:, :], in0=gt[0:64, :], in1=gc[:, :], op=ALU.subtract),
                 reads=[gt_b, gc_b], writes=[ekt_b])
            P.op("act", E("activation", out=ekt[:, :], in_=ekt[:, :], func=AF.Exp), reads=[ekt_b], writes=[ekt_b])
            P.op("act", E("activation", out=gl[:, :], in_=gt[:, :], func=AF.Exp), reads=[gt_b], writes=[gl_b])
            for hh in range(2):
                h0 = hh * 8
                k0 = hh * 4
                hs = slice(h0, h0 + 8)
                if DBGSTEP < 2:
                    continue
                d1, d1_b = Dg1[hh]
                d2, d2_b = Dg2[hh]
                P.op("dve", E("tensor_tensor", out=d1[:, :, :], in0=bc(identI.unsqueeze(1), [64, 8, 64]),
                              in1=bc(gc[:, hs].unsqueeze(2), [64, 8, 64]), op=ALU.mult),
                     reads=[K.ident_b, gc_b], writes=[d1_b])
                P.op("dve", E("tensor_tensor", out=d2[:, :, :], in0=bc(identI.unsqueeze(1), [64, 8, 64]),
                              in1=bc(gb[:, hs].unsqueeze(2), [64, 8, 64]), op=ALU.mult),
                     reads=[K.ident_b, gb_b], writes=[d2_b])
                if DBGSTEP < 2.2:
                    continue
                pG, pG_b = bank[1]
                pGb, pGb_b = bank[2]
                P.op("pe", E("matmul", out=pG[:, :], lhsT=K.ones[0:64, :], rhs=d1[:, :, :].rearrange("p h c -> p (h c)"),
                             start=True, stop=True), reads=[K.ones_b, d1_b], writes=[pG_b])
                P.op("pe", E("matmul", out=pGb[0:64, :], lhsT=K.ones[0:64, 0:64], rhs=d2[:, :, :].rearrange("p h c -> p (h c)"),
                             start=True, stop=True), reads=[K.ones_b, d2_b], writes=[pGb_b])
                if DBGSTEP < 2.4:
                    continue
                pG3 = pG[:, :].rearrange("p (h c) -> p h c", c=64)
                pGb3 = pGb[:, :].rearrange("p (h c) -> p h c", c=64)
                a1, a1_b = t1[hh]
                a2, a2_b = t2[hh]
                ei, ei_b = EI[hh]
                es, es_b = ES[hh]
                egr, egr_b = EG[hh]
                gcb = bc(gc[:, hs].unsqueeze(2), [64, 8, 64])
                P.op("dve", E("tensor_tensor", out=a1[:, :, :], in0=pG3[0:64], in1=gcb, op=ALU.subtract),
                     reads=[pG_b, gc_b], writes=[a1_b])
                P.op("dve", E("tensor_tensor", out=a2[:, :, :], in0=pGb3[0:64], in1=gcb, op=ALU.subtract),
                     reads=[pGb_b, gc_b], writes=[a2_b])
                if DBGSTEP < 2.5:
                    continue
                P.op("act", E("activation", out=egr[:, :, :], in_=pG3, func=AF.Exp), reads=[pG_b], writes=[egr_b])
                if DBGSTEP < 2.6:
                    continue
                P.op(POOLENG, E("tensor_tensor", out=a1[:, :, :], in0=a1[:, :, :], in1=bc(NEGI.unsqueeze(1), [64, 8, 64]),
                               op=ALU.add), reads=[a1_b, cB_b], writes=[a1_b])
                P.op(POOLENG, E("tensor_tensor", out=a2[:, :, :], in0=a2[:, :, :], in1=bc(NEGS.unsqueeze(1), [64, 8, 64]),
                               op=ALU.add), reads=[a2_b, cB_b], writes=[a2_b])
                if DBGSTEP < 2.7:
                    continue
                P.op("act", E("activation", out=ei[:, :, :], in_=a1[:, :, :], func=EXPF), reads=[a1_b], writes=[ei_b])
                P.op("act", E("activation", out=es[:, :, :], in_=a2[:, :, :], func=EXPF), reads=[a2_b], writes=[es_b])
                if DBGSTEP < 3:
                    continue
                qgt, qg_b = qg[hh]
                P.op("dve", E("tensor_tensor", out=qgt[:, :, :].rearrange("p (k two) c -> p k two c", two=2),
                              in0=bc(qf[:, k0:k0 + 4, cs].unsqueeze(2), [128, 4, 2, 64]),
                              in1=egr[:, :, :].rearrange("p (k two) c -> p k two c", two=2), op=ALU.mult),
                     reads=[qf_b, egr_b], writes=[qg_b])
                pK, pK_b = bank[3]
                for kh in range(4):
                    P.op("pe", E("matmul", out=pK[0:64, kh * 128:kh * 128 + 64], lhsT=kf[:, k0 + kh, cs], rhs=kf[:, k0 + kh, cs],
                                 start=True, stop=True), reads=[kf_b], writes=[pK_b])
                    P.op("pe", E("matmul", out=pK[0:64, kh * 128 + 64:kh * 128 + 128], lhsT=kf[:, k0 + kh, cs],
                                 rhs=qf[:, k0 + kh, cs], start=True, stop=True), reads=[kf_b, qf_b], writes=[pK_b])
                pK4 = pK[0:64, :].rearrange("p (k j c) -> p k j c", j=2, c=64)
                p0, p0_b = Pm[0][hh]
                qk, qk_b = qkT[hh]
                P.op("dve", E("tensor_tensor", out=p0[:, :, :].rearrange("p (k two) c -> p k two c", two=2),
                              in0=bc(pK4[:, :, 0, :].unsqueeze(2), [64, 4, 2, 64]),
                              in1=es[:, :, :].rearrange("p (k two) c -> p k two c", two=2), op=ALU.mult),
                     reads=[pK_b, es_b], writes=[p0_b])
                P.op("dve", E("tensor_tensor", out=qk[:, :, :].rearrange("p (k two) c -> p k two c", two=2),
                              in0=bc(pK4[:, :, 1, :].unsqueeze(2), [64, 4, 2, 64]),
                              in1=ei[:, :, :].rearrange("p (k two) c -> p k two c", two=2), op=ALU.mult),
                     reads=[pK_b, ei_b], writes=[qk_b])
                P.op(POOLENG, E("tensor_scalar", out=p0[:, :, :], in0=p0[:, :, :], scalar1=-1.0, scalar2=None, op0=ALU.mult),
                     reads=[p0_b], writes=[p0_b])
                if DBGSTEP < 4:
                    continue
                pT, pT_b = bank[7]
                pTv = pT[:, :].bitcast(BF16)
                for h in range(8):
                    P.op("pe", E("transpose", out=pTv[0:64, h * 128:(h + 1) * 128], in_=vf[:, h0 + h, cs], identity=identb[:, :]),
                         reads=[vf_b, identb_b], writes=[pT_b])
                vbt, vb_b = vb[hh]
                P.op("dve", E("tensor_tensor", out=vbt[:, :, :], in0=pTv[0:64, :].rearrange("p (h v) -> p h v", v=128),
                              in1=bc(beta[:, c, hs].unsqueeze(2), [64, 8, 128]), op=ALU.mult),
                     reads=[pT_b, sm_b], writes=[vb_b])
                pT2, pT2_b = bank[6]
                pT2v = pT2[:, :].bitcast(BF16)
                for kh in range(4):
                    P.op("pe", E("transpose", out=pT2v[0:64, kh * 128:(kh + 1) * 128], in_=kf[:, k0 + kh, cs], identity=identb[:, :]),
                         reads=[kf_b, identb_b], writes=[pT2_b])
                ktm4 = bc(pT2v[0:64, 0:512].rearrange("p (k d) -> p k d", d=128).unsqueeze(2), [64, 4, 2, 128])
                kbt, kb_b = kbg[hh]
                ktt, kt_b = kt[hh]
                P.op("dve", E("tensor_tensor", out=kbt[:, :, :].rearrange("p (k two) d -> p k two d", two=2), in0=ktm4,
                              in1=bc(beg[:, hs].unsqueeze(2), [64, 8, 128]).rearrange("p (k two) d -> p k two d", two=2),
                              op=ALU.mult), reads=[pT2_b, beg_b], writes=[kb_b])
                P.op("dve", E("tensor_tensor", out=ktt[:, :, :].rearrange("p (k two) d -> p k two d", two=2), in0=ktm4,
                              in1=bc(ekt[:, hs].unsqueeze(2), [64, 8, 128]).rearrange("p (k two) d -> p k two d", two=2),
                              op=ALU.mult), reads=[pT2_b, ekt_b], writes=[kt_b])
                if DBGSTEP < 5:
                    continue
                pA, pA_b = bank[4]
                pB, pB_b = bank[5]
                pC, pC_b = bank[6]
                pAv = pA[:, :].bitcast(BF16)
                for h in range(8):
                    P.op("pe", E("transpose", out=pAv[0:64, h * 64:(h + 1) * 64], in_=p0[:, h, :], identity=identb[0:64, 0:64]),
                         reads=[p0_b, identb_b], writes=[pA_b])
                q0, q0_b = PT[0][hh]
                P.op("act", E("activation", out=q0[:, :, :], in_=pAv[0:64, 0:512].rearrange("p (h c) -> p h c", c=64), func=AF.Copy),
                     reads=[pA_b], writes=[q0_b])
                rf, rf_b = Rf[hh]
                rb, rb_b = Rb[hh]
                P.op("dve", E("tensor_tensor", out=rf[:, :, :], in0=p0[:, :, :], in1=bc(identI.unsqueeze(1), [64, 8, 64]),
                              op=ALU.add), reads=[p0_b, K.ident_b], writes=[rf_b])
                P.op("act", E("activation", out=rb[:, :, :], in_=rf[:, :, :], func=AF.Copy), reads=[rf_b], writes=[rb_b])
                cur = 0
                for lvl in range(6):
                    pc_, pc_b = Pm[cur][hh]
                    pt_, pt_b = PT[cur][hh]
                    pn_, pn_b = Pm[1 - cur][hh]
                    ptn_, ptn_b = PT[1 - cur][hh]
                    if lvl >= 1:
                        for h in range(8):
                            P.op("pe", E("matmul", out=pC[0:64, h * 64:(h + 1) * 64], lhsT=pt_[:, h, :], rhs=rb[:, h, :],
                                         start=True, stop=True), reads=[pt_b, rb_b], writes=[pC_b])
                    if lvl <= 4:
                        for h in range(8):
                            P.op("pe", E("matmul", out=pA[0:64, h * 64:(h + 1) * 64], lhsT=pt_[:, h, :], rhs=pc_[:, h, :],
                                         start=True, stop=True), reads=[pt_b, pc_b], writes=[pA_b])
                        for h in range(8):
                            P.op("pe", E("matmul", out=pB[0:64, h * 64:(h + 1) * 64], lhsT=pc_[:, h, :], rhs=pt_[:, h, :],
                                         start=True, stop=True), reads=[pt_b, pc_b], writes=[pB_b])
                    if lvl >= 1:
                        P.op("dve", E("tensor_tensor", out=rf[:, :, :], in0=pC[0:64, :].rearrange("p (h c) -> p h c", c=64),
                                      in1=rf[:, :, :], op=ALU.add), reads=[pC_b, rf_b], writes=[rf_b])
                        P.op("act", E("activation", out=rb[:, :, :], in_=rf[:, :, :], func=AF.Copy), reads=[rf_b], writes=[rb_b])
                    if lvl <= 4:
                        P.op("act", E("activation", out=pn_[:, :, :], in_=pA[0:64, :].rearrange("p (h c) -> p h c", c=64),
                                      func=AF.Copy), reads=[pA_b], writes=[pn_b])
                        P.op("dve", E("tensor_copy", out=ptn_[:, :, :], in_=pB[0:64, :].rearrange("p (h c) -> p h c", c=64)),
                             reads=[pB_b], writes=[ptn_b])
                        cur = 1 - cur
                if DBGSTEP < 6:
                    continue
                pW, pW_b = bank[2]
                for h in range(8):
                    P.op("pe", E("matmul", out=pW[:, h * 64:(h + 1) * 64], lhsT=kbt[:, h, :], rhs=rb[:, h, :], start=True, stop=True),
                         reads=[kb_b, rb_b], writes=[pW_b])
                nw, nw_b = nwT[hh]
                P.op("act", E("activation", out=nw[:, :, :], in_=pW[:, :].rearrange("p (h c) -> p h c", c=64), func=AF.Copy,
                              scale=-1.0), reads=[pW_b], writes=[nw_b])
                if DBGSTEP < 7:
                    continue
                vnt, vn_b = vn[hh]
                for q in range(2):
                    pV, pV_b = bank[4 + q]
                    for h4 in range(4):
                        h = q * 4 + h4
                        P.op("pe", E("matmul", out=pV[0:64, h4 * 128:(h4 + 1) * 128], lhsT=rb[:, h, :], rhs=vbt[:, h, :],
                                     start=True, stop=False), reads=[rb_b, vb_b], writes=[pV_b])
                        P.op("pe", E("matmul", out=pV[0:64, h4 * 128:(h4 + 1) * 128], lhsT=nw[:, h, :], rhs=Sb[:, h0 + h, :],
                                     start=False, stop=True), reads=[nw_b, Sb_b[hh]], writes=[pV_b])
                    if q == 0:
                        P.op("act", E("activation", out=vnt[:, 0:4, :], in_=pV[0:64, :].rearrange("p (h v) -> p h v", v=128),
                                      func=AF.Copy), reads=[pV_b], writes=[vn_b])
                    else:
                        P.op("dve", E("tensor_copy", out=vnt[:, 4:8, :], in_=pV[0:64, :].rearrange("p (h v) -> p h v", v=128)),
                             reads=[pV_b], writes=[vn_b])
                if DBGSTEP < 8:
                    continue
                pO, pO_b = bank[1]
                for h in range(8):
                    P.op("pe", E("matmul", out=pO[:, h * 64:(h + 1) * 64], lhsT=Sb[:, h0 + h, :], rhs=qgt[:, h, :],
                                 start=True, stop=False), reads=[Sb_b[hh], qg_b], writes=[pO_b])
                    P.op("pe", E("matmul", out=pO[:, h * 64:(h + 1) * 64], lhsT=vnt[:, h, :], rhs=qk[:, h, :],
                                 start=False, stop=True), reads=[vn_b, qk_b], writes=[pO_b])
                if DBGSTEP < 9:
                    continue
                for q in range(2):
                    pS, pS_b = bank[4 + q]
                    for h4 in range(4):
                        h = q * 4 + h4
                        P.op("pe", E("matmul", out=pS[:, h4 * 128:(h4 + 1) * 128], lhsT=ktt[:, h, :], rhs=vnt[:, h, :],
                                     start=True, stop=True), reads=[kt_b, vn_b], writes=[pS_b])
                    sv = S[:, h0 + q * 4:h0 + q * 4 + 4, :]
                    P.op("dve", E("tensor_tensor", out=sv, in0=sv, in1=bc(gl[:, h0 + q * 4:h0 + q * 4 + 4].unsqueeze(2), [128, 4, 128]),
                                  op=ALU.mult), reads=[S_b[hh], gl_b], writes=[S_b[hh]])
                    P.op("dve", E("tensor_tensor", out=sv, in0=pS[:, :].rearrange("p (h v) -> p h v", v=128), in1=sv, op=ALU.add),
                         reads=[pS_b, S_b[hh]], writes=[S_b[hh]])
                    P.op("act", E("activation", out=Sb[:, h0 + q * 4:h0 + q * 4 + 4, :], in_=sv, func=AF.Copy),
                         reads=[S_b[hh]], writes=[Sb_b[hh]])
                if DBGSTEP < 10:
                    continue
                ob, ob_b = osb[hh]
                oq, oq_b = osq[hh]
                orr, or_b = orst[hh]
                pO3 = pO[:, :].rearrange("p (h c) -> p h c", c=64)
                P.op("dve", E("tensor_copy", out=ob[:, :, :], in_=pO3), reads=[pO_b], writes=[ob_b])
                P.op("act", E("activation", out=oq[:, :, :], in_=pO3, func=AF.Square), reads=[pO_b], writes=[oq_b])
                pQ, pQ_b = bank[3]
                P.op("pe", E("matmul", out=pQ[:, :], lhsT=K.ones[:, :], rhs=oq[:, :, :].rearrange("p h c -> p (h c)"),
                             start=True, stop=True), reads=[K.ones_b, oq_b], writes=[pQ_b])
                rsqrt_op(P, K, orr[:, :, :].rearrange("p h c -> p (h c)"), or_b, pQ[:, :], pQ_b, 1.0 / 128, EPS)
                P.op("dve", E("scalar_tensor_tensor", out=ob[:, :, :].rearrange("p h c -> p (h c)"),
                              in0=ob[:, :, :].rearrange("p h c -> p (h c)"), scalar=vec[:, 128:129],
                              in1=orr[:, :, :].rearrange("p h c -> p (h c)"), op0=ALU.mult, op1=ALU.mult),
                     reads=[ob_b, or_b, vec_b], writes=[ob_b])
                P.op("dve", E("tensor_tensor", out=ogt[:, hs, cs], in0=ob[:, :, :], in1=zs[:, hs, cs], op=ALU.mult),
                     reads=[ob_b, zs_b], writes=[og_b])
        if DBGSTEP < 10:
            P.op("dve", E("tensor_copy", out=ogt[:, :, :], in_=vf[:, :, :]), reads=[vf_b], writes=[og_b])
        if xch is None:
            P.dma("sp", ol, og.rearrange("f p t -> p f t")[:, :, t0:t0 + T], ogt[:, :, :], reads=[og_b])
        else:
            P.dma("sp", ol, xch["og_loc"][ti].rearrange("(f p) t -> p f t", p=128), ogt[:, :, :],
                  reads=[og_b], writes=[xch["og_loc_b"][ti]])
            P.cc_allgather(xch["og_loc"][ti], xch["og_loc_b"][ti], xch["og_all"][ti], xch["og_all_b"][ti], PAIRS)


def build_B(ntile=None):
    nc = bass.Bass("TRN2", target_bir_lowering=False)
    io = {}

    def inp(name, shape, dt=F32):
        io[name] = nc.dram_tensor(name, list(shape), dt, kind="ExternalInput").ap()

    inp("hn1g", [NF, 128, TG], BF16)
    inp("ident", [128, 128])
    inp("vecB", [128, 129])
    inp("rowB", [128, 32])
    inp("cB", [64, 192])
    inp("cB2", [128, 256])
    inp("gw_in", [D, 6176])
    io["og"] = nc.dram_tensor("og", [NF, 128, TG], BF16, kind="ExternalOutput").ap()
    P = Prog(nc)
    K = Common(P, nc, io["ident"])
    phase_B(P, K, io, ntile)
    P.emit()
    P.close()
    return nc


def const_cB():
    t = np.arange(64)
    U = (t[:, None] <= t[None, :]).astype(np.float32)
    negi = np.where(t[None, :] >= t[:, None], 0.0, NEG).astype(np.float32)
    negs = np.where(t[None, :] > t[:, None], 0.0, NEG).astype(np.float32)
    return np.ascontiguousarray(np.concatenate([U, negi, negs], axis=1))


def const_cB2():
    c = const_cB()
    c = np.concatenate([c, np.eye(64, dtype=np.float32)], axis=1)
    return np.ascontiguousarray(np.concatenate([c, c], axis=0))


def host_weights_B(inp, r):
    w = np.asarray(inp["gdn_w_in"], np.float32)[0]
    cw = np.asarray(inp["gdn_conv_w"], np.float32)[0]
    qs = slice(r * 1024, (r + 1) * 1024)
    ks = slice(2048 + r * 1024, 2048 + (r + 1) * 1024)
    vs = slice(4096 + r * 2048, 4096 + (r + 1) * 2048)
    zs = slice(8192 + r * 2048, 8192 + (r + 1) * 2048)
    bs = slice(12288 + r * 16, 12288 + (r + 1) * 16)
    as_ = slice(12320 + r * 16, 12320 + (r + 1) * 16)
    gw = np.ascontiguousarray(np.concatenate([w[:, qs], w[:, ks], w[:, vs], w[:, zs], w[:, bs], w[:, as_]], axis=1))
    cwl = np.concatenate([cw[:, qs], cw[:, ks], cw[:, vs]], axis=1)
    vec = np.concatenate([_fm(cwl[j]) for j in range(4)] + [np.asarray(inp["gdn_norm_w"], np.float32)[0].reshape(128, 1)], axis=1)
    al = np.asarray(inp["gdn_a_log"], np.float32)[0][r * 16:(r + 1) * 16]
    dtb = np.asarray(inp["gdn_dt_bias"], np.float32)[0][r * 16:(r + 1) * 16]
    row = np.ascontiguousarray(np.broadcast_to(np.concatenate([al, dtb])[None, :], (128, 32)))
    return {"gw_in": gw, "vecB": np.ascontiguousarray(vec.astype(np.float32)), "rowB": row.astype(np.float32),
            "cB": const_cB(), "cB2": const_cB2(), "ident": np.eye(128, dtype=np.float32)}


def _run(nc, maps):
    res = run_bass_kernel_spmd(nc, maps, core_ids=list(range(8)))
    return res.results


def _kernel_unfused(**inputs):
    resA = _run(build_A(), host_inputs_A(inputs))
    wB = [host_weights_B(inputs, r) for r in range(2)]
    mapsB = []
    for c in range(8):
        b, r = c // 2, c % 2
        h0 = np.asarray(resA[2 * b]["hn1"])
        h1_ = np.asarray(resA[2 * b + 1]["hn1"])
        seq = np.concatenate([np.zeros((NF, 128, 48), dtype=h0.dtype), h0, h1_[:, :, 16:]], axis=2)
        m = dict(wB[r])
        m["hn1g"] = np.ascontiguousarray(seq)
        mapsB.append(m)
    resB = _run(build_B(), mapsB)
    shared = {
        "ident": np.eye(128, dtype=np.float32),
        "vecC": np.ascontiguousarray(np.concatenate([_fm(inputs["ffn_norm"][1]), _fm(inputs["final_norm"])], axis=1)),
        "gdn_w_out": np.ascontiguousarray(np.asarray(inputs["gdn_w_out"], np.float32)[0]),
        "wg": np.ascontiguousarray(np.asarray(inputs["ffn_w_gate"], np.float32)[1]),
        "wu": np.ascontiguousarray(np.asarray(inputs["ffn_w_up"], np.float32)[1]),
        "wd": np.ascontiguousarray(np.asarray(inputs["ffn_w_down"], np.float32)[1]),
    }
    mapsC = []
    for c in range(8):
        b, r = c // 2, c % 2
        lo = 64 + r * 2048
        ogc = np.concatenate([np.asarray(resB[2 * b]["og"])[:, :, lo:lo + 2048],
                              np.asarray(resB[2 * b + 1]["og"])[:, :, lo:lo + 2048]], axis=0)
        m = dict(shared)
        m["ogc"] = np.ascontiguousarray(ogc)
        m["h1c"] = np.ascontiguousarray(np.asarray(resA[c]["h1"])[:, :, 16:])
        mapsC.append(m)
    resC = _run(build_C(), mapsC)
    out = np.empty((BATCH, SEQ, D), np.float32)
    for c in range(8):
        b, r = c // 2, c % 2
        out[b, r * 2048:(r + 1) * 2048] = np.asarray(resC[c]["out"])
    return out


def build_fused():
    nc = bass.Bass("TRN2", target_bir_lowering=False)
    io = {}

    def inp(name, shape, dt=F32):
        io[name] = nc.dram_tensor(name, list(shape), dt, kind="ExternalInput").ap()

    inp("xa", [TA, D])
    inp("ident", [128, 128])
    inp("vecA", [128, 96])
    inp("sc_w_in", [D, 3 * D])
    inp("sc_w_out", [D, D])
    inp("wg0", [D, DFF])
    inp("wu0", [D, DFF])
    inp("wd0", [DFF, D])
    inp("vecB", [128, 129])
    inp("rowB", [128, 32])
    inp("cB", [64, 192])
    inp("cB2", [128, 256])
    inp("gw_in", [D, 6176])
    inp("vecC", [128, 32])
    inp("mk", [128, 2])
    inp("gdn_w_out", [2 * D, D])
    inp("wg1", [D, DFF])
    inp("wu1", [D, DFF])
    inp("wd1", [DFF, D])
    io["out"] = nc.dram_tensor("out", [TC, D], F32, kind="ExternalOutput").ap()
    nA = len(TA_TILES)
    nB = TG // TB
    h1 = nc.dram_tensor("h1_loc", [NF, 128, TA], F32).ap()
    xch = {
        "h1_b": Buf("h1"),
        "hn1_loc": [nc.dram_tensor("hn1_loc%d" % i, [D, XWS[i % 2]], BF16).ap() for i in range(2 * nA)],
        "hn1_all": [nc.dram_tensor("hn1_all%d" % i, [2 * D, XWS[i % 2]], BF16).ap() for i in range(2 * nA)],
        "og_loc": [nc.dram_tensor("og_loc%d" % i, [D, TB], BF16).ap() for i in range(nB)],
        "og_all": [nc.dram_tensor("og_all%d" % i, [2 * D, TB], BF16).ap() for i in range(nB)],
    }
    for k in ("hn1_loc", "hn1_all", "og_loc", "og_all"):
        xch[k + "_b"] = [Buf("%s%d" % (k, i)) for i in range(len(xch[k]))]
    P = Prog(nc)
    K = Common(P, nc, io["ident"], wslots=4)
    ioA = {"xa": io["xa"], "vecA": io["vecA"], "sc_w_in": io["sc_w_in"], "sc_w_out": io["sc_w_out"],
           "wg": io["wg0"], "wu": io["wu0"], "wd": io["wd0"], "h1": h1}
    phase_A(P, K, ioA, None, xch)
    P.emit(final=False)
    P.close()
    K = Common(P, nc, io["ident"])
    ioB = {"vecB": io["vecB"], "rowB": io["rowB"], "cB": io["cB"], "cB2": io["cB2"], "gw_in": io["gw_in"]}
    phase_B(P, K, ioB, None, xch)
    P.emit(final=False)
    P.close()
    K = Common(P, nc, io["ident"], wslots=6)
    ioC = {"h1c": h1, "vecC": io["vecC"], "mk": io["mk"], "gdn_w_out": io["gdn_w_out"],
           "wg": io["wg1"], "wu": io["wu1"], "wd": io["wd1"], "out": io["out"]}
    phase_C(P, K, ioC, 512, xch)
    P.emit(final=True)
    P.finish()
    return nc


def host_inputs_fused(inp):
    mapsA = host_inputs_A(inp)
    wB = [host_weights_B(inp, r) for r in range(2)]
    shared = {
        "vecC": np.ascontiguousarray(np.concatenate([_fm(inp["ffn_norm"][1]), _fm(inp["final_norm"])], axis=1)),
        "gdn_w_out": np.ascontiguousarray(np.asarray(inp["gdn_w_out"], np.float32)[0]),
        "wg1": np.ascontiguousarray(np.asarray(inp["ffn_w_gate"], np.float32)[1]),
        "wu1": np.ascontiguousarray(np.asarray(inp["ffn_w_up"], np.float32)[1]),
        "wd1": np.ascontiguousarray(np.asarray(inp["ffn_w_down"], np.float32)[1]),
    }
    maps = []
    for c in range(8):
        r = c % 2
        a = mapsA[c]
        m = {"xa": a["xa"], "ident": a["ident"], "vecA": a["vecA"], "sc_w_in": a["sc_w_in"], "sc_w_out": a["sc_w_out"],
             "wg0": a["wg"], "wu0": a["wu"], "wd0": a["wd"]}
        for k in ("vecB", "rowB", "cB", "cB2", "gw_in"):
            m[k] = wB[r][k]
        m.update(shared)
        mk = np.zeros((128, 2), np.float32)
        mk[:, r] = 1.0
        m["mk"] = mk
        maps.append(m)
    return maps


def kernel_unfused(**inputs):
    return _kernel_unfused(**inputs)


def kernel(**inputs):
    nc = build_fused()
    res = run_bass_kernel_spmd(nc, host_inputs_fused(inputs), core_ids=list(range(8))).results
    out = np.empty((BATCH, SEQ, D), np.float32)
    for c in range(8):
        b, r = c // 2, c % 2
        out[b, r * 2048:(r + 1) * 2048] = np.asarray(res[c]["out"])
    return out
```

```python
import numpy as np
import ml_dtypes
import concourse.bass as bass
import concourse.mybir as mybir
from concourse.bass_utils import run_bass_kernel_spmd

F32 = mybir.dt.float32
BF16 = mybir.dt.bfloat16
AF = mybir.ActivationFunctionType
ALU = mybir.AluOpType
AX = mybir.AxisListType

D = 2048
NF = 16
DFF = 5632
NCF = 44
SEQ = 4096
BATCH = 4
N_META = 16
EPS = 1e-6
TA = 2064
TC = 2048
TG = 4160
PAIRS = [[0, 1], [2, 3], [4, 5], [6, 7]]
TA_TILES = (528, 512, 512, 512)
XWS = (512, 64)

ENGS = ("pe", "act", "dve", "pool", "sp")


class Buf:
    __slots__ = ("name", "last_w", "readers", "excl")

    def __init__(self, name="", excl=False):
        self.name = name
        self.last_w = None
        self.readers = []
        self.excl = excl


class Ins:
    __slots__ = ("eng", "emit", "deps", "inc", "ordinal", "lane", "lane_val", "is_dma", "cc_inc", "phase")

    def __init__(self, eng, emit):
        self.eng = eng
        self.emit = emit
        self.deps = []
        self.inc = False
        self.ordinal = None
        self.lane = None
        self.lane_val = None
        self.is_dma = False
        self.cc_inc = 16
        self.phase = 0


class Lane:
    def __init__(self, name):
        self.name = name
        self.sem = None
        self.count = 0
        self.last = None


class Prog:
    def __init__(self, nc):
        self.nc = nc
        self.streams = {e: [] for e in ENGS}
        self.lanes = []
        self._ctx = []
        self.out_lanes = []
        self.phase = 0
        self.sems = None
        self.ord = {e: 0 for e in ENGS}
        self._semctx = []

    def sb(self, name, shape, dt):
        g = self.nc.sbuf_tensor("%s_p%d" % (name, self.phase), list(shape), dt)
        t = g.__enter__()
        self._ctx.append(g)
        return t

    def ps(self, name, shape, dt=F32):
        g = self.nc.psum_tensor("%s_p%d" % (name, self.phase), list(shape), dt)
        t = g.__enter__()
        self._ctx.append(g)
        return t

    def lane(self, name, out=False):
        l = Lane("%s_p%d" % (name, self.phase))
        self.lanes.append(l)
        if out:
            self.out_lanes.append(l)
        return l

    def _track(self, ins, reads, writes):
        deps = ins.deps
        for b in reads:
            if b.last_w is not None:
                deps.append(b.last_w)
            if b.excl:
                for r in b.readers:
                    if r.eng != ins.eng:
                        deps.append(r)
        for b in writes:
            if b.last_w is not None:
                deps.append(b.last_w)
            deps.extend(b.readers)
        for b in reads:
            if not ins.is_dma:
                b.readers = [r for r in b.readers if r.is_dma or r.eng != ins.eng]
            b.readers.append(ins)
        for b in writes:
            b.last_w = ins
            b.readers = []

    def op(self, eng, emit, reads=(), writes=()):
        ins = Ins(eng, emit)
        ins.phase = self.phase
        self._track(ins, reads, writes)
        self.streams[eng].append(ins)
        return ins

    def dma(self, eng, lane, out, in_, reads=(), writes=()):
        return self.dma_group(eng, lane, [(out, in_)], reads, writes)

    def dma_group(self, eng, lane, pairs, reads=(), writes=()):
        pairs = list(pairs)
        ins = Ins(eng, lambda e: [e.dma_start(out=o, in_=i) for (o, i) in pairs])
        ins.is_dma = True
        ins.phase = self.phase
        ins.lane = lane
        if lane.last is not None:
            ins.deps.append(lane.last)
        lane.count += 16 * len(pairs)
        ins.lane_val = lane.count
        lane.last = ins
        self._track(ins, reads, writes)
        self.streams[eng].append(ins)
        return ins

    def cc_allgather(self, in_ap, in_buf, out_ap, out_buf, groups):
        lane = self.lane("cc%d" % len(self.lanes))
        ins = Ins("pool", lambda e: [e.collective_compute("AllGather", ALU.bypass, replica_groups=groups,
                                                          ins=[in_ap], outs=[out_ap])])
        ins.is_dma = True
        ins.phase = self.phase
        ins.lane = lane
        ins.cc_inc = 1
        lane.count += 1
        ins.lane_val = lane.count
        lane.last = ins
        self._track(ins, [in_buf], [out_buf])
        self.streams["pool"].append(ins)
        return ins

    def emit(self, final=True):
        nc = self.nc
        cur = self.phase
        for e in ENGS:
            for ins in self.streams[e]:
                for d in ins.deps:
                    if not d.is_dma and d.phase == cur:
                        d.inc = True
        for e in ENGS:
            c = self.ord[e]
            for ins in self.streams[e]:
                if ins.inc and not ins.is_dma:
                    c += 1
                    ins.ordinal = c
            self.ord[e] = c
        if self.sems is None:
            self.sems = {}
            for e in ENGS:
                g = nc.semaphore("sem_" + e)
                self.sems[e] = g.__enter__()
                self._semctx.append(g)
        sems = self.sems
        for l in self.lanes:
            if l.sem is None:
                g = nc.semaphore("lane_" + l.name)
                l.sem = g.__enter__()
                self._semctx.append(g)
        out_lanes = self.out_lanes if final else []

        def run(ename, eng):
            seen = {}
            for ins in self.streams[ename]:
                need = {}
                for d in ins.deps:
                    if d.is_dma:
                        key = ("l", id(d.lane))
                        sem = d.lane.sem
                        val = d.lane_val
                    else:
                        if d.phase != cur:
                            continue
                        if d.eng == ename and ename == "pe":
                            continue
                        key = ("e", d.eng)
                        sem = sems[d.eng]
                        val = d.ordinal
                    if seen.get(key, 0) >= val:
                        continue
                    if key not in need or need[key][1] < val:
                        need[key] = (sem, val)
                for key, (sem, val) in need.items():
                    eng.wait_ge(sem, val)
                    seen[key] = val
                bi = ins.emit(eng)
                if ins.is_dma:
                    for b1 in bi:
                        b1.then_inc(ins.lane.sem, getattr(ins, "cc_inc", 16))
                elif ins.inc:
                    bi.then_inc(sems[ename], 1)
            if ename == "sp":
                for l in out_lanes:
                    if l.count:
                        eng.wait_ge(l.sem, l.count)

        with nc.Block() as block:
            @block.tensor
            def _(t):
                run("pe", t)

            @block.scalar
            def _(t):
                run("act", t)

            @block.vector
            def _(t):
                run("dve", t)

            @block.gpsimd
            def _(t):
                run("pool", t)

            @block.sync
            def _(t):
                run("sp", t)
        self.streams = {e: [] for e in ENGS}
        self.phase += 1

    def close(self):
        for g in reversed(self._ctx):
            g.__exit__(None, None, None)
        self._ctx = []

    def finish(self):
        self.close()
        for g in reversed(self._semctx):
            g.__exit__(None, None, None)
        self._semctx = []


class Ring:
    def __init__(self, P, name, n, shape, dt, lanes=False, psum=False):
        self.slots = []
        for i in range(n):
            t = (P.ps if psum else P.sb)("%s%d" % (name, i), shape, dt)
            self.slots.append((t, Buf("%s%d" % (name, i), excl=psum), P.lane("%s%d" % (name, i)) if lanes else None))
        self.i = 0

    def next(self):
        s = self.slots[self.i % len(self.slots)]
        self.i += 1
        return s


def E(method, **kw):
    return lambda e: getattr(e, method)(**kw)


def mm_group(P, out_ap, out_buf, pairs, reads):
    n = len(pairs)
    for i, (l, r) in enumerate(pairs):
        P.op("pe", E("matmul", out=out_ap, lhsT=l, rhs=r, start=(i == 0), stop=(i == n - 1)),
             reads=reads, writes=[out_buf])


def make_segs(T, maxn=512):
    nseg = (T + maxn - 1) // maxn
    base = (T + nseg - 1) // nseg
    segs = []
    a = 0
    while a < T:
        b = min(T, a + base)
        segs.append((a, b))
        a = b
    return segs


class Common:
    def __init__(self, P, nc, ident_dram, wslots=3, wsize=6144, ntmp=4):
        self.P = P
        self.psum = Ring(P, "bank", 8, [128, 512], F32, psum=True)
        self.wring = Ring(P, "wr", wslots, [128, wsize], BF16, lanes=True)
        self.ident = P.sb("ident_sb", [128, 128], F32)
        self.ident_b = Buf("ident")
        self.ones = P.sb("ones_sb", [128, 128], F32)
        self.ones_b = Buf("ones")
        self.cl = P.lane("const")
        P.dma("sp", self.cl, self.ident[:], ident_dram, writes=[self.ident_b])
        P.op("dve", E("memset", ap=self.ones[:], constant=1.0), writes=[self.ones_b])
        self.epsc = P.sb("epsc", [128, 2], F32)
        self.eps_b = Buf("epsc")
        self.eps_col = {EPS: 0}
        P.op("dve", E("memset", ap=self.epsc[:], constant=EPS), writes=[self.eps_b])
        self.sq = Ring(P, "sq", 3, [128, 512], F32)
        self.tmp = Ring(P, "tmp", ntmp, [128, 512], F32) if ntmp else None


def rsqrt_op(P, K, out, out_b, in_, in_b, scale, eps):
    P.op("act", E("activation", out=out, in_=in_, func=AF.Ln, bias=K.epsc[:in_.shape[0], K.eps_col[eps]:K.eps_col[eps] + 1],
                  scale=scale), reads=[in_b, K.eps_b], writes=[out_b])
    P.op("act", E("activation", out=out, in_=out, func=AF.Exp, scale=-0.5), reads=[out_b], writes=[out_b])


def rmsnorm_fm(P, K, src, src_b, dst, dst_b, segs, vec, vec_b, wcol, rstd, rstd_b):
    for (a, b) in segs:
        n = b - a
        pb, pbb, _ = K.psum.next()
        for f in range(NF):
            sq, sqb, _ = K.sq.next()
            P.op("act", E("activation", out=sq[:, :n], in_=src[:, f, a:b], func=AF.Square),
                 reads=[src_b], writes=[sqb])
            P.op("pe", E("matmul", out=pb[:, :n], lhsT=K.ones[:], rhs=sq[:, :n], start=(f == 0), stop=(f == NF - 1)),
                 reads=[sqb, K.ones_b], writes=[pbb])
        rsqrt_op(P, K, rstd[:, a:b], rstd_b, pb[:, :n], pbb, 1.0 / D, EPS)
        for f in range(NF):
            P.op("dve", E("scalar_tensor_tensor", out=dst[:, f, a:b], in0=src[:, f, a:b],
                          scalar=vec[:, wcol + f:wcol + f + 1], in1=rstd[:, a:b], op0=ALU.mult, op1=ALU.mult),
                 reads=[src_b, rstd_b, vec_b], writes=[dst_b])


def ffn_fm(P, K, hT, hT_b, hn, hn_b, act, act_b, segs, wg, wu, wd):
    wgv = wg.rearrange("(k p) m -> p k m", p=128)
    wuv = wu.rearrange("(k p) m -> p k m", p=128)
    wdv = wd.rearrange("(k p) m -> p k m", p=128)
    for c in range(NCF):
        wt, wb, wl = K.wring.next()
        wv = wt[:, 0:4096].rearrange("p (k j m) -> p k j m", k=NF, j=2)
        P.dma_group("pool", wl, [(wv[:, :, 0, :], wgv[:, :, c * 128:(c + 1) * 128]),
                                 (wv[:, :, 1, :], wuv[:, :, c * 128:(c + 1) * 128])], writes=[wb])
        for (a, b) in segs:
            n = b - a
            pg, pgb, _ = K.psum.next()
            pu, pub, _ = K.psum.next()
            mm_group(P, pg[:, :n], pgb, [(wv[:, k, 0, :], hn[:, k, a:b]) for k in range(NF)], [wb, hn_b])
            mm_group(P, pu[:, :n], pub, [(wv[:, k, 1, :], hn[:, k, a:b]) for k in range(NF)], [wb, hn_b])
            st, stb, _ = K.tmp.next()
            P.op("act", E("activation", out=st[:, :n], in_=pg[:, :n], func=AF.Silu), reads=[pgb], writes=[stb])
            P.op("dve", E("tensor_tensor", out=act[:, c, a:b], in0=pu[:, :n], in1=st[:, :n], op=ALU.mult),
                 reads=[pub, stb], writes=[act_b])
    for m in range(NF):
        wt, wb, wl = K.wring.next()
        wv = wt[:, 0:NCF * 128].rearrange("p (k m) -> p k m", k=NCF)
        P.dma("pool", wl, wv, wdv[:, :, m * 128:(m + 1) * 128], writes=[wb])
        for (a, b) in segs:
            n = b - a
            pb, pbb, _ = K.psum.next()
            mm_group(P, pb[:, :n], pbb, [(wv[:, k, :], act[:, k, a:b]) for k in range(NCF)], [wb, act_b])
            P.op("dve", E("tensor_tensor", out=hT[:, m, a:b], in0=pb[:, :n], in1=hT[:, m, a:b], op=ALU.add),
                 reads=[pbb, hT_b], writes=[hT_b])


def phase_A(P, K, io, tile_T=516, xch=None):
    xa, w_in, w_out = io["xa"], io["sc_w_in"], io["sc_w_out"]
    h1, hn1 = io["h1"], io.get("hn1")
    T = max(TA_TILES)
    ntile = len(TA_TILES)
    vec = P.sb("vecA_sb", [128, 96], F32)
    vec_b = Buf("vecA")
    P.dma("sp", K.cl, vec[:], io["vecA"], writes=[vec_b])
    hT_full = P.sb("hT", [128, NF, T], F32)
    hT_b = Buf("hT")
    hn_full = P.sb("hn", [128, NF, T], BF16)
    hn_b = Buf("hn")
    act_full = P.sb("act", [128, NCF, T], BF16)
    act_b = Buf("act")
    rstd_full = P.sb("rstd", [128, T], F32)
    rstd_b = Buf("rstd")
    xs = Ring(P, "xs", 2, [128, D], F32, lanes=True)
    cur = Ring(P, "cu", 2, [128, T + 2], F32)
    bsb = Ring(P, "bsb", 2, [128, T], F32)
    acc = Ring(P, "acc", 2, [128, T], F32)
    halo = P.sb("halo", [128, NF, 2], F32)
    halo_b = [Buf("halo%d" % f) for f in range(NF)]
    P.op("dve", E("memset", ap=halo[:], constant=0.0), writes=halo_b)
    st_lane = P.lane("stA", out=(xch is None))
    w_in_v = w_in.rearrange("(k p) m -> p k m", p=128)
    w_out_v = w_out.rearrange("(k p) m -> p k m", p=128)
    h1_v = h1.rearrange("f p t -> p f t")
    hn1_v = hn1.rearrange("f p t -> p f t") if hn1 is not None else None

    for ti in range(ntile):
        T = TA_TILES[ti]
        t0 = sum(TA_TILES[:ti])
        segs = [(0, T)] if T <= 512 else [(0, T - 512), (T - 512, T)]
        hT = hT_full[:, :, 0:T]
        hn = hn_full[:, :, 0:T]
        act = act_full[:, :, 0:T]
        y = act[:, 0:NF, :]
        rstd = rstd_full[:, 0:T]
        for g0 in range(0, T, 128):
            gs = min(128, T - g0)
            xt, xb, xl = xs.next()
            P.dma("sp", xl, xt[:gs, :], xa[t0 + g0:t0 + g0 + gs, :], writes=[xb])
            for fq in range(4):
                pb, pbb, _ = K.psum.next()
                for j in range(4):
                    f = fq * 4 + j
                    P.op("pe", E("transpose", out=pb[:, j * 128:j * 128 + gs], in_=xt[:gs, f * 128:(f + 1) * 128],
                                 identity=K.ident[:gs, :gs]), reads=[xb, K.ident_b], writes=[pbb])
                P.op("act", E("activation", out=hT[:, fq * 4:(fq + 1) * 4, g0:g0 + gs],
                              in_=pb[:, :].rearrange("p (j t) -> p j t", t=128)[:, :, :gs], func=AF.Copy),
                     reads=[pbb], writes=[hT_b])
        rmsnorm_fm(P, K, hT, hT_b, hn, hn_b, segs, vec, vec_b, 0, rstd, rstd_b)
        for f in range(NF):
            wt, wb, wl = K.wring.next()
            wv = wt[:, 0:6144].rearrange("p (k j m) -> p k j m", k=NF, j=3)
            P.dma_group("pool", wl, [(wv[:, :, j, :], w_in_v[:, :, j * D + f * 128:j * D + (f + 1) * 128])
                                     for j in range(3)], writes=[wb])
            cu, cub, _ = cur.next()
            bs, bsbb, _ = bsb.next()
            ac, acb, _ = acc.next()
            cu, bs, ac = cu[:, 0:T + 2], bs[:, 0:T], ac[:, 0:T]
            P.op("act", E("activation", out=cu[:, 0:2], in_=halo[:, f, :], func=AF.Copy),
                 reads=[halo_b[f]], writes=[cub])
            for (a, b) in segs:
                n = b - a
                pbk = [K.psum.next() for _ in range(3)]
                for j in range(3):
                    mm_group(P, pbk[j][0][:, :n], pbk[j][1],
                             [(wv[:, k, j, :], hn[:, k, a:b]) for k in range(NF)], [wb, hn_b])
                ut, utb, _ = K.tmp.next()
                P.op("act", E("activation", out=ut[:, :n], in_=pbk[2][0][:, :n], func=AF.Copy),
                     reads=[pbk[2][1]], writes=[utb])
                P.op("dve", E("tensor_tensor", out=cu[:, 2 + a:2 + b], in0=pbk[1][0][:, :n], in1=ut[:, :n], op=ALU.mult),
                     reads=[pbk[1][1], utb], writes=[cub])
                P.op("act", E("activation", out=bs[:, a:b], in_=pbk[0][0][:, :n], func=AF.Copy),
                     reads=[pbk[0][1]], writes=[bsbb])
            c0, c1, c2 = 32 + f, 48 + f, 64 + f
            P.op("dve", E("tensor_scalar", out=ac[:, :], in0=cu[:, 2:2 + T], scalar1=vec[:, c2:c2 + 1], scalar2=None,
                          op0=ALU.mult), reads=[cub, vec_b], writes=[acb])
            P.op("dve", E("scalar_tensor_tensor", out=ac[:, :], in0=cu[:, 1:1 + T], scalar=vec[:, c1:c1 + 1],
                          in1=ac[:, :], op0=ALU.mult, op1=ALU.add), reads=[cub, vec_b, acb], writes=[acb])
            P.op("dve", E("scalar_tensor_tensor", out=ac[:, :], in0=cu[:, 0:T], scalar=vec[:, c0:c0 + 1],
                          in1=ac[:, :], op0=ALU.mult, op1=ALU.add), reads=[cub, vec_b, acb], writes=[acb])
            P.op("dve", E("tensor_tensor", out=y[:, f, :], in0=ac[:, :], in1=bs[:, :], op=ALU.mult),
                 reads=[acb, bsbb], writes=[act_b])
            P.op("act", E("activation", out=halo[:, f, :], in_=cu[:, T:T + 2], func=AF.Copy),
                 reads=[cub], writes=[halo_b[f]])
        for mp in range(NF // 2):
            wt, wb, wl = K.wring.next()
            wv = wt[:, 0:4096].rearrange("p (k m) -> p k m", k=NF)
            P.dma("pool", wl, wv, w_out_v[:, :, mp * 256:(mp + 1) * 256], writes=[wb])
            for mi in range(2):
                m = mp * 2 + mi
                for (a, b) in segs:
                    n = b - a
                    pb, pbb, _ = K.psum.next()
                    mm_group(P, pb[:, :n], pbb, [(wv[:, k, mi * 128:(mi + 1) * 128], y[:, k, a:b]) for k in range(NF)],
                             [wb, act_b])
                    P.op("dve", E("tensor_tensor", out=hT[:, m, a:b], in0=pb[:, :n], in1=hT[:, m, a:b], op=ALU.add),
                         reads=[pbb, hT_b], writes=[hT_b])
        rmsnorm_fm(P, K, hT, hT_b, hn, hn_b, segs, vec, vec_b, 16, rstd, rstd_b)
        ffn_fm(P, K, hT, hT_b, hn, hn_b, act, act_b, segs, io["wg"], io["wu"], io["wd"])
        if xch is None:
            P.dma("sp", st_lane, h1_v[:, :, t0:t0 + T], hT[:, :, :], reads=[hT_b])
            rmsnorm_fm(P, K, hT, hT_b, hn, hn_b, segs, vec, vec_b, 80, rstd, rstd_b)
            P.dma("sp", st_lane, hn1_v[:, :, t0:t0 + T], hn[:, :, :], reads=[hn_b])
        else:
            P.dma("sp", st_lane, h1_v[:, :, t0:t0 + T], hT[:, :, :], reads=[hT_b], writes=[xch["h1_b"]])
            rmsnorm_fm(P, K, hT, hT_b, hn, hn_b, segs, vec, vec_b, 80, rstd, rstd_b)
            for part, (c0, c1) in enumerate(((0, 512), (T - 64, T))):
                if part == 1 and T <= 512:
                    continue
                xi = 2 * ti + part
                P.dma("sp", st_lane, xch["hn1_loc"][xi].rearrange("(f p) t -> p f t", p=128)[:, :, 0:c1 - c0],
                      hn[:, :, c0:c1], reads=[hn_b], writes=[xch["hn1_loc_b"][xi]])
                P.cc_allgather(xch["hn1_loc"][xi], xch["hn1_loc_b"][xi], xch["hn1_all"][xi], xch["hn1_all_b"][xi], PAIRS)
    return st_lane


def phase_C(P, K, io, tile_T=512, xch=None):
    h1c, og, gw_out, out = io["h1c"], io.get("ogc"), io["gdn_w_out"], io["out"]
    T = tile_T
    ntile = TC // T
    segs = make_segs(T)
    vec = P.sb("vecC_sb", [128, 32], F32)
    vec_b = Buf("vecC")
    P.dma("sp", K.cl, vec[:], io["vecC"], writes=[vec_b])
    hT = P.sb("hTc", [128, NF, T], F32)
    hT_b = Buf("hTc")
    hn = P.sb("hnc", [128, NF, T], BF16)
    hn_b = Buf("hnc")
    act = P.sb("actc", [128, NCF, T], BF16)
    act_b = Buf("actc")
    ogt = act[:, 0:32, :]
    rstd = P.sb("rstdc", [128, T], F32)
    rstd_b = Buf("rstdc")
    ot = Ring(P, "ot", 2, [128, D], F32, lanes=True)
    ld = P.lane("ldC")
    h1_v = h1c.rearrange("f p t -> p f t")
    og_v = og.rearrange("f p t -> p f t") if og is not None else None
    wo_v = gw_out.rearrange("(k p) m -> p k m", p=128)
    hoff = 0 if xch is None else 16
    if xch is not None:
        mk = P.sb("mk_sb", [128, 2], F32)
        mk_b = Buf("mk")
        P.dma("sp", K.cl, mk[:], io["mk"], writes=[mk_b])
    for l in ot.slots:
        P.out_lanes.append(l[2])
    for ti in range(ntile):
        t0 = ti * T
        if xch is None:
            P.dma("sp", ld, hT[:, :, :], h1_v[:, :, t0:t0 + T], writes=[hT_b])
            P.dma_group("sp", ld, [(ogt[:, 0:16, :], og_v[:, 0:16, t0:t0 + T]),
                                   (ogt[:, 16:32, :], og_v[:, 16:32, t0:t0 + T])], writes=[act_b])
        else:
            P.dma("sp", ld, hT[:, :, :], h1_v[:, :, hoff + t0:hoff + t0 + T], reads=[xch["h1_b"]], writes=[hT_b])
            for hf in range(2):
                for cand, dst, dst_b in ((0, ogt, act_b), (1, hn, hn_b)):
                    pieces = []
                    s0 = 64 + cand * 2048 + t0
                    end = s0 + T
                    while s0 < end:
                        bt, cq = s0 // TB, s0 % TB
                        n = min(end - s0, TB - cq)
                        src = xch["og_all"][bt].rearrange("(rf p) t -> p rf t", p=128)[:, hf * 16:(hf + 1) * 16, cq:cq + n]
                        d0 = s0 - (64 + cand * 2048 + t0)
                        dd = dst[:, hf * 16:(hf + 1) * 16, d0:d0 + n] if cand == 0 else dst[:, :, d0:d0 + n]
                        pieces.append((dd, src, xch["og_all_b"][bt]))
                        s0 += n
                    P.dma_group("sp", ld, [(d_, s_) for d_, s_, _ in pieces], reads=[b_ for _, _, b_ in pieces],
                                writes=[dst_b])
                oh = ogt[:, hf * 16:(hf + 1) * 16, :]
                P.op("dve", E("tensor_scalar", out=oh, in0=oh, scalar1=mk[:, 0:1], scalar2=None, op0=ALU.mult),
                     reads=[act_b, mk_b], writes=[act_b])
                P.op("dve", E("scalar_tensor_tensor", out=oh, in0=hn[:, :, :], scalar=mk[:, 1:2], in1=oh,
                              op0=ALU.mult, op1=ALU.add), reads=[hn_b, act_b, mk_b], writes=[act_b])
        for m in range(NF):
            wt, wb, wl = K.wring.next()
            wv = wt[:, 0:4096].rearrange("p (k m) -> p k m", k=32)
            P.dma("pool", wl, wv, wo_v[:, :, m * 128:(m + 1) * 128], writes=[wb])
            for (a, b) in segs:
                n = b - a
                pb, pbb, _ = K.psum.next()
                mm_group(P, pb[:, :n], pbb, [(wv[:, k, :], ogt[:, k, a:b]) for k in range(32)], [wb, act_b])
                P.op("dve", E("tensor_tensor", out=hT[:, m, a:b], in0=pb[:, :n], in1=hT[:, m, a:b], op=ALU.add),
                     reads=[pbb, hT_b], writes=[hT_b])
        rmsnorm_fm(P, K, hT, hT_b, hn, hn_b, segs, vec, vec_b, 0, rstd, rstd_b)
        ffn_fm(P, K, hT, hT_b, hn, hn_b, act, act_b, segs, io["wg"], io["wu"], io["wd"])
        rmsnorm_fm(P, K, hT, hT_b, hT, hT_b, segs, vec, vec_b, 16, rstd, rstd_b)
        for g0 in range(0, T, 128):
            o_t, ob, ol = ot.next()
            for fq in range(4):
                pb, pbb, _ = K.psum.next()
                for j in range(4):
                    f = fq * 4 + j
                    P.op("pe", E("transpose", out=pb[:, j * 128:(j + 1) * 128], in_=hT[:, f, g0:g0 + 128],
                                 identity=K.ident[:, :]), reads=[hT_b, K.ident_b], writes=[pbb])
                if fq % 2:
                    P.op("act", E("activation", out=o_t[:, fq * 512:(fq + 1) * 512], in_=pb[:, :], func=AF.Copy),
                         reads=[pbb], writes=[ob])
                else:
                    P.op("dve", E("tensor_copy", out=o_t[:, fq * 512:(fq + 1) * 512], in_=pb[:, :]),
                         reads=[pbb], writes=[ob])
            P.dma("sp", ol, out[t0 + g0:t0 + g0 + 128, :], o_t[:, :], reads=[ob])


def _fm(v):
    v = np.asarray(v, np.float32)
    return np.ascontiguousarray(v.reshape(-1, 128).T)


def build_A(tile_T=None):
    nc = bass.Bass("TRN2", target_bir_lowering=False)
    io = {}

    def inp(name, shape, dt=F32):
        io[name] = nc.dram_tensor(name, list(shape), dt, kind="ExternalInput").ap()

    inp("xa", [TA, D])
    inp("ident", [128, 128])
    inp("vecA", [128, 96])
    inp("sc_w_in", [D, 3 * D])
    inp("sc_w_out", [D, D])
    inp("wg", [D, DFF])
    inp("wu", [D, DFF])
    inp("wd", [DFF, D])
    io["h1"] = nc.dram_tensor("h1", [NF, 128, TA], F32, kind="ExternalOutput").ap()
    io["hn1"] = nc.dram_tensor("hn1", [NF, 128, TA], BF16, kind="ExternalOutput").ap()
    P = Prog(nc)
    K = Common(P, nc, io["ident"])
    phase_A(P, K, io, tile_T)
    P.emit()
    P.close()
    return nc


def host_inputs_A(inp):
    x = np.asarray(inp["x"], np.float32)
    meta = np.asarray(inp["meta_tokens"], np.float32)
    conv = np.asarray(inp["sc_conv_w"], np.float32)[0]
    vecA = np.concatenate([_fm(inp["mixer_norm"][0]), _fm(inp["ffn_norm"][0]),
                           _fm(conv[0]), _fm(conv[1]), _fm(conv[2]), _fm(inp["mixer_norm"][1])], axis=1)
    shared = {
        "ident": np.eye(128, dtype=np.float32),
        "vecA": np.ascontiguousarray(vecA),
        "sc_w_in": np.ascontiguousarray(np.asarray(inp["sc_w_in"], np.float32)[0]),
        "sc_w_out": np.ascontiguousarray(np.asarray(inp["sc_w_out"], np.float32)[0]),
        "wg": np.ascontiguousarray(np.asarray(inp["ffn_w_gate"], np.float32)[0]),
        "wu": np.ascontiguousarray(np.asarray(inp["ffn_w_up"], np.float32)[0]),
        "wd": np.ascontiguousarray(np.asarray(inp["ffn_w_down"], np.float32)[0]),
    }
    maps = []
    for c in range(8):
        b, r = c // 2, c % 2
        if r == 0:
            xa = np.concatenate([meta, x[b, 0:2048]], axis=0)
        else:
            xa = x[b, 2032:4096]
        m = dict(shared)
        m["xa"] = np.ascontiguousarray(xa)
        maps.append(m)
    return maps


def build_C(tile_T=512):
    nc = bass.Bass("TRN2", target_bir_lowering=False)
    io = {}

    def inp(name, shape, dt=F32):
        io[name] = nc.dram_tensor(name, list(shape), dt, kind="ExternalInput").ap()

    inp("h1c", [NF, 128, TC])
    inp("ogc", [32, 128, TC], BF16)
    inp("ident", [128, 128])
    inp("vecC", [128, 32])
    inp("gdn_w_out", [2 * D, D])
    inp("wg", [D, DFF])
    inp("wu", [D, DFF])
    inp("wd", [DFF, D])
    io["out"] = nc.dram_tensor("out", [TC, D], F32, kind="ExternalOutput").ap()
    P = Prog(nc)
    K = Common(P, nc, io["ident"])
    phase_C(P, K, io, tile_T)
    P.emit()
    P.close()
    return nc


NEG = -30000.0
DBG = ""
DBGSTEP = 99
POOLENG = "dve"
STACKED = True
EXPF = AF.Exp
TB = 320
NCH = TB // 64
DK = 128


def phase_B(P, K, io, ntile=None, xch=None):
    hn1g, gw_in, og = io.get("hn1g"), io["gw_in"], io.get("og")
    T = TB
    ntile = ntile or TG // T
    nc = P.nc
    vec = P.sb("vecB_sb", [128, 129], F32)
    vec_b = Buf("vecB")
    P.dma("sp", K.cl, vec[:], io["vecB"], writes=[vec_b])
    row = P.sb("rowB_sb", [128, 32], F32)
    row_b = Buf("rowB")
    P.dma("sp", K.cl, row[:], io["rowB"], writes=[row_b])
    cB = P.sb("cB_sb", [64, 192], F32)
    cB_b = Buf("cB")
    P.dma("sp", K.cl, cB[:], io["cB"], writes=[cB_b])
    Umat = cB[:, 0:64]
    NEGI = cB[:, 64:128]
    NEGS = cB[:, 128:192]
    identb = P.sb("identb", [128, 128], BF16)
    identb_b = Buf("identb")
    P.op("dve", E("tensor_copy", out=identb[:], in_=K.ident[:]), reads=[K.ident_b], writes=[identb_b])
    onec = P.sb("onec", [128, 1], F32)
    onec_b = Buf("onec")
    P.op("dve", E("memset", ap=onec[:], constant=1.0), writes=[onec_b])
    nea = P.sb("nea", [64, 16], F32)
    nea_b = Buf("nea")
    P.op("act", E("activation", out=nea[:], in_=row[0:64, 0:16], func=AF.Exp), reads=[row_b], writes=[nea_b])
    P.op("dve", E("tensor_scalar", out=nea[:], in0=nea[:], scalar1=-1.0, scalar2=None, op0=ALU.mult),
         reads=[nea_b], writes=[nea_b])
    wba = P.sb("wba", [128, NF, 32], BF16)
    wba_b = Buf("wba")
    gw_v = gw_in.rearrange("(k p) m -> p k m", p=128)
    P.dma("pool", P.lane("wba"), wba[:], gw_v[:, :, 6144:6176], writes=[wba_b])
    hnr = Ring(P, "hnB", 2, [128, NF, T], BF16, lanes=True)
    qf = P.sb("qf", [128, 8, T], BF16)
    kf = P.sb("kf", [128, 8, T], BF16)
    vf = P.sb("vf", [128, 16, T], BF16)
    zs = P.sb("zs", [128, 16, T], BF16)
    qf_b, kf_b, vf_b, zs_b = Buf("qf"), Buf("kf"), Buf("vf"), Buf("zs")
    ogr = Ring(P, "ogB", 2, [128, 16, T], BF16, lanes=True)
    if xch is None:
        for sl in ogr.slots:
            P.out_lanes.append(sl[2])
    pcr = Ring(P, "pc", 2, [128, T + 3], F32)
    accr = Ring(P, "accB", 2, [128, T], F32)
    silr = Ring(P, "sil", 2, [128, T], F32)
    rinr = Ring(P, "rin", 2, [128, T], F32)
    halo = P.sb("haloB", [128, 32, 3], F32)
    halo_b = [Buf("haloB%d" % i) for i in range(32)]
    P.op("dve", E("memset", ap=halo[:], constant=0.0), writes=halo_b)
    beta = P.sb("beta", [64, NCH, 16], F32)
    lb = P.sb("lb", [64, NCH, 16], F32)
    gtm = P.sb("gtm", [64, NCH, 16], F32)
    sm_b = Buf("small")
    S = P.sb("S", [128, 16, 128], F32)
    S_b = [Buf("S0"), Buf("S1")]
    Sb = P.sb("Sbf", [128, 16, 128], BF16)
    Sb_b = [Buf("Sb0"), Buf("Sb1")]
    P.op("dve", E("memset", ap=S[:], constant=0.0), writes=S_b)
    P.op("dve", E("memset", ap=Sb[:], constant=0.0), writes=Sb_b)
    def mk(name, shape, dt):
        if STACKED:
            shape = [shape[0]] + [1] * (len(shape) - 1)
        return [(P.sb("%s%d" % (name, i), shape, dt), Buf("%s%d" % (name, i))) for i in range(2)]
    gcs = mk("gcs", [64, 16], F32)
    gbs = mk("gbs", [64, 16], F32)
    egs = mk("egs", [64, 16], F32)
    begs = mk("begs", [64, 16], F32)
    ekts = mk("ekts", [64, 16], F32)
    gtot = mk("gtot", [128, 16], F32)
    gls = mk("gls", [128, 16], F32)
    Dg1 = mk("Dg1", [64, 8, 64], F32)
    Dg2 = mk("Dg2", [64, 8, 64], F32)
    t1, t2, EI, ES = Dg1, Dg2, Dg1, Dg2
    EG = mk("EG", [128, 8, 64], F32)
    qg = mk("qg", [128, 8, 64], BF16)
    qkT = mk("qkT", [64, 8, 64], BF16)
    Pm = [mk("Pm%d" % j, [64, 8, 64], BF16) for j in range(2)]
    PT = [mk("PT%d" % j, [64, 8, 64], BF16) for j in range(2)]
    Rf = mk("Rf", [64, 8, 64], F32)
    Rb = mk("Rb", [64, 8, 64], BF16)
    vb = mk("vb", [64, 8, 128], BF16)
    kbg = mk("kbg", [64, 8, 128], BF16)
    kt = mk("kt", [64, 8, 128], BF16)
    nwT = mk("nwT", [128, 8, 64], BF16)
    vn = mk("vn", [64, 8, 128], BF16)
    osb = mk("osb", [128, 8, 64], F32)
    osq = mk("osq", [128, 8, 64], F32)
    orst = osq
    bank = [(K.psum.slots[i][0], K.psum.slots[i][1]) for i in range(8)]
    identI = K.ident[0:64, 0:64]

    def bc(ap, shape):
        return ap.to_broadcast(list(shape))


    if STACKED:
        cS = P.sb("cBst_sb", [128, 256], F32)
        cS_b = Buf("cBst")
        P.dma("sp", K.cl, cS[:], io["cB2"], writes=[cS_b])
        U_st, NEGI_st, NEGS_st, I_st = cS[:, 0:64], cS[:, 64:128], cS[:, 128:192], cS[:, 192:256]
        blk1m = P.sb("blk1", [128, 128], F32)
        sel = [P.sb("sel%d" % i, [128, 128], F32) for i in range(2)]
        cm_b = Buf("cmats")
        P.op("dve", E("memset", ap=blk1m[:], constant=0.0), writes=[cm_b])
        P.op("dve", E("memset", ap=blk1m[0:64, 0:64], constant=1.0), writes=[cm_b])
        P.op("dve", E("memset", ap=blk1m[64:128, 64:128], constant=1.0), writes=[cm_b])
        for i in range(2):
            P.op("dve", E("memset", ap=sel[i][:], constant=0.0), writes=[cm_b])
            P.op("dve", E("memset", ap=sel[i][i * 64:(i + 1) * 64, :], constant=1.0), writes=[cm_b])
        dtb_st = P.sb("dtb_st", [128, 8], F32)
        nea_st = P.sb("nea_st", [128, 8], F32)
        st_b = Buf("stconst")
        for i in range(2):
            ps_ = slice(i * 64, (i + 1) * 64)
            P.op("act", E("activation", out=dtb_st[ps_, :], in_=row[ps_, 16 + i * 8:24 + i * 8], func=AF.Copy),
                 reads=[row_b], writes=[st_b])
            P.op("act", E("activation", out=nea_st[ps_, :], in_=row[ps_, i * 8:(i + 1) * 8], func=AF.Exp),
                 reads=[row_b], writes=[st_b])
        P.op("dve", E("tensor_scalar", out=nea_st[:], in0=nea_st[:], scalar1=-1.0, scalar2=None, op0=ALU.mult),
             reads=[st_b], writes=[st_b])
        betaS = P.sb("betaS", [128, NCH, 8], F32)
        lbS = P.sb("lbS", [128, NCH, 8], F32)
        gS = P.sb("gS", [128, NCH, 8], F32)
        smS_b = Buf("smallS")

        def one(name, shape, dt):
            return P.sb(name, shape, dt), Buf(name)
        gcS, gcS_b = one("gcS", [128, 8], F32)
        gbS, gbS_b = one("gbS", [128, 8], F32)
        egS, egS_b = one("egS", [128, 8], F32)
        BEG = [one("begS%d" % i, [128, 8], F32) for i in range(2)]
        EKT = [one("ektS%d" % i, [128, 8], F32) for i in range(2)]
        GLB = [one("glS%d" % i, [128, 16], F32) for i in range(2)]
        QKB = [one("QKs%d" % i, [128, 8, 64], BF16) for i in range(2)]
        P0B = [one("P0s%d" % i, [128, 8, 64], BF16) for i in range(2)]
        QGS = [[one("QGs%d_%d" % (i, h), [128, 8, 64], BF16) for h in range(2)] for i in range(2)]
        gtS, gtS_b = one("gtS", [128, 8], F32)
        D1, D1_b = one("D1", [128, 8, 64], F32)
        D2, D2_b = one("D2", [128, 8, 64], F32)
        EGs = [one("EGs%d" % i, [128, 8, 64], F32) for i in range(2)]
        PmS = [one("PmS%d" % i, [128, 8, 64], BF16) for i in range(2)]
        PTS = [one("PTS%d" % i, [128, 8, 64], BF16) for i in range(2)]
        RfS, RfS_b = one("RfS", [128, 8, 64], F32)
        RbS, RbS_b = one("RbS", [128, 8, 64], BF16)
        VB, VB_b = one("VBs", [128, 8, 128], BF16)
        KBG, KBG_b = one("KBGs", [128, 8, 128], BF16)
        KT, KT_b = one("KTs", [128, 8, 128], BF16)
        NW = [one("NWs%d" % i, [128, 8, 64], BF16) for i in range(2)]
        VN, VN_b = one("VNs", [128, 8, 128], BF16)
        OB = [one("OBs%d" % i, [128, 8, 64], F32) for i in range(2)]
        OQ = [one("OQs%d" % i, [128, 8, 64], F32) for i in range(2)]

    def stacked_tail(ti, t0, hn, hn_b):
        HS = (slice(0, 64), slice(64, 128))
        pba, pba_b = bank[0]
        wba4 = wba[:, :, :].rearrange("p k (two h) -> p k two h", two=2)
        for c in range(NCH):
            for hh in range(2):
                mm_group(P, pba[HS[hh], c * 16:(c + 1) * 16], pba_b,
                         [(hn[:, k, c * 64:(c + 1) * 64], wba4[:, k, :, hh * 8:(hh + 1) * 8]) for k in range(NF)],
                         [hn_b, wba_b])
        pv = pba[:, 0:NCH * 16].rearrange("p (c two j) -> p c two j", two=2, j=8)
        P.op("act", E("activation", out=betaS[:, :, :], in_=pv[:, :, 0, :], func=AF.Sigmoid), reads=[pba_b], writes=[smS_b])
        P.op("act", E("activation", out=lbS[:, :, :], in_=betaS[:, :, :], func=AF.Ln), reads=[smS_b], writes=[smS_b])
        P.op("dve", E("tensor_tensor", out=gS[:, :, :], in0=pv[:, :, 1, :], in1=bc(dtb_st[:, :].unsqueeze(1), [128, NCH, 8]),
                      op=ALU.add), reads=[pba_b, st_b], writes=[smS_b])
        P.op("act", E("activation", out=gS[:, :, :], in_=gS[:, :, :], func=AF.Exp), reads=[smS_b], writes=[smS_b])
        P.op("act", E("activation", out=gS[:, :, :], in_=gS[:, :, :], func=AF.Ln, bias=onec[:, 0:1]),
             reads=[smS_b, onec_b], writes=[smS_b])
        P.op("dve", E("tensor_tensor", out=gS[:, :, :], in0=gS[:, :, :], in1=bc(nea_st[:, :].unsqueeze(1), [128, NCH, 8]),
                      op=ALU.mult), reads=[smS_b, st_b], writes=[smS_b])
        ogt, og_b, ol = ogr.next()
        v3 = lambda t: t[:, :].rearrange("p (h c) -> p h c", c=64)
        pair = lambda t: t[:, :, :].rearrange("p (k two) c -> p k two c", two=2)
        pair2 = lambda t: t.rearrange("p (k two) d -> p k two d", two=2)

        def pro(c):
            cp = c % 2
            cs = slice(c * 64, (c + 1) * 64)
            begS, begS_b = BEG[cp]
            ektS, ektS_b = EKT[cp]
            glS, glS_b = GLB[cp]
            QK, QK_b = QKB[cp]
            p0, p0_b = P0B[cp]
            pm, pm_b = bank[0]
            for hh in range(2):
                P.op("pe", E("matmul", out=pm[HS[hh], 256:264], lhsT=U_st[HS[hh], :], rhs=gS[HS[hh], c, :], start=True, stop=True),
                     reads=[cS_b, smS_b], writes=[pm_b])
            P.op("pe", E("matmul", out=pm[:, 264:272], lhsT=blk1m[:, :], rhs=gS[:, c, :], start=True, stop=True),
                 reads=[cm_b, smS_b], writes=[pm_b])
            for hh in range(2):
                P.op("pe", E("matmul", out=pm[:, 272 + hh * 8:280 + hh * 8], lhsT=sel[hh][:, :], rhs=gS[:, c, :], start=True, stop=True),
                     reads=[cm_b, smS_b], writes=[pm_b])
            yield
            P.op("dve", E("tensor_copy", out=gcS[:, :], in_=pm[:, 256:264]), reads=[pm_b], writes=[gcS_b])
            P.op("dve", E("tensor_copy", out=gtS[:, :], in_=pm[:, 264:272]), reads=[pm_b], writes=[gtS_b])
            P.op("act", E("activation", out=glS[:, :], in_=pm[:, 272:288], func=AF.Exp), reads=[pm_b], writes=[glS_b])
            P.op("dve", E("tensor_tensor", out=gbS[:, :], in0=gcS[:, :], in1=lbS[:, c, :], op=ALU.add),
                 reads=[gcS_b, smS_b], writes=[gbS_b])
            P.op("act", E("activation", out=egS[:, :], in_=gcS[:, :], func=AF.Exp), reads=[gcS_b], writes=[egS_b])
            P.op("dve", E("tensor_tensor", out=begS[:, :], in0=egS[:, :], in1=betaS[:, c, :], op=ALU.mult),
                 reads=[egS_b, smS_b], writes=[begS_b])
            P.op("dve", E("tensor_tensor", out=ektS[:, :], in0=gtS[:, :], in1=gcS[:, :], op=ALU.subtract),
                 reads=[gtS_b, gcS_b], writes=[ektS_b])
            P.op("act", E("activation", out=ektS[:, :], in_=ektS[:, :], func=AF.Exp), reads=[ektS_b], writes=[ektS_b])
            P.op("dve", E("tensor_tensor", out=D1[:, :, :], in0=bc(I_st.unsqueeze(1), [128, 8, 64]),
                          in1=bc(gcS[:, :].unsqueeze(2), [128, 8, 64]), op=ALU.mult), reads=[cS_b, gcS_b], writes=[D1_b])
            P.op("pool", E("tensor_tensor", out=D2[:, :, :], in0=bc(I_st.unsqueeze(1), [128, 8, 64]),
                           in1=bc(gbS[:, :].unsqueeze(2), [128, 8, 64]), op=ALU.mult), reads=[cS_b, gbS_b], writes=[D2_b])
            yield
            d1f = D1[:, :, :].rearrange("p h c -> p (h c)")
            d2f = D2[:, :, :].rearrange("p h c -> p (h c)")
            pG, pG_b = bank[1]
            pGb, pGb_b = bank[2]
            pR, pR_b = bank[3]
            P.op("pe", E("matmul", out=pR[:, :], lhsT=sel[0][:, :], rhs=d1f, start=True, stop=True), reads=[cm_b, D1_b], writes=[pR_b])
            P.op("pe", E("matmul", out=pG[:, :], lhsT=blk1m[:, :], rhs=d1f, start=True, stop=True), reads=[cm_b, D1_b], writes=[pG_b])
            P.op("pe", E("matmul", out=pGb[:, :], lhsT=blk1m[:, :], rhs=d2f, start=True, stop=True), reads=[cm_b, D2_b], writes=[pGb_b])
            yield
            P.op("act", E("activation", out=EGs[0][0][:, :, :], in_=v3(pR), func=AF.Exp), reads=[pR_b], writes=[EGs[0][1]])
            P.op("pe", E("matmul", out=pR[:, :], lhsT=sel[1][:, :], rhs=d1f, start=True, stop=True), reads=[cm_b, D1_b], writes=[pR_b])
            gcb = bc(gcS[:, :].unsqueeze(2), [128, 8, 64])
            P.op("dve", E("tensor_tensor", out=D1[:, :, :], in0=v3(pG), in1=gcb, op=ALU.subtract), reads=[pG_b, gcS_b], writes=[D1_b])
            P.op("dve", E("tensor_tensor", out=D2[:, :, :], in0=v3(pGb), in1=gcb, op=ALU.subtract), reads=[pGb_b, gcS_b], writes=[D2_b])
            P.op("act", E("activation", out=EGs[1][0][:, :, :], in_=v3(pR), func=AF.Exp), reads=[pR_b], writes=[EGs[1][1]])
            yield
            P.op("pool", E("tensor_tensor", out=D1[:, :, :], in0=D1[:, :, :], in1=bc(NEGI_st.unsqueeze(1), [128, 8, 64]), op=ALU.add),
                 reads=[D1_b, cS_b], writes=[D1_b])
            P.op("pool", E("tensor_tensor", out=D2[:, :, :], in0=D2[:, :, :], in1=bc(NEGS_st.unsqueeze(1), [128, 8, 64]), op=ALU.add),
                 reads=[D2_b, cS_b], writes=[D2_b])
            P.op("act", E("activation", out=D1[:, :, :], in_=D1[:, :, :], func=AF.Exp), reads=[D1_b], writes=[D1_b])
            P.op("act", E("activation", out=D2[:, :, :], in_=D2[:, :, :], func=AF.Exp), reads=[D2_b], writes=[D2_b])
            for hh in range(2):
                qg_, qg_b = QGS[cp][hh]
                P.op("dve" if hh == 0 else "pool",
                     E("tensor_tensor", out=pair(qg_), in0=bc(qf[:, hh * 4:hh * 4 + 4, cs].unsqueeze(2), [128, 4, 2, 64]),
                       in1=pair(EGs[hh][0]), op=ALU.mult), reads=[qf_b, EGs[hh][1]], writes=[qg_b])
            yield
            pK, pK_b = bank[0]
            for hh in range(2):
                for kh in range(4):
                    kg = hh * 4 + kh
                    P.op("pe", E("matmul", out=pK[HS[hh], kh * 128:kh * 128 + 64], lhsT=kf[:, kg, cs], rhs=kf[:, kg, cs],
                                 start=True, stop=True), reads=[kf_b], writes=[pK_b])
                    P.op("pe", E("matmul", out=pK[HS[hh], kh * 128 + 64:kh * 128 + 128], lhsT=kf[:, kg, cs], rhs=qf[:, kg, cs],
                                 start=True, stop=True), reads=[kf_b, qf_b], writes=[pK_b])
            yield
            pK4 = pK[:, :].rearrange("p (k j c) -> p k j c", j=2, c=64)
            P.op("dve", E("tensor_tensor", out=pair(p0), in0=bc(pK4[:, :, 0, :].unsqueeze(2), [128, 4, 2, 64]), in1=pair(D2), op=ALU.mult),
                 reads=[pK_b, D2_b], writes=[p0_b])
            P.op("dve", E("tensor_tensor", out=pair(QK), in0=bc(pK4[:, :, 1, :].unsqueeze(2), [128, 4, 2, 64]), in1=pair(D1), op=ALU.mult),
                 reads=[pK_b, D1_b], writes=[QK_b])
            P.op("act", E("activation", out=p0[:, :, :], in_=p0[:, :, :], func=AF.Copy, scale=-1.0), reads=[p0_b], writes=[p0_b])
            yield

        def inv(c, filler):
            cp = c % 2
            cs = slice(c * 64, (c + 1) * 64)
            begS, begS_b = BEG[cp]
            ektS, ektS_b = EKT[cp]
            p0, p0_b = P0B[cp]
            pA, pA_b = bank[4]
            pB, pB_b = bank[5]
            pC, pC_b = bank[6]
            pT, pT_b = bank[7]
            pAv = pA[:, :].bitcast(BF16)
            for hh in range(2):
                for j in range(8):
                    P.op("pe", E("transpose", out=pAv[HS[hh], j * 64:(j + 1) * 64], in_=p0[HS[hh], j, :],
                                 identity=identb[HS[hh], hh * 64:(hh + 1) * 64]), reads=[p0_b, identb_b], writes=[pA_b])
            q0, q0_b = PTS[0]
            P.op("act", E("activation", out=q0[:, :, :], in_=pAv[:, 0:512].rearrange("p (h c) -> p h c", c=64), func=AF.Copy),
                 reads=[pA_b], writes=[q0_b])
            P.op("dve", E("tensor_tensor", out=RfS[:, :, :], in0=p0[:, :, :], in1=bc(I_st.unsqueeze(1), [128, 8, 64]), op=ALU.add),
                 reads=[p0_b, cS_b], writes=[RfS_b])
            P.op("act", E("activation", out=RbS[:, :, :], in_=RfS[:, :, :], func=AF.Copy), reads=[RfS_b], writes=[RbS_b])
            pTv = pT[:, :].bitcast(BF16)
            for hh in range(2):
                for j in range(8):
                    P.op("pe", E("transpose", out=pTv[HS[hh], j * 128:(j + 1) * 128], in_=vf[:, hh * 8 + j, cs], identity=identb[:, :]),
                         reads=[vf_b, identb_b], writes=[pT_b])
            P.op("dve", E("tensor_tensor", out=VB[:, :, :], in0=pTv[:, :].rearrange("p (h v) -> p h v", v=128),
                          in1=bc(betaS[:, c, :].unsqueeze(2), [128, 8, 128]), op=ALU.mult), reads=[pT_b, smS_b], writes=[VB_b])
            next(filler, None)
            pcur, pcur_b = p0, p0_b
            pt_, pt_b = PTS[0]
            nxt = 0
            for lvl in range(6):
                pn_, pn_b = PmS[nxt]
                ptn_, ptn_b = PTS[1 - (lvl % 2)]
                if lvl >= 1:
                    for hh in range(2):
                        for j in range(8):
                            P.op("pe", E("matmul", out=pC[HS[hh], j * 64:(j + 1) * 64], lhsT=pt_[HS[hh], j, :], rhs=RbS[HS[hh], j, :],
                                         start=True, stop=True), reads=[pt_b, RbS_b], writes=[pC_b])
                if lvl <= 4:
                    for hh in range(2):
                        for j in range(8):
                            P.op("pe", E("matmul", out=pA[HS[hh], j * 64:(j + 1) * 64], lhsT=pt_[HS[hh], j, :], rhs=pcur[HS[hh], j, :],
                                         start=True, stop=True), reads=[pt_b, pcur_b], writes=[pA_b])
                    for hh in range(2):
                        for j in range(8):
                            P.op("pe", E("matmul", out=pB[HS[hh], j * 64:(j + 1) * 64], lhsT=pcur[HS[hh], j, :], rhs=pt_[HS[hh], j, :],
                                         start=True, stop=True), reads=[pt_b, pcur_b], writes=[pB_b])
                if lvl == 1:
                    pT2v = pT[:, :].bitcast(BF16)
                    for hh in range(2):
                        for kh in range(4):
                            P.op("pe", E("transpose", out=pT2v[HS[hh], kh * 128:(kh + 1) * 128], in_=kf[:, hh * 4 + kh, cs], identity=identb[:, :]),
                                 reads=[kf_b, identb_b], writes=[pT_b])
                    ktm4 = bc(pT2v[:, 0:512].rearrange("p (k d) -> p k d", d=128).unsqueeze(2), [128, 4, 2, 128])
                    P.op("dve", E("tensor_tensor", out=pair2(KBG[:, :, :]), in0=ktm4,
                                  in1=pair2(bc(begS[:, :].unsqueeze(2), [128, 8, 128])), op=ALU.mult), reads=[pT_b, begS_b], writes=[KBG_b])
                    P.op("dve", E("tensor_tensor", out=pair2(KT[:, :, :]), in0=ktm4,
                                  in1=pair2(bc(ektS[:, :].unsqueeze(2), [128, 8, 128])), op=ALU.mult), reads=[pT_b, ektS_b], writes=[KT_b])
                next(filler, None)
                if lvl >= 1:
                    P.op("pool", E("tensor_tensor", out=RbS[:, :, :], in0=RfS[:, :, :], in1=RfS[:, :, :], op=ALU.bypass),
                         reads=[RfS_b], writes=[RbS_b]) if False else None
                    P.op("dve", E("tensor_tensor", out=RfS[:, :, :], in0=v3(pC), in1=RfS[:, :, :], op=ALU.add),
                         reads=[pC_b, RfS_b], writes=[RfS_b])
                    P.op("act", E("activation", out=RbS[:, :, :], in_=RfS[:, :, :], func=AF.Copy), reads=[RfS_b], writes=[RbS_b])
                if lvl <= 4:
                    P.op("act", E("activation", out=pn_[:, :, :], in_=v3(pA), func=AF.Copy), reads=[pA_b], writes=[pn_b])
                    P.op("dve", E("tensor_copy", out=ptn_[:, :, :], in_=v3(pB)), reads=[pB_b], writes=[ptn_b])
                    pcur, pcur_b = pn_, pn_b
                    pt_, pt_b = ptn_, ptn_b
                    nxt = 1 - nxt
                next(filler, None)

        def tail(c):
            cp = c % 2
            cs = slice(c * 64, (c + 1) * 64)
            glS, glS_b = GLB[cp]
            QK, QK_b = QKB[cp]
            for hh in range(2):
                pW, pW_b = bank[4 + hh]
                for j in range(8):
                    P.op("pe", E("matmul", out=pW[:, j * 64:(j + 1) * 64], lhsT=KBG[HS[hh], j, :], rhs=RbS[HS[hh], j, :],
                                 start=True, stop=True), reads=[KBG_b, RbS_b], writes=[pW_b])
                if hh == 0:
                    P.op("act", E("activation", out=NW[hh][0][:, :, :], in_=v3(pW), func=AF.Copy, scale=-1.0),
                         reads=[pW_b], writes=[NW[hh][1]])
                else:
                    P.op("dve", E("tensor_scalar", out=NW[hh][0][:, :, :], in0=v3(pW), scalar1=-1.0, scalar2=None, op0=ALU.mult),
                         reads=[pW_b], writes=[NW[hh][1]])
            for q in range(2):
                pV, pV_b = bank[6 + q]
                for hh in range(2):
                    for j4 in range(4):
                        j = q * 4 + j4
                        P.op("pe", E("matmul", out=pV[HS[hh], j4 * 128:(j4 + 1) * 128], lhsT=RbS[HS[hh], j, :], rhs=VB[HS[hh], j, :],
                                     start=True, stop=False), reads=[RbS_b, VB_b], writes=[pV_b])
                        P.op("pe", E("matmul", out=pV[HS[hh], j4 * 128:(j4 + 1) * 128], lhsT=NW[hh][0][:, j, :], rhs=Sb[:, hh * 8 + j, :],
                                     start=False, stop=True), reads=[NW[hh][1], Sb_b[hh]], writes=[pV_b])
                if q == 0:
                    P.op("act", E("activation", out=VN[:, 0:4, :], in_=pV[:, :].rearrange("p (h v) -> p h v", v=128), func=AF.Copy),
                         reads=[pV_b], writes=[VN_b])
                else:
                    P.op("dve", E("tensor_copy", out=VN[:, 4:8, :], in_=pV[:, :].rearrange("p (h v) -> p h v", v=128)),
                         reads=[pV_b], writes=[VN_b])
            pO = [bank[0], bank[1]]
            for hh in range(2):
                qg_, qg_b = QGS[cp][hh]
                for j in range(8):
                    P.op("pe", E("matmul", out=pO[hh][0][:, j * 64:(j + 1) * 64], lhsT=Sb[:, hh * 8 + j, :], rhs=qg_[:, j, :],
                                 start=True, stop=False), reads=[Sb_b[hh], qg_b], writes=[pO[hh][1]])
                    P.op("pe", E("matmul", out=pO[hh][0][:, j * 64:(j + 1) * 64], lhsT=VN[HS[hh], j, :], rhs=QK[HS[hh], j, :],
                                 start=False, stop=True), reads=[VN_b, QK_b], writes=[pO[hh][1]])
            for hh in range(2):
                for q in range(2):
                    pS, pS_b = bank[2 + q] if hh == 0 else bank[4 + q]
                    for j4 in range(4):
                        j = q * 4 + j4
                        P.op("pe", E("matmul", out=pS[:, j4 * 128:(j4 + 1) * 128], lhsT=KT[HS[hh], j, :], rhs=VN[HS[hh], j, :],
                                     start=True, stop=True), reads=[KT_b, VN_b], writes=[pS_b])
                    h0 = hh * 8 + q * 4
                    sv = S[:, h0:h0 + 4, :]
                    e1 = "dve" if q == 0 else "pool"
                    P.op(e1, E("tensor_tensor", out=sv, in0=sv, in1=bc(glS[:, h0:h0 + 4].unsqueeze(2), [128, 4, 128]), op=ALU.mult),
                         reads=[S_b[hh], glS_b], writes=[S_b[hh]])
                    P.op("dve", E("tensor_tensor", out=sv, in0=pS[:, :].rearrange("p (h v) -> p h v", v=128), in1=sv, op=ALU.add),
                         reads=[pS_b, S_b[hh]], writes=[S_b[hh]])
                    P.op("act", E("activation", out=Sb[:, h0:h0 + 4, :], in_=sv, func=AF.Copy), reads=[S_b[hh]], writes=[Sb_b[hh]])
            for hh in range(2):
                ob, ob_b = OB[hh]
                oq, oq_b = OQ[hh]
                pO3 = v3(pO[hh][0])
                hs = slice(hh * 8, (hh + 1) * 8)
                P.op("dve", E("tensor_copy", out=ob[:, :, :], in_=pO3), reads=[pO[hh][1]], writes=[ob_b])
                P.op("act", E("activation", out=oq[:, :, :], in_=pO3, func=AF.Square), reads=[pO[hh][1]], writes=[oq_b])
                pQ, pQ_b = bank[6 + hh]
                P.op("pe", E("matmul", out=pQ[:, :], lhsT=K.ones[:, :], rhs=oq[:, :, :].rearrange("p h c -> p (h c)"),
                             start=True, stop=True), reads=[K.ones_b, oq_b], writes=[pQ_b])
                rsqrt_op(P, K, oq[:, :, :].rearrange("p h c -> p (h c)"), oq_b, pQ[:, :], pQ_b, 1.0 / 128, EPS)
                P.op("dve", E("scalar_tensor_tensor", out=ob[:, :, :].rearrange("p h c -> p (h c)"),
                              in0=ob[:, :, :].rearrange("p h c -> p (h c)"), scalar=vec[:, 128:129],
                              in1=oq[:, :, :].rearrange("p h c -> p (h c)"), op0=ALU.mult, op1=ALU.mult),
                     reads=[ob_b, oq_b, vec_b], writes=[ob_b])
                P.op("pool", E("tensor_tensor", out=ogt[:, hs, cs], in0=ob[:, :, :], in1=zs[:, hs, cs], op=ALU.mult),
                     reads=[ob_b, zs_b], writes=[og_b])

        for _ in pro(0):
            pass
        for c in range(NCH):
            filler = pro(c + 1) if c + 1 < NCH else iter(())
            inv(c, filler)
            for _ in filler:
                pass
            tail(c)
        if xch is None:
            P.dma("sp", ol, og.rearrange("f p t -> p f t")[:, :, t0:t0 + T], ogt[:, :, :], reads=[og_b])
        else:
            P.dma("sp", ol, xch["og_loc"][ti].rearrange("(f p) t -> p f t", p=128), ogt[:, :, :],
                  reads=[og_b], writes=[xch["og_loc_b"][ti]])
            P.cc_allgather(xch["og_loc"][ti], xch["og_loc_b"][ti], xch["og_all"][ti], xch["og_all_b"][ti], PAIRS)

    for ti in range(ntile):
        t0 = ti * T
        hn, hn_b, hl = hnr.next()
        if xch is None:
            P.dma("sp", hl, hn[:, :, :], hn1g.rearrange("f p t -> p f t")[:, :, t0:t0 + T], writes=[hn_b])
        else:
            pieces = []
            s0 = t0
            while s0 < t0 + T:
                if s0 < 48:
                    s1 = min(48, t0 + T)
                    P.op("dve", E("memset", ap=hn[:, :, s0 - t0:s1 - t0], constant=0.0), writes=[hn_b])
                    s0 = s1
                    continue
                rk, j = (0, s0 - 48) if s0 < 2112 else (1, s0 - 2112 + 16)
                lim = 2112 if s0 < 2112 else TG
                q = max(i for i in range(len(TA_TILES)) if sum(TA_TILES[:i]) <= j)
                cq = j - sum(TA_TILES[:q])
                part, pc0, plim = (0, cq, 512) if cq < 512 else (1, cq - (TA_TILES[q] - 64), 64)
                n = min(t0 + T - s0, lim - s0, plim - pc0)
                xi = 2 * q + part
                src = xch["hn1_all"][xi].rearrange("(r f p) t -> r p f t", r=2, p=128)[rk][:, :, pc0:pc0 + n]
                pieces.append((hn[:, :, s0 - t0:s0 - t0 + n], src, xch["hn1_all_b"][xi]))
                s0 += n
            P.dma_group("sp", hl, [(d_, s_) for d_, s_, _ in pieces], reads=[b_ for _, _, b_ in pieces], writes=[hn_b])
        for blk in range(24):
            wt, wb, wl = K.wring.next()
            wv = wt[:, 0:4096].rearrange("p (k m) -> p k m", k=NF)
            P.dma("pool", wl, wv, gw_v[:, :, blk * 256:(blk + 1) * 256], writes=[wb])
            for mi in range(2):
                mc = blk * 2 + mi
                pb, pbb, _ = K.psum.next()
                mm_group(P, pb[:, :T], pbb, [(wv[:, k, mi * 128:(mi + 1) * 128], hn[:, k, :]) for k in range(NF)],
                         [wb, hn_b])
                if mc >= 32:
                    h = mc - 32
                    P.op("act", E("activation", out=zs[:, h, :], in_=pb[:, :T], func=AF.Silu), reads=[pbb], writes=[zs_b])
                    continue
                pc, pcb, _ = pcr.next()
                ac, acb, _ = accr.next()
                P.op("act", E("activation", out=pc[:, 3:3 + T], in_=pb[:, :T], func=AF.Copy), reads=[pbb], writes=[pcb])
                P.op("act", E("activation", out=pc[:, 0:3], in_=halo[:, mc, :], func=AF.Copy),
                     reads=[halo_b[mc]], writes=[pcb])
                P.op("dve", E("tensor_scalar", out=ac[:, :], in0=pc[:, 3:3 + T], scalar1=vec[:, 96 + mc:97 + mc],
                              scalar2=None, op0=ALU.mult), reads=[pcb, vec_b], writes=[acb])
                for j in range(3):
                    P.op("dve", E("scalar_tensor_tensor", out=ac[:, :], in0=pc[:, j:j + T],
                                  scalar=vec[:, j * 32 + mc:j * 32 + mc + 1], in1=ac[:, :], op0=ALU.mult, op1=ALU.add),
                         reads=[pcb, vec_b, acb], writes=[acb])
                P.op("act", E("activation", out=halo[:, mc, :], in_=pc[:, T:T + 3], func=AF.Copy),
                     reads=[pcb], writes=[halo_b[mc]])
                if mc >= 16:
                    h = mc - 16
                    P.op("act", E("activation", out=vf[:, h, :], in_=ac[:, :], func=AF.Silu), reads=[acb], writes=[vf_b])
                    continue
                sl, slb, _ = silr.next()
                P.op("act", E("activation", out=sl[:, :], in_=ac[:, :], func=AF.Silu), reads=[acb], writes=[slb])
                sq, sqb, _ = K.sq.next()
                P.op("act", E("activation", out=sq[:, :T], in_=sl[:, :], func=AF.Square), reads=[slb], writes=[sqb])
                p2, p2b, _ = K.psum.next()
                P.op("pe", E("matmul", out=p2[:, :T], lhsT=K.ones[:], rhs=sq[:, :T], start=True, stop=True),
                     reads=[sqb, K.ones_b], writes=[p2b])
                ri, rib, _ = rinr.next()
                rsqrt_op(P, K, ri[:, :], rib, p2[:, :T], p2b, 1.0, EPS)
                if mc < 8:
                    P.op("dve", E("scalar_tensor_tensor", out=qf[:, mc, :], in0=sl[:, :], scalar=DK ** -0.5, in1=ri[:, :],
                                  op0=ALU.mult, op1=ALU.mult), reads=[slb, rib], writes=[qf_b])
                else:
                    P.op("dve", E("tensor_tensor", out=kf[:, mc - 8, :], in0=sl[:, :], in1=ri[:, :], op=ALU.mult),
                         reads=[slb, rib], writes=[kf_b])
        if STACKED:
            stacked_tail(ti, t0, hn, hn_b)
            continue
        if DBG == "s2":
            ogt, og_b, ol = ogr.next()
            P.op("dve", E("tensor_copy", out=ogt[:, :, :], in_=vf[:, :, :]), reads=[vf_b, qf_b, kf_b, zs_b], writes=[og_b])
            P.dma("sp", ol, og.rearrange("f p t -> p f t")[:, :, t0:t0 + T], ogt[:, :, :], reads=[og_b])
            continue
        pba, pba_b = bank[0]
        for c in range(NCH):
            mm_group(P, pba[0:64, c * 32:(c + 1) * 32], pba_b,
                     [(hn[:, k, c * 64:(c + 1) * 64], wba[:, k, :]) for k in range(NF)], [hn_b, wba_b])
        pv = pba[0:64, 0:NCH * 32].rearrange("p (c j) -> p c j", j=32)
        P.op("act", E("activation", out=beta[:, :, :], in_=pv[:, :, 0:16], func=AF.Sigmoid), reads=[pba_b], writes=[sm_b])
        P.op("act", E("activation", out=lb[:, :, :], in_=beta[:, :, :], func=AF.Ln), reads=[sm_b], writes=[sm_b])
        P.op("dve", E("tensor_tensor", out=gtm[:, :, :], in0=pv[:, :, 16:32],
                      in1=bc(row[0:64, 16:32].unsqueeze(1), [64, NCH, 16]), op=ALU.add),
             reads=[pba_b, row_b], writes=[sm_b])
        P.op("act", E("activation", out=gtm[:, :, :], in_=gtm[:, :, :], func=AF.Exp), reads=[sm_b], writes=[sm_b])
        P.op("act", E("activation", out=gtm[:, :, :], in_=gtm[:, :, :], func=AF.Ln, bias=onec[0:64, 0:1]),
             reads=[sm_b, onec_b], writes=[sm_b])
        P.op("dve", E("tensor_tensor", out=gtm[:, :, :], in0=gtm[:, :, :],
                      in1=bc(nea[:, :].unsqueeze(1), [64, NCH, 16]), op=ALU.mult), reads=[sm_b, nea_b], writes=[sm_b])
        ogt, og_b, ol = ogr.next()
        if DBG == "s2b":
            P.op("dve", E("tensor_copy", out=ogt[:, :, :], in_=vf[:, :, :]), reads=[vf_b, qf_b, kf_b, zs_b, sm_b], writes=[og_b])
            P.dma("sp", ol, og.rearrange("f p t -> p f t")[:, :, t0:t0 + T], ogt[:, :, :], reads=[og_b])
            continue
        for c in range(NCH):
            cs = slice(c * 64, (c + 1) * 64)
            pm, pm_b = bank[0]
            P.op("pe", E("matmul", out=pm[0:64, 256:272], lhsT=Umat, rhs=gtm[:, c, :], start=True, stop=True),
                 reads=[cB_b, sm_b], writes=[pm_b])
            P.op("pe", E("matmul", out=pm[:, 272:288], lhsT=K.ones[0:64, :], rhs=gtm[:, c, :], start=True, stop=True),
                 reads=[K.ones_b, sm_b], writes=[pm_b])
            gc, gc_b = gcs[0]
            gb, gb_b = gbs[0]
            eg, eg_b = egs[0]
            beg, beg_b = begs[0]
            ekt, ekt_b = ekts[0]
            gt, gt_b = gtot[0]
            gl, gl_b = gls[0]
            P.op("dve", E("tensor_copy", out=gc[:, :], in_=pm[0:64, 256:272]), reads=[pm_b], writes=[gc_b])
            P.op("act", E("activation", out=gt[:, :], in_=pm[:, 272:288], func=AF.Copy), reads=[pm_b], writes=[gt_b])
            P.op("dve", E("tensor_tensor", out=gb[:, :], in0=gc[:, :], in1=lb[:, c, :], op=ALU.add),
                 reads=[gc_b, sm_b], writes=[gb_b])
            P.op("act", E("activation", out=eg[:, :], in_=gc[:, :], func=AF.Exp), reads=[gc_b], writes=[eg_b])
            P.op("dve", E("tensor_tensor", out=beg[:, :], in0=eg[:, :], in1=beta[:, c, :], op=ALU.mult),
                 reads=[eg_b, sm_b], writes=[beg_b])
            P.op("dve", E("tensor_tensor", out=ekt[:, :], in0=gt[0:64, :], in1=gc[:, :], op=ALU.subtract),
                 reads=[gt_b, gc_b], writes=[ekt_b])
            P.op("act", E("activation", out=ekt[:, :], in_=ekt[:, :], func=AF.Exp), reads=[ekt_b], writes=[ekt_b])
            P.op("act", E("activation", out=gl[:, :], in_=gt[:, :], func=AF.Exp), reads=[gt_b], writes=[gl_b])
            for hh in range(2):
                h0 = hh * 8
                k0 = hh * 4
                hs = slice(h0, h0 + 8)
                if DBGSTEP < 2:
                    continue
                d1, d1_b = Dg1[hh]
                d2, d2_b = Dg2[hh]
                P.op("dve", E("tensor_tensor", out=d1[:, :, :], in0=bc(identI.unsqueeze(1), [64, 8, 64]),
                              in1=bc(gc[:, hs].unsqueeze(2), [64, 8, 64]), op=ALU.mult),
                     reads=[K.ident_b, gc_b], writes=[d1_b])
                P.op("dve", E("tensor_tensor", out=d2[:, :, :], in0=bc(identI.unsqueeze(1), [64, 8, 64]),
                              in1=bc(gb[:, hs].unsqueeze(2), [64, 8, 64]), op=ALU.mult),
                     reads=[K.ident_b, gb_b], writes=[d2_b])
                if DBGSTEP < 2.2:
                    continue
                pG, pG_b = bank[1]
                pGb, pGb_b = bank[2]
                P.op("pe", E("matmul", out=pG[:, :], lhsT=K.ones[0:64, :], rhs=d1[:, :, :].rearrange("p h c -> p (h c)"),
                             start=True, stop=True), reads=[K.ones_b, d1_b], writes=[pG_b])
                P.op("pe", E("matmul", out=pGb[0:64, :], lhsT=K.ones[0:64, 0:64], rhs=d2[:, :, :].rearrange("p h c -> p (h c)"),
                             start=True, stop=True), reads=[K.ones_b, d2_b], writes=[pGb_b])
                if DBGSTEP < 2.4:
                    continue
                pG3 = pG[:, :].rearrange("p (h c) -> p h c", c=64)
                pGb3 = pGb[:, :].rearrange("p (h c) -> p h c", c=64)
                a1, a1_b = t1[hh]
                a2, a2_b = t2[hh]
                ei, ei_b = EI[hh]
                es, es_b = ES[hh]
                egr, egr_b = EG[hh]
                gcb = bc(gc[:, hs].unsqueeze(2), [64, 8, 64])
                P.op("dve", E("tensor_tensor", out=a1[:, :, :], in0=pG3[0:64], in1=gcb, op=ALU.subtract),
                     reads=[pG_b, gc_b], writes=[a1_b])
                P.op("dve", E("tensor_tensor", out=a2[:, :, :], in0=pGb3[0:64], in1=gcb, op=ALU.subtract),
                     reads=[pGb_b, gc_b], writes=[a2_b])
                if DBGSTEP < 2.5:
                    continue
                P.op("act", E("activation", out=egr[:, :, :], in_=pG3, func=AF.Exp), reads=[pG_b], writes=[egr_b])
                if DBGSTEP < 2.6:
                    continue
                P.op(POOLENG, E("tensor_tensor", out=a1[:, :, :], in0=a1[:, :, :], in1=bc(NEGI.unsqueeze(1), [64, 8, 64]),
                               op=ALU.add), reads=[a1_b, cB_b], writes=[a1_b])
                P.op(POOLENG, E("tensor_tensor", out=a2[:, :, :], in0=a2[:, :, :], in1=bc(NEGS.unsqueeze(1), [64, 8, 64]),
                               op=ALU.add), reads=[a2_b, cB_b], writes=[a2_b])
                if DBGSTEP < 2.7:
                    continue
                P.op("act", E("activation", out=ei[:, :, :], in_=a1[:, :, :], func=EXPF), reads=[a1_b], writes=[ei_b])
                P.op("act", E("activation", out=es[:, :, :], in_=a2[:, :, :], func=EXPF), reads=[a2_b], writes=[es_b])
                if DBGSTEP < 3:
                    continue
                qgt, qg_b = qg[hh]
                P.op("dve", E("tensor_tensor", out=qgt[:, :, :].rearrange("p (k two) c -> p k two c", two=2),
                              in0=bc(qf[:, k0:k0 + 4, cs].unsqueeze(2), [128, 4, 2, 64]),
                              in1=egr[:, :, :].rearrange("p (k two) c -> p k two c", two=2), op=ALU.mult),
                     reads=[qf_b, egr_b], writes=[qg_b])
                pK, pK_b = bank[3]
                for kh in range(4):
                    P.op("pe", E("matmul", out=pK[0:64, kh * 128:kh * 128 + 64], lhsT=kf[:, k0 + kh, cs], rhs=kf[:, k0 + kh, cs],
                                 start=True, stop=True), reads=[kf_b], writes=[pK_b])
                    P.op("pe", E("matmul", out=pK[0:64, kh * 128 + 64:kh * 128 + 128], lhsT=kf[:, k0 + kh, cs],
                                 rhs=qf[:, k0 + kh, cs], start=True, stop=True), reads=[kf_b, qf_b], writes=[pK_b])
                pK4 = pK[0:64, :].rearrange("p (k j c) -> p k j c", j=2, c=64)
                p0, p0_b = Pm[0][hh]
                qk, qk_b = qkT[hh]
                P.op("dve", E("tensor_tensor", out=p0[:, :, :].rearrange("p (k two) c -> p k two c", two=2),
                              in0=bc(pK4[:, :, 0, :].unsqueeze(2), [64, 4, 2, 64]),
                              in1=es[:, :, :].rearrange("p (k two) c -> p k two c", two=2), op=ALU.mult),
                     reads=[pK_b, es_b], writes=[p0_b])
                P.op("dve", E("tensor_tensor", out=qk[:, :, :].rearrange("p (k two) c -> p k two c", two=2),
                              in0=bc(pK4[:, :, 1, :].unsqueeze(2), [64, 4, 2, 64]),
                              in1=ei[:, :, :].rearrange("p (k two) c -> p k two c", two=2), op=ALU.mult),
                     reads=[pK_b, ei_b], writes=[qk_b])
                P.op(POOLENG, E("tensor_scalar", out=p0[:, :, :], in0=p0[:, :, :], scalar1=-1.0, scalar2=None, op0=ALU.mult),
                     reads=[p0_b], writes=[p0_b])
                if DBGSTEP < 4:
                    continue
                pT, pT_b = bank[7]
                pTv = pT[:, :].bitcast(BF16)
                for h in range(8):
                    P.op("pe", E("transpose", out=pTv[0:64, h * 128:(h + 1) * 128], in_=vf[:, h0 + h, cs], identity=identb[:, :]),
                         reads=[vf_b, identb_b], writes=[pT_b])
                vbt, vb_b = vb[hh]
                P.op("dve", E("tensor_tensor", out=vbt[:, :, :], in0=pTv[0:64, :].rearrange("p (h v) -> p h v", v=128),
                              in1=bc(beta[:, c, hs].unsqueeze(2), [64, 8, 128]), op=ALU.mult),
                     reads=[pT_b, sm_b], writes=[vb_b])
                pT2, pT2_b = bank[6]
                pT2v = pT2[:, :].bitcast(BF16)
                for kh in range(4):
                    P.op("pe", E("transpose", out=pT2v[0:64, kh * 128:(kh + 1) * 128], in_=kf[:, k0 + kh, cs], identity=identb[:, :]),
                         reads=[kf_b, identb_b], writes=[pT2_b])
                ktm4 = bc(pT2v[0:64, 0:512].rearrange("p (k d) -> p k d", d=128).unsqueeze(2), [64, 4, 2, 128])
                kbt, kb_b = kbg[hh]
                ktt, kt_b = kt[hh]
                P.op("dve", E("tensor_tensor", out=kbt[:, :, :].rearrange("p (k two) d -> p k two d", two=2), in0=ktm4,
                              in1=bc(beg[:, hs].unsqueeze(2), [64, 8, 128]).rearrange("p (k two) d -> p k two d", two=2),
                              op=ALU.mult), reads=[pT2_b, beg_b], writes=[kb_b])
                P.op("dve", E("tensor_tensor", out=ktt[:, :, :].rearrange("p (k two) d -> p k two d", two=2), in0=ktm4,
                              in1=bc(ekt[:, hs].unsqueeze(2), [64, 8, 128]).rearrange("p (k two) d -> p k two d", two=2),
                              op=ALU.mult), reads=[pT2_b, ekt_b], writes=[kt_b])
                if DBGSTEP < 5:
                    continue
                pA, pA_b = bank[4]
                pB, pB_b = bank[5]
                pC, pC_b = bank[6]
                pAv = pA[:, :].bitcast(BF16)
                for h in range(8):
                    P.op("pe", E("transpose", out=pAv[0:64, h * 64:(h + 1) * 64], in_=p0[:, h, :], identity=identb[0:64, 0:64]),
                         reads=[p0_b, identb_b], writes=[pA_b])
                q0, q0_b = PT[0][hh]
                P.op("act", E("activation", out=q0[:, :, :], in_=pAv[0:64, 0:512].rearrange("p (h c) -> p h c", c=64), func=AF.Copy),
                     reads=[pA_b], writes=[q0_b])
                rf, rf_b = Rf[hh]
                rb, rb_b = Rb[hh]
                P.op("dve", E("tensor_tensor", out=rf[:, :, :], in0=p0[:, :, :], in1=bc(identI.unsqueeze(1), [64, 8, 64]),
                              op=ALU.add), reads=[p0_b, K.ident_b], writes=[rf_b])
                P.op("act", E("activation", out=rb[:, :, :], in_=rf[:, :, :], func=AF.Copy), reads=[rf_b], writes=[rb_b])
                cur = 0
                for lvl in range(6):
                    pc_, pc_b = Pm[cur][hh]
                    pt_, pt_b = PT[cur][hh]
                    pn_, pn_b = Pm[1 - cur][hh]
                    ptn_, ptn_b = PT[1 - cur][hh]
                    if lvl >= 1:
                        for h in range(8):
                            P.op("pe", E("matmul", out=pC[0:64, h * 64:(h + 1) * 64], lhsT=pt_[:, h, :], rhs=rb[:, h, :],
                                         start=True, stop=True), reads=[pt_b, rb_b], writes=[pC_b])
                    if lvl <= 4:
                        for h in range(8):
                            P.op("pe", E("matmul", out=pA[0:64, h * 64:(h + 1) * 64], lhsT=pt_[:, h, :], rhs=pc_[:, h, :],
                                         start=True, stop=True), reads=[pt_b, pc_b], writes=[pA_b])
                        for h in range(8):
                            P.op("pe", E("matmul", out=pB[0:64, h * 64:(h + 1) * 64], lhsT=pc_[:, h, :], rhs=pt_[:, h, :],
                                         start=True, stop=True), reads=[pt_b, pc_b], writes=[pB_b])
                    if lvl >= 1:
                        P.op("dve", E("tensor_tensor", out=rf[:, :, :], in0=pC[0:64, :].rearrange("p (h c) -> p h c", c=64),
                                      in1=rf[:, :, :], op=ALU.add), reads=[pC_b, rf_b], writes=[rf_b])
                        P.op("act", E("activation", out=rb[:, :, :], in_=rf[:, :, :], func=AF.Copy), reads=[rf_b], writes=[rb_b])
                    if lvl <= 4:
                        P.op("act", E("activation", out=pn_[:, :, :], in_=pA[0:64, :].rearrange("p (h c) -> p h c", c=64),
                                      func=AF.Copy), reads=[pA_b], writes=[pn_b])
                        P.op("dve", E("tensor_copy", out=ptn_[:, :, :], in_=pB[0:64, :].rearrange("p (h c) -> p h c", c=64)),
                             reads=[pB_b], writes=[ptn_b])
                        cur = 1 - cur
                if DBGSTEP < 6:
                    continue
                pW, pW_b = bank[2]
                for h in range(8):
                    P.op("pe", E("matmul", out=pW[:, h * 64:(h + 1) * 64], lhsT=kbt[:, h, :], rhs=rb[:, h, :], start=True, stop=True),
                         reads=[kb_b, rb_b], writes=[pW_b])
                nw, nw_b = nwT[hh]
                P.op("act", E("activation", out=nw[:, :, :], in_=pW[:, :].rearrange("p (h c) -> p h c", c=64), func=AF.Copy,
                              scale=-1.0), reads=[pW_b], writes=[nw_b])
                if DBGSTEP < 7:
                    continue
                vnt, vn_b = vn[hh]
                for q in range(2):
                    pV, pV_b = bank[4 + q]
                    for h4 in range(4):
                        h = q * 4 + h4
                        P.op("pe", E("matmul", out=pV[0:64, h4 * 128:(h4 + 1) * 128], lhsT=rb[:, h, :], rhs=vbt[:, h, :],
                                     start=True, stop=False), reads=[rb_b, vb_b], writes=[pV_b])
                        P.op("pe", E("matmul", out=pV[0:64, h4 * 128:(h4 + 1) * 128], lhsT=nw[:, h, :], rhs=Sb[:, h0 + h, :],
                                     start=False, stop=True), reads=[nw_b, Sb_b[hh]], writes=[pV_b])
                    if q == 0:
                        P.op("act", E("activation", out=vnt[:, 0:4, :], in_=pV[0:64, :].rearrange("p (h v) -> p h v", v=128),
                                      func=AF.Copy), reads=[pV_b], writes=[vn_b])
                    else:
                        P.op("dve", E("tensor_copy", out=vnt[:, 4:8, :], in_=pV[0:64, :].rearrange("p (h v) -> p h v", v=128)),
                             reads=[pV_b], writes=[vn_b])
                if DBGSTEP < 8:
                    continue
                pO, pO_b = bank[1]
                for h in range(8):
                    P.op("pe", E("matmul", out=pO[:, h * 64:(h + 1) * 64], lhsT=Sb[:, h0 + h, :], rhs=qgt[:, h, :],
                                 start=True, stop=False), reads=[Sb_b[hh], qg_b], writes=[pO_b])
                    P.op("pe", E("matmul", out=pO[:, h * 64:(h + 1) * 64], lhsT=vnt[:, h, :], rhs=qk[:, h, :],
                                 start=False, stop=True), reads=[vn_b, qk_b], writes=[pO_b])
                if DBGSTEP < 9:
                    continue
                for q in range(2):
                    pS, pS_b = bank[4 + q]
                    for h4 in range(4):
                        h = q * 4 + h4
                        P.op("pe", E("matmul", out=pS[:, h4 * 128:(h4 + 1) * 128], lhsT=ktt[:, h, :], rhs=vnt[:, h, :],
                                     start=True, stop=True), reads=[kt_b, vn_b], writes=[pS_b])
                    sv = S[:, h0 + q * 4:h0 + q * 4 + 4, :]
                    P.op("dve", E("tensor_tensor", out=sv, in0=sv, in1=bc(gl[:, h0 + q * 4:h0 + q * 4 + 4].unsqueeze(2), [128, 4, 128]),
                                  op=ALU.mult), reads=[S_b[hh], gl_b], writes=[S_b[hh]])
                    P.op("dve", E("tensor_tensor", out=sv, in0=pS[:, :].rearrange("p (h v) -> p h v", v=128), in1=sv, op=ALU.add),
                         reads=[pS_b, S_b[hh]], writes=[S_b[hh]])
                    P.op("act", E("activation", out=Sb[:, h0 + q * 4:h0 + q * 4 + 4, :], in_=sv, func=AF.Copy),
                         reads=[S_b[hh]], writes=[Sb_b[hh]])
                if DBGSTEP < 10:
                    continue
                ob, ob_b = osb[hh]
                oq, oq_b = osq[hh]
                orr, or_b = orst[hh]
                pO3 = pO[:, :].rearrange("p (h c) -> p h c", c=64)
                P.op("dve", E("tensor_copy", out=ob[:, :, :], in_=pO3), reads=[pO_b], writes=[ob_b])
                P.op("act", E("activation", out=oq[:, :, :], in_=pO3, func=AF.Square), reads=[pO_b], writes=[oq_b])
                pQ, pQ_b = bank[3]
                P.op("pe", E("matmul", out=pQ[:, :], lhsT=K.ones[:, :], rhs=oq[:, :, :].rearrange("p h c -> p (h c)"),
                             start=True, stop=True), reads=[K.ones_b, oq_b], writes=[pQ_b])
                rsqrt_op(P, K, orr[:, :, :].rearrange("p h c -> p (h c)"), or_b, pQ[:, :], pQ_b, 1.0 / 128, EPS)
                P.op("dve", E("scalar_tensor_tensor", out=ob[:, :, :].rearrange("p h c -> p (h c)"),
                              in0=ob[:, :, :].rearrange("p h c -> p (h c)"), scalar=vec[:, 128:129],
                              in1=orr[:, :, :].rearrange("p h c -> p (h c)"), op0=ALU.mult, op1=ALU.mult),
                     reads=[ob_b, or_b, vec_b], writes=[ob_b])
                P.op("dve", E("tensor_tensor", out=ogt[:, hs, cs], in0=ob[:, :, :], in1=zs[:, hs, cs], op=ALU.mult),
                     reads=[ob_b, zs_b], writes=[og_b])
        if DBGSTEP < 10:
            P.op("dve", E("tensor_copy", out=ogt[:, :, :], in_=vf[:, :, :]), reads=[vf_b], writes=[og_b])
        if xch is None:
            P.dma("sp", ol, og.rearrange("f p t -> p f t")[:, :, t0:t0 + T], ogt[:, :, :], reads=[og_b])
        else:
            P.dma("sp", ol, xch["og_loc"][ti].rearrange("(f p) t -> p f t", p=128), ogt[:, :, :],
                  reads=[og_b], writes=[xch["og_loc_b"][ti]])
            P.cc_allgather(xch["og_loc"][ti], xch["og_loc_b"][ti], xch["og_all"][ti], xch["og_all_b"][ti], PAIRS)


def build_B(ntile=None):
    nc = bass.Bass("TRN2", target_bir_lowering=False)
    io = {}

    def inp(name, shape, dt=F32):
        io[name] = nc.dram_tensor(name, list(shape), dt, kind="ExternalInput").ap()

    inp("hn1g", [NF, 128, TG], BF16)
    inp("ident", [128, 128])
    inp("vecB", [128, 129])
    inp("rowB", [128, 32])
    inp("cB", [64, 192])
    inp("cB2", [128, 256])
    inp("gw_in", [D, 6176])
    io["og"] = nc.dram_tensor("og", [NF, 128, TG], BF16, kind="ExternalOutput").ap()
    P = Prog(nc)
    K = Common(P, nc, io["ident"])
    phase_B(P, K, io, ntile)
    P.emit()
    P.close()
    return nc


def const_cB():
    t = np.arange(64)
    U = (t[:, None] <= t[None, :]).astype(np.float32)
    negi = np.where(t[None, :] >= t[:, None], 0.0, NEG).astype(np.float32)
    negs = np.where(t[None, :] > t[:, None], 0.0, NEG).astype(np.float32)
    return np.ascontiguousarray(np.concatenate([U, negi, negs], axis=1))


def const_cB2():
    c = const_cB()
    c = np.concatenate([c, np.eye(64, dtype=np.float32)], axis=1)
    return np.ascontiguousarray(np.concatenate([c, c], axis=0))


def host_weights_B(inp, r):
    w = np.asarray(inp["gdn_w_in"], np.float32)[0]
    cw = np.asarray(inp["gdn_conv_w"], np.float32)[0]
    qs = slice(r * 1024, (r + 1) * 1024)
    ks = slice(2048 + r * 1024, 2048 + (r + 1) * 1024)
    vs = slice(4096 + r * 2048, 4096 + (r + 1) * 2048)
    zs = slice(8192 + r * 2048, 8192 + (r + 1) * 2048)
    bs = slice(12288 + r * 16, 12288 + (r + 1) * 16)
    as_ = slice(12320 + r * 16, 12320 + (r + 1) * 16)
    gw = np.ascontiguousarray(np.concatenate([w[:, qs], w[:, ks], w[:, vs], w[:, zs], w[:, bs], w[:, as_]], axis=1))
    cwl = np.concatenate([cw[:, qs], cw[:, ks], cw[:, vs]], axis=1)
    vec = np.concatenate([_fm(cwl[j]) for j in range(4)] + [np.asarray(inp["gdn_norm_w"], np.float32)[0].reshape(128, 1)], axis=1)
    al = np.asarray(inp["gdn_a_log"], np.float32)[0][r * 16:(r + 1) * 16]
    dtb = np.asarray(inp["gdn_dt_bias"], np.float32)[0][r * 16:(r + 1) * 16]
    row = np.ascontiguousarray(np.broadcast_to(np.concatenate([al, dtb])[None, :], (128, 32)))
    return {"gw_in": gw, "vecB": np.ascontiguousarray(vec.astype(np.float32)), "rowB": row.astype(np.float32),
            "cB": const_cB(), "cB2": const_cB2(), "ident": np.eye(128, dtype=np.float32)}


def _run(nc, maps):
    res = run_bass_kernel_spmd(nc, maps, core_ids=list(range(8)))
    return res.results


def _kernel_unfused(**inputs):
    resA = _run(build_A(), host_inputs_A(inputs))
    wB = [host_weights_B(inputs, r) for r in range(2)]
    mapsB = []
    for c in range(8):
        b, r = c // 2, c % 2
        h0 = np.asarray(resA[2 * b]["hn1"])
        h1_ = np.asarray(resA[2 * b + 1]["hn1"])
        seq = np.concatenate([np.zeros((NF, 128, 48), dtype=h0.dtype), h0, h1_[:, :, 16:]], axis=2)
        m = dict(wB[r])
        m["hn1g"] = np.ascontiguousarray(seq)
        mapsB.append(m)
    resB = _run(build_B(), mapsB)
    shared = {
        "ident": np.eye(128, dtype=np.float32),
        "vecC": np.ascontiguousarray(np.concatenate([_fm(inputs["ffn_norm"][1]), _fm(inputs["final_norm"])], axis=1)),
        "gdn_w_out": np.ascontiguousarray(np.asarray(inputs["gdn_w_out"], np.float32)[0]),
        "wg": np.ascontiguousarray(np.asarray(inputs["ffn_w_gate"], np.float32)[1]),
        "wu": np.ascontiguousarray(np.asarray(inputs["ffn_w_up"], np.float32)[1]),
        "wd": np.ascontiguousarray(np.asarray(inputs["ffn_w_down"], np.float32)[1]),
    }
    mapsC = []
    for c in range(8):
        b, r = c // 2, c % 2
        lo = 64 + r * 2048
        ogc = np.concatenate([np.asarray(resB[2 * b]["og"])[:, :, lo:lo + 2048],
                              np.asarray(resB[2 * b + 1]["og"])[:, :, lo:lo + 2048]], axis=0)
        m = dict(shared)
        m["ogc"] = np.ascontiguousarray(ogc)
        m["h1c"] = np.ascontiguousarray(np.asarray(resA[c]["h1"])[:, :, 16:])
        mapsC.append(m)
    resC = _run(build_C(), mapsC)
    out = np.empty((BATCH, SEQ, D), np.float32)
    for c in range(8):
        b, r = c // 2, c % 2
        out[b, r * 2048:(r + 1) * 2048] = np.asarray(resC[c]["out"])
    return out


def build_fused():
    nc = bass.Bass("TRN2", target_bir_lowering=False)
    io = {}

    def inp(name, shape, dt=F32):
        io[name] = nc.dram_tensor(name, list(shape), dt, kind="ExternalInput").ap()

    inp("xa", [TA, D])
    inp("ident", [128, 128])
    inp("vecA", [128, 96])
    inp("sc_w_in", [D, 3 * D])
    inp("sc_w_out", [D, D])
    inp("wg0", [D, DFF])
    inp("wu0", [D, DFF])
    inp("wd0", [DFF, D])
    inp("vecB", [128, 129])
    inp("rowB", [128, 32])
    inp("cB", [64, 192])
    inp("cB2", [128, 256])
    inp("gw_in", [D, 6176])
    inp("vecC", [128, 32])
    inp("mk", [128, 2])
    inp("gdn_w_out", [2 * D, D])
    inp("wg1", [D, DFF])
    inp("wu1", [D, DFF])
    inp("wd1", [DFF, D])
    io["out"] = nc.dram_tensor("out", [TC, D], F32, kind="ExternalOutput").ap()
    nA = len(TA_TILES)
    nB = TG // TB
    h1 = nc.dram_tensor("h1_loc", [NF, 128, TA], F32).ap()
    xch = {
        "h1_b": Buf("h1"),
        "hn1_loc": [nc.dram_tensor("hn1_loc%d" % i, [D, XWS[i % 2]], BF16).ap() for i in range(2 * nA)],
        "hn1_all": [nc.dram_tensor("hn1_all%d" % i, [2 * D, XWS[i % 2]], BF16).ap() for i in range(2 * nA)],
        "og_loc": [nc.dram_tensor("og_loc%d" % i, [D, TB], BF16).ap() for i in range(nB)],
        "og_all": [nc.dram_tensor("og_all%d" % i, [2 * D, TB], BF16).ap() for i in range(nB)],
    }
    for k in ("hn1_loc", "hn1_all", "og_loc", "og_all"):
        xch[k + "_b"] = [Buf("%s%d" % (k, i)) for i in range(len(xch[k]))]
    P = Prog(nc)
    K = Common(P, nc, io["ident"], wslots=4)
    ioA = {"xa": io["xa"], "vecA": io["vecA"], "sc_w_in": io["sc_w_in"], "sc_w_out": io["sc_w_out"],
           "wg": io["wg0"], "wu": io["wu0"], "wd": io["wd0"], "h1": h1}
    phase_A(P, K, ioA, None, xch)
    P.emit(final=False)
    P.close()
    K = Common(P, nc, io["ident"], wslots=5, wsize=4096, ntmp=0)
    ioB = {"vecB": io["vecB"], "rowB": io["rowB"], "cB": io["cB"], "cB2": io["cB2"], "gw_in": io["gw_in"]}
    phase_B(P, K, ioB, None, xch)
    P.emit(final=False)
    P.close()
    K = Common(P, nc, io["ident"], wslots=6)
    ioC = {"h1c": h1, "vecC": io["vecC"], "mk": io["mk"], "gdn_w_out": io["gdn_w_out"],
           "wg": io["wg1"], "wu": io["wu1"], "wd": io["wd1"], "out": io["out"]}
    phase_C(P, K, ioC, 512, xch)
    P.emit(final=True)
    P.finish()
    return nc


def host_inputs_fused(inp):
    mapsA = host_inputs_A(inp)
    wB = [host_weights_B(inp, r) for r in range(2)]
    shared = {
        "vecC": np.ascontiguousarray(np.concatenate([_fm(inp["ffn_norm"][1]), _fm(inp["final_norm"])], axis=1)),
        "gdn_w_out": np.ascontiguousarray(np.asarray(inp["gdn_w_out"], np.float32)[0]),
        "wg1": np.ascontiguousarray(np.asarray(inp["ffn_w_gate"], np.float32)[1]),
        "wu1": np.ascontiguousarray(np.asarray(inp["ffn_w_up"], np.float32)[1]),
        "wd1": np.ascontiguousarray(np.asarray(inp["ffn_w_down"], np.float32)[1]),
    }
    maps = []
    for c in range(8):
        r = c % 2
        a = mapsA[c]
        m = {"xa": a["xa"], "ident": a["ident"], "vecA": a["vecA"], "sc_w_in": a["sc_w_in"], "sc_w_out": a["sc_w_out"],
             "wg0": a["wg"], "wu0": a["wu"], "wd0": a["wd"]}
        for k in ("vecB", "rowB", "cB", "cB2", "gw_in"):
            m[k] = wB[r][k]
        m.update(shared)
        mk = np.zeros((128, 2), np.float32)
        mk[:, r] = 1.0
        m["mk"] = mk
        maps.append(m)
    return maps


def kernel_unfused(**inputs):
    return _kernel_unfused(**inputs)


def kernel(**inputs):
    nc = build_fused()
    res = run_bass_kernel_spmd(nc, host_inputs_fused(inputs), core_ids=list(range(8))).results
    out = np.empty((BATCH, SEQ, D), np.float32)
    for c in range(8):
        b, r = c // 2, c % 2
        out[b, r * 2048:(r + 1) * 2048] = np.asarray(res[c]["out"])
    return out
```
